# Optimizing a Trainium2 kernel written in Bass

```python
import jax
import jax.numpy as jnp
from jax import lax
import numpy as np

D_MODEL = 1024
BATCH = 16
SEQ = 256
DEPTH = 1
DEC_BATCH = 4
DEC_SEQ = 2048
PAST_LEN = 512

GRID_W = 64
CHUNK = 64
RET_HEADS = 4
RET_DK = 128
RET_DV = 256
DN_HEADS = 4
DN_DK = 128
DN_DV = 256
CONV_WIDTH = 3
ROPE_BASE = 10000.0
EPS = 1e-6
D_FF = (((8 * D_MODEL + 2) // 3) + 255) // 256 * 256
RET_QK = RET_HEADS * RET_DK
RET_VW = RET_HEADS * RET_DV
DN_QK = DN_HEADS * DN_DK
DN_VW = DN_HEADS * DN_DV
DN_CONV_CH = 2 * DN_QK + DN_VW
IN_WIDTHS = (RET_QK, RET_QK, RET_VW, RET_VW, DN_QK, DN_QK, DN_VW, DN_VW, 2 * DN_HEADS, 2 * DN_HEADS, D_MODEL, D_MODEL)
IN_COLS = 2 * RET_QK + 2 * RET_VW + 2 * DN_QK + 2 * DN_VW + 4 * DN_HEADS + 2 * D_MODEL

kernel_name = 'hybrid_retention_gdn_diffusion_step'


def _split_cols(t, widths):
    idx, acc = [], 0
    for w in widths[:-1]:
        acc += w
        idx.append(acc)
    return jnp.split(t, idx, axis=-1)


def _rmsnorm(x, w):
    xf = x.astype(jnp.float32)
    y = xf * lax.rsqrt(jnp.mean(xf * xf, axis=-1, keepdims=True) + EPS)
    return (y * w.astype(jnp.float32)).astype(x.dtype)


def _l2norm(x):
    xf = x.astype(jnp.float32)
    return xf * lax.rsqrt(jnp.sum(xf * xf, axis=-1, keepdims=True) + EPS)


def _flip(t):
    return t[:, ::-1]


def _to_chunks(t):
    B, L = t.shape[:2]
    t = t.reshape((B, L // CHUNK, CHUNK) + t.shape[2:])
    perm = (1, 0, 3, 2) + tuple(range(4, t.ndim))
    return jnp.transpose(t, perm)


def _from_chunks(t):
    n, B, H, C, d = t.shape
    return jnp.transpose(t, (1, 0, 3, 2, 4)).reshape(B, n * C, H, d)


def _grid_rope(L):
    rows = L // GRID_W
    row_idx = jnp.repeat(jnp.arange(rows, dtype=jnp.float32), GRID_W)
    col_idx = (jnp.arange(L) % GRID_W).astype(jnp.float32)
    n_freq = RET_DK // 4
    freqs = ROPE_BASE ** (-jnp.arange(n_freq, dtype=jnp.float32) / n_freq)
    ang = jnp.concatenate([row_idx[:, None] * freqs, col_idx[:, None] * freqs], axis=-1)
    return jnp.cos(ang), jnp.sin(ang)


def _apply_rope(x, cos, sin):
    xp = x.astype(jnp.float32).reshape(x.shape[:-1] + (x.shape[-1] // 2, 2))
    x0, x1 = xp[..., 0], xp[..., 1]
    c, s = cos[None, :, None, :], sin[None, :, None, :]
    out = jnp.stack([x0 * c - x1 * s, x0 * s + x1 * c], axis=-1)
    return out.reshape(x.shape).astype(x.dtype)


def _short_conv(x, w):
    C = x.shape[-1]
    pad = CONV_WIDTH // 2
    return lax.conv_general_dilated(x, w[:, None, :].astype(x.dtype), window_strides=(1,), padding=[(pad, pad)], dimension_numbers=('NWC', 'WIO', 'NWC'), feature_group_count=C)


def _retention_chunked(q, k, v, log_gamma, s0):
    qc, kc, vc = (_to_chunks(t.astype(jnp.float32)) for t in (q, k, v))
    pos = jnp.arange(CHUNK, dtype=jnp.float32)
    diff = pos[:, None] - pos[None, :]
    lower = diff >= 0
    lg = log_gamma.astype(jnp.float32)[:, None, None]
    intra = jnp.where(lower, jnp.exp(lg * jnp.where(lower, diff, 0.0)), 0.0)
    q_dec = jnp.exp(lg[:, :, 0] * (pos + 1.0))
    k_dec = jnp.exp(lg[:, :, 0] * (CHUNK - 1.0 - pos))
    c_dec = jnp.exp(lg[:, 0, 0] * CHUNK)

    def step(S, inp):
        qi, ki, vi = inp
        scores = jnp.einsum('bhid,bhjd->bhij', qi, ki) * intra
        o = jnp.einsum('bhij,bhjv->bhiv', scores, vi) + jnp.einsum('bhid,bhdv->bhiv', qi, S) * q_dec[..., None]
        S = S * c_dec[:, None, None] + jnp.einsum('bhjd,bhjv->bhdv', ki * k_dec[..., None], vi)
        return S, o

    S, oc = lax.scan(step, s0.astype(jnp.float32), (qc, kc, vc))
    return _from_chunks(oc), S


def _gated_delta_chunked(q, k, v, beta, log_alpha, s0):
    qc, kc, vc = (_to_chunks(t.astype(jnp.float32)) for t in (q, k, v))
    bc, ac = _to_chunks(beta.astype(jnp.float32)), _to_chunks(log_alpha.astype(jnp.float32))
    tril = jnp.tril(jnp.ones((CHUNK, CHUNK), dtype=bool))
    strict = jnp.tril(jnp.ones((CHUNK, CHUNK), dtype=bool), -1)
    eye = jnp.eye(CHUNK, dtype=jnp.float32)

    def step(S, inp):
        qi, ki, vi, bi, ai = inp
        g = jnp.cumsum(ai, axis=-1)
        decay = jnp.exp(jnp.where(tril, g[..., :, None] - g[..., None, :], -jnp.inf))
        kb = ki * bi[..., None]
        a_mat = eye + jnp.where(strict, jnp.einsum('bhid,bhjd->bhij', kb, ki) * decay, 0.0)
        u = lax.linalg.triangular_solve(a_mat, vi * bi[..., None], left_side=True, lower=True, unit_diagonal=True)
        w = lax.linalg.triangular_solve(a_mat, kb * jnp.exp(g)[..., None], left_side=True, lower=True, unit_diagonal=True)
        v_new = u - jnp.einsum('bhcd,bhdv->bhcv', w, S)
        scores = jnp.einsum('bhid,bhjd->bhij', qi, ki) * decay
        o = jnp.einsum('bhcd,bhdv->bhcv', qi * jnp.exp(g)[..., None], S) + jnp.einsum('bhij,bhjv->bhiv', scores, v_new)
        g_last = g[..., -1:]
        S = S * jnp.exp(g_last)[..., None] + jnp.einsum('bhcd,bhcv->bhdv', ki * jnp.exp(g_last - g)[..., None], v_new)
        return S, o

    S, oc = lax.scan(step, s0.astype(jnp.float32), (qc, kc, vc, bc, ac))
    return _from_chunks(oc), S


def _token_mixers(h, p, s_ret0, s_dn0, rope):
    B, L, _ = h.shape
    proj = h @ p['w_in']
    r_q, r_k, r_v, r_g, d_q, d_k, d_v, d_z, d_b, d_a, g_r, g_d = _split_cols(proj, IN_WIDTHS)

    rq = r_q.reshape(B, L, RET_HEADS, RET_DK) * (RET_DK ** -0.5)
    rk = r_k.reshape(B, L, RET_HEADS, RET_DK)
    rv = r_v.reshape(B, L, RET_HEADS, RET_DV)
    if rope is not None:
        rq = _apply_rope(rq, rope[0], rope[1])
        rk = _apply_rope(rk, rope[0], rope[1])
    log_gamma = jax.nn.log_sigmoid(p['ret_decay_logit'].astype(jnp.float32))
    o_f, sr_f = _retention_chunked(rq, rk, rv, log_gamma[0], s_ret0[:, 0])
    o_b, sr_b = _retention_chunked(_flip(rq), _flip(rk), _flip(rv), log_gamma[1], s_ret0[:, 1])
    o_r = o_f + _flip(o_b)
    mu = jnp.mean(o_r, axis=-1, keepdims=True)
    var = jnp.mean(jnp.square(o_r - mu), axis=-1, keepdims=True)
    o_r = (o_r - mu) * lax.rsqrt(var + EPS) * p['ret_gn_w'].astype(jnp.float32).reshape(RET_HEADS, RET_DV)
    y_r = (jax.nn.silu(r_g) * o_r.reshape(B, L, RET_VW).astype(h.dtype)) @ p['w_ret_o']

    qkv = jax.nn.silu(_short_conv(jnp.concatenate([d_q, d_k, d_v], axis=-1), p['conv_w']))
    dq, dk, dv = _split_cols(qkv, (DN_QK, DN_QK, DN_VW))
    dq = _l2norm(dq.reshape(B, L, DN_HEADS, DN_DK)) * (DN_DK ** -0.5)
    dk = _l2norm(dk.reshape(B, L, DN_HEADS, DN_DK))
    dv = dv.reshape(B, L, DN_HEADS, DN_DV)
    beta = jax.nn.sigmoid(d_b.astype(jnp.float32)).reshape(B, L, 2, DN_HEADS)
    log_alpha = -jnp.exp(p['dn_a_log'].astype(jnp.float32)) * jax.nn.softplus(d_a.astype(jnp.float32).reshape(B, L, 2, DN_HEADS) + p['dn_dt_bias'].astype(jnp.float32))
    od_f, sd_f = _gated_delta_chunked(dq, dk, dv, beta[:, :, 0], log_alpha[:, :, 0], s_dn0[:, 0])
    od_b, sd_b = _gated_delta_chunked(_flip(dq), _flip(dk), _flip(dv), _flip(beta[:, :, 1]), _flip(log_alpha[:, :, 1]), s_dn0[:, 1])
    o_d = od_f + _flip(od_b)
    o_d = o_d * lax.rsqrt(jnp.mean(o_d * o_d, axis=-1, keepdims=True) + EPS) * p['dn_norm_w'].astype(jnp.float32)
    o_d = o_d * jax.nn.silu(d_z.astype(jnp.float32).reshape(B, L, DN_HEADS, DN_DV))
    y_d = o_d.reshape(B, L, DN_VW).astype(h.dtype) @ p['w_dn_o']

    merged = jax.nn.sigmoid(g_r) * y_r + jax.nn.sigmoid(g_d) * y_d
    out = merged @ p['w_out']
    return out, jnp.stack([sr_f, sr_b], axis=1), jnp.stack([sd_f, sd_b], axis=1)


def _layer(x, cond, s_ret0, s_dn0, rope, p):
    mod = (jax.nn.silu(cond) @ p['w_mod'] + p['b_mod'])[:, None, :]
    sh1, sc1, g1, sh2, sc2, g2 = jnp.split(mod, 6, axis=-1)
    nw = p['norm_w']
    h = _rmsnorm(x, nw[0]) * (1 + sc1) + sh1
    m, s_ret, s_dn = _token_mixers(h, p, s_ret0, s_dn0, rope)
    x = x + g1 * _rmsnorm(m, nw[1])
    h = _rmsnorm(x, nw[2]) * (1 + sc2) + sh2
    gate, up = jnp.split(h @ p['w_ffn_in'], 2, axis=-1)
    f = (jax.nn.silu(gate) * up) @ p['w_ffn_out']
    x = x + g2 * _rmsnorm(f, nw[3])
    return x, s_ret, s_dn


def setup_inputs(seed: int = 0) -> dict:
    key = jax.random.key(seed)
    ks = jax.random.split(key, 24)
    f32 = jnp.float32

    def nrm(k, shape, scale):
        return jax.random.normal(k, shape, f32) * scale

    heads_exp = 5.0 + jnp.arange(RET_HEADS, dtype=f32)
    ret_logit0 = jnp.log(jnp.exp2(heads_exp) - 1.0)
    dt = jnp.exp(jax.random.uniform(ks[12], (DEPTH, 2, DN_HEADS), f32, np.log(1e-3), np.log(1e-1)))
    return {
        'x_prompt': nrm(ks[0], (BATCH, SEQ, D_MODEL), 1.0),
        'x_sample': nrm(ks[1], (DEC_BATCH, DEC_SEQ, D_MODEL), 1.0),
        'c': nrm(ks[2], (DEC_BATCH, D_MODEL), 1.0),
        'state_ret': nrm(ks[3], (DEC_BATCH, DEPTH, 2, RET_HEADS, RET_DK, RET_DV), 0.1),
        'state_dn': nrm(ks[4], (DEC_BATCH, DEPTH, 2, DN_HEADS, DN_DK, DN_DV), 0.1),
        'c_ctx': nrm(ks[5], (D_MODEL,), 1.0),
        'w_mod': nrm(ks[6], (DEPTH, D_MODEL, 6 * D_MODEL), 0.5 * D_MODEL ** -0.5),
        'b_mod': nrm(ks[7], (DEPTH, 6 * D_MODEL), 0.02),
        'norm_w': 1.0 + nrm(ks[8], (DEPTH, 4, D_MODEL), 0.02),
        'w_in': nrm(ks[9], (DEPTH, D_MODEL, IN_COLS), D_MODEL ** -0.5),
        'conv_w': nrm(ks[10], (DEPTH, CONV_WIDTH, DN_CONV_CH), CONV_WIDTH ** -0.5),
        'ret_decay_logit': ret_logit0 + nrm(ks[11], (DEPTH, 2, RET_HEADS), 0.1),
        'ret_gn_w': 1.0 + nrm(ks[13], (DEPTH, RET_VW), 0.02),
        'dn_a_log': jnp.log(jax.random.uniform(ks[14], (DEPTH, 2, DN_HEADS), f32, 1.0, 16.0)),
        'dn_dt_bias': dt + jnp.log(-jnp.expm1(-dt)),
        'dn_norm_w': 1.0 + nrm(ks[15], (DEPTH, DN_DV), 0.02),
        'w_ret_o': nrm(ks[16], (DEPTH, RET_VW, D_MODEL), RET_VW ** -0.5),
        'w_dn_o': nrm(ks[17], (DEPTH, DN_VW, D_MODEL), DN_VW ** -0.5),
        'w_out': nrm(ks[18], (DEPTH, D_MODEL, D_MODEL), D_MODEL ** -0.5),
        'w_ffn_in': nrm(ks[19], (DEPTH, D_MODEL, 2 * D_FF), D_MODEL ** -0.5),
        'w_ffn_out': nrm(ks[20], (DEPTH, D_FF, D_MODEL), D_FF ** -0.5),
    }


def reference(x_prompt, x_sample, c, state_ret, state_dn, c_ctx, w_mod, b_mod, norm_w, w_in, conv_w, ret_decay_logit, ret_gn_w, dn_a_log, dn_dt_bias, dn_norm_w, w_ret_o, w_dn_o, w_out, w_ffn_in, w_ffn_out):
    rope = _grid_rope(x_sample.shape[1])
    ctx_cond = c_ctx[None, :]
    n_ctx = x_prompt.shape[0]
    zero_ret = jnp.zeros((n_ctx, 2, RET_HEADS, RET_DK, RET_DV), jnp.float32)
    zero_dn = jnp.zeros((n_ctx, 2, DN_HEADS, DN_DK, DN_DV), jnp.float32)
    y_prompt, y_sample = x_prompt, x_sample
    ret_states, dn_states = [], []
    for l in range(DEPTH):
        p = {'w_mod': w_mod[l], 'b_mod': b_mod[l], 'norm_w': norm_w[l], 'w_in': w_in[l], 'conv_w': conv_w[l],
             'ret_decay_logit': ret_decay_logit[l], 'ret_gn_w': ret_gn_w[l], 'dn_a_log': dn_a_log[l],
             'dn_dt_bias': dn_dt_bias[l], 'dn_norm_w': dn_norm_w[l], 'w_ret_o': w_ret_o[l], 'w_dn_o': w_dn_o[l],
             'w_out': w_out[l], 'w_ffn_in': w_ffn_in[l], 'w_ffn_out': w_ffn_out[l]}
        y_prompt, s_ret, s_dn = _layer(y_prompt, ctx_cond, zero_ret, zero_dn, None, p)
        ret_states.append(s_ret)
        dn_states.append(s_dn)
        y_sample, _, _ = _layer(y_sample, c, state_ret[:, l], state_dn[:, l], rope, p)
    new_state_ret = jnp.stack(ret_states, axis=1).astype(x_prompt.dtype)
    new_state_dn = jnp.stack(dn_states, axis=1).astype(x_prompt.dtype)
    return (y_prompt, y_sample, new_state_ret, new_state_dn)
```

```python
import numpy as np
from contextlib import ExitStack
import concourse.bass as bass
import concourse.mybir as mybir
from concourse.bass_utils import run_bass_kernel_spmd

F32 = mybir.dt.float32
F32R = mybir.dt.float32r
BF16 = mybir.dt.bfloat16
AF = mybir.ActivationFunctionType
ALU = mybir.AluOpType

T = 2048
D = 1024
NT = 16
EPS = 1e-6
BIG = 30000.0
DFF = 2816
NFF = 22
DBGR = 0

BLK_R = 0
BLK_D = 4
BLK_GR = 8
BLK_GD = 9
BLK_RO = 10
BLK_DO = 11
BLK_OUT = 12
BLK_FF = 13
NBLK = 19


def _cst_layout():
    names = [("ONES", 128),
             ("BLKM", 128), ("I128", 128), ("TRILO128", 128), ("TRIUP128", 128), ("NEGONES128", 128),
             ("MASKF128", 256), ("MASKB128", 256), ("SMF128", 256), ("SMB128", 256),
             ("DPOS8", 128), ("DNEG8", 128), ("GE8", 128), ("LE8", 128), ("POS1_8", 128), ("POS128M", 128),
             ("KPF", 1), ("KPB", 1), ("EPS", 1), ("EPS128", 1), ("ONE", 1), ("C64", 1), ("ZERO", 1)]
    off = {}
    o = 0
    for n, w in names:
        off[n] = (o, w)
        o += w
    return off, o


CST_OFF, NCST = _cst_layout()


def _build_cst():
    c = np.zeros((128, NCST), np.float32)
    p = np.arange(128)[:, None]
    pp = p % 64

    def put(name, arr):
        if name not in CST_OFF:
            return
        o, w = CST_OFF[name]
        c[:, o:o + w] = arr

    f64 = np.arange(64)[None, :]
    f128 = np.arange(128)[None, :]
    put("I2", (pp == f64))
    put("TRILO", (pp <= f64))
    put("TRIUP", (pp >= f64))
    put("ONES", np.ones((128, 128)))
    put("NEGONES", -np.ones((128, 64)))
    put("SELH0", np.broadcast_to(p < 64, (128, 128)))
    put("SELH1", np.broadcast_to(p >= 64, (128, 128)))
    put("BD1", (p // 64 == f128 // 64))
    put("BDLO", (p // 64 == f128 // 64) & (pp <= f128 % 64))
    put("BDUP", (p // 64 == f128 // 64) & (pp >= f128 % 64))
    ge = (f64 >= pp).astype(np.float32)
    le = (f64 <= pp).astype(np.float32)
    gt = (f64 > pp).astype(np.float32)
    lt = (f64 < pp).astype(np.float32)
    nge = -BIG * (1 - ge)
    nle = -BIG * (1 - le)
    put("MASKF", np.concatenate([nge, nle], 1))
    put("MASKB", np.concatenate([nle, nge], 1))
    put("SMF", np.concatenate([-gt, -lt], 1))
    put("SMB", np.concatenate([-lt, -gt], 1))
    put("DPOS", np.maximum(f64 - pp, 0))
    put("DNEG", np.maximum(pp - f64, 0))
    put("GE", ge)
    put("LE", le)
    put("POS1", np.broadcast_to(f64 + 1.0, (128, 64)))
    put("POS64M", np.broadcast_to(64.0 - f64, (128, 64)))
    put("I128", (p == f128))
    put("TRILO128", (p <= f128))
    put("TRIUP128", (p >= f128))
    put("NEGONES128", -np.ones((128, 128)))
    ge8 = (f128 >= p).astype(np.float32)
    le8 = (f128 <= p).astype(np.float32)
    gt8 = (f128 > p).astype(np.float32)
    lt8 = (f128 < p).astype(np.float32)
    put("BLKM", (f128 // 64 == p // 64))
    put("MASKF128", np.concatenate([-BIG * (1 - ge8), -BIG * (1 - le8)], 1))
    put("MASKB128", np.concatenate([-BIG * (1 - le8), -BIG * (1 - ge8)], 1))
    put("SMF128", np.concatenate([-gt8, -lt8], 1))
    put("SMB128", np.concatenate([-lt8, -gt8], 1))
    put("DPOS8", np.maximum(f128 - p, 0))
    put("DNEG8", np.maximum(p - f128, 0))
    put("GE8", (f128 >= p))
    put("LE8", (f128 <= p))
    put("POS1_8", np.broadcast_to(f128 + 1.0, (128, 128)))
    put("POS128M", np.broadcast_to(128.0 - f128, (128, 128)))
    put("KPF", 127.0 - p)
    put("KPB", p * 1.0)
    put("EPS", np.full((128, 1), EPS))
    put("EPS128", np.full((128, 1), 128 * EPS))
    put("ONE", np.ones((128, 1)))
    put("C64", np.full((128, 1), 64.0))
    return c


class Sched:
    EPOCH = 3000
    NDMA = 24

    def __init__(self, nc, es):
        self.nc = nc
        self.es = es
        self.eng = {"pe": nc.tensor, "dve": nc.vector, "act": nc.scalar, "pool": nc.gpsimd, "sp": nc.sync}
        self.count = {e: 0 for e in self.eng}
        self.sems = {e: [] for e in self.eng}
        self.waited = {e: {} for e in self.eng}
        self.res = {}
        self.pending = {e: ([], []) for e in self.eng}
        self.dsem = [es.enter_context(nc.semaphore("dsem%d" % i)) for i in range(self.NDMA)]
        self.duse = [0] * self.NDMA
        self.qpool = {"sp": list(range(0, 16)), "pool": list(range(16, self.NDMA)), "act": list(range(0, 16))}
        self.qi = {"sp": 0, "pool": 0, "act": 0}
        self.di = 0
        self.semid = {}
        self.all_dma_events = []

    def _sid(self, sem):
        k = id(sem)
        if k not in self.semid:
            self.semid[k] = len(self.semid)
        return self.semid[k]

    def _tl_sem(self, e, epoch):
        while len(self.sems[e]) <= epoch:
            self.sems[e].append(self.es.enter_context(self.nc.semaphore("tl_%s_%d" % (e, len(self.sems[e])))))
        return self.sems[e][epoch]

    def _wait(self, e, ev):
        if ev is None:
            return
        sem, val = ev[0], ev[1]
        sid = self._sid(sem)
        if self.waited[e].get(sid, 0) >= val:
            return
        self.eng[e].wait_ge(sem, val)
        self.waited[e][sid] = val

    def _deps(self, e, reads, writes):
        evs = []
        for k in reads:
            r = self.res.get(k)
            if r is not None and r["w"] is not None:
                evs.append(r["w"])
        for k in writes:
            r = self.res.get(k)
            if r is not None:
                if r["w"] is not None:
                    evs.append(r["w"])
                evs.extend(r["r"].values())
        for ev in evs:
            if ev[0] == "PENDING":
                if ev[1] != e:
                    raise RuntimeError("dependency on pending (no-inc) instruction of %s" % ev[1])
                continue
            self._wait(e, ev)

    def _commit(self, ev, reads, writes, tl):
        for k in writes:
            self.res[k] = {"w": ev, "r": {}}
        for k in reads:
            r = self.res.setdefault(k, {"w": None, "r": {}})
            r["r"][tl] = ev

    def op(self, e, method, reads, writes, inc=True, **kw):
        if e != "pe":
            writes = list(writes) + [k for k in reads if k.startswith("ps")]
        self._deps(e, reads, writes)
        inst = getattr(self.eng[e], method)(**kw)
        if not inc:
            pr, pw = self.pending[e]
            pr.extend(reads)
            pw.extend(writes)
            for k in writes:
                self.res[k] = {"w": ("PENDING", e), "r": {}}
            return inst
        i = self.count[e]
        self.count[e] += 1
        sem = self._tl_sem(e, i // self.EPOCH)
        val = i % self.EPOCH + 1
        inst.then_inc(sem, 1)
        ev = (sem, val)
        pr, pw = self.pending[e]
        self._commit(ev, list(reads) + pr, list(writes) + pw, "tl_" + e + str(i // self.EPOCH))
        self.pending[e] = ([], [])
        return inst

    def dma(self, q, out, in_, reads, writes, **kw):
        self._deps(q, reads, writes)
        pl = self.qpool[q]
        s = pl[self.qi[q] % len(pl)]
        self.qi[q] += 1
        self.di += 1
        sem = self.dsem[s]
        self._wait(q, (sem, 16 * self.duse[s]))
        self.duse[s] += 1
        val = 16 * self.duse[s]
        self.eng[q].dma_start(out=out, in_=in_, **kw).then_inc(sem, 16)
        ev = (sem, val)
        self._commit(ev, reads, writes, "dma%d_%d" % (s, self.duse[s]))
        self.all_dma_events.append(ev)
        if len(self.all_dma_events) > 2 * self.NDMA:
            self.all_dma_events = self.all_dma_events[-self.NDMA:]

    def barrier(self):
        evs = []
        for e in self.eng:
            if self.count[e] > 0:
                i = self.count[e] - 1
                evs.append((self.sems[e][i // self.EPOCH], i % self.EPOCH + 1))
        for s in range(self.NDMA):
            if self.duse[s] > 0:
                evs.append((self.dsem[s], 16 * self.duse[s]))
        for e in self.eng:
            for ev in evs:
                self._wait(e, ev)
        self.res = {}


class _Stop(Exception):
    pass


def build_program(upto=99):
    nc = bass.Bass("TRN2", target_bir_lowering=False)
    try:
        _build_body(nc, upto)
    except _Stop:
        pass
    return nc


def _build_body(nc, upto):

    def din(name, shape, dt=F32):
        return nc.dram_tensor(name, shape, dt, kind="ExternalInput").ap()

    def dout(name, shape, dt=F32):
        return nc.dram_tensor(name, shape, dt, kind="ExternalOutput").ap()

    x_d = din("x", [T, D])
    cond_d = din("cond", [D])
    wmod_d = din("w_mod", [D, 6 * D])
    bmod_d = din("b_mod", [6 * D])
    normw_d = din("norm_w", [4 * D])
    wblk_d = din("wblk", [NBLK, D, D])
    wba_d = din("w_ba", [D, 16])
    wffo_d = din("w_ffo", [DFF, D])
    convw_d = din("conv_w", [3, 2048])
    rdl_d = din("ret_decay_logit", [8])
    gnw_d = din("ret_gn_w", [1024])
    alog_d = din("dn_a_log", [8])
    dtb_d = din("dn_dt_bias", [8])
    dnw_d = din("dn_norm_w", [256])
    sret_d = din("st_ret", [2, 4, 128, 256])
    sdn_d = din("st_dn", [2, 4, 128, 256])
    ropec_d = din("rope_c", [128, T])
    ropes_d = din("rope_s", [128, T])
    flag_d = din("flag", [128, 1])
    cst_d = din("cst", [128, NCST])
    ident_d = din("ident", [128, 128])
    y_d = dout("y", [T, D])
    sret_o = dout("sret_o", [4, 2, 4, 128, 256])
    sdn_o = dout("sdn_o", [4, 2, 4, 128, 256])
    yrg_d = nc.dram_tensor("yrg_scr", [8, 128, T], BF16, kind="Internal").ap()
    modscr_d = nc.dram_tensor("mod_scr", [1, 4 * D], F32, kind="Internal").ap()

    with ExitStack() as es:
        S = Sched(nc, es)

        dbgn = [0]

        def DBG(name, ap, shape, dt, reads):
            d = nc.dram_tensor("dbg_" + name, shape, dt, kind="ExternalOutput").ap()
            S.dma("sp", d, ap, reads, [])

        def STOP(k):
            if upto == k:
                S.barrier()
                print("STOP at", k, "INSTR", dict(S.count), "dma", S.di)
                raise _Stop()

        def MEM(tag):
            pass

        sbn = [0]

        def SB(stack, name, shape, dt):
            sbn[0] += 1
            return stack.enter_context(nc.sbuf_tensor("s%d_%s" % (sbn[0], name), shape, dt))

        psb = [es.enter_context(nc.psum_tensor("psb%d" % i, [128, 512], F32)) for i in range(7)]
        psT = es.enter_context(nc.psum_tensor("psT", [128, 1024], BF16))
        psi = [0]

        held = set()

        def PS():
            while True:
                i = psi[0] % 7
                psi[0] += 1
                if i not in held:
                    return psb[i], "psb%d" % i

        cst = SB(es, "cst", [128, NCST], F32)
        S.dma("sp", cst[:], cst_d[:, :], [], ["cst"])

        def C(name, rows=slice(0, 128), w=None):
            o, ww = CST_OFF[name]
            if w is None:
                w = ww
            return cst[rows, o:o + w]

        identb = SB(es, "identb", [128, 128], BF16)
        S.dma("pool", identb[:], ident_d[:, :], [], ["identb"])
        flag = SB(es, "flag", [128, 1], F32)
        S.dma("sp", flag[:], flag_d[:, :], [], ["flag"])
        hT = SB(es, "hT", [128, 8, T], BF16)
        wbuf = [SB(es, "wbuf%d" % i, [128, 8, D], BF16) for i in range(2)]
        wslot = [0]
        s01 = ExitStack()
        s01.__enter__()
        modA = SB(s01, "modA", [128, 2 * D], F32)
        modb = SB(s01, "modb", [128, 4 * D], F32)

        def load_w(blk):
            i = wslot[0] % 2
            wslot[0] += 1
            S.dma("pool", wbuf[i][:], wblk_d[blk].rearrange("(k p) c -> p k c", p=128), [], ["wbuf%d" % i])
            return wbuf[i], "wbuf%d" % i

        def mm(ps, pk, lhsT, rhs, rk, start, stop, last=None):
            if last is None:
                last = stop
            S.op("pe", "matmul", rk + ["cst"], [pk], inc=last, out=ps, lhsT=lhsT, rhs=rhs, start=start, stop=stop)

        MEM("global")
        s0 = ExitStack()
        s0.__enter__()
        normw = SB(s0, "normw", [128, 4 * D], F32)
        wm = [SB(s0, "wm%d" % i, [128, 8, 512], F32) for i in range(2)]
        cond = SB(s0, "cond", [128, 8], F32)
        csil = SB(s0, "csil", [128, 8], F32)
        cb = SB(s0, "cb", [128, 8, 128], F32)
        S.dma("sp", modA[:], bmod_d[0:2 * D].partition_broadcast(128), [], ["modA"])
        S.dma("sp", modb[:], bmod_d[2 * D:6 * D].partition_broadcast(128), [], ["modb"])
        S.dma("sp", normw[:], normw_d.partition_broadcast(128), [], ["normw"])
        S.dma("sp", cond[:], cond_d.rearrange("(k p) -> p k", p=128), [], ["cond"], allow_slow_non_contiguous=True)
        S.op("act", "activation", ["cond"], ["csil"], out=csil[:], in_=cond[:], func=AF.Silu)
        S.op("dve", "tensor_copy", ["csil"], ["cb"], out=cb[:], in_=csil[:].unsqueeze(2).to_broadcast([128, 8, 128]))

        def mod_cols(cg):
            w_ = wm[cg % 2]
            wk = "wm%d" % (cg % 2)
            S.dma("sp", w_[:], wmod_d[:, cg * 512:(cg + 1) * 512].rearrange("(k p) c -> p k c", p=128), [], [wk])
            ps, pk = PS()
            for kc in range(8):
                mm(ps[:, :], pk, cb[:, kc, :], w_[:, kc, :], ["cb", wk], kc == 0, kc == 7)
            if cg < 4:
                msl = modA[:, cg * 512:(cg + 1) * 512]
                mk = "modA"
            else:
                msl = modb[:, (cg - 4) * 512:(cg - 3) * 512]
                mk = "modb"
            S.op("dve", "tensor_tensor", [pk, mk], [mk], out=msl, in0=msl, in1=ps[:, :], op=ALU.add)

        for cg in range(4):
            mod_cols(cg)
        S.op("dve", "scalar_tensor_tensor", ["modA", "normw"], ["modA"], out=modA[:, 1024:2048], in0=modA[:, 1024:2048],
             scalar=1.0, in1=normw[:, 0:1024], op0=ALU.add, op1=ALU.mult)

        def stage0b():
            for cg in range(4, 12):
                mod_cols(cg)
                yield
            S.op("dve", "tensor_tensor", ["modb", "normw"], ["modb"], out=modb[:, 0:1024], in0=modb[:, 0:1024],
                 in1=normw[:, 1024:2048], op=ALU.mult)
            S.op("dve", "scalar_tensor_tensor", ["modb", "normw"], ["modb"], out=modb[:, 2048:3072], in0=modb[:, 2048:3072],
                 scalar=1.0, in1=normw[:, 2048:3072], op0=ALU.add, op1=ALU.mult)
            S.op("dve", "tensor_tensor", ["modb", "normw"], ["modb"], out=modb[:, 3072:4096], in0=modb[:, 3072:4096],
                 in1=normw[:, 3072:4096], op=ALU.mult)
            S.dma("sp", modscr_d[0:1, :], modb[0:1, :], ["modb"], ["modscr"])
            yield

        B1 = modA[:, 0:1024]
        A1 = modA[:, 1024:2048]

        def interleave(factories, width, extra=None, extras=None):
            extras = list(extras or [])
            active = []
            free = list(range(width))
            it = iter(factories)
            pending = True
            while pending or active:
                while free and pending:
                    try:
                        f = next(it)
                    except StopIteration:
                        pending = False
                        break
                    bi = free.pop(0)
                    active.append((f(bi), bi))
                if extra is not None:
                    try:
                        next(extra)
                    except StopIteration:
                        extra = None
                for eg_ in list(extras):
                    try:
                        next(eg_)
                    except StopIteration:
                        extras.remove(eg_)
                for item in list(active):
                    try:
                        next(item[0])
                    except StopIteration:
                        active.remove(item)
                        free.append(item[1])
            while extra is not None:
                try:
                    next(extra)
                except StopIteration:
                    extra = None
            while extras:
                for eg_ in list(extras):
                    try:
                        next(eg_)
                    except StopIteration:
                        extras.remove(eg_)

        def norm_to_hT_gen(src, srck, tt, A, B, tmp32, tk, hb, hk, small, smk, modk="modb"):
            S.op("act", "activation", [srck], [tk, smk + "0"], out=tmp32[:], in_=src, func=AF.Square, accum_out=small[:, 0:1])
            yield
            S.op("act", "activation", [smk + "0", "cst"], [smk + "1"], out=small[:, 1:2], in_=small[:, 0:1], func=AF.Sqrt,
                 scale=1.0 / D, bias=C("EPS"))
            yield
            S.op("dve", "reciprocal", [smk + "1"], [smk + "2"], out=small[:, 2:3], in_=small[:, 1:2])
            S.op("dve", "scalar_tensor_tensor", [srck, smk + "2", modk], [tk], out=tmp32[:], in0=src, scalar=small[:, 2:3],
                 in1=A, op0=ALU.mult, op1=ALU.mult)
            yield
            S.op("dve", "tensor_tensor", [tk, modk], [hk], out=hb[:], in0=tmp32[:], in1=B, op=ALU.add)
            yield
            for kc in range(8):
                S.op("pe", "transpose", [hk, "identb"], ["psT"], inc=(kc == 7), out=psT[:, kc * 128:(kc + 1) * 128],
                     in_=hb[:, kc * 128:(kc + 1) * 128], identity=identb[:])
            S.op("act", "activation", ["psT"], ["hT"], out=hT[:, :, tt * 128:(tt + 1) * 128],
                 in_=psT[:, :].rearrange("p (k t) -> p k t", k=8), func=AF.Copy)
            yield

        with ExitStack() as s1:
            W1 = 4
            xb = [SB(s1, "xb%d" % i, [128, D], F32) for i in range(W1)]
            tmp32 = [SB(s1, "tmp32_%d" % i, [128, D], F32) for i in range(W1)]
            hb = [SB(s1, "hb%d" % i, [128, D], BF16) for i in range(W1)]
            small = [SB(s1, "small%d" % i, [128, 4], F32) for i in range(W1)]

            def st1(tt):
                def g(bi):
                    xk = "xb%d" % bi
                    S.dma("sp", xb[bi][:], x_d[tt * 128:(tt + 1) * 128, :], [], [xk])
                    yield
                    yield from norm_to_hT_gen(xb[bi][:], xk, tt, A1, B1, tmp32[bi], "tmp%d" % bi, hb[bi], "hb%d" % bi, small[bi], "sm%d_" % bi, "modA")
                return g
            interleave([st1(tt) for tt in range(NT)], W1, extra=stage0b())
            S.barrier()
        s0.__exit__(None, None, None)
        s01.__exit__(None, None, None)
        STOP(1)

        MEM("before mixers")
        with ExitStack() as sB:
            zT = SB(sB, "zT", [128, 8, T], BF16)
            par = SB(sB, "par", [128, 64], F32)
            lg = SB(sB, "lg", [128, 8], F32)
            dnw = SB(sB, "dnw", [128, 256], F32)
            S.dma("sp", dnw[:], dnw_d.partition_broadcast(128), [], ["dnw"])
            STOP(1.5)
            S32 = [[SB(sB, "S32_%d_%d" % (r, i), [128, 256], F32) for i in range(2)] for r in range(2)]
            S16 = [[SB(sB, "S16_%d_%d" % (r, i), [128, 256], BF16) for i in range(2)] for r in range(2)]

            def state_boundary(r, cur, seg, h, out_d):
                k = "S32_%d_%d" % (r, cur)
                k2 = "S32_%d_%d" % (r, 1 - cur)
                if seg < 4:
                    S.dma("sp", out_d[seg, r, h], S32[r][cur][:], [k], [])
                S.op("dve", "tensor_scalar", [k, "flag"], [k2], out=S32[r][1 - cur][:], in0=S32[r][cur][:], scalar1=flag[:, 0:1],
                     scalar2=None, op0=ALU.mult)
                return 1 - cur

            MEM("before sR")
            with ExitStack() as sR:
                cdec = SB(sR, "cdec", [128, 8], F32)
                kdec = SB(sR, "kdec", [128, 8], F32)
                qpat = SB(sR, "qpat", [128, 8, 128], F32)
                MT = SB(sR, "MT", [128, 4, 128], F32)
                S.dma("sp", par[:, 0:8], rdl_d.partition_broadcast(128), [], ["par"])
                S.op("act", "activation", ["par"], ["par1"], out=par[:, 8:16], in_=par[:, 0:8], func=AF.Exp, scale=-1.0)
                S.op("act", "activation", ["par1", "cst"], ["par2"], out=par[:, 16:24], in_=par[:, 8:16], func=AF.Ln, bias=C("ONE"))
                S.op("dve", "tensor_scalar", ["par2"], ["lg"], out=lg[:], in0=par[:, 16:24], scalar1=-1.0, scalar2=None, op0=ALU.mult)
                S.op("act", "activation", ["lg"], ["cdec"], out=cdec[:], in_=lg[:], func=AF.Exp, scale=128.0)
                mtmp = SB(sR, "mtmp", [128, 2, 128], F32)
                for h in range(4):
                    S.op("act", "activation", ["lg", "cst"], ["kdec"], out=kdec[:, h:h + 1], in_=C("KPF"), func=AF.Exp, scale=lg[:, h:h + 1])
                    S.op("act", "activation", ["lg", "cst"], ["kdec"], out=kdec[:, 4 + h:5 + h], in_=C("KPB"), func=AF.Exp, scale=lg[:, 4 + h:5 + h])
                    S.op("act", "activation", ["lg", "cst"], ["qpat"], out=qpat[:, h, :], in_=C("POS1_8"), func=AF.Exp, scale=lg[:, h:h + 1])
                    S.op("act", "activation", ["lg", "cst"], ["qpat"], out=qpat[:, 4 + h, :], in_=C("POS128M"), func=AF.Exp, scale=lg[:, 4 + h:5 + h])
                    S.op("act", "activation", ["lg", "cst"], ["mtmp"], out=mtmp[:, 0, :], in_=C("DPOS8"), func=AF.Exp, scale=lg[:, h:h + 1])
                    S.op("act", "activation", ["lg", "cst"], ["mtmp"], out=mtmp[:, 1, :], in_=C("DNEG8"), func=AF.Exp, scale=lg[:, 4 + h:5 + h])
                    S.op("dve", "tensor_tensor", ["mtmp", "cst"], ["mtmp"], out=mtmp[:, 0, :], in0=mtmp[:, 0, :], in1=C("GE8"), op=ALU.mult)
                    S.op("dve", "tensor_tensor", ["mtmp", "cst"], ["mtmp"], out=mtmp[:, 1, :], in0=mtmp[:, 1, :], in1=C("LE8"), op=ALU.mult)
                    S.op("dve", "tensor_tensor", ["mtmp"], ["MT"], out=MT[:, h, :], in0=mtmp[:, 0, :], in1=mtmp[:, 1, :], op=ALU.add)

                gnw = SB(sR, "gnw", [128, 1024], F32)
                S.dma("sp", gnw[:], gnw_d.partition_broadcast(128), [], ["gnw"])
                ropec = SB(sR, "ropec", [128, T], BF16)
                ropes = SB(sR, "ropes", [128, T], BF16)
                S.dma("pool", ropec[:], ropec_d[:, :], [], ["ropec"])
                S.dma("pool", ropes[:], ropes_d[:, :], [], ["ropes"])
                qT = SB(sR, "qT", [128, T], BF16)
                kT = SB(sR, "kT", [128, T], BF16)
                qfT = SB(sR, "qfT", [128, T], BF16)
                qbT = SB(sR, "qbT", [128, T], BF16)
                kft = SB(sR, "kft", [128, NT, 128], BF16)
                kbt = SB(sR, "kbt", [128, NT, 128], BF16)
                vt = SB(sR, "vt", [128, NT, 256], BF16)
                WB = 5
                rg = [SB(sR, "rg%d" % i, [128, 256], BF16) for i in range(WB)]
                of = SB(sR, "of", [128, NT, 256], BF16)
                ob = SB(sR, "ob", [128, NT, 256], BF16)
                sT = SB(sR, "sT", [128, NT, 128], BF16)
                tA = SB(sR, "tA", [128, 512], F32)
                tB = SB(sR, "tB", [128, 512], F32)
                o32 = [SB(sR, "o32_%d" % i, [128, 256], F32) for i in range(WB)]
                t32 = [SB(sR, "t32_%d" % i, [128, 256], F32) for i in range(WB)]
                zb = [SB(sR, "zb%d" % i, [128, 256], BF16) for i in range(WB)]
                sm = [SB(sR, "sm%d" % i, [128, 8], F32) for i in range(WB)]
                MEM("sR after alloc")
                pendingRB = []
                nextW = load_w(BLK_R)
                for h in range(4):
                    W, wk = nextW
                    def ret_pre(h=h, W=W, wk=wk):
                        for tg in range(4):
                            ts_ = slice(tg * 512, (tg + 1) * 512)
                            for (dst, dk_, c0, scale) in ((qT, "qT", 0, 128.0 ** -0.5), (kT, "kT", 256, 1.0)):
                                psa, pka = PS()
                                for kc in range(8):
                                    mm(psa[:, :], pka, W[:, kc, c0:c0 + 128], hT[:, kc, ts_], [wk, "hT"], kc == 0, kc == 7)
                                psb2, pkb = PS()
                                for kc in range(8):
                                    mm(psb2[:, :], pkb, W[:, kc, c0 + 128:c0 + 256], hT[:, kc, ts_], [wk, "hT"], kc == 0, kc == 7)
                                S.op("dve", "scalar_tensor_tensor", [pka, "ropec"], ["tA"], out=tA[:], in0=psa[:, :], scalar=scale,
                                     in1=ropec[:, ts_], op0=ALU.mult, op1=ALU.mult)
                                S.op("dve", "scalar_tensor_tensor", [pkb, "ropes"], ["tB"], out=tB[:], in0=psb2[:, :], scalar=scale,
                                     in1=ropes[:, ts_], op0=ALU.mult, op1=ALU.mult)
                                S.op("dve", "tensor_tensor", ["tA", "tB"], [dk_], out=dst[:, ts_], in0=tA[:], in1=tB[:], op=ALU.add)
                                yield
                        for tt in range(NT):
                            ps, pk = PS()
                            for kc in range(8):
                                mm(ps[:, 0:256], pk, hT[:, kc, tt * 128:(tt + 1) * 128], W[:, kc, 512:768], [wk, "hT"], kc == 0, kc == 7)
                            S.op("act", "activation", [pk], ["vt"], out=vt[:, tt, :], in_=ps[:, 0:256], func=AF.Copy)
                            if tt % 2 == 1:
                                yield
                        for g4 in range(4):
                            for j in range(4):
                                tt = g4 * 4 + j
                                S.op("pe", "transpose", ["kT", "identb"], ["psT"], inc=(j == 3), out=psT[:, j * 128:(j + 1) * 128],
                                     in_=kT[:, tt * 128:(tt + 1) * 128], identity=identb[:])
                            pv = psT[:, 0:512].rearrange("p (t d) -> p t d", t=4)
                            S.op("dve", "tensor_scalar", ["psT", "kdec"], ["kft"], out=kft[:, g4 * 4:(g4 + 1) * 4, :], in0=pv,
                                 scalar1=kdec[:, h:h + 1], scalar2=None, op0=ALU.mult)
                            S.op("act", "activation", ["psT", "kdec"], ["kbt"], out=kbt[:, g4 * 4:(g4 + 1) * 4, :], in_=pv, func=AF.Copy,
                                 scale=kdec[:, 4 + h:5 + h])
                            yield
                        S.op("dve", "tensor_tensor", ["qT", "qpat"], ["qfT"], out=qfT[:, :].rearrange("p (n i) -> p n i", i=128),
                             in0=qT[:, :].rearrange("p (n i) -> p n i", i=128),
                             in1=qpat[:, h:h + 1, :].to_broadcast([128, 16, 128]), op=ALU.mult)
                        S.op("dve", "tensor_tensor", ["qT", "qpat"], ["qbT"], out=qbT[:, :].rearrange("p (n i) -> p n i", i=128),
                             in0=qT[:, :].rearrange("p (n i) -> p n i", i=128),
                             in1=qpat[:, 4 + h:5 + h, :].to_broadcast([128, 16, 128]), op=ALU.mult)
                        yield
                        for g4 in range(4):
                            ps, pk = PS()
                            for c in range(4):
                                tt = g4 * 4 + c
                                tsl = slice(tt * 128, (tt + 1) * 128)
                                mm(ps[:, c * 128:(c + 1) * 128], pk, kT[:, tsl], qT[:, tsl], ["kT", "qT"], True, True, last=(c == 3))
                            S.op("dve", "tensor_tensor", [pk, "MT"], ["sT"], out=sT[:, g4 * 4:(g4 + 1) * 4, :], in0=ps[:, :].rearrange("p (t i) -> p t i", t=4),
                                 in1=MT[:, h:h + 1, :].to_broadcast([128, 4, 128]), op=ALU.mult)
                            yield
                    interleave([lambda bi: ret_pre()], 1, extras=pendingRB)
                    pendingRB = []
                    nextW = load_w(BLK_R + h + 1) if h < 3 else load_w(BLK_RO)
                    STOP(1.8)
                    cur = [0, 0]
                    c16 = [0, 0]
                    for r in range(2):
                        S.dma("sp", S32[r][0][:], sret_d[r, h], [], ["S32_%d_0" % r])
                        S.op("act", "activation", ["S32_%d_0" % r], ["S16_%d_0" % r], out=S16[r][0][:], in_=S32[r][0][:], func=AF.Copy)
                    for t in range(NT):
                        for r in range(2):
                            tt = t if r == 0 else NT - 1 - t
                            tsl = slice(tt * 128, (tt + 1) * 128)
                            sk = "S32_%d_%d" % (r, cur[r])
                            s16k = "S16_%d_%d" % (r, c16[r])
                            s16n = "S16_%d_%d" % (r, 1 - c16[r])
                            Scur = S16[r][c16[r]]
                            Snew = S16[r][1 - c16[r]]
                            pss, pks = PS()
                            kk = kft if r == 0 else kbt
                            mm(pss[:, 0:256], pks, kk[:, tt, :], vt[:, tt, :], ["kft", "kbt", "vt"], True, True)
                            pso, pko = PS()
                            if r == 0:
                                mm(pso[:, 0:256], pko, sT[:, tt, :], vt[:, tt, :], ["sT", "vt"], True, False)
                                mm(pso[:, 0:256], pko, qfT[:, tsl], Scur[:], ["qfT", s16k], False, True)
                                S.op("act", "activation", [pko], ["of"], out=of[:, tt, :], in_=pso[:, 0:256], func=AF.Copy)
                            else:
                                mm(pso[:, 0:256], pko, qbT[:, tsl], Scur[:], ["qbT", s16k], True, True)
                                S.op("act", "activation", [pko], ["ob"], out=ob[:, tt, :], in_=pso[:, 0:256], func=AF.Copy)
                            bnd = (tt % 2 == 1) if r == 0 else (tt % 2 == 0)
                            if not bnd:
                                S.op("dve", "scalar_tensor_tensor", [pks, sk, "cdec"], [s16n], out=Snew[:], in0=S32[r][cur[r]][:],
                                     scalar=cdec[:, 4 * r + h:4 * r + h + 1], in1=pss[:, 0:256], op0=ALU.mult, op1=ALU.add)
                            S.op("dve", "scalar_tensor_tensor", [pks, sk, "cdec"], [sk], out=S32[r][cur[r]][:], in0=S32[r][cur[r]][:],
                                 scalar=cdec[:, 4 * r + h:4 * r + h + 1], in1=pss[:, 0:256], op0=ALU.mult, op1=ALU.add)
                            if bnd:
                                cur[r] = state_boundary(r, cur[r], tt // 2, h, sret_o)
                                S.op("act", "activation", ["S32_%d_%d" % (r, cur[r])], [s16n], out=Snew[:], in_=S32[r][cur[r]][:], func=AF.Copy)
                            c16[r] = 1 - c16[r]
                    STOP(1.9)
                    def retB(tt, h=h, W=W, wk=wk):
                        def g(bi):
                            o32_, t32_, zb_, sm_, rg_ = o32[bi], t32[bi], zb[bi], sm[bi], rg[bi]
                            q_ = "_%d" % bi
                            ps, pk = PS()
                            for kc in range(8):
                                mm(ps[:, 0:256], pk, hT[:, kc, tt * 128:(tt + 1) * 128], W[:, kc, 768:1024], [wk, "hT"], kc == 0, kc == 7)
                            S.op("act", "activation", [pk], ["rg" + q_], out=rg_[:], in_=ps[:, 0:256], func=AF.Silu)
                            S.op("dve", "scalar_tensor_tensor", ["of", "ob"], ["o32" + q_, "sm0" + q_], out=o32_[:], in0=of[:, tt, :], scalar=1.0,
                                 in1=ob[:, tt, :], op0=ALU.mult, op1=ALU.add, accum_out=sm_[:, 0:1])
                            yield
                            S.op("dve", "tensor_scalar", ["sm0" + q_], ["sm1" + q_], out=sm_[:, 1:2], in0=sm_[:, 0:1], scalar1=-1.0 / 256, scalar2=None, op0=ALU.mult)
                            yield
                            S.op("act", "activation", ["o32" + q_, "sm1" + q_], ["t32" + q_, "sm2" + q_], out=t32_[:], in_=o32_[:], func=AF.Square, bias=sm_[:, 1:2],
                                 accum_out=sm_[:, 2:3])
                            yield
                            S.op("act", "activation", ["sm2" + q_, "cst"], ["sm3" + q_], out=sm_[:, 3:4], in_=sm_[:, 2:3], func=AF.Sqrt, scale=1.0 / 256, bias=C("EPS"))
                            yield
                            S.op("dve", "reciprocal", ["sm3" + q_], ["sm4" + q_], out=sm_[:, 4:5], in_=sm_[:, 3:4])
                            S.op("dve", "scalar_tensor_tensor", ["o32" + q_, "sm1" + q_, "gnw"], ["t32" + q_], out=t32_[:], in0=o32_[:], scalar=sm_[:, 1:2],
                                 in1=gnw[:, h * 256:(h + 1) * 256], op0=ALU.add, op1=ALU.mult)
                            S.op("dve", "scalar_tensor_tensor", ["t32" + q_, "sm4" + q_, "rg" + q_], ["zb" + q_], out=zb_[:], in0=t32_[:], scalar=sm_[:, 4:5],
                                 in1=rg_[:], op0=ALU.mult, op1=ALU.mult)
                            yield
                            for c in range(2):
                                S.op("pe", "transpose", ["zb" + q_, "identb"], ["psT"], inc=(c == 1), out=psT[:, c * 128:(c + 1) * 128],
                                     in_=zb_[:, c * 128:(c + 1) * 128], identity=identb[:])
                            S.op("act", "activation", ["psT"], ["zT"], out=zT[:, 2 * h:2 * h + 2, tt * 128:(tt + 1) * 128],
                                 in_=psT[:, 0:256].rearrange("p (c t) -> p c t", c=2), func=AF.Copy)
                            yield
                        return g
                    def laneRB(tts, bi, retB=retB):
                        for tt in tts:
                            yield from retB(tt)(bi)
                    lanesR = [laneRB(list(range(bi, NT, WB)), bi) for bi in range(WB)]
                    if h < 3:
                        pendingRB = lanesR
                    else:
                        interleave([], 1, extras=lanesR)
                S.barrier()

            def out_gate(blk_o, blk_g, first, mT=None, pre=None):
                with ExitStack() as sg:
                    sig = SB(sg, "sig", [128, 512], F32)
                    yt = [SB(sg, "yt%d" % i, [128, 512], BF16) for i in range(2)]
                    tg32 = SB(sg, "tg32", [128, 512], F32)
                    Wo, wko = pre if pre is not None else load_w(blk_o)
                    Wg, wkg = load_w(blk_g)
                    it = 0
                    for fo in range(8):
                        for tg in range(4):
                            ts_ = slice(tg * 512, (tg + 1) * 512)
                            ps1, pk1 = PS()
                            for kc in range(8):
                                mm(ps1[:, :], pk1, Wo[:, kc, fo * 128:(fo + 1) * 128], zT[:, kc, ts_], [wko, "zT"], kc == 0, kc == 7)
                            ps2, pk2 = PS()
                            for kc in range(8):
                                mm(ps2[:, :], pk2, Wg[:, kc, fo * 128:(fo + 1) * 128], hT[:, kc, ts_], [wkg, "hT"], kc == 0, kc == 7)
                            S.op("act", "activation", [pk2], ["sig"], out=sig[:], in_=ps2[:, :], func=AF.Sigmoid)
                            yk = "yt%d" % (it % 2)
                            y_ = yt[it % 2]
                            it += 1
                            if first:
                                S.op("dve", "tensor_tensor", [pk1, "sig"], [yk], out=y_[:], in0=ps1[:, :], in1=sig[:], op=ALU.mult)
                                S.dma("sp", yrg_d[fo, :, ts_], y_[:], [yk], ["yrg%d_%d" % (fo, tg)])
                            else:
                                S.dma("sp", y_[:], yrg_d[fo, :, ts_], ["yrg%d_%d" % (fo, tg)], [yk])
                                S.op("dve", "tensor_tensor", [pk1, "sig"], ["tg32"], out=tg32[:], in0=ps1[:, :], in1=sig[:], op=ALU.mult)
                                S.op("dve", "tensor_tensor", ["tg32", yk], ["mT"], out=mT[:, fo, ts_], in0=tg32[:], in1=y_[:], op=ALU.add)
                    S.barrier()

            STOP(2)
            out_gate(BLK_RO, BLK_GR, True, pre=nextW)
            STOP(3)

            with ExitStack() as sD:
                bap = SB(sD, "bap", [128, NT, 16], F32)
                beta = SB(sD, "beta", [128, NT, 8], F32)
                la = SB(sD, "la", [128, NT, 8], F32)
                Gtok = SB(sD, "Gtok", [128, NT, 8], F32)
                eg = SB(sD, "eg", [128, NT, 8], F32)
                kh = SB(sD, "kh", [128, NT, 8], F32)
                bw = SB(sD, "bw", [128, NT, 8], F32)
                EGL = SB(sD, "EGL", [128, NT, 8], F32)
                wba = SB(sD, "wba", [128, 8, 16], BF16)
                dpar = SB(sD, "dpar", [128, 32], F32)
                convw = SB(sD, "convw", [128, 3, 16], F32)
                ncw = SB(sD, "ncw", [128, 2, 16], F32)
                S.dma("pool", wba[:], wba_d.rearrange("(k p) c -> p k c", p=128), [], ["wba"])
                S.dma("sp", dpar[:, 0:8], alog_d.partition_broadcast(128), [], ["dpar"])
                S.dma("sp", dpar[:, 8:16], dtb_d.partition_broadcast(128), [], ["dpar"])
                S.dma("sp", convw[:], convw_d.rearrange("t (c p) -> p t c", p=128), [], ["convw"], allow_slow_non_contiguous=True)
                S.op("dve", "tensor_scalar", ["flag"], ["dpar2"], out=dpar[:, 24:25], in0=flag[:, 0:1], scalar1=-1.0, scalar2=None, op0=ALU.add)
                S.op("dve", "tensor_scalar", ["convw", "dpar2"], ["ncw"], out=ncw[:, 0, :], in0=convw[:, 0, :], scalar1=dpar[:, 24:25], scalar2=None, op0=ALU.mult)
                S.op("dve", "tensor_scalar", ["convw", "dpar2"], ["ncw"], out=ncw[:, 1, :], in0=convw[:, 2, :], scalar1=dpar[:, 24:25], scalar2=None, op0=ALU.mult)
                S.op("act", "activation", ["dpar"], ["dpar1"], out=dpar[:, 16:24], in_=dpar[:, 0:8], func=AF.Exp)
                for tt in range(NT):
                    ps, pk = PS()
                    for kc in range(8):
                        mm(ps[:, 0:16], pk, hT[:, kc, tt * 128:(tt + 1) * 128], wba[:, kc, :], ["wba", "hT"], kc == 0, kc == 7)
                    S.op("act", "activation", [pk], ["bap"], out=bap[:, tt, :], in_=ps[:, 0:16], func=AF.Copy)
                S.op("act", "activation", ["bap"], ["beta"], out=beta[:], in_=bap[:, :, 0:8], func=AF.Sigmoid)
                S.op("dve", "tensor_tensor", ["bap", "dpar"], ["la"], out=la[:], in0=bap[:, :, 8:16],
                     in1=dpar[:, 8:16].unsqueeze(1).to_broadcast([128, NT, 8]), op=ALU.add)
                S.op("act", "activation", ["la"], ["la"], out=la[:], in_=la[:], func=AF.Exp)
                S.op("act", "activation", ["la", "cst"], ["la"], out=la[:], in_=la[:], func=AF.Ln, bias=C("ONE"))
                S.op("dve", "scalar_tensor_tensor", ["la", "dpar1"], ["la"], out=la[:], in0=la[:], scalar=-1.0,
                     in1=dpar[:, 16:24].unsqueeze(1).to_broadcast([128, NT, 8]), op0=ALU.mult, op1=ALU.mult)
                ps, pk = PS()
                mm(ps[:, 0:64].rearrange("p (t c) -> p t c", c=4), pk, C("TRILO128"), la[:, :, 0:4], ["la"], True, True, last=False)
                mm(ps[:, 64:128].rearrange("p (t c) -> p t c", c=4), pk, C("TRIUP128"), la[:, :, 4:8], ["la"], True, True)
                S.op("act", "activation", [pk], ["Gtok"], out=Gtok[:, :, 0:4], in_=ps[:, 0:64].rearrange("p (t c) -> p t c", c=4), func=AF.Copy)
                S.op("act", "activation", [pk], ["Gtok"], out=Gtok[:, :, 4:8], in_=ps[:, 64:128].rearrange("p (t c) -> p t c", c=4), func=AF.Copy)
                ps, pk = PS()
                mm(ps[:, 0:128], pk, C("ONES"), la[:].rearrange("p t c -> p (t c)"), ["la"], True, True)
                S.op("dve", "tensor_tensor", [pk, "Gtok"], ["kh"], out=kh[:].rearrange("p t c -> p (t c)"), in0=ps[:, 0:128],
                     in1=Gtok[:].rearrange("p t c -> p (t c)"), op=ALU.subtract)
                S.op("act", "activation", [pk], ["EGL"], out=EGL[:].rearrange("p t c -> p (t c)"), in_=ps[:, 0:128], func=AF.Exp)
                S.op("act", "activation", ["kh"], ["kh"], out=kh[:], in_=kh[:], func=AF.Exp)
                S.op("act", "activation", ["Gtok"], ["eg"], out=eg[:], in_=Gtok[:], func=AF.Exp)
                S.op("dve", "tensor_tensor", ["eg", "beta"], ["bw"], out=bw[:], in0=eg[:], in1=beta[:], op=ALU.mult)

                dqT = SB(sD, "dqT", [128, T], BF16)
                dkT = SB(sD, "dkT", [128, T], BF16)
                ktok = SB(sD, "ktok", [128, NT, 128], BF16)
                vt = SB(sD, "dvt", [128, NT, 256], BF16)
                WBD = 4
                zg = [SB(sD, "zg%d" % i, [128, 256], BF16) for i in range(WBD)]
                odf = SB(sD, "odf", [128, NT, 256], BF16)
                odb = SB(sD, "odb", [128, NT, 256], BF16)
                o32 = [SB(sD, "do32_%d" % i, [128, 256], F32) for i in range(WBD)]
                t32 = [SB(sD, "dt32_%d" % i, [128, 256], F32) for i in range(WBD)]
                zb = [SB(sD, "dzb%d" % i, [128, 256], BF16) for i in range(WBD)]
                sm = [SB(sD, "dsm%d" % i, [128, 8], F32) for i in range(WBD)]
                RING = 3
                NSET = 4

                pendingB = []
                nextW = load_w(BLK_D)
                for h in range(4):
                    W, wk = nextW
                    with ExitStack() as sconv:
                        WC = 2
                        xpre_l = [SB(sconv, "xpre%d" % i, [128, T], BF16) for i in range(WC)]
                        cv_l = [SB(sconv, "cv%d" % i, [128, T], F32) for i in range(WC)]
                        cvr_l = [SB(sconv, "cvr%d" % i, [128, 512], F32R) for i in range(WC)]
                        ysb_l = xpre_l
                        rin_l = [SB(sconv, "rin%d" % i, [128, 512], F32) for i in range(WC)]
                        onesR2 = SB(sconv, "onesR2", [128, 128], F32R)
                        S.op("dve", "tensor_copy", ["cst"], ["onesR2"], out=onesR2[:], in_=C("ONES"))

                        def conv_job(kind, ct, c0, h=h, W=W, wk=wk):
                            def g(bi):
                                xpre, cv, ysb, rin, cvr = xpre_l[bi], cv_l[bi], ysb_l[bi], rin_l[bi], cvr_l[bi]
                                q_ = "_%d" % bi
                                xk, ck, yk, rk_ = "xpre" + q_, "cv" + q_, "xpre" + q_, "rin" + q_
                                for tg in range(4):
                                    ts_ = slice(tg * 512, (tg + 1) * 512)
                                    ps, pk = PS()
                                    for kc in range(8):
                                        mm(ps[:, :], pk, W[:, kc, c0:c0 + 128], hT[:, kc, ts_], [wk, "hT"], kc == 0, kc == 7)
                                    S.op("act", "activation", [pk], [xk], out=xpre[:, ts_], in_=ps[:, :], func=AF.Copy)
                                    yield
                                w0 = convw[:, 0, ct:ct + 1]
                                w1 = convw[:, 1, ct:ct + 1]
                                w2 = convw[:, 2, ct:ct + 1]
                                S.op("dve", "tensor_scalar", [xk, "convw"], [ck], out=cv[:], in0=xpre[:], scalar1=w1, scalar2=None, op0=ALU.mult)
                                S.op("dve", "scalar_tensor_tensor", [xk, "convw", ck], [ck], out=cv[:, 1:T], in0=xpre[:, 0:T - 1], scalar=w0,
                                     in1=cv[:, 1:T], op0=ALU.mult, op1=ALU.add)
                                S.op("dve", "scalar_tensor_tensor", [xk, "convw", ck], [ck], out=cv[:, 0:T - 1], in0=xpre[:, 1:T], scalar=w2,
                                     in1=cv[:, 0:T - 1], op0=ALU.mult, op1=ALU.add)
                                xv = xpre[:, :].rearrange("p (s t) -> p s t", t=256)
                                cvv = cv[:, :].rearrange("p (s t) -> p s t", t=256)
                                S.op("dve", "scalar_tensor_tensor", [xk, "ncw", ck], [ck], out=cvv[:, 1:8, 0:1], in0=xv[:, 0:7, 255:256],
                                     scalar=ncw[:, 0, ct:ct + 1], in1=cvv[:, 1:8, 0:1], op0=ALU.mult, op1=ALU.add)
                                S.op("dve", "scalar_tensor_tensor", [xk, "ncw", ck], [ck], out=cvv[:, 0:7, 255:256], in0=xv[:, 1:8, 0:1],
                                     scalar=ncw[:, 1, ct:ct + 1], in1=cvv[:, 0:7, 255:256], op0=ALU.mult, op1=ALU.add)
                                yield
                                S.op("act", "activation", [ck], [yk], out=ysb[:], in_=cv[:], func=AF.Silu)
                                yield
                                if kind in ("q", "k"):
                                    dst, dk_ = (dqT, "dqT") if kind == "q" else (dkT, "dkT")
                                    epsn, sc = ("EPS128", 128.0) if kind == "q" else ("EPS", 1.0)
                                    for tg in range(4):
                                        ts_ = slice(tg * 512, (tg + 1) * 512)
                                        S.op("act", "activation", [yk], ["cvr" + q_], out=cvr[:], in_=ysb[:, ts_], func=AF.Square)
                                        yield
                                        ps, pk = PS()
                                        mm(ps[:, :], pk, onesR2[:], cvr[:], ["cvr" + q_, "onesR2"], True, True)
                                        S.op("act", "activation", [pk, "cst"], [rk_], out=rin[:], in_=ps[:, :], func=AF.Sqrt, scale=sc, bias=C(epsn))
                                        yield
                                        S.op("dve", "reciprocal", [rk_], [rk_], out=rin[:], in_=rin[:])
                                        S.op("dve", "tensor_tensor", [yk, rk_], [dk_], out=dst[:, ts_], in0=ysb[:, ts_], in1=rin[:], op=ALU.mult)
                                        yield
                                    if kind == "k":
                                        for g4 in range(4):
                                            for j in range(4):
                                                tt = g4 * 4 + j
                                                S.op("pe", "transpose", ["dkT", "identb"], ["psT"], inc=(j == 3), out=psT[:, j * 128:(j + 1) * 128],
                                                     in_=dkT[:, tt * 128:(tt + 1) * 128], identity=identb[:])
                                            S.op("act", "activation", ["psT"], ["ktok"], out=ktok[:, g4 * 4:(g4 + 1) * 4, :],
                                                 in_=psT[:, 0:512].rearrange("p (t d) -> p t d", t=4), func=AF.Copy)
                                            yield
                                else:
                                    c = kind
                                    for g4 in range(4):
                                        for j in range(4):
                                            tt = g4 * 4 + j
                                            S.op("pe", "transpose", [yk, "identb"], ["psT"], inc=(j == 3), out=psT[:, j * 128:(j + 1) * 128],
                                                 in_=ysb[:, tt * 128:(tt + 1) * 128], identity=identb[:])
                                        S.op("act", "activation", ["psT"], ["dvt"], out=vt[:, g4 * 4:(g4 + 1) * 4, c * 128:(c + 1) * 128],
                                             in_=psT[:, 0:512].rearrange("p (t d) -> p t d", t=4), func=AF.Copy)
                                        yield
                            return g
                        interleave([conv_job("q", h, 0), conv_job("k", 4 + h, 128), conv_job(0, 8 + 2 * h, 256), conv_job(1, 9 + 2 * h, 384)], WC,
                                   extras=pendingB)
                        pendingB = []
                        S.barrier()
                    nextW = load_w(BLK_D + h + 1) if h < 3 else load_w(BLK_DO)

                    with ExitStack() as sscan:
                        ring = {}
                        for r in range(2):
                            for i in range(RING):
                                ring[(r, i)] = dict(
                                    sT=SB(sscan, "r_sT%d%d" % (r, i), [128, 128], BF16),
                                    nwT=SB(sscan, "r_nwT%d%d" % (r, i), [128, 128], BF16),
                                    u=SB(sscan, "r_u%d%d" % (r, i), [128, 256], F32),
                                    qgT=SB(sscan, "r_qgT%d%d" % (r, i), [128, 128], BF16),
                                    khat=SB(sscan, "r_kh%d%d" % (r, i), [128, 128], BF16))
                        TS = []
                        for i in range(NSET):
                            TS.append(dict(
                                aTri=SB(sscan, "aTri%d" % i, [128, 128], F32R),
                                DD=SB(sscan, "DD%d" % i, [128, 256], F32),
                                Z=SB(sscan, "Z%d" % i, [128, 384], F32R),
                                Noff=SB(sscan, "Noff%d" % i, [128, 128], F32R),
                                bv=SB(sscan, "bv%d" % i, [128, 256], F32R), kbeg=SB(sscan, "kbeg%d" % i, [128, 128], F32R),
                                egrow=SB(sscan, "egrow%d" % i, [128, 128], F32)))
                        vn = [SB(sscan, "vn%d" % r, [128, 256], BF16) for r in range(2)]
                        identR = SB(sscan, "identR", [128, 128], F32R)
                        S.op("dve", "tensor_copy", ["cst"], ["identR"], out=identR[:], in_=C("I128"))
                        onesR = SB(sscan, "onesR", [128, 128], F32R)
                        negonesR = SB(sscan, "negonesR", [128, 128], F32R)
                        S.op("dve", "tensor_copy", ["cst"], ["onesR"], out=onesR[:], in_=C("ONES"))
                        S.op("dve", "tensor_copy", ["cst"], ["negonesR"], out=negonesR[:], in_=C("NEGONES128"))

                        def prep_gen(k, r, tsi):
                            tt = k if r == 0 else NT - 1 - k
                            c = 4 * r + h
                            X = TS[tsi]
                            q_ = "_%d" % tsi
                            slot = ring[(r, k % RING)]
                            sk = "ring%d%d" % (r, k % RING)
                            tsl = slice(tt * 128, (tt + 1) * 128)
                            tri = C("TRILO128") if r == 0 else C("TRIUP128")
                            aTri, DD, Z = X["aTri"], X["DD"], X["Z"]
                            DS = DD[:, 128:256]
                            bv, kbeg, egrow = X["bv"], X["kbeg"], X["egrow"]
                            S.op("act", "activation", ["la", "cst"], ["aTri" + q_], out=aTri[:], in_=tri, func=AF.Copy, scale=la[:, tt, c:c + 1])
                            S.op("act", "activation", ["ktok", "kh"], [sk + "khat"], out=slot["khat"][:], in_=ktok[:, tt, :], func=AF.Copy, scale=kh[:, tt, c:c + 1])
                            S.op("dve", "tensor_scalar", ["ktok", "bw"], ["kbeg" + q_], out=kbeg[:], in0=ktok[:, tt, :], scalar1=bw[:, tt, c:c + 1], scalar2=None, op0=ALU.mult)
                            S.op("act", "activation", ["dvt", "beta"], ["bv" + q_], out=bv[:], in_=vt[:, tt, :], func=AF.Copy, scale=beta[:, tt, c:c + 1])
                            yield
                            while nhold[0] >= 2:
                                yield
                            nhold[0] += 1
                            psQ, pkQ = PS()
                            held.add(int(pkQ[3:]))
                            psG, pkG = PS()
                            held.add(int(pkG[3:]))
                            mm(psQ[:, 0:128], pkQ, onesR[:], aTri[:], ["aTri" + q_, "onesR"], True, False)
                            mm(psQ[:, 0:128], pkQ, aTri[:], negonesR[:], ["aTri" + q_, "negonesR"], False, True, last=False)
                            mm(psQ[:, 128:256], pkQ, onesR[:], aTri[:], ["aTri" + q_, "onesR"], True, True)
                            mm(psG[:, 0:128], pkG, dkT[:, tsl], dkT[:, tsl], ["dkT"], True, True, last=False)
                            mm(psG[:, 128:256], pkG, dkT[:, tsl], dqT[:, tsl], ["dkT", "dqT"], True, True)
                            mask = C("MASKF128") if r == 0 else C("MASKB128")
                            smask = C("SMF128") if r == 0 else C("SMB128")
                            S.op("dve", "tensor_tensor", [pkQ, "cst"], ["DD" + q_], out=DD[:, 0:128], in0=psQ[:, 0:128], in1=mask[:, 0:128], op=ALU.add)
                            S.op("dve", "scalar_tensor_tensor", [pkQ, "cst"], ["DD" + q_], out=DD[:, 128:256], in0=psQ[:, 0:128], scalar=-1.0,
                                 in1=mask[:, 128:256], op0=ALU.mult, op1=ALU.add)
                            S.op("act", "activation", [pkQ], ["egrow" + q_], out=egrow[:], in_=psQ[:, 128:256], func=AF.Exp)
                            S.op("act", "activation", ["DD" + q_], ["DD" + q_], out=DD[:], in_=DD[:], func=AF.Exp)
                            held.discard(int(pkQ[3:]))
                            yield
                            S.op("dve", "tensor_tensor", ["DD" + q_, "cst"], ["DD" + q_], out=DS, in0=DD[:, 128:256], in1=smask[:, 128:256], op=ALU.mult)
                            S.op("dve", "tensor_tensor", [pkG, "DD" + q_], [sk + "sT"], out=slot["sT"][:], in0=psG[:, 128:256], in1=DD[:, 0:128], op=ALU.mult)
                            S.op("dve", "tensor_tensor", ["dqT", "egrow" + q_], [sk + "qgT"], out=slot["qgT"][:], in0=dqT[:, tsl], in1=egrow[:], op=ALU.mult)
                            kZ = "Z" + q_
                            Noff = X["Noff"][:]
                            S.op("dve", "scalar_tensor_tensor", [pkG, "beta", "DD" + q_], ["egrow" + q_], out=egrow[:], in0=psG[:, 0:128], scalar=beta[:, tt, c:c + 1],
                                 in1=DS, op0=ALU.mult, op1=ALU.mult)
                            S.op("dve", "tensor_tensor", ["egrow" + q_, "cst"], [kZ], out=Z[:, 256:384], in0=egrow[:], in1=C("BLKM"), op=ALU.mult)
                            S.op("dve", "tensor_tensor", ["egrow" + q_, kZ], ["Noff" + q_], out=Noff, in0=egrow[:], in1=Z[:, 256:384], op=ALU.subtract)
                            mm(psG[:, 256:384], pkG, Z[:, 256:384], identR[:], [kZ, "identR"], True, True)
                            S.op("act", "activation", [pkG], [kZ], out=Z[:, 0:128], in_=psG[:, 256:384], func=AF.Copy)
                            S.op("dve", "tensor_tensor", [kZ, "identR"], [kZ], out=Z[:, 128:256], in0=Z[:, 0:128], in1=identR[:], op=ALU.add)
                            held.discard(int(pkG[3:]))
                            nhold[0] -= 1
                            yield
                            for lev in range(1, 7):
                                psA, pkA = PS()
                                if lev == 1:
                                    mm(psA[:, 0:128], pkA, Z[:, 256:384], Z[:, 0:128], [kZ], True, True, last=False)
                                else:
                                    mm(psA[:, 0:256], pkA, Z[:, 256:384], Z[:, 0:256], [kZ], True, True, last=(lev == 6))
                                if lev < 6:
                                    mm(psA[:, 256:384], pkA, Z[:, 0:128], Z[:, 256:384], [kZ], True, True)
                                    S.op("act", "activation", [pkA], [kZ], out=Z[:, :].rearrange("p (b f) -> p b f", b=3)[:, 0:3:2, :],
                                         in_=psA[:, 0:384].rearrange("p (b f) -> p b f", b=3)[:, 0:3:2, :], func=AF.Copy)
                                if lev >= 2:
                                    S.op("dve", "tensor_tensor", [pkA, kZ], [kZ], out=Z[:, 128:256], in0=psA[:, 128:256], in1=Z[:, 128:256], op=ALU.add)
                                yield
                            psM, pkM = PS()
                            mm(psM[:, 0:128], pkM, Noff, Z[:, 128:256], ["Noff" + q_, kZ], True, True, last=False)
                            mm(psM[:, 256:384], pkM, Z[:, 128:256], identR[:], [kZ, "identR"], True, True)
                            S.op("act", "activation", [pkM], [kZ], out=Z[:, :].rearrange("p (b f) -> p b f", b=3)[:, 0:3:2, :],
                                 in_=psM[:, 0:384].rearrange("p (b f) -> p b f", b=3)[:, 0:3:2, :], func=AF.Copy)
                            yield
                            psM2, pkM2 = PS()
                            mm(psM2[:, 0:128], pkM2, Z[:, 256:384], Z[:, 0:128], [kZ], True, True)
                            S.op("dve", "tensor_tensor", [pkM2, kZ], [kZ], out=Z[:, 128:256], in0=psM2[:, 0:128], in1=Z[:, 128:256], op=ALU.add)
                            yield
                            TT_ = Z[:, 128:256]
                            tk = kZ
                            psU, pkU = PS()
                            mm(psU[:, 0:256], pkU, TT_, bv[:], [tk, "bv" + q_], True, True, last=False)
                            mm(psU[:, 256:384], pkU, kbeg[:], TT_, [tk, "kbeg" + q_], True, True)
                            S.op("dve", "tensor_copy", [pkU], [sk + "u"], out=slot["u"][:], in_=psU[:, 0:256])
                            S.op("act", "activation", [pkU], [sk + "nwT"], out=slot["nwT"][:], in_=psU[:, 256:384], func=AF.Copy, scale=-1.0)
                            yield

                        cur = [0, 0]
                        c16 = [0, 0]
                        ready = {}
                        done_tiles = [0, 0]
                        nhold = [0]

                        def chain_gen(r):
                            c = 4 * r + h
                            od = odf if r == 0 else odb
                            odk = "odf" if r == 0 else "odb"
                            vk = "vn%d" % r
                            for k in range(NT):
                                tt = k if r == 0 else NT - 1 - k
                                while not ready.get((k, r)):
                                    yield
                                slot = ring[(r, k % RING)]
                                sk = "ring%d%d" % (r, k % RING)
                                s32k = "S32_%d_%d" % (r, cur[r])
                                s16k = "S16_%d_%d" % (r, c16[r])
                                s16n = "S16_%d_%d" % (r, 1 - c16[r])
                                Scur = S16[r][c16[r]]
                                Snew = S16[r][1 - c16[r]]
                                psV, pkV = PS()
                                mm(psV[:, 0:256], pkV, slot["nwT"][:], Scur[:], [sk + "nwT", s16k], True, True)
                                S.op("dve", "tensor_tensor", [pkV, sk + "u"], [vk], out=vn[r][:], in0=psV[:, 0:256], in1=slot["u"][:], op=ALU.add)
                                yield
                                psO, pkO = PS()
                                psS, pkS = PS()
                                mm(psS[:, 0:256], pkS, slot["khat"][:], vn[r][:], [sk + "khat", vk], True, True)
                                mm(psO[:, 0:256], pkO, slot["qgT"][:], Scur[:], [sk + "qgT", s16k], True, False)
                                mm(psO[:, 0:256], pkO, slot["sT"][:], vn[r][:], [sk + "sT", vk], False, True)
                                bnd = (tt % 2 == 1) if r == 0 else (tt % 2 == 0)
                                S.op("dve", "scalar_tensor_tensor", [pkS, s32k, "EGL"], [s32k], out=S32[r][cur[r]][:], in0=S32[r][cur[r]][:],
                                     scalar=EGL[:, tt, c:c + 1], in1=psS[:, 0:256], op0=ALU.mult, op1=ALU.add)
                                if bnd:
                                    cur[r] = state_boundary(r, cur[r], tt // 2, h, sdn_o)
                                S.op("act", "activation", ["S32_%d_%d" % (r, cur[r])], [s16n], out=Snew[:], in_=S32[r][cur[r]][:], func=AF.Copy)
                                S.op("act", "activation", [pkO], [odk], out=od[:, tt, :], in_=psO[:, 0:256], func=AF.Copy)
                                c16[r] = 1 - c16[r]
                                yield
                                done_tiles[r] += 1

                        for r in range(2):
                            S.dma("sp", S32[r][0][:], sdn_d[r, h], [], ["S32_%d_0" % r])
                            S.op("act", "activation", ["S32_%d_0" % r], ["S16_%d_0" % r], out=S16[r][0][:], in_=S32[r][0][:], func=AF.Copy)
                        chains = [chain_gen(0), chain_gen(1)]
                        chain_alive = [True, True]
                        nextk = [0, 0]
                        free_sets = list(range(NSET))
                        active = []
                        turn = 0
                        while any(chain_alive):
                            launched = True
                            while free_sets and launched:
                                launched = False
                                for rr in ((0, 1) if turn % 2 == 0 else (1, 0)):
                                    if free_sets and nextk[rr] < NT and nextk[rr] < done_tiles[rr] + RING:
                                        tsi = free_sets.pop(0)
                                        active.append((prep_gen(nextk[rr], rr, tsi), nextk[rr], rr, tsi))
                                        nextk[rr] += 1
                                        launched = True
                                        turn += 1
                            def chain_turn():
                                for r in range(2):
                                    if chain_alive[r]:
                                        try:
                                            next(chains[r])
                                        except StopIteration:
                                            chain_alive[r] = False
                            chain_turn()
                            CT = 2
                            for ii, item in enumerate(list(active)):
                                if ii > 0 and ii % CT == 0:
                                    chain_turn()
                                g, k, rr, tsi = item
                                try:
                                    next(g)
                                except StopIteration:
                                    active.remove(item)
                                    free_sets.append(tsi)
                                    ready[(k, rr)] = True
                        S.barrier()
                    def delB(tt, h=h, W=W, wk=wk):
                        def g(bi):
                            o32_, t32_, zb_, sm_, zg_ = o32[bi], t32[bi], zb[bi], sm[bi], zg[bi]
                            q_ = "_%d" % bi
                            ps, pk = PS()
                            for kc in range(8):
                                mm(ps[:, 0:256], pk, hT[:, kc, tt * 128:(tt + 1) * 128], W[:, kc, 512:768], [wk, "hT"], kc == 0, kc == 7)
                            S.op("act", "activation", [pk], ["zg" + q_], out=zg_[:], in_=ps[:, 0:256], func=AF.Silu)
                            S.op("dve", "tensor_tensor", ["odf", "odb"], ["do32" + q_], out=o32_[:], in0=odf[:, tt, :], in1=odb[:, tt, :], op=ALU.add)
                            yield
                            S.op("act", "activation", ["do32" + q_], ["dt32" + q_, "dsm0" + q_], out=t32_[:], in_=o32_[:], func=AF.Square, accum_out=sm_[:, 0:1])
                            yield
                            S.op("act", "activation", ["dsm0" + q_, "cst"], ["dsm1" + q_], out=sm_[:, 1:2], in_=sm_[:, 0:1], func=AF.Sqrt, scale=1.0 / 256, bias=C("EPS"))
                            yield
                            S.op("dve", "reciprocal", ["dsm1" + q_], ["dsm2" + q_], out=sm_[:, 2:3], in_=sm_[:, 1:2])
                            S.op("dve", "scalar_tensor_tensor", ["do32" + q_, "dsm2" + q_, "dnw"], ["dt32" + q_], out=t32_[:], in0=o32_[:], scalar=sm_[:, 2:3],
                                 in1=dnw[:], op0=ALU.mult, op1=ALU.mult)
                            S.op("dve", "tensor_tensor", ["dt32" + q_, "zg" + q_], ["dzb" + q_], out=zb_[:], in0=t32_[:], in1=zg_[:], op=ALU.mult)
                            yield
                            for c2 in range(2):
                                S.op("pe", "transpose", ["dzb" + q_, "identb"], ["psT"], inc=(c2 == 1), out=psT[:, c2 * 128:(c2 + 1) * 128],
                                     in_=zb_[:, c2 * 128:(c2 + 1) * 128], identity=identb[:])
                            S.op("act", "activation", ["psT"], ["zT"], out=zT[:, 2 * h:2 * h + 2, tt * 128:(tt + 1) * 128],
                                 in_=psT[:, 0:256].rearrange("p (c t) -> p c t", c=2), func=AF.Copy)
                            yield
                        return g
                    def laneB(tts, bi, delB=delB):
                        for tt in tts:
                            yield from delB(tt)(bi)
                    lanes = [laneB(list(range(bi, NT, WBD)), bi) for bi in range(WBD)]
                    if h < 3:
                        pendingB = lanes
                    else:
                        interleave([], 1, extras=lanes)
                S.barrier()

            STOP(4)
            mT = SB(sB, "mT", [128, 8, T], BF16)
            out_gate(BLK_DO, BLK_GD, False, mT, pre=nextW)

            with ExitStack() as s4:
                W4 = 4
                xb = [SB(s4, "x4b%d" % i, [128, D], F32) for i in range(W4)]
                x1 = xb
                tmp32 = [SB(s4, "tmp32b%d" % i, [128, D], F32) for i in range(W4)]
                hb = [SB(s4, "hb4_%d" % i, [128, D], BF16) for i in range(W4)]
                small = [SB(s4, "small4_%d" % i, [128, 8], F32) for i in range(W4)]
                small_n = [SB(s4, "small_n%d" % i, [128, 4], F32) for i in range(W4)]
                junk4 = hb
                modb4 = SB(s4, "modb4", [128, 3 * D], F32)
                S.dma("sp", modb4[:], modscr_d[0, 0:3 * D].partition_broadcast(128), [], ["modb"])
                G1 = modb4[:, 0:1024]
                B2 = modb4[:, 1024:2048]
                A2 = modb4[:, 2048:3072]
                Wo, wko = load_w(BLK_OUT)

                def st4(tt):
                    def g(bi):
                        q_ = "_%d" % bi
                        xk = "x4b" + q_
                        x1k = xk
                        tk = "tmp4" + q_
                        sm_ = small[bi]
                        tsl = slice(tt * 128, (tt + 1) * 128)
                        S.dma("sp", xb[bi][:], x_d[tsl, :], [], [xk])
                        pa, pka = PS()
                        pb_, pkb = PS()
                        for kc in range(8):
                            mm(pa[:, :], pka, mT[:, kc, tsl], Wo[:, kc, 0:512], ["mT", wko], kc == 0, kc == 7)
                        for kc in range(8):
                            mm(pb_[:, :], pkb, mT[:, kc, tsl], Wo[:, kc, 512:1024], ["mT", wko], kc == 0, kc == 7)
                        S.op("act", "activation", [pka], ["hb4" + q_, "sa" + q_], out=junk4[bi][:, 0:512], in_=pa[:, :], func=AF.Square, accum_out=sm_[:, 0:1])
                        S.op("dve", "tensor_copy", [pka], [tk], out=tmp32[bi][:, 0:512], in_=pa[:, :])
                        S.op("act", "activation", [pkb], ["hb4" + q_, "sb" + q_], out=junk4[bi][:, 512:1024], in_=pb_[:, :], func=AF.Square, accum_out=sm_[:, 1:2])
                        S.op("dve", "tensor_copy", [pkb], [tk], out=tmp32[bi][:, 512:1024], in_=pb_[:, :])
                        yield
                        S.op("dve", "tensor_tensor", ["sa" + q_, "sb" + q_], ["sc" + q_], out=sm_[:, 2:3], in0=sm_[:, 0:1], in1=sm_[:, 1:2], op=ALU.add)
                        yield
                        S.op("act", "activation", ["sc" + q_, "cst"], ["sd" + q_], out=sm_[:, 3:4], in_=sm_[:, 2:3], func=AF.Sqrt, scale=1.0 / D, bias=C("EPS"))
                        yield
                        S.op("dve", "reciprocal", ["sd" + q_], ["se" + q_], out=sm_[:, 4:5], in_=sm_[:, 3:4])
                        S.op("dve", "scalar_tensor_tensor", ["se" + q_, "modb", tk], [tk], out=tmp32[bi][:], in0=tmp32[bi][:], scalar=sm_[:, 4:5],
                             in1=G1, op0=ALU.mult, op1=ALU.mult)
                        yield
                        S.op("dve", "tensor_tensor", [tk, xk], [x1k], out=x1[bi][:], in0=tmp32[bi][:], in1=xb[bi][:], op=ALU.add)
                        S.dma("sp", y_d[tsl, :], x1[bi][:], [x1k], ["y%d" % tt])
                        yield
                        yield from norm_to_hT_gen(x1[bi][:], x1k, tt, A2, B2, tmp32[bi], tk, hb[bi], "hb4" + q_, small_n[bi], "sn" + q_)
                    return g
                interleave([st4(tt) for tt in range(NT)], W4)
                S.barrier()

        STOP(6)
        with ExitStack() as s5:
            wffo = SB(s5, "wffo", [128, NFF, D], BF16)
            S.dma("pool", wffo[:], wffo_d.rearrange("(j p) c -> p j c", p=128), [], ["wffo"])
            aT = SB(s5, "aT", [128, NFF, 1024], BF16)
            sg = SB(s5, "sg", [128, 512], F32)
            x1b = [SB(s5, "x5b%d" % i, [128, D], F32) for i in range(2)]
            yb = [SB(s5, "y5b%d" % i, [128, D], F32) for i in range(2)]
            tmp32 = SB(s5, "tmp32c", [128, D], F32)
            small = SB(s5, "small5", [128, 8], F32)
            G2 = SB(s5, "modb5", [128, D], F32)
            S.dma("sp", G2[:], modscr_d[0, 3 * D:4 * D].partition_broadcast(128), [], ["modb"])
            for th in range(2):
                for b in range(6):
                    W, wk = load_w(BLK_FF + b)
                    nj = 4 if b < 5 else 2
                    for jj in range(nj):
                        j = b * 4 + jj
                        for tg in range(2):
                            ts_ = slice(th * 1024 + tg * 512, th * 1024 + (tg + 1) * 512)
                            pg, pkg = PS()
                            pu, pku = PS()
                            for kc in range(8):
                                mm(pg[:, :], pkg, W[:, kc, jj * 128:(jj + 1) * 128], hT[:, kc, ts_], [wk, "hT"], kc == 0, kc == 7)
                            for kc in range(8):
                                mm(pu[:, :], pku, W[:, kc, 512 + jj * 128:512 + (jj + 1) * 128], hT[:, kc, ts_], [wk, "hT"], kc == 0, kc == 7)
                            S.op("act", "activation", [pkg], ["sg"], out=sg[:], in_=pg[:, :], func=AF.Silu)
                            S.op("dve", "tensor_tensor", [pku, "sg"], ["aT"], out=aT[:, j, tg * 512:(tg + 1) * 512], in0=pu[:, :], in1=sg[:], op=ALU.mult)
                for t8 in range(8):
                    tt = th * 8 + t8
                    tsl = slice(tt * 128, (tt + 1) * 128)
                    xk = "x5b%d" % (tt % 2)
                    yk = "y5b%d" % (tt % 2)
                    S.dma("sp", x1b[tt % 2][:], y_d[tsl, :], ["y%d" % tt], [xk])
                    pa, pka = PS()
                    pb_, pkb = PS()
                    for j in range(NFF):
                        mm(pa[:, :], pka, aT[:, j, t8 * 128:(t8 + 1) * 128], wffo[:, j, 0:512], ["aT", "wffo"], j == 0, j == NFF - 1)
                    for j in range(NFF):
                        mm(pb_[:, :], pkb, aT[:, j, t8 * 128:(t8 + 1) * 128], wffo[:, j, 512:1024], ["aT", "wffo"], j == 0, j == NFF - 1)
                    S.op("act", "activation", [pka], ["junk5", "sm5a"], out=tmp32[:, 0:512], in_=pa[:, :], func=AF.Square, accum_out=small[:, 0:1])
                    S.op("act", "activation", [pkb], ["junk5", "sm5b"], out=tmp32[:, 512:1024], in_=pb_[:, :], func=AF.Square, accum_out=small[:, 1:2])
                    S.op("dve", "tensor_tensor", ["sm5a", "sm5b"], ["sm5c"], out=small[:, 2:3], in0=small[:, 0:1], in1=small[:, 1:2], op=ALU.add)
                    S.op("act", "activation", ["sm5c", "cst"], ["sm5d"], out=small[:, 3:4], in_=small[:, 2:3], func=AF.Sqrt, scale=1.0 / D, bias=C("EPS"))
                    S.op("dve", "reciprocal", ["sm5d"], ["sm5e"], out=small[:, 4:5], in_=small[:, 3:4])
                    S.op("dve", "scalar_tensor_tensor", [pka, "sm5e", "modb", "junk5"], ["junk5"], out=tmp32[:, 0:512], in0=pa[:, :], scalar=small[:, 4:5],
                         in1=G2[:, 0:512], op0=ALU.mult, op1=ALU.mult)
                    S.op("dve", "scalar_tensor_tensor", [pkb, "sm5e", "modb", "junk5"], ["junk5"], out=tmp32[:, 512:1024], in0=pb_[:, :], scalar=small[:, 4:5],
                         in1=G2[:, 512:1024], op0=ALU.mult, op1=ALU.mult)
                    S.op("dve", "tensor_tensor", ["junk5", xk], [yk], out=yb[tt % 2][:], in0=tmp32[:], in1=x1b[tt % 2][:], op=ALU.add)
                    S.dma("sp", y_d[tsl, :], yb[tt % 2][:], [yk], ["y%d" % tt])
            S.barrier()
        print("INSTR", dict(S.count), "dma", S.di)


_PROG = {}
_RETURN_MAPS = [False]


def _rope_tables():
    L = 2048
    rows = L // 64
    row_idx = np.repeat(np.arange(rows, dtype=np.float32), 64)
    col_idx = (np.arange(L) % 64).astype(np.float32)
    n_freq = 32
    freqs = (np.float32(10000.0) ** (-np.arange(n_freq, dtype=np.float32) / np.float32(n_freq))).astype(np.float32)
    ang = np.concatenate([row_idx[:, None] * freqs, col_idx[:, None] * freqs], axis=-1).astype(np.float32)
    cos = np.cos(ang).astype(np.float32)
    sin = np.sin(ang).astype(np.float32)
    d = np.arange(128)
    c = cos[:, d // 2].T.copy()
    s = sin[:, d // 2].T.copy()
    s[0::2, :] *= -1.0
    return np.ascontiguousarray(c), np.ascontiguousarray(s)


def _pack_weights(w_in, w_ret_o, w_dn_o, w_out, w_ffn_in):
    blk = np.zeros((NBLK, D, D), np.float32)
    swap = np.arange(128) ^ 1
    for h in range(4):
        q = w_in[:, h * 128:(h + 1) * 128]
        k = w_in[:, 512 + h * 128:512 + (h + 1) * 128]
        blk[BLK_R + h, :, 0:128] = q
        blk[BLK_R + h, :, 128:256] = q[:, swap]
        blk[BLK_R + h, :, 256:384] = k
        blk[BLK_R + h, :, 384:512] = k[:, swap]
        blk[BLK_R + h, :, 512:768] = w_in[:, 1024 + h * 256:1024 + (h + 1) * 256]
        blk[BLK_R + h, :, 768:1024] = w_in[:, 2048 + h * 256:2048 + (h + 1) * 256]
        blk[BLK_D + h, :, 0:128] = w_in[:, 3072 + h * 128:3072 + (h + 1) * 128]
        blk[BLK_D + h, :, 128:256] = w_in[:, 3584 + h * 128:3584 + (h + 1) * 128]
        blk[BLK_D + h, :, 256:512] = w_in[:, 4096 + h * 256:4096 + (h + 1) * 256]
        blk[BLK_D + h, :, 512:768] = w_in[:, 5120 + h * 256:5120 + (h + 1) * 256]
    blk[BLK_GR] = w_in[:, 6160:7184]
    blk[BLK_GD] = w_in[:, 7184:8208]
    blk[BLK_RO] = w_ret_o
    blk[BLK_DO] = w_dn_o
    blk[BLK_OUT] = w_out
    for b in range(6):
        n = 512 if b < 5 else 256
        blk[BLK_FF + b, :, 0:n] = w_ffn_in[:, b * 512:b * 512 + n]
        blk[BLK_FF + b, :, 512:512 + n] = w_ffn_in[:, DFF + b * 512:DFF + b * 512 + n]
    w_ba = np.ascontiguousarray(w_in[:, 6144:6160])
    return blk, w_ba


def kernel(x_prompt, x_sample, c, state_ret, state_dn, c_ctx, w_mod, b_mod, norm_w, w_in, conv_w, ret_decay_logit,
           ret_gn_w, dn_a_log, dn_dt_bias, dn_norm_w, w_ret_o, w_dn_o, w_out, w_ffn_in, w_ffn_out):
    f = lambda a: np.ascontiguousarray(np.asarray(a, dtype=np.float32))
    x_prompt, x_sample, c, state_ret, state_dn, c_ctx = map(f, (x_prompt, x_sample, c, state_ret, state_dn, c_ctx))
    w_in0 = f(w_in)[0]
    blk, w_ba = _pack_weights(w_in0, f(w_ret_o)[0], f(w_dn_o)[0], f(w_out)[0], f(w_ffn_in)[0])
    ropec, ropes = _rope_tables()
    ones_c = np.ones((128, T), np.float32)
    zeros_c = np.zeros((128, T), np.float32)
    shared = {
        "w_mod": f(w_mod)[0], "b_mod": f(b_mod)[0], "norm_w": f(norm_w)[0].reshape(-1), "wblk": blk, "w_ba": w_ba,
        "w_ffo": f(w_ffn_out)[0], "conv_w": f(conv_w)[0], "ret_decay_logit": f(ret_decay_logit)[0].reshape(-1),
        "ret_gn_w": f(ret_gn_w)[0], "dn_a_log": f(dn_a_log)[0].reshape(-1), "dn_dt_bias": f(dn_dt_bias)[0].reshape(-1),
        "dn_norm_w": f(dn_norm_w)[0], "cst": _build_cst(), "ident": np.eye(128, dtype=np.float32),
    }
    in_maps = []
    for core in range(8):
        m = dict(shared)
        if core < 4:
            m["x"] = x_sample[core]
            m["cond"] = c[core]
            m["st_ret"] = np.ascontiguousarray(state_ret[core, 0])
            m["st_dn"] = np.ascontiguousarray(state_dn[core, 0])
            m["rope_c"] = ropec
            m["rope_s"] = ropes
            m["flag"] = np.ones((128, 1), np.float32)
        else:
            j = core - 4
            xs = np.zeros((T, D), np.float32)
            xs[0:1024] = x_prompt[4 * j:4 * j + 4].reshape(1024, D)
            xs[1024:2048] = xs[0:1024]
            m["x"] = xs
            m["cond"] = c_ctx
            m["st_ret"] = np.zeros((2, 4, 128, 256), np.float32)
            m["st_dn"] = np.zeros((2, 4, 128, 256), np.float32)
            m["rope_c"] = ones_c
            m["rope_s"] = zeros_c
            m["flag"] = np.zeros((128, 1), np.float32)
        in_maps.append(m)
    if _RETURN_MAPS[0]:
        return in_maps
    if "nc" not in _PROG:
        _PROG["nc"] = build_program()
    res = run_bass_kernel_spmd(_PROG["nc"], in_maps, core_ids=list(range(8)))
    rr = res.results
    y_sample = np.stack([rr[i]["y"] for i in range(4)], 0)
    y_prompt = np.concatenate([rr[4 + j]["y"][0:1024].reshape(4, 256, D) for j in range(4)], 0)
    new_ret = np.concatenate([rr[4 + j]["sret_o"] for j in range(4)], 0)[:, None]
    new_dn = np.concatenate([rr[4 + j]["sdn_o"] for j in range(4)], 0)[:, None]
    return (y_prompt.astype(np.float32), y_sample.astype(np.float32), new_ret.astype(np.float32), new_dn.astype(np.float32))
```

```python
import numpy as np
from contextlib import ExitStack
import concourse.bass as bass
import concourse.mybir as mybir
from concourse.bass_utils import run_bass_kernel_spmd

F32 = mybir.dt.float32
F32R = mybir.dt.float32r
BF16 = mybir.dt.bfloat16
AF = mybir.ActivationFunctionType
ALU = mybir.AluOpType

T = 2048
D = 1024
NT = 16
EPS = 1e-6
BIG = 30000.0
DFF = 2816
NFF = 22
DBGR = 0

BLK_R = 0
BLK_D = 4
BLK_GR = 8
BLK_GD = 9
BLK_RO = 10
BLK_DO = 11
BLK_OUT = 12
BLK_FF = 13
NBLK = 19


def _cst_layout():
    names = [("ONES", 128),
             ("BLKM", 128), ("I128", 128), ("TRILO128", 128), ("TRIUP128", 128), ("NEGONES128", 128),
             ("MASKF128", 256), ("MASKB128", 256), ("SMF128", 256), ("SMB128", 256),
             ("DPOS8", 128), ("DNEG8", 128), ("GE8", 128), ("LE8", 128), ("POS1_8", 128), ("POS128M", 128),
             ("KPF", 1), ("KPB", 1), ("EPS", 1), ("EPS128", 1), ("ONE", 1), ("C64", 1), ("ZERO", 1)]
    off = {}
    o = 0
    for n, w in names:
        off[n] = (o, w)
        o += w
    return off, o


CST_OFF, NCST = _cst_layout()


def _build_cst():
    c = np.zeros((128, NCST), np.float32)
    p = np.arange(128)[:, None]
    pp = p % 64

    def put(name, arr):
        if name not in CST_OFF:
            return
        o, w = CST_OFF[name]
        c[:, o:o + w] = arr

    f64 = np.arange(64)[None, :]
    f128 = np.arange(128)[None, :]
    put("I2", (pp == f64))
    put("TRILO", (pp <= f64))
    put("TRIUP", (pp >= f64))
    put("ONES", np.ones((128, 128)))
    put("NEGONES", -np.ones((128, 64)))
    put("SELH0", np.broadcast_to(p < 64, (128, 128)))
    put("SELH1", np.broadcast_to(p >= 64, (128, 128)))
    put("BD1", (p // 64 == f128 // 64))
    put("BDLO", (p // 64 == f128 // 64) & (pp <= f128 % 64))
    put("BDUP", (p // 64 == f128 // 64) & (pp >= f128 % 64))
    ge = (f64 >= pp).astype(np.float32)
    le = (f64 <= pp).astype(np.float32)
    gt = (f64 > pp).astype(np.float32)
    lt = (f64 < pp).astype(np.float32)
    nge = -BIG * (1 - ge)
    nle = -BIG * (1 - le)
    put("MASKF", np.concatenate([nge, nle], 1))
    put("MASKB", np.concatenate([nle, nge], 1))
    put("SMF", np.concatenate([-gt, -lt], 1))
    put("SMB", np.concatenate([-lt, -gt], 1))
    put("DPOS", np.maximum(f64 - pp, 0))
    put("DNEG", np.maximum(pp - f64, 0))
    put("GE", ge)
    put("LE", le)
    put("POS1", np.broadcast_to(f64 + 1.0, (128, 64)))
    put("POS64M", np.broadcast_to(64.0 - f64, (128, 64)))
    put("I128", (p == f128))
    put("TRILO128", (p <= f128))
    put("TRIUP128", (p >= f128))
    put("NEGONES128", -np.ones((128, 128)))
    ge8 = (f128 >= p).astype(np.float32)
    le8 = (f128 <= p).astype(np.float32)
    gt8 = (f128 > p).astype(np.float32)
    lt8 = (f128 < p).astype(np.float32)
    put("BLKM", (f128 // 64 == p // 64))
    put("MASKF128", np.concatenate([-BIG * (1 - ge8), -BIG * (1 - le8)], 1))
    put("MASKB128", np.concatenate([-BIG * (1 - le8), -BIG * (1 - ge8)], 1))
    put("SMF128", np.concatenate([-gt8, -lt8], 1))
    put("SMB128", np.concatenate([-lt8, -gt8], 1))
    put("DPOS8", np.maximum(f128 - p, 0))
    put("DNEG8", np.maximum(p - f128, 0))
    put("GE8", (f128 >= p))
    put("LE8", (f128 <= p))
    put("POS1_8", np.broadcast_to(f128 + 1.0, (128, 128)))
    put("POS128M", np.broadcast_to(128.0 - f128, (128, 128)))
    put("KPF", 127.0 - p)
    put("KPB", p * 1.0)
    put("EPS", np.full((128, 1), EPS))
    put("EPS128", np.full((128, 1), 128 * EPS))
    put("ONE", np.ones((128, 1)))
    put("C64", np.full((128, 1), 64.0))
    return c


class Sched:
    EPOCH = 3000
    NDMA = 24

    def __init__(self, nc, es):
        self.nc = nc
        self.es = es
        self.eng = {"pe": nc.tensor, "dve": nc.vector, "act": nc.scalar, "pool": nc.gpsimd, "sp": nc.sync}
        self.count = {e: 0 for e in self.eng}
        self.sems = {e: [] for e in self.eng}
        self.waited = {e: {} for e in self.eng}
        self.res = {}
        self.pending = {e: ([], []) for e in self.eng}
        self.dsem = [es.enter_context(nc.semaphore("dsem%d" % i)) for i in range(self.NDMA)]
        self.duse = [0] * self.NDMA
        self.qpool = {"sp": list(range(0, 16)), "pool": list(range(16, self.NDMA)), "act": list(range(0, 16))}
        self.qi = {"sp": 0, "pool": 0, "act": 0}
        self.di = 0
        self.semid = {}
        self.all_dma_events = []

    def _sid(self, sem):
        k = id(sem)
        if k not in self.semid:
            self.semid[k] = len(self.semid)
        return self.semid[k]

    def _tl_sem(self, e, epoch):
        while len(self.sems[e]) <= epoch:
            self.sems[e].append(self.es.enter_context(self.nc.semaphore("tl_%s_%d" % (e, len(self.sems[e])))))
        return self.sems[e][epoch]

    def _wait(self, e, ev):
        if ev is None:
            return
        sem, val = ev[0], ev[1]
        sid = self._sid(sem)
        if self.waited[e].get(sid, 0) >= val:
            return
        self.eng[e].wait_ge(sem, val)
        self.waited[e][sid] = val

    def _deps(self, e, reads, writes):
        evs = []
        for k in reads:
            r = self.res.get(k)
            if r is not None and r["w"] is not None:
                evs.append(r["w"])
        for k in writes:
            r = self.res.get(k)
            if r is not None:
                if r["w"] is not None:
                    evs.append(r["w"])
                evs.extend(r["r"].values())
        for ev in evs:
            if ev[0] == "PENDING":
                if ev[1] != e:
                    raise RuntimeError("dependency on pending (no-inc) instruction of %s" % ev[1])
                continue
            self._wait(e, ev)

    def _commit(self, ev, reads, writes, tl):
        for k in writes:
            self.res[k] = {"w": ev, "r": {}}
        for k in reads:
            r = self.res.setdefault(k, {"w": None, "r": {}})
            r["r"][tl] = ev

    def op(self, e, method, reads, writes, inc=True, **kw):
        if e != "pe":
            writes = list(writes) + [k for k in reads if k.startswith("ps")]
        self._deps(e, reads, writes)
        inst = getattr(self.eng[e], method)(**kw)
        if not inc:
            pr, pw = self.pending[e]
            pr.extend(reads)
            pw.extend(writes)
            for k in writes:
                self.res[k] = {"w": ("PENDING", e), "r": {}}
            return inst
        i = self.count[e]
        self.count[e] += 1
        sem = self._tl_sem(e, i // self.EPOCH)
        val = i % self.EPOCH + 1
        inst.then_inc(sem, 1)
        ev = (sem, val)
        pr, pw = self.pending[e]
        self._commit(ev, list(reads) + pr, list(writes) + pw, "tl_" + e + str(i // self.EPOCH))
        self.pending[e] = ([], [])
        return inst

    def dma(self, q, out, in_, reads, writes, **kw):
        self._deps(q, reads, writes)
        pl = self.qpool[q]
        s = pl[self.qi[q] % len(pl)]
        self.qi[q] += 1
        self.di += 1
        sem = self.dsem[s]
        self._wait(q, (sem, 16 * self.duse[s]))
        self.duse[s] += 1
        val = 16 * self.duse[s]
        self.eng[q].dma_start(out=out, in_=in_, **kw).then_inc(sem, 16)
        ev = (sem, val)
        self._commit(ev, reads, writes, "dma%d_%d" % (s, self.duse[s]))
        self.all_dma_events.append(ev)
        if len(self.all_dma_events) > 2 * self.NDMA:
            self.all_dma_events = self.all_dma_events[-self.NDMA:]

    def barrier(self):
        evs = []
        for e in self.eng:
            if self.count[e] > 0:
                i = self.count[e] - 1
                evs.append((self.sems[e][i // self.EPOCH], i % self.EPOCH + 1))
        for s in range(self.NDMA):
            if self.duse[s] > 0:
                evs.append((self.dsem[s], 16 * self.duse[s]))
        for e in self.eng:
            for ev in evs:
                self._wait(e, ev)
        self.res = {}


class _Stop(Exception):
    pass


def build_program(upto=99):
    nc = bass.Bass("TRN2", target_bir_lowering=False)
    try:
        _build_body(nc, upto)
    except _Stop:
        pass
    return nc


def _build_body(nc, upto):

    def din(name, shape, dt=F32):
        return nc.dram_tensor(name, shape, dt, kind="ExternalInput").ap()

    def dout(name, shape, dt=F32):
        return nc.dram_tensor(name, shape, dt, kind="ExternalOutput").ap()

    x_d = din("x", [T, D])
    cond_d = din("cond", [D])
    wmod_d = din("w_mod", [D, 6 * D])
    bmod_d = din("b_mod", [6 * D])
    normw_d = din("norm_w", [4 * D])
    wblk_d = din("wblk", [NBLK, D, D])
    wba_d = din("w_ba", [D, 16])
    wffo_d = din("w_ffo", [DFF, D])
    convw_d = din("conv_w", [3, 2048])
    rdl_d = din("ret_decay_logit", [8])
    gnw_d = din("ret_gn_w", [1024])
    alog_d = din("dn_a_log", [8])
    dtb_d = din("dn_dt_bias", [8])
    dnw_d = din("dn_norm_w", [256])
    sret_d = din("st_ret", [2, 4, 128, 256])
    sdn_d = din("st_dn", [2, 4, 128, 256])
    ropec_d = din("rope_c", [128, T])
    ropes_d = din("rope_s", [128, T])
    flag_d = din("flag", [128, 1])
    cst_d = din("cst", [128, NCST])
    ident_d = din("ident", [128, 128])
    y_d = dout("y", [T, D])
    sret_o = dout("sret_o", [4, 2, 4, 128, 256])
    sdn_o = dout("sdn_o", [4, 2, 4, 128, 256])
    yrg_d = nc.dram_tensor("yrg_scr", [8, 128, T], BF16, kind="Internal").ap()
    modscr_d = nc.dram_tensor("mod_scr", [1, 4 * D], F32, kind="Internal").ap()

    with ExitStack() as es:
        S = Sched(nc, es)

        dbgn = [0]

        def DBG(name, ap, shape, dt, reads):
            d = nc.dram_tensor("dbg_" + name, shape, dt, kind="ExternalOutput").ap()
            S.dma("sp", d, ap, reads, [])

        def STOP(k):
            if upto == k:
                S.barrier()
                print("STOP at", k, "INSTR", dict(S.count), "dma", S.di)
                raise _Stop()

        def MEM(tag):
            pass

        sbn = [0]

        def SB(stack, name, shape, dt):
            sbn[0] += 1
            return stack.enter_context(nc.sbuf_tensor("s%d_%s" % (sbn[0], name), shape, dt))

        psb = [es.enter_context(nc.psum_tensor("psb%d" % i, [128, 512], F32)) for i in range(7)]
        psT = es.enter_context(nc.psum_tensor("psT", [128, 1024], BF16))
        psi = [0]

        held = set()

        def PS():
            while True:
                i = psi[0] % 7
                psi[0] += 1
                if i not in held:
                    return psb[i], "psb%d" % i

        cst = SB(es, "cst", [128, NCST], F32)
        S.dma("sp", cst[:], cst_d[:, :], [], ["cst"])

        def C(name, rows=slice(0, 128), w=None):
            o, ww = CST_OFF[name]
            if w is None:
                w = ww
            return cst[rows, o:o + w]

        identb = SB(es, "identb", [128, 128], BF16)
        S.dma("pool", identb[:], ident_d[:, :], [], ["identb"])
        flag = SB(es, "flag", [128, 1], F32)
        S.dma("sp", flag[:], flag_d[:, :], [], ["flag"])
        hT = SB(es, "hT", [128, 8, T], BF16)
        wbuf = [SB(es, "wbuf%d" % i, [128, 8, D], BF16) for i in range(2)]
        wslot = [0]
        s01 = ExitStack()
        s01.__enter__()
        modA = SB(s01, "modA", [128, 2 * D], F32)
        modb = SB(s01, "modb", [128, 4 * D], F32)

        def load_w(blk):
            i = wslot[0] % 2
            wslot[0] += 1
            S.dma("pool", wbuf[i][:], wblk_d[blk].rearrange("(k p) c -> p k c", p=128), [], ["wbuf%d" % i])
            return wbuf[i], "wbuf%d" % i

        def mm(ps, pk, lhsT, rhs, rk, start, stop, last=None):
            if last is None:
                last = stop
            S.op("pe", "matmul", rk + ["cst"], [pk], inc=last, out=ps, lhsT=lhsT, rhs=rhs, start=start, stop=stop)

        MEM("global")
        s0 = ExitStack()
        s0.__enter__()
        normw = SB(s0, "normw", [128, 4 * D], F32)
        wm = [SB(s0, "wm%d" % i, [128, 8, 512], F32) for i in range(2)]
        cond = SB(s0, "cond", [128, 8], F32)
        csil = SB(s0, "csil", [128, 8], F32)
        cb = SB(s0, "cb", [128, 8, 128], F32)
        S.dma("sp", modA[:], bmod_d[0:2 * D].partition_broadcast(128), [], ["modA"])
        S.dma("sp", modb[:], bmod_d[2 * D:6 * D].partition_broadcast(128), [], ["modb"])
        S.dma("sp", normw[:], normw_d.partition_broadcast(128), [], ["normw"])
        S.dma("sp", cond[:], cond_d.rearrange("(k p) -> p k", p=128), [], ["cond"], allow_slow_non_contiguous=True)
        S.op("act", "activation", ["cond"], ["csil"], out=csil[:], in_=cond[:], func=AF.Silu)
        S.op("dve", "tensor_copy", ["csil"], ["cb"], out=cb[:], in_=csil[:].unsqueeze(2).to_broadcast([128, 8, 128]))

        def mod_cols(cg):
            w_ = wm[cg % 2]
            wk = "wm%d" % (cg % 2)
            S.dma("sp", w_[:], wmod_d[:, cg * 512:(cg + 1) * 512].rearrange("(k p) c -> p k c", p=128), [], [wk])
            ps, pk = PS()
            for kc in range(8):
                mm(ps[:, :], pk, cb[:, kc, :], w_[:, kc, :], ["cb", wk], kc == 0, kc == 7)
            if cg < 4:
                msl = modA[:, cg * 512:(cg + 1) * 512]
                mk = "modA"
            else:
                msl = modb[:, (cg - 4) * 512:(cg - 3) * 512]
                mk = "modb"
            S.op("dve", "tensor_tensor", [pk, mk], [mk], out=msl, in0=msl, in1=ps[:, :], op=ALU.add)

        for cg in range(4):
            mod_cols(cg)
        S.op("dve", "scalar_tensor_tensor", ["modA", "normw"], ["modA"], out=modA[:, 1024:2048], in0=modA[:, 1024:2048],
             scalar=1.0, in1=normw[:, 0:1024], op0=ALU.add, op1=ALU.mult)

        def stage0b():
            for cg in range(4, 12):
                mod_cols(cg)
                yield
            S.op("dve", "tensor_tensor", ["modb", "normw"], ["modb"], out=modb[:, 0:1024], in0=modb[:, 0:1024],
                 in1=normw[:, 1024:2048], op=ALU.mult)
            S.op("dve", "scalar_tensor_tensor", ["modb", "normw"], ["modb"], out=modb[:, 2048:3072], in0=modb[:, 2048:3072],
                 scalar=1.0, in1=normw[:, 2048:3072], op0=ALU.add, op1=ALU.mult)
            S.op("dve", "tensor_tensor", ["modb", "normw"], ["modb"], out=modb[:, 3072:4096], in0=modb[:, 3072:4096],
                 in1=normw[:, 3072:4096], op=ALU.mult)
            S.dma("sp", modscr_d[0:1, :], modb[0:1, :], ["modb"], ["modscr"])
            yield

        B1 = modA[:, 0:1024]
        A1 = modA[:, 1024:2048]

        def interleave(factories, width, extra=None, extras=None):
            extras = list(extras or [])
            active = []
            free = list(range(width))
            it = iter(factories)
            pending = True
            while pending or active:
                while free and pending:
                    try:
                        f = next(it)
                    except StopIteration:
                        pending = False
                        break
                    bi = free.pop(0)
                    active.append((f(bi), bi))
                if extra is not None:
                    try:
                        next(extra)
                    except StopIteration:
                        extra = None
                for eg_ in list(extras):
                    try:
                        next(eg_)
                    except StopIteration:
                        extras.remove(eg_)
                for item in list(active):
                    try:
                        next(item[0])
                    except StopIteration:
                        active.remove(item)
                        free.append(item[1])
            while extra is not None:
                try:
                    next(extra)
                except StopIteration:
                    extra = None
            while extras:
                for eg_ in list(extras):
                    try:
                        next(eg_)
                    except StopIteration:
                        extras.remove(eg_)

        def norm_to_hT_gen(src, srck, tt, A, B, tmp32, tk, hb, hk, small, smk, modk="modb"):
            S.op("act", "activation", [srck], [tk, smk + "0"], out=tmp32[:], in_=src, func=AF.Square, accum_out=small[:, 0:1])
            yield
            S.op("act", "activation", [smk + "0", "cst"], [smk + "1"], out=small[:, 1:2], in_=small[:, 0:1], func=AF.Sqrt,
                 scale=1.0 / D, bias=C("EPS"))
            yield
            S.op("dve", "reciprocal", [smk + "1"], [smk + "2"], out=small[:, 2:3], in_=small[:, 1:2])
            S.op("dve", "scalar_tensor_tensor", [srck, smk + "2", modk], [tk], out=tmp32[:], in0=src, scalar=small[:, 2:3],
                 in1=A, op0=ALU.mult, op1=ALU.mult)
            yield
            S.op("dve", "tensor_tensor", [tk, modk], [hk], out=hb[:], in0=tmp32[:], in1=B, op=ALU.add)
            yield
            for kc in range(8):
                S.op("pe", "transpose", [hk, "identb"], ["psT"], inc=(kc == 7), out=psT[:, kc * 128:(kc + 1) * 128],
                     in_=hb[:, kc * 128:(kc + 1) * 128], identity=identb[:])
            S.op("act", "activation", ["psT"], ["hT"], out=hT[:, :, tt * 128:(tt + 1) * 128],
                 in_=psT[:, :].rearrange("p (k t) -> p k t", k=8), func=AF.Copy)
            yield

        with ExitStack() as s1:
            W1 = 4
            xb = [SB(s1, "xb%d" % i, [128, D], F32) for i in range(W1)]
            tmp32 = [SB(s1, "tmp32_%d" % i, [128, D], F32) for i in range(W1)]
            hb = [SB(s1, "hb%d" % i, [128, D], BF16) for i in range(W1)]
            small = [SB(s1, "small%d" % i, [128, 4], F32) for i in range(W1)]

            def st1(tt):
                def g(bi):
                    xk = "xb%d" % bi
                    S.dma("sp", xb[bi][:], x_d[tt * 128:(tt + 1) * 128, :], [], [xk])
                    yield
                    yield from norm_to_hT_gen(xb[bi][:], xk, tt, A1, B1, tmp32[bi], "tmp%d" % bi, hb[bi], "hb%d" % bi, small[bi], "sm%d_" % bi, "modA")
                return g
            interleave([st1(tt) for tt in range(NT)], W1, extra=stage0b())
            S.barrier()
        s0.__exit__(None, None, None)
        s01.__exit__(None, None, None)
        STOP(1)

        MEM("before mixers")
        with ExitStack() as sB:
            zT = SB(sB, "zT", [128, 8, T], BF16)
            par = SB(sB, "par", [128, 64], F32)
            lg = SB(sB, "lg", [128, 8], F32)
            dnw = SB(sB, "dnw", [128, 256], F32)
            S.dma("sp", dnw[:], dnw_d.partition_broadcast(128), [], ["dnw"])
            STOP(1.5)
            S32 = [[SB(sB, "S32_%d_%d" % (r, i), [128, 256], F32) for i in range(2)] for r in range(2)]
            S16 = [[SB(sB, "S16_%d_%d" % (r, i), [128, 256], BF16) for i in range(2)] for r in range(2)]

            def state_boundary(r, cur, seg, h, out_d):
                k = "S32_%d_%d" % (r, cur)
                k2 = "S32_%d_%d" % (r, 1 - cur)
                if seg < 4:
                    S.dma("sp", out_d[seg, r, h], S32[r][cur][:], [k], [])
                S.op("dve", "tensor_scalar", [k, "flag"], [k2], out=S32[r][1 - cur][:], in0=S32[r][cur][:], scalar1=flag[:, 0:1],
                     scalar2=None, op0=ALU.mult)
                return 1 - cur

            MEM("before sR")
            with ExitStack() as sR:
                cdec = SB(sR, "cdec", [128, 8], F32)
                kdec = SB(sR, "kdec", [128, 8], F32)
                qpat = SB(sR, "qpat", [128, 8, 128], F32)
                MT = SB(sR, "MT", [128, 4, 128], F32)
                S.dma("sp", par[:, 0:8], rdl_d.partition_broadcast(128), [], ["par"])
                S.op("act", "activation", ["par"], ["par1"], out=par[:, 8:16], in_=par[:, 0:8], func=AF.Exp, scale=-1.0)
                S.op("act", "activation", ["par1", "cst"], ["par2"], out=par[:, 16:24], in_=par[:, 8:16], func=AF.Ln, bias=C("ONE"))
                S.op("dve", "tensor_scalar", ["par2"], ["lg"], out=lg[:], in0=par[:, 16:24], scalar1=-1.0, scalar2=None, op0=ALU.mult)
                S.op("act", "activation", ["lg"], ["cdec"], out=cdec[:], in_=lg[:], func=AF.Exp, scale=128.0)
                mtmp = SB(sR, "mtmp", [128, 2, 128], F32)
                for h in range(4):
                    S.op("act", "activation", ["lg", "cst"], ["kdec"], out=kdec[:, h:h + 1], in_=C("KPF"), func=AF.Exp, scale=lg[:, h:h + 1])
                    S.op("act", "activation", ["lg", "cst"], ["kdec"], out=kdec[:, 4 + h:5 + h], in_=C("KPB"), func=AF.Exp, scale=lg[:, 4 + h:5 + h])
                    S.op("act", "activation", ["lg", "cst"], ["qpat"], out=qpat[:, h, :], in_=C("POS1_8"), func=AF.Exp, scale=lg[:, h:h + 1])
                    S.op("act", "activation", ["lg", "cst"], ["qpat"], out=qpat[:, 4 + h, :], in_=C("POS128M"), func=AF.Exp, scale=lg[:, 4 + h:5 + h])
                    S.op("act", "activation", ["lg", "cst"], ["mtmp"], out=mtmp[:, 0, :], in_=C("DPOS8"), func=AF.Exp, scale=lg[:, h:h + 1])
                    S.op("act", "activation", ["lg", "cst"], ["mtmp"], out=mtmp[:, 1, :], in_=C("DNEG8"), func=AF.Exp, scale=lg[:, 4 + h:5 + h])
                    S.op("dve", "tensor_tensor", ["mtmp", "cst"], ["mtmp"], out=mtmp[:, 0, :], in0=mtmp[:, 0, :], in1=C("GE8"), op=ALU.mult)
                    S.op("dve", "tensor_tensor", ["mtmp", "cst"], ["mtmp"], out=mtmp[:, 1, :], in0=mtmp[:, 1, :], in1=C("LE8"), op=ALU.mult)
                    S.op("dve", "tensor_tensor", ["mtmp"], ["MT"], out=MT[:, h, :], in0=mtmp[:, 0, :], in1=mtmp[:, 1, :], op=ALU.add)

                gnw = SB(sR, "gnw", [128, 1024], F32)
                S.dma("sp", gnw[:], gnw_d.partition_broadcast(128), [], ["gnw"])
                ropec = SB(sR, "ropec", [128, T], BF16)
                ropes = SB(sR, "ropes", [128, T], BF16)
                S.dma("pool", ropec[:], ropec_d[:, :], [], ["ropec"])
                S.dma("pool", ropes[:], ropes_d[:, :], [], ["ropes"])
                qT = SB(sR, "qT", [128, T], BF16)
                kT = SB(sR, "kT", [128, T], BF16)
                qfT = SB(sR, "qfT", [128, T], BF16)
                qbT = SB(sR, "qbT", [128, T], BF16)
                kft = SB(sR, "kft", [128, NT, 128], BF16)
                kbt = SB(sR, "kbt", [128, NT, 128], BF16)
                vt = SB(sR, "vt", [128, NT, 256], BF16)
                WB = 5
                rg = [SB(sR, "rg%d" % i, [128, 256], BF16) for i in range(WB)]
                of = SB(sR, "of", [128, NT, 256], BF16)
                ob = SB(sR, "ob", [128, NT, 256], BF16)
                sT = SB(sR, "sT", [128, NT, 128], BF16)
                tA = SB(sR, "tA", [128, 512], F32)
                tB = SB(sR, "tB", [128, 512], F32)
                o32 = [SB(sR, "o32_%d" % i, [128, 256], F32) for i in range(WB)]
                t32 = [SB(sR, "t32_%d" % i, [128, 256], F32) for i in range(WB)]
                zb = [SB(sR, "zb%d" % i, [128, 256], BF16) for i in range(WB)]
                sm = [SB(sR, "sm%d" % i, [128, 8], F32) for i in range(WB)]
                MEM("sR after alloc")
                pendingRB = []
                nextW = load_w(BLK_R)
                for h in range(4):
                    W, wk = nextW
                    def ret_pre(h=h, W=W, wk=wk):
                        for tg in range(4):
                            ts_ = slice(tg * 512, (tg + 1) * 512)
                            for (dst, dk_, c0, scale) in ((qT, "qT", 0, 128.0 ** -0.5), (kT, "kT", 256, 1.0)):
                                psa, pka = PS()
                                for kc in range(8):
                                    mm(psa[:, :], pka, W[:, kc, c0:c0 + 128], hT[:, kc, ts_], [wk, "hT"], kc == 0, kc == 7)
                                psb2, pkb = PS()
                                for kc in range(8):
                                    mm(psb2[:, :], pkb, W[:, kc, c0 + 128:c0 + 256], hT[:, kc, ts_], [wk, "hT"], kc == 0, kc == 7)
                                S.op("dve", "scalar_tensor_tensor", [pka, "ropec"], ["tA"], out=tA[:], in0=psa[:, :], scalar=scale,
                                     in1=ropec[:, ts_], op0=ALU.mult, op1=ALU.mult)
                                S.op("dve", "scalar_tensor_tensor", [pkb, "ropes"], ["tB"], out=tB[:], in0=psb2[:, :], scalar=scale,
                                     in1=ropes[:, ts_], op0=ALU.mult, op1=ALU.mult)
                                S.op("dve", "tensor_tensor", ["tA", "tB"], [dk_], out=dst[:, ts_], in0=tA[:], in1=tB[:], op=ALU.add)
                                yield
                        for tt in range(NT):
                            ps, pk = PS()
                            for kc in range(8):
                                mm(ps[:, 0:256], pk, hT[:, kc, tt * 128:(tt + 1) * 128], W[:, kc, 512:768], [wk, "hT"], kc == 0, kc == 7)
                            S.op("act", "activation", [pk], ["vt"], out=vt[:, tt, :], in_=ps[:, 0:256], func=AF.Copy)
                            if tt % 2 == 1:
                                yield
                        for g4 in range(4):
                            for j in range(4):
                                tt = g4 * 4 + j
                                S.op("pe", "transpose", ["kT", "identb"], ["psT"], inc=(j == 3), out=psT[:, j * 128:(j + 1) * 128],
                                     in_=kT[:, tt * 128:(tt + 1) * 128], identity=identb[:])
                            pv = psT[:, 0:512].rearrange("p (t d) -> p t d", t=4)
                            S.op("dve", "tensor_scalar", ["psT", "kdec"], ["kft"], out=kft[:, g4 * 4:(g4 + 1) * 4, :], in0=pv,
                                 scalar1=kdec[:, h:h + 1], scalar2=None, op0=ALU.mult)
                            S.op("act", "activation", ["psT", "kdec"], ["kbt"], out=kbt[:, g4 * 4:(g4 + 1) * 4, :], in_=pv, func=AF.Copy,
                                 scale=kdec[:, 4 + h:5 + h])
                            yield
                        S.op("dve", "tensor_tensor", ["qT", "qpat"], ["qfT"], out=qfT[:, :].rearrange("p (n i) -> p n i", i=128),
                             in0=qT[:, :].rearrange("p (n i) -> p n i", i=128),
                             in1=qpat[:, h:h + 1, :].to_broadcast([128, 16, 128]), op=ALU.mult)
                        S.op("dve", "tensor_tensor", ["qT", "qpat"], ["qbT"], out=qbT[:, :].rearrange("p (n i) -> p n i", i=128),
                             in0=qT[:, :].rearrange("p (n i) -> p n i", i=128),
                             in1=qpat[:, 4 + h:5 + h, :].to_broadcast([128, 16, 128]), op=ALU.mult)
                        yield
                        for g4 in range(4):
                            ps, pk = PS()
                            for c in range(4):
                                tt = g4 * 4 + c
                                tsl = slice(tt * 128, (tt + 1) * 128)
                                mm(ps[:, c * 128:(c + 1) * 128], pk, kT[:, tsl], qT[:, tsl], ["kT", "qT"], True, True, last=(c == 3))
                            S.op("dve", "tensor_tensor", [pk, "MT"], ["sT"], out=sT[:, g4 * 4:(g4 + 1) * 4, :], in0=ps[:, :].rearrange("p (t i) -> p t i", t=4),
                                 in1=MT[:, h:h + 1, :].to_broadcast([128, 4, 128]), op=ALU.mult)
                            yield
                    interleave([lambda bi: ret_pre()], 1, extras=pendingRB)
                    pendingRB = []
                    nextW = load_w(BLK_R + h + 1) if h < 3 else load_w(BLK_RO)
                    STOP(1.8)
                    cur = [0, 0]
                    c16 = [0, 0]
                    for r in range(2):
                        S.dma("sp", S32[r][0][:], sret_d[r, h], [], ["S32_%d_0" % r])
                        S.op("act", "activation", ["S32_%d_0" % r], ["S16_%d_0" % r], out=S16[r][0][:], in_=S32[r][0][:], func=AF.Copy)
                    for t in range(NT):
                        for r in range(2):
                            tt = t if r == 0 else NT - 1 - t
                            tsl = slice(tt * 128, (tt + 1) * 128)
                            sk = "S32_%d_%d" % (r, cur[r])
                            s16k = "S16_%d_%d" % (r, c16[r])
                            s16n = "S16_%d_%d" % (r, 1 - c16[r])
                            Scur = S16[r][c16[r]]
                            Snew = S16[r][1 - c16[r]]
                            pss, pks = PS()
                            kk = kft if r == 0 else kbt
                            mm(pss[:, 0:256], pks, kk[:, tt, :], vt[:, tt, :], ["kft", "kbt", "vt"], True, True)
                            pso, pko = PS()
                            if r == 0:
                                mm(pso[:, 0:256], pko, sT[:, tt, :], vt[:, tt, :], ["sT", "vt"], True, False)
                                mm(pso[:, 0:256], pko, qfT[:, tsl], Scur[:], ["qfT", s16k], False, True)
                                S.op("act", "activation", [pko], ["of"], out=of[:, tt, :], in_=pso[:, 0:256], func=AF.Copy)
                            else:
                                mm(pso[:, 0:256], pko, qbT[:, tsl], Scur[:], ["qbT", s16k], True, True)
                                S.op("act", "activation", [pko], ["ob"], out=ob[:, tt, :], in_=pso[:, 0:256], func=AF.Copy)
                            bnd = (tt % 2 == 1) if r == 0 else (tt % 2 == 0)
                            if not bnd:
                                S.op("dve", "scalar_tensor_tensor", [pks, sk, "cdec"], [s16n], out=Snew[:], in0=S32[r][cur[r]][:],
                                     scalar=cdec[:, 4 * r + h:4 * r + h + 1], in1=pss[:, 0:256], op0=ALU.mult, op1=ALU.add)
                            S.op("dve", "scalar_tensor_tensor", [pks, sk, "cdec"], [sk], out=S32[r][cur[r]][:], in0=S32[r][cur[r]][:],
                                 scalar=cdec[:, 4 * r + h:4 * r + h + 1], in1=pss[:, 0:256], op0=ALU.mult, op1=ALU.add)
                            if bnd:
                                cur[r] = state_boundary(r, cur[r], tt // 2, h, sret_o)
                                S.op("act", "activation", ["S32_%d_%d" % (r, cur[r])], [s16n], out=Snew[:], in_=S32[r][cur[r]][:], func=AF.Copy)
                            c16[r] = 1 - c16[r]
                    STOP(1.9)
                    def retB(tt, h=h, W=W, wk=wk):
                        def g(bi):
                            o32_, t32_, zb_, sm_, rg_ = o32[bi], t32[bi], zb[bi], sm[bi], rg[bi]
                            q_ = "_%d" % bi
                            ps, pk = PS()
                            for kc in range(8):
                                mm(ps[:, 0:256], pk, hT[:, kc, tt * 128:(tt + 1) * 128], W[:, kc, 768:1024], [wk, "hT"], kc == 0, kc == 7)
                            S.op("act", "activation", [pk], ["rg" + q_], out=rg_[:], in_=ps[:, 0:256], func=AF.Silu)
                            S.op("dve", "scalar_tensor_tensor", ["of", "ob"], ["o32" + q_, "sm0" + q_], out=o32_[:], in0=of[:, tt, :], scalar=1.0,
                                 in1=ob[:, tt, :], op0=ALU.mult, op1=ALU.add, accum_out=sm_[:, 0:1])
                            yield
                            S.op("dve", "tensor_scalar", ["sm0" + q_], ["sm1" + q_], out=sm_[:, 1:2], in0=sm_[:, 0:1], scalar1=-1.0 / 256, scalar2=None, op0=ALU.mult)
                            yield
                            S.op("act", "activation", ["o32" + q_, "sm1" + q_], ["t32" + q_, "sm2" + q_], out=t32_[:], in_=o32_[:], func=AF.Square, bias=sm_[:, 1:2],
                                 accum_out=sm_[:, 2:3])
                            yield
                            S.op("act", "activation", ["sm2" + q_, "cst"], ["sm3" + q_], out=sm_[:, 3:4], in_=sm_[:, 2:3], func=AF.Sqrt, scale=1.0 / 256, bias=C("EPS"))
                            yield
                            S.op("dve", "reciprocal", ["sm3" + q_], ["sm4" + q_], out=sm_[:, 4:5], in_=sm_[:, 3:4])
                            S.op("dve", "scalar_tensor_tensor", ["o32" + q_, "sm1" + q_, "gnw"], ["t32" + q_], out=t32_[:], in0=o32_[:], scalar=sm_[:, 1:2],
                                 in1=gnw[:, h * 256:(h + 1) * 256], op0=ALU.add, op1=ALU.mult)
                            S.op("dve", "scalar_tensor_tensor", ["t32" + q_, "sm4" + q_, "rg" + q_], ["zb" + q_], out=zb_[:], in0=t32_[:], scalar=sm_[:, 4:5],
                                 in1=rg_[:], op0=ALU.mult, op1=ALU.mult)
                            yield
                            for c in range(2):
                                S.op("pe", "transpose", ["zb" + q_, "identb"], ["psT"], inc=(c == 1), out=psT[:, c * 128:(c + 1) * 128],
                                     in_=zb_[:, c * 128:(c + 1) * 128], identity=identb[:])
                            S.op("act", "activation", ["psT"], ["zT"], out=zT[:, 2 * h:2 * h + 2, tt * 128:(tt + 1) * 128],
                                 in_=psT[:, 0:256].rearrange("p (c t) -> p c t", c=2), func=AF.Copy)
                            yield
                        return g
                    def laneRB(tts, bi, retB=retB):
                        for tt in tts:
                            yield from retB(tt)(bi)
                    lanesR = [laneRB(list(range(bi, NT, WB)), bi) for bi in range(WB)]
                    if h < 3:
                        pendingRB = lanesR
                    else:
                        interleave([], 1, extras=lanesR)
                S.barrier()

            def out_gate(blk_o, blk_g, first, mT=None, pre=None):
                with ExitStack() as sg:
                    sig = SB(sg, "sig", [128, 512], F32)
                    yt = [SB(sg, "yt%d" % i, [128, 512], BF16) for i in range(2)]
                    tg32 = SB(sg, "tg32", [128, 512], F32)
                    Wo, wko = pre if pre is not None else load_w(blk_o)
                    Wg, wkg = load_w(blk_g)
                    it = 0
                    for fo in range(8):
                        for tg in range(4):
                            ts_ = slice(tg * 512, (tg + 1) * 512)
                            ps1, pk1 = PS()
                            for kc in range(8):
                                mm(ps1[:, :], pk1, Wo[:, kc, fo * 128:(fo + 1) * 128], zT[:, kc, ts_], [wko, "zT"], kc == 0, kc == 7)
                            ps2, pk2 = PS()
                            for kc in range(8):
                                mm(ps2[:, :], pk2, Wg[:, kc, fo * 128:(fo + 1) * 128], hT[:, kc, ts_], [wkg, "hT"], kc == 0, kc == 7)
                            S.op("act", "activation", [pk2], ["sig"], out=sig[:], in_=ps2[:, :], func=AF.Sigmoid)
                            yk = "yt%d" % (it % 2)
                            y_ = yt[it % 2]
                            it += 1
                            if first:
                                S.op("dve", "tensor_tensor", [pk1, "sig"], [yk], out=y_[:], in0=ps1[:, :], in1=sig[:], op=ALU.mult)
                                S.dma("sp", yrg_d[fo, :, ts_], y_[:], [yk], ["yrg%d_%d" % (fo, tg)])
                            else:
                                S.dma("sp", y_[:], yrg_d[fo, :, ts_], ["yrg%d_%d" % (fo, tg)], [yk])
                                S.op("dve", "tensor_tensor", [pk1, "sig"], ["tg32"], out=tg32[:], in0=ps1[:, :], in1=sig[:], op=ALU.mult)
                                S.op("dve", "tensor_tensor", ["tg32", yk], ["mT"], out=mT[:, fo, ts_], in0=tg32[:], in1=y_[:], op=ALU.add)
                    S.barrier()

            STOP(2)
            out_gate(BLK_RO, BLK_GR, True, pre=nextW)
            STOP(3)

            with ExitStack() as sD:
                bap = SB(sD, "bap", [128, NT, 16], F32)
                beta = SB(sD, "beta", [128, NT, 8], F32)
                la = SB(sD, "la", [128, NT, 8], F32)
                Gtok = SB(sD, "Gtok", [128, NT, 8], F32)
                eg = SB(sD, "eg", [128, NT, 8], F32)
                kh = SB(sD, "kh", [128, NT, 8], F32)
                bw = SB(sD, "bw", [128, NT, 8], F32)
                EGL = SB(sD, "EGL", [128, NT, 8], F32)
                wba = SB(sD, "wba", [128, 8, 16], BF16)
                dpar = SB(sD, "dpar", [128, 32], F32)
                convw = SB(sD, "convw", [128, 3, 16], F32)
                ncw = SB(sD, "ncw", [128, 2, 16], F32)
                S.dma("pool", wba[:], wba_d.rearrange("(k p) c -> p k c", p=128), [], ["wba"])
                S.dma("sp", dpar[:, 0:8], alog_d.partition_broadcast(128), [], ["dpar"])
                S.dma("sp", dpar[:, 8:16], dtb_d.partition_broadcast(128), [], ["dpar"])
                S.dma("sp", convw[:], convw_d.rearrange("t (c p) -> p t c", p=128), [], ["convw"], allow_slow_non_contiguous=True)
                S.op("dve", "tensor_scalar", ["flag"], ["dpar2"], out=dpar[:, 24:25], in0=flag[:, 0:1], scalar1=-1.0, scalar2=None, op0=ALU.add)
                S.op("dve", "tensor_scalar", ["convw", "dpar2"], ["ncw"], out=ncw[:, 0, :], in0=convw[:, 0, :], scalar1=dpar[:, 24:25], scalar2=None, op0=ALU.mult)
                S.op("dve", "tensor_scalar", ["convw", "dpar2"], ["ncw"], out=ncw[:, 1, :], in0=convw[:, 2, :], scalar1=dpar[:, 24:25], scalar2=None, op0=ALU.mult)
                S.op("act", "activation", ["dpar"], ["dpar1"], out=dpar[:, 16:24], in_=dpar[:, 0:8], func=AF.Exp)
                for tt in range(NT):
                    ps, pk = PS()
                    for kc in range(8):
                        mm(ps[:, 0:16], pk, hT[:, kc, tt * 128:(tt + 1) * 128], wba[:, kc, :], ["wba", "hT"], kc == 0, kc == 7)
                    S.op("act", "activation", [pk], ["bap"], out=bap[:, tt, :], in_=ps[:, 0:16], func=AF.Copy)
                S.op("act", "activation", ["bap"], ["beta"], out=beta[:], in_=bap[:, :, 0:8], func=AF.Sigmoid)
                S.op("dve", "tensor_tensor", ["bap", "dpar"], ["la"], out=la[:], in0=bap[:, :, 8:16],
                     in1=dpar[:, 8:16].unsqueeze(1).to_broadcast([128, NT, 8]), op=ALU.add)
                S.op("act", "activation", ["la"], ["la"], out=la[:], in_=la[:], func=AF.Exp)
                S.op("act", "activation", ["la", "cst"], ["la"], out=la[:], in_=la[:], func=AF.Ln, bias=C("ONE"))
                S.op("dve", "scalar_tensor_tensor", ["la", "dpar1"], ["la"], out=la[:], in0=la[:], scalar=-1.0,
                     in1=dpar[:, 16:24].unsqueeze(1).to_broadcast([128, NT, 8]), op0=ALU.mult, op1=ALU.mult)
                ps, pk = PS()
                mm(ps[:, 0:64].rearrange("p (t c) -> p t c", c=4), pk, C("TRILO128"), la[:, :, 0:4], ["la"], True, True, last=False)
                mm(ps[:, 64:128].rearrange("p (t c) -> p t c", c=4), pk, C("TRIUP128"), la[:, :, 4:8], ["la"], True, True)
                S.op("act", "activation", [pk], ["Gtok"], out=Gtok[:, :, 0:4], in_=ps[:, 0:64].rearrange("p (t c) -> p t c", c=4), func=AF.Copy)
                S.op("act", "activation", [pk], ["Gtok"], out=Gtok[:, :, 4:8], in_=ps[:, 64:128].rearrange("p (t c) -> p t c", c=4), func=AF.Copy)
                ps, pk = PS()
                mm(ps[:, 0:128], pk, C("ONES"), la[:].rearrange("p t c -> p (t c)"), ["la"], True, True)
                S.op("dve", "tensor_tensor", [pk, "Gtok"], ["kh"], out=kh[:].rearrange("p t c -> p (t c)"), in0=ps[:, 0:128],
                     in1=Gtok[:].rearrange("p t c -> p (t c)"), op=ALU.subtract)
                S.op("act", "activation", [pk], ["EGL"], out=EGL[:].rearrange("p t c -> p (t c)"), in_=ps[:, 0:128], func=AF.Exp)
                S.op("act", "activation", ["kh"], ["kh"], out=kh[:], in_=kh[:], func=AF.Exp)
                S.op("act", "activation", ["Gtok"], ["eg"], out=eg[:], in_=Gtok[:], func=AF.Exp)
                S.op("dve", "tensor_tensor", ["eg", "beta"], ["bw"], out=bw[:], in0=eg[:], in1=beta[:], op=ALU.mult)

                dqT = SB(sD, "dqT", [128, T], BF16)
                dkT = SB(sD, "dkT", [128, T], BF16)
                ktok = SB(sD, "ktok", [128, NT, 128], BF16)
                vt = SB(sD, "dvt", [128, NT, 256], BF16)
                WBD = 4
                zg = [SB(sD, "zg%d" % i, [128, 256], BF16) for i in range(WBD)]
                odf = SB(sD, "odf", [128, NT, 256], BF16)
                odb = SB(sD, "odb", [128, NT, 256], BF16)
                o32 = [SB(sD, "do32_%d" % i, [128, 256], F32) for i in range(WBD)]
                t32 = [SB(sD, "dt32_%d" % i, [128, 256], F32) for i in range(WBD)]
                zb = [SB(sD, "dzb%d" % i, [128, 256], BF16) for i in range(WBD)]
                sm = [SB(sD, "dsm%d" % i, [128, 8], F32) for i in range(WBD)]
                RING = 3
                NSET = 4

                pendingB = []
                nextW = load_w(BLK_D)
                for h in range(4):
                    W, wk = nextW
                    with ExitStack() as sconv:
                        WC = 2
                        xpre_l = [SB(sconv, "xpre%d" % i, [128, T], BF16) for i in range(WC)]
                        cv_l = [SB(sconv, "cv%d" % i, [128, T], F32) for i in range(WC)]
                        cvr_l = [SB(sconv, "cvr%d" % i, [128, 512], F32R) for i in range(WC)]
                        ysb_l = xpre_l
                        rin_l = [SB(sconv, "rin%d" % i, [128, 512], F32) for i in range(WC)]
                        onesR2 = SB(sconv, "onesR2", [128, 128], F32R)
                        S.op("dve", "tensor_copy", ["cst"], ["onesR2"], out=onesR2[:], in_=C("ONES"))

                        def conv_job(kind, ct, c0, h=h, W=W, wk=wk):
                            def g(bi):
                                xpre, cv, ysb, rin, cvr = xpre_l[bi], cv_l[bi], ysb_l[bi], rin_l[bi], cvr_l[bi]
                                q_ = "_%d" % bi
                                xk, ck, yk, rk_ = "xpre" + q_, "cv" + q_, "xpre" + q_, "rin" + q_
                                for tg in range(4):
                                    ts_ = slice(tg * 512, (tg + 1) * 512)
                                    ps, pk = PS()
                                    for kc in range(8):
                                        mm(ps[:, :], pk, W[:, kc, c0:c0 + 128], hT[:, kc, ts_], [wk, "hT"], kc == 0, kc == 7)
                                    S.op("act", "activation", [pk], [xk], out=xpre[:, ts_], in_=ps[:, :], func=AF.Copy)
                                    yield
                                w0 = convw[:, 0, ct:ct + 1]
                                w1 = convw[:, 1, ct:ct + 1]
                                w2 = convw[:, 2, ct:ct + 1]
                                S.op("dve", "tensor_scalar", [xk, "convw"], [ck], out=cv[:], in0=xpre[:], scalar1=w1, scalar2=None, op0=ALU.mult)
                                S.op("dve", "scalar_tensor_tensor", [xk, "convw", ck], [ck], out=cv[:, 1:T], in0=xpre[:, 0:T - 1], scalar=w0,
                                     in1=cv[:, 1:T], op0=ALU.mult, op1=ALU.add)
                                S.op("dve", "scalar_tensor_tensor", [xk, "convw", ck], [ck], out=cv[:, 0:T - 1], in0=xpre[:, 1:T], scalar=w2,
                                     in1=cv[:, 0:T - 1], op0=ALU.mult, op1=ALU.add)
                                xv = xpre[:, :].rearrange("p (s t) -> p s t", t=256)
                                cvv = cv[:, :].rearrange("p (s t) -> p s t", t=256)
                                S.op("dve", "scalar_tensor_tensor", [xk, "ncw", ck], [ck], out=cvv[:, 1:8, 0:1], in0=xv[:, 0:7, 255:256],
                                     scalar=ncw[:, 0, ct:ct + 1], in1=cvv[:, 1:8, 0:1], op0=ALU.mult, op1=ALU.add)
                                S.op("dve", "scalar_tensor_tensor", [xk, "ncw", ck], [ck], out=cvv[:, 0:7, 255:256], in0=xv[:, 1:8, 0:1],
                                     scalar=ncw[:, 1, ct:ct + 1], in1=cvv[:, 0:7, 255:256], op0=ALU.mult, op1=ALU.add)
                                yield
                                S.op("act", "activation", [ck], [yk], out=ysb[:], in_=cv[:], func=AF.Silu)
                                yield
                                if kind in ("q", "k"):
                                    dst, dk_ = (dqT, "dqT") if kind == "q" else (dkT, "dkT")
                                    epsn, sc = ("EPS128", 128.0) if kind == "q" else ("EPS", 1.0)
                                    for tg in range(4):
                                        ts_ = slice(tg * 512, (tg + 1) * 512)
                                        S.op("act", "activation", [yk], ["cvr" + q_], out=cvr[:], in_=ysb[:, ts_], func=AF.Square)
                                        yield
                                        ps, pk = PS()
                                        mm(ps[:, :], pk, onesR2[:], cvr[:], ["cvr" + q_, "onesR2"], True, True)
                                        S.op("act", "activation", [pk, "cst"], [rk_], out=rin[:], in_=ps[:, :], func=AF.Sqrt, scale=sc, bias=C(epsn))
                                        yield
                                        S.op("dve", "reciprocal", [rk_], [rk_], out=rin[:], in_=rin[:])
                                        S.op("dve", "tensor_tensor", [yk, rk_], [dk_], out=dst[:, ts_], in0=ysb[:, ts_], in1=rin[:], op=ALU.mult)
                                        yield
                                    if kind == "k":
                                        for g4 in range(4):
                                            for j in range(4):
                                                tt = g4 * 4 + j
                                                S.op("pe", "transpose", ["dkT", "identb"], ["psT"], inc=(j == 3), out=psT[:, j * 128:(j + 1) * 128],
                                                     in_=dkT[:, tt * 128:(tt + 1) * 128], identity=identb[:])
                                            S.op("act", "activation", ["psT"], ["ktok"], out=ktok[:, g4 * 4:(g4 + 1) * 4, :],
                                                 in_=psT[:, 0:512].rearrange("p (t d) -> p t d", t=4), func=AF.Copy)
                                            yield
                                else:
                                    c = kind
                                    for g4 in range(4):
                                        for j in range(4):
                                            tt = g4 * 4 + j
                                            S.op("pe", "transpose", [yk, "identb"], ["psT"], inc=(j == 3), out=psT[:, j * 128:(j + 1) * 128],
                                                 in_=ysb[:, tt * 128:(tt + 1) * 128], identity=identb[:])
                                        S.op("act", "activation", ["psT"], ["dvt"], out=vt[:, g4 * 4:(g4 + 1) * 4, c * 128:(c + 1) * 128],
                                             in_=psT[:, 0:512].rearrange("p (t d) -> p t d", t=4), func=AF.Copy)
                                        yield
                            return g
                        interleave([conv_job("q", h, 0), conv_job("k", 4 + h, 128), conv_job(0, 8 + 2 * h, 256), conv_job(1, 9 + 2 * h, 384)], WC,
                                   extras=pendingB)
                        pendingB = []
                        S.barrier()
                    nextW = load_w(BLK_D + h + 1) if h < 3 else load_w(BLK_DO)

                    with ExitStack() as sscan:
                        ring = {}
                        for r in range(2):
                            for i in range(RING):
                                ring[(r, i)] = dict(
                                    sT=SB(sscan, "r_sT%d%d" % (r, i), [128, 128], BF16),
                                    nwT=SB(sscan, "r_nwT%d%d" % (r, i), [128, 128], BF16),
                                    u=SB(sscan, "r_u%d%d" % (r, i), [128, 256], F32),
                                    qgT=SB(sscan, "r_qgT%d%d" % (r, i), [128, 128], BF16),
                                    khat=SB(sscan, "r_kh%d%d" % (r, i), [128, 128], BF16))
                        TS = []
                        for i in range(NSET):
                            TS.append(dict(
                                aTri=SB(sscan, "aTri%d" % i, [128, 128], F32R),
                                DD=SB(sscan, "DD%d" % i, [128, 256], F32),
                                Z=SB(sscan, "Z%d" % i, [128, 384], F32R),
                                Noff=SB(sscan, "Noff%d" % i, [128, 128], F32R),
                                bv=SB(sscan, "bv%d" % i, [128, 256], F32R), kbeg=SB(sscan, "kbeg%d" % i, [128, 128], F32R),
                                egrow=SB(sscan, "egrow%d" % i, [128, 128], F32)))
                        vn = [SB(sscan, "vn%d" % r, [128, 256], BF16) for r in range(2)]
                        identR = SB(sscan, "identR", [128, 128], F32R)
                        S.op("dve", "tensor_copy", ["cst"], ["identR"], out=identR[:], in_=C("I128"))
                        onesR = SB(sscan, "onesR", [128, 128], F32R)
                        negonesR = SB(sscan, "negonesR", [128, 128], F32R)
                        S.op("dve", "tensor_copy", ["cst"], ["onesR"], out=onesR[:], in_=C("ONES"))
                        S.op("dve", "tensor_copy", ["cst"], ["negonesR"], out=negonesR[:], in_=C("NEGONES128"))

                        def prep_gen(k, r, tsi):
                            tt = k if r == 0 else NT - 1 - k
                            c = 4 * r + h
                            X = TS[tsi]
                            q_ = "_%d" % tsi
                            slot = ring[(r, k % RING)]
                            sk = "ring%d%d" % (r, k % RING)
                            tsl = slice(tt * 128, (tt + 1) * 128)
                            tri = C("TRILO128") if r == 0 else C("TRIUP128")
                            aTri, DD, Z = X["aTri"], X["DD"], X["Z"]
                            DS = DD[:, 128:256]
                            bv, kbeg, egrow = X["bv"], X["kbeg"], X["egrow"]
                            S.op("act", "activation", ["la", "cst"], ["aTri" + q_], out=aTri[:], in_=tri, func=AF.Copy, scale=la[:, tt, c:c + 1])
                            S.op("act", "activation", ["ktok", "kh"], [sk + "khat"], out=slot["khat"][:], in_=ktok[:, tt, :], func=AF.Copy, scale=kh[:, tt, c:c + 1])
                            S.op("dve", "tensor_scalar", ["ktok", "bw"], ["kbeg" + q_], out=kbeg[:], in0=ktok[:, tt, :], scalar1=bw[:, tt, c:c + 1], scalar2=None, op0=ALU.mult)
                            S.op("act", "activation", ["dvt", "beta"], ["bv" + q_], out=bv[:], in_=vt[:, tt, :], func=AF.Copy, scale=beta[:, tt, c:c + 1])
                            yield
                            while nhold[0] >= 2:
                                yield
                            nhold[0] += 1
                            psQ, pkQ = PS()
                            held.add(int(pkQ[3:]))
                            psG, pkG = PS()
                            held.add(int(pkG[3:]))
                            mm(psQ[:, 0:128], pkQ, onesR[:], aTri[:], ["aTri" + q_, "onesR"], True, False)
                            mm(psQ[:, 0:128], pkQ, aTri[:], negonesR[:], ["aTri" + q_, "negonesR"], False, True, last=False)
                            mm(psQ[:, 128:256], pkQ, onesR[:], aTri[:], ["aTri" + q_, "onesR"], True, True)
                            mm(psG[:, 0:128], pkG, dkT[:, tsl], dkT[:, tsl], ["dkT"], True, True, last=False)
                            mm(psG[:, 128:256], pkG, dkT[:, tsl], dqT[:, tsl], ["dkT", "dqT"], True, True)
                            mask = C("MASKF128") if r == 0 else C("MASKB128")
                            smask = C("SMF128") if r == 0 else C("SMB128")
                            S.op("dve", "tensor_tensor", [pkQ, "cst"], ["DD" + q_], out=DD[:, 0:128], in0=psQ[:, 0:128], in1=mask[:, 0:128], op=ALU.add)
                            S.op("dve", "scalar_tensor_tensor", [pkQ, "cst"], ["DD" + q_], out=DD[:, 128:256], in0=psQ[:, 0:128], scalar=-1.0,
                                 in1=mask[:, 128:256], op0=ALU.mult, op1=ALU.add)
                            S.op("act", "activation", [pkQ], ["egrow" + q_], out=egrow[:], in_=psQ[:, 128:256], func=AF.Exp)
                            S.op("act", "activation", ["DD" + q_], ["DD" + q_], out=DD[:], in_=DD[:], func=AF.Exp)
                            held.discard(int(pkQ[3:]))
                            yield
                            S.op("dve", "tensor_tensor", ["DD" + q_, "cst"], ["DD" + q_], out=DS, in0=DD[:, 128:256], in1=smask[:, 128:256], op=ALU.mult)
                            S.op("dve", "tensor_tensor", [pkG, "DD" + q_], [sk + "sT"], out=slot["sT"][:], in0=psG[:, 128:256], in1=DD[:, 0:128], op=ALU.mult)
                            S.op("dve", "tensor_tensor", ["dqT", "egrow" + q_], [sk + "qgT"], out=slot["qgT"][:], in0=dqT[:, tsl], in1=egrow[:], op=ALU.mult)
                            kZ = "Z" + q_
                            Noff = X["Noff"][:]
                            S.op("dve", "scalar_tensor_tensor", [pkG, "beta", "DD" + q_], ["egrow" + q_], out=egrow[:], in0=psG[:, 0:128], scalar=beta[:, tt, c:c + 1],
                                 in1=DS, op0=ALU.mult, op1=ALU.mult)
                            S.op("dve", "tensor_tensor", ["egrow" + q_, "cst"], [kZ], out=Z[:, 256:384], in0=egrow[:], in1=C("BLKM"), op=ALU.mult)
                            S.op("dve", "tensor_tensor", ["egrow" + q_, kZ], ["Noff" + q_], out=Noff, in0=egrow[:], in1=Z[:, 256:384], op=ALU.subtract)
                            mm(psG[:, 256:384], pkG, Z[:, 256:384], identR[:], [kZ, "identR"], True, True)
                            S.op("act", "activation", [pkG], [kZ], out=Z[:, 0:128], in_=psG[:, 256:384], func=AF.Copy)
                            S.op("dve", "tensor_tensor", [kZ, "identR"], [kZ], out=Z[:, 128:256], in0=Z[:, 0:128], in1=identR[:], op=ALU.add)
                            held.discard(int(pkG[3:]))
                            nhold[0] -= 1
                            yield
                            for lev in range(1, 7):
                                psA, pkA = PS()
                                if lev == 1:
                                    mm(psA[:, 0:128], pkA, Z[:, 256:384], Z[:, 0:128], [kZ], True, True, last=False)
                                else:
                                    mm(psA[:, 0:256], pkA, Z[:, 256:384], Z[:, 0:256], [kZ], True, True, last=(lev == 6))
                                if lev < 6:
                                    mm(psA[:, 256:384], pkA, Z[:, 0:128], Z[:, 256:384], [kZ], True, True)
                                    S.op("act", "activation", [pkA], [kZ], out=Z[:, :].rearrange("p (b f) -> p b f", b=3)[:, 0:3:2, :],
                                         in_=psA[:, 0:384].rearrange("p (b f) -> p b f", b=3)[:, 0:3:2, :], func=AF.Copy)
                                if lev >= 2:
                                    S.op("dve", "tensor_tensor", [pkA, kZ], [kZ], out=Z[:, 128:256], in0=psA[:, 128:256], in1=Z[:, 128:256], op=ALU.add)
                                yield
                            psM, pkM = PS()
                            mm(psM[:, 0:128], pkM, Noff, Z[:, 128:256], ["Noff" + q_, kZ], True, True, last=False)
                            mm(psM[:, 256:384], pkM, Z[:, 128:256], identR[:], [kZ, "identR"], True, True)
                            S.op("act", "activation", [pkM], [kZ], out=Z[:, :].rearrange("p (b f) -> p b f", b=3)[:, 0:3:2, :],
                                 in_=psM[:, 0:384].rearrange("p (b f) -> p b f", b=3)[:, 0:3:2, :], func=AF.Copy)
                            yield
                            psM2, pkM2 = PS()
                            mm(psM2[:, 0:128], pkM2, Z[:, 256:384], Z[:, 0:128], [kZ], True, True)
                            S.op("dve", "tensor_tensor", [pkM2, kZ], [kZ], out=Z[:, 128:256], in0=psM2[:, 0:128], in1=Z[:, 128:256], op=ALU.add)
                            yield
                            TT_ = Z[:, 128:256]
                            tk = kZ
                            psU, pkU = PS()
                            mm(psU[:, 0:256], pkU, TT_, bv[:], [tk, "bv" + q_], True, True, last=False)
                            mm(psU[:, 256:384], pkU, kbeg[:], TT_, [tk, "kbeg" + q_], True, True)
                            S.op("dve", "tensor_copy", [pkU], [sk + "u"], out=slot["u"][:], in_=psU[:, 0:256])
                            S.op("act", "activation", [pkU], [sk + "nwT"], out=slot["nwT"][:], in_=psU[:, 256:384], func=AF.Copy, scale=-1.0)
                            yield

                        cur = [0, 0]
                        c16 = [0, 0]
                        ready = {}
                        done_tiles = [0, 0]
                        nhold = [0]

                        def chain_gen(r):
                            c = 4 * r + h
                            od = odf if r == 0 else odb
                            odk = "odf" if r == 0 else "odb"
                            vk = "vn%d" % r
                            for k in range(NT):
                                tt = k if r == 0 else NT - 1 - k
                                while not ready.get((k, r)):
                                    yield
                                slot = ring[(r, k % RING)]
                                sk = "ring%d%d" % (r, k % RING)
                                s32k = "S32_%d_%d" % (r, cur[r])
                                s16k = "S16_%d_%d" % (r, c16[r])
                                s16n = "S16_%d_%d" % (r, 1 - c16[r])
                                Scur = S16[r][c16[r]]
                                Snew = S16[r][1 - c16[r]]
                                psV, pkV = PS()
                                mm(psV[:, 0:256], pkV, slot["nwT"][:], Scur[:], [sk + "nwT", s16k], True, True)
                                S.op("dve", "tensor_tensor", [pkV, sk + "u"], [vk], out=vn[r][:], in0=psV[:, 0:256], in1=slot["u"][:], op=ALU.add)
                                yield
                                psO, pkO = PS()
                                psS, pkS = PS()
                                mm(psS[:, 0:256], pkS, slot["khat"][:], vn[r][:], [sk + "khat", vk], True, True)
                                mm(psO[:, 0:256], pkO, slot["qgT"][:], Scur[:], [sk + "qgT", s16k], True, False)
                                mm(psO[:, 0:256], pkO, slot["sT"][:], vn[r][:], [sk + "sT", vk], False, True)
                                bnd = (tt % 2 == 1) if r == 0 else (tt % 2 == 0)
                                S.op("dve", "scalar_tensor_tensor", [pkS, s32k, "EGL"], [s32k], out=S32[r][cur[r]][:], in0=S32[r][cur[r]][:],
                                     scalar=EGL[:, tt, c:c + 1], in1=psS[:, 0:256], op0=ALU.mult, op1=ALU.add)
                                if bnd:
                                    cur[r] = state_boundary(r, cur[r], tt // 2, h, sdn_o)
                                S.op("act", "activation", ["S32_%d_%d" % (r, cur[r])], [s16n], out=Snew[:], in_=S32[r][cur[r]][:], func=AF.Copy)
                                S.op("act", "activation", [pkO], [odk], out=od[:, tt, :], in_=psO[:, 0:256], func=AF.Copy)
                                c16[r] = 1 - c16[r]
                                yield
                                done_tiles[r] += 1

                        for r in range(2):
                            S.dma("sp", S32[r][0][:], sdn_d[r, h], [], ["S32_%d_0" % r])
                            S.op("act", "activation", ["S32_%d_0" % r], ["S16_%d_0" % r], out=S16[r][0][:], in_=S32[r][0][:], func=AF.Copy)
                        chains = [chain_gen(0), chain_gen(1)]
                        chain_alive = [True, True]
                        nextk = [0, 0]
                        free_sets = list(range(NSET))
                        active = []
                        turn = 0
                        while any(chain_alive):
                            launched = True
                            while free_sets and launched:
                                launched = False
                                for rr in ((0, 1) if turn % 2 == 0 else (1, 0)):
                                    if free_sets and nextk[rr] < NT and nextk[rr] < done_tiles[rr] + RING:
                                        tsi = free_sets.pop(0)
                                        active.append((prep_gen(nextk[rr], rr, tsi), nextk[rr], rr, tsi))
                                        nextk[rr] += 1
                                        launched = True
                                        turn += 1
                            def chain_turn():
                                for r in range(2):
                                    if chain_alive[r]:
                                        try:
                                            next(chains[r])
                                        except StopIteration:
                                            chain_alive[r] = False
                            chain_turn()
                            CT = 2
                            for ii, item in enumerate(list(active)):
                                if ii > 0 and ii % CT == 0:
                                    chain_turn()
                                g, k, rr, tsi = item
                                try:
                                    next(g)
                                except StopIteration:
                                    active.remove(item)
                                    free_sets.append(tsi)
                                    ready[(k, rr)] = True
                        S.barrier()
                    def delB(tt, h=h, W=W, wk=wk):
                        def g(bi):
                            o32_, t32_, zb_, sm_, zg_ = o32[bi], t32[bi], zb[bi], sm[bi], zg[bi]
                            q_ = "_%d" % bi
                            ps, pk = PS()
                            for kc in range(8):
                                mm(ps[:, 0:256], pk, hT[:, kc, tt * 128:(tt + 1) * 128], W[:, kc, 512:768], [wk, "hT"], kc == 0, kc == 7)
                            S.op("act", "activation", [pk], ["zg" + q_], out=zg_[:], in_=ps[:, 0:256], func=AF.Silu)
                            S.op("dve", "tensor_tensor", ["odf", "odb"], ["do32" + q_], out=o32_[:], in0=odf[:, tt, :], in1=odb[:, tt, :], op=ALU.add)
                            yield
                            S.op("act", "activation", ["do32" + q_], ["dt32" + q_, "dsm0" + q_], out=t32_[:], in_=o32_[:], func=AF.Square, accum_out=sm_[:, 0:1])
                            yield
                            S.op("act", "activation", ["dsm0" + q_, "cst"], ["dsm1" + q_], out=sm_[:, 1:2], in_=sm_[:, 0:1], func=AF.Sqrt, scale=1.0 / 256, bias=C("EPS"))
                            yield
                            S.op("dve", "reciprocal", ["dsm1" + q_], ["dsm2" + q_], out=sm_[:, 2:3], in_=sm_[:, 1:2])
                            S.op("dve", "scalar_tensor_tensor", ["do32" + q_, "dsm2" + q_, "dnw"], ["dt32" + q_], out=t32_[:], in0=o32_[:], scalar=sm_[:, 2:3],
                                 in1=dnw[:], op0=ALU.mult, op1=ALU.mult)
                            S.op("dve", "tensor_tensor", ["dt32" + q_, "zg" + q_], ["dzb" + q_], out=zb_[:], in0=t32_[:], in1=zg_[:], op=ALU.mult)
                            yield
                            for c2 in range(2):
                                S.op("pe", "transpose", ["dzb" + q_, "identb"], ["psT"], inc=(c2 == 1), out=psT[:, c2 * 128:(c2 + 1) * 128],
                                     in_=zb_[:, c2 * 128:(c2 + 1) * 128], identity=identb[:])
                            S.op("act", "activation", ["psT"], ["zT"], out=zT[:, 2 * h:2 * h + 2, tt * 128:(tt + 1) * 128],
                                 in_=psT[:, 0:256].rearrange("p (c t) -> p c t", c=2), func=AF.Copy)
                            yield
                        return g
                    def laneB(tts, bi, delB=delB):
                        for tt in tts:
                            yield from delB(tt)(bi)
                    lanes = [laneB(list(range(bi, NT, WBD)), bi) for bi in range(WBD)]
                    if h < 3:
                        pendingB = lanes
                    else:
                        interleave([], 1, extras=lanes)
                S.barrier()

            STOP(4)
            mT = SB(sB, "mT", [128, 8, T], BF16)
            out_gate(BLK_DO, BLK_GD, False, mT, pre=nextW)

            with ExitStack() as s4:
                W4 = 4
                xb = [SB(s4, "x4b%d" % i, [128, D], F32) for i in range(W4)]
                x1 = xb
                tmp32 = [SB(s4, "tmp32b%d" % i, [128, D], F32) for i in range(W4)]
                hb = [SB(s4, "hb4_%d" % i, [128, D], BF16) for i in range(W4)]
                small = [SB(s4, "small4_%d" % i, [128, 8], F32) for i in range(W4)]
                small_n = [SB(s4, "small_n%d" % i, [128, 4], F32) for i in range(W4)]
                junk4 = hb
                modb4 = SB(s4, "modb4", [128, 3 * D], F32)
                S.dma("sp", modb4[:], modscr_d[0, 0:3 * D].partition_broadcast(128), [], ["modb"])
                G1 = modb4[:, 0:1024]
                B2 = modb4[:, 1024:2048]
                A2 = modb4[:, 2048:3072]
                Wo, wko = load_w(BLK_OUT)
                preFF = load_w(BLK_FF)

                def st4(tt):
                    def g(bi):
                        q_ = "_%d" % bi
                        xk = "x4b" + q_
                        x1k = xk
                        tk = "tmp4" + q_
                        sm_ = small[bi]
                        tsl = slice(tt * 128, (tt + 1) * 128)
                        S.dma("sp", xb[bi][:], x_d[tsl, :], [], [xk])
                        pa, pka = PS()
                        pb_, pkb = PS()
                        for kc in range(8):
                            mm(pa[:, :], pka, mT[:, kc, tsl], Wo[:, kc, 0:512], ["mT", wko], kc == 0, kc == 7)
                        for kc in range(8):
                            mm(pb_[:, :], pkb, mT[:, kc, tsl], Wo[:, kc, 512:1024], ["mT", wko], kc == 0, kc == 7)
                        S.op("act", "activation", [pka], ["hb4" + q_, "sa" + q_], out=junk4[bi][:, 0:512], in_=pa[:, :], func=AF.Square, accum_out=sm_[:, 0:1])
                        S.op("dve", "tensor_copy", [pka], [tk], out=tmp32[bi][:, 0:512], in_=pa[:, :])
                        S.op("act", "activation", [pkb], ["hb4" + q_, "sb" + q_], out=junk4[bi][:, 512:1024], in_=pb_[:, :], func=AF.Square, accum_out=sm_[:, 1:2])
                        S.op("dve", "tensor_copy", [pkb], [tk], out=tmp32[bi][:, 512:1024], in_=pb_[:, :])
                        yield
                        S.op("dve", "tensor_tensor", ["sa" + q_, "sb" + q_], ["sc" + q_], out=sm_[:, 2:3], in0=sm_[:, 0:1], in1=sm_[:, 1:2], op=ALU.add)
                        yield
                        S.op("act", "activation", ["sc" + q_, "cst"], ["sd" + q_], out=sm_[:, 3:4], in_=sm_[:, 2:3], func=AF.Sqrt, scale=1.0 / D, bias=C("EPS"))
                        yield
                        S.op("dve", "reciprocal", ["sd" + q_], ["se" + q_], out=sm_[:, 4:5], in_=sm_[:, 3:4])
                        S.op("dve", "scalar_tensor_tensor", ["se" + q_, "modb", tk], [tk], out=tmp32[bi][:], in0=tmp32[bi][:], scalar=sm_[:, 4:5],
                             in1=G1, op0=ALU.mult, op1=ALU.mult)
                        yield
                        S.op("dve", "tensor_tensor", [tk, xk], [x1k], out=x1[bi][:], in0=tmp32[bi][:], in1=xb[bi][:], op=ALU.add)
                        S.dma("sp", y_d[tsl, :], x1[bi][:], [x1k], ["y%d" % tt])
                        yield
                        yield from norm_to_hT_gen(x1[bi][:], x1k, tt, A2, B2, tmp32[bi], tk, hb[bi], "hb4" + q_, small_n[bi], "sn" + q_)
                    return g
                interleave([st4(tt) for tt in range(NT)], W4)
                S.barrier()

        STOP(6)
        with ExitStack() as s5:
            wffo = SB(s5, "wffo", [128, NFF, D], BF16)
            S.dma("pool", wffo[:], wffo_d.rearrange("(j p) c -> p j c", p=128), [], ["wffo"])
            aT = SB(s5, "aT", [128, NFF, 1024], BF16)
            sg = SB(s5, "sg", [128, 512], F32)
            x1b = [SB(s5, "x5b%d" % i, [128, D], F32) for i in range(2)]
            yb = [SB(s5, "y5b%d" % i, [128, D], F32) for i in range(2)]
            tmp32 = SB(s5, "tmp32c", [128, D], F32)
            small = SB(s5, "small5", [128, 8], F32)
            G2 = SB(s5, "modb5", [128, D], F32)
            S.dma("sp", G2[:], modscr_d[0, 3 * D:4 * D].partition_broadcast(128), [], ["modb"])
            for th in range(2):
                for b in range(6):
                    W, wk = preFF if (th == 0 and b == 0) else load_w(BLK_FF + b)
                    nj = 4 if b < 5 else 2
                    for jj in range(nj):
                        j = b * 4 + jj
                        for tg in range(2):
                            ts_ = slice(th * 1024 + tg * 512, th * 1024 + (tg + 1) * 512)
                            pg, pkg = PS()
                            pu, pku = PS()
                            for kc in range(8):
                                mm(pg[:, :], pkg, W[:, kc, jj * 128:(jj + 1) * 128], hT[:, kc, ts_], [wk, "hT"], kc == 0, kc == 7)
                            for kc in range(8):
                                mm(pu[:, :], pku, W[:, kc, 512 + jj * 128:512 + (jj + 1) * 128], hT[:, kc, ts_], [wk, "hT"], kc == 0, kc == 7)
                            S.op("act", "activation", [pkg], ["sg"], out=sg[:], in_=pg[:, :], func=AF.Silu)
                            S.op("dve", "tensor_tensor", [pku, "sg"], ["aT"], out=aT[:, j, tg * 512:(tg + 1) * 512], in0=pu[:, :], in1=sg[:], op=ALU.mult)
                for t8 in range(8):
                    tt = th * 8 + t8
                    tsl = slice(tt * 128, (tt + 1) * 128)
                    xk = "x5b%d" % (tt % 2)
                    yk = "y5b%d" % (tt % 2)
                    S.dma("sp", x1b[tt % 2][:], y_d[tsl, :], ["y%d" % tt], [xk])
                    pa, pka = PS()
                    pb_, pkb = PS()
                    for j in range(NFF):
                        mm(pa[:, :], pka, aT[:, j, t8 * 128:(t8 + 1) * 128], wffo[:, j, 0:512], ["aT", "wffo"], j == 0, j == NFF - 1)
                    for j in range(NFF):
                        mm(pb_[:, :], pkb, aT[:, j, t8 * 128:(t8 + 1) * 128], wffo[:, j, 512:1024], ["aT", "wffo"], j == 0, j == NFF - 1)
                    S.op("act", "activation", [pka], ["junk5", "sm5a"], out=tmp32[:, 0:512], in_=pa[:, :], func=AF.Square, accum_out=small[:, 0:1])
                    S.op("act", "activation", [pkb], ["junk5", "sm5b"], out=tmp32[:, 512:1024], in_=pb_[:, :], func=AF.Square, accum_out=small[:, 1:2])
                    S.op("dve", "tensor_tensor", ["sm5a", "sm5b"], ["sm5c"], out=small[:, 2:3], in0=small[:, 0:1], in1=small[:, 1:2], op=ALU.add)
                    S.op("act", "activation", ["sm5c", "cst"], ["sm5d"], out=small[:, 3:4], in_=small[:, 2:3], func=AF.Sqrt, scale=1.0 / D, bias=C("EPS"))
                    S.op("dve", "reciprocal", ["sm5d"], ["sm5e"], out=small[:, 4:5], in_=small[:, 3:4])
                    S.op("dve", "scalar_tensor_tensor", [pka, "sm5e", "modb", "junk5"], ["junk5"], out=tmp32[:, 0:512], in0=pa[:, :], scalar=small[:, 4:5],
                         in1=G2[:, 0:512], op0=ALU.mult, op1=ALU.mult)
                    S.op("dve", "scalar_tensor_tensor", [pkb, "sm5e", "modb", "junk5"], ["junk5"], out=tmp32[:, 512:1024], in0=pb_[:, :], scalar=small[:, 4:5],
                         in1=G2[:, 512:1024], op0=ALU.mult, op1=ALU.mult)
                    S.op("dve", "tensor_tensor", ["junk5", xk], [yk], out=yb[tt % 2][:], in0=tmp32[:], in1=x1b[tt % 2][:], op=ALU.add)
                    S.dma("sp", y_d[tsl, :], yb[tt % 2][:], [yk], ["y%d" % tt])
            S.barrier()
        print("INSTR", dict(S.count), "dma", S.di)


_PROG = {}
_RETURN_MAPS = [False]


def _rope_tables():
    L = 2048
    rows = L // 64
    row_idx = np.repeat(np.arange(rows, dtype=np.float32), 64)
    col_idx = (np.arange(L) % 64).astype(np.float32)
    n_freq = 32
    freqs = (np.float32(10000.0) ** (-np.arange(n_freq, dtype=np.float32) / np.float32(n_freq))).astype(np.float32)
    ang = np.concatenate([row_idx[:, None] * freqs, col_idx[:, None] * freqs], axis=-1).astype(np.float32)
    cos = np.cos(ang).astype(np.float32)
    sin = np.sin(ang).astype(np.float32)
    d = np.arange(128)
    c = cos[:, d // 2].T.copy()
    s = sin[:, d // 2].T.copy()
    s[0::2, :] *= -1.0
    return np.ascontiguousarray(c), np.ascontiguousarray(s)


def _pack_weights(w_in, w_ret_o, w_dn_o, w_out, w_ffn_in):
    blk = np.zeros((NBLK, D, D), np.float32)
    swap = np.arange(128) ^ 1
    for h in range(4):
        q = w_in[:, h * 128:(h + 1) * 128]
        k = w_in[:, 512 + h * 128:512 + (h + 1) * 128]
        blk[BLK_R + h, :, 0:128] = q
        blk[BLK_R + h, :, 128:256] = q[:, swap]
        blk[BLK_R + h, :, 256:384] = k
        blk[BLK_R + h, :, 384:512] = k[:, swap]
        blk[BLK_R + h, :, 512:768] = w_in[:, 1024 + h * 256:1024 + (h + 1) * 256]
        blk[BLK_R + h, :, 768:1024] = w_in[:, 2048 + h * 256:2048 + (h + 1) * 256]
        blk[BLK_D + h, :, 0:128] = w_in[:, 3072 + h * 128:3072 + (h + 1) * 128]
        blk[BLK_D + h, :, 128:256] = w_in[:, 3584 + h * 128:3584 + (h + 1) * 128]
        blk[BLK_D + h, :, 256:512] = w_in[:, 4096 + h * 256:4096 + (h + 1) * 256]
        blk[BLK_D + h, :, 512:768] = w_in[:, 5120 + h * 256:5120 + (h + 1) * 256]
    blk[BLK_GR] = w_in[:, 6160:7184]
    blk[BLK_GD] = w_in[:, 7184:8208]
    blk[BLK_RO] = w_ret_o
    blk[BLK_DO] = w_dn_o
    blk[BLK_OUT] = w_out
    for b in range(6):
        n = 512 if b < 5 else 256
        blk[BLK_FF + b, :, 0:n] = w_ffn_in[:, b * 512:b * 512 + n]
        blk[BLK_FF + b, :, 512:512 + n] = w_ffn_in[:, DFF + b * 512:DFF + b * 512 + n]
    w_ba = np.ascontiguousarray(w_in[:, 6144:6160])
    return blk, w_ba


def kernel(x_prompt, x_sample, c, state_ret, state_dn, c_ctx, w_mod, b_mod, norm_w, w_in, conv_w, ret_decay_logit,
           ret_gn_w, dn_a_log, dn_dt_bias, dn_norm_w, w_ret_o, w_dn_o, w_out, w_ffn_in, w_ffn_out):
    f = lambda a: np.ascontiguousarray(np.asarray(a, dtype=np.float32))
    x_prompt, x_sample, c, state_ret, state_dn, c_ctx = map(f, (x_prompt, x_sample, c, state_ret, state_dn, c_ctx))
    w_in0 = f(w_in)[0]
    blk, w_ba = _pack_weights(w_in0, f(w_ret_o)[0], f(w_dn_o)[0], f(w_out)[0], f(w_ffn_in)[0])
    ropec, ropes = _rope_tables()
    ones_c = np.ones((128, T), np.float32)
    zeros_c = np.zeros((128, T), np.float32)
    shared = {
        "w_mod": f(w_mod)[0], "b_mod": f(b_mod)[0], "norm_w": f(norm_w)[0].reshape(-1), "wblk": blk, "w_ba": w_ba,
        "w_ffo": f(w_ffn_out)[0], "conv_w": f(conv_w)[0], "ret_decay_logit": f(ret_decay_logit)[0].reshape(-1),
        "ret_gn_w": f(ret_gn_w)[0], "dn_a_log": f(dn_a_log)[0].reshape(-1), "dn_dt_bias": f(dn_dt_bias)[0].reshape(-1),
        "dn_norm_w": f(dn_norm_w)[0], "cst": _build_cst(), "ident": np.eye(128, dtype=np.float32),
    }
    in_maps = []
    for core in range(8):
        m = dict(shared)
        if core < 4:
            m["x"] = x_sample[core]
            m["cond"] = c[core]
            m["st_ret"] = np.ascontiguousarray(state_ret[core, 0])
            m["st_dn"] = np.ascontiguousarray(state_dn[core, 0])
            m["rope_c"] = ropec
            m["rope_s"] = ropes
            m["flag"] = np.ones((128, 1), np.float32)
        else:
            j = core - 4
            xs = np.zeros((T, D), np.float32)
            xs[0:1024] = x_prompt[4 * j:4 * j + 4].reshape(1024, D)
            xs[1024:2048] = xs[0:1024]
            m["x"] = xs
            m["cond"] = c_ctx
            m["st_ret"] = np.zeros((2, 4, 128, 256), np.float32)
            m["st_dn"] = np.zeros((2, 4, 128, 256), np.float32)
            m["rope_c"] = ones_c
            m["rope_s"] = zeros_c
            m["flag"] = np.zeros((128, 1), np.float32)
        in_maps.append(m)
    if _RETURN_MAPS[0]:
        return in_maps
    if "nc" not in _PROG:
        _PROG["nc"] = build_program()
    res = run_bass_kernel_spmd(_PROG["nc"], in_maps, core_ids=list(range(8)))
    rr = res.results
    y_sample = np.stack([rr[i]["y"] for i in range(4)], 0)
    y_prompt = np.concatenate([rr[4 + j]["y"][0:1024].reshape(4, 256, D) for j in range(4)], 0)
    new_ret = np.concatenate([rr[4 + j]["sret_o"] for j in range(4)], 0)[:, None]
    new_dn = np.concatenate([rr[4 + j]["sdn_o"] for j in range(4)], 0)[:, None]
    return (y_prompt.astype(np.float32), y_sample.astype(np.float32), new_ret.astype(np.float32), new_dn.astype(np.float32))
```

```python
import numpy as np
from contextlib import ExitStack
import concourse.bass as bass
import concourse.mybir as mybir
from concourse.bass_utils import run_bass_kernel_spmd

F32 = mybir.dt.float32
F32R = mybir.dt.float32r
BF16 = mybir.dt.bfloat16
AF = mybir.ActivationFunctionType
ALU = mybir.AluOpType

T = 2048
D = 1024
NT = 16
EPS = 1e-6
BIG = 30000.0
DFF = 2816
NFF = 22
DBGR = 0

BLK_R = 0
BLK_D = 4
BLK_GR = 8
BLK_GD = 9
BLK_RO = 10
BLK_DO = 11
BLK_OUT = 12
BLK_FF = 13
NBLK = 19


def _cst_layout():
    names = [("ONES", 128),
             ("BLKM", 128), ("I128", 128), ("TRILO128", 128), ("TRIUP128", 128), ("NEGONES128", 128),
             ("MASKF128", 256), ("MASKB128", 256), ("SMF128", 256), ("SMB128", 256),
             ("DPOS8", 128), ("DNEG8", 128), ("GE8", 128), ("LE8", 128), ("POS1_8", 128), ("POS128M", 128),
             ("KPF", 1), ("KPB", 1), ("EPS", 1), ("EPS128", 1), ("ONE", 1), ("C64", 1), ("ZERO", 1)]
    off = {}
    o = 0
    for n, w in names:
        off[n] = (o, w)
        o += w
    return off, o


CST_OFF, NCST = _cst_layout()


def _build_cst():
    c = np.zeros((128, NCST), np.float32)
    p = np.arange(128)[:, None]
    pp = p % 64

    def put(name, arr):
        if name not in CST_OFF:
            return
        o, w = CST_OFF[name]
        c[:, o:o + w] = arr

    f64 = np.arange(64)[None, :]
    f128 = np.arange(128)[None, :]
    put("I2", (pp == f64))
    put("TRILO", (pp <= f64))
    put("TRIUP", (pp >= f64))
    put("ONES", np.ones((128, 128)))
    put("NEGONES", -np.ones((128, 64)))
    put("SELH0", np.broadcast_to(p < 64, (128, 128)))
    put("SELH1", np.broadcast_to(p >= 64, (128, 128)))
    put("BD1", (p // 64 == f128 // 64))
    put("BDLO", (p // 64 == f128 // 64) & (pp <= f128 % 64))
    put("BDUP", (p // 64 == f128 // 64) & (pp >= f128 % 64))
    ge = (f64 >= pp).astype(np.float32)
    le = (f64 <= pp).astype(np.float32)
    gt = (f64 > pp).astype(np.float32)
    lt = (f64 < pp).astype(np.float32)
    nge = -BIG * (1 - ge)
    nle = -BIG * (1 - le)
    put("MASKF", np.concatenate([nge, nle], 1))
    put("MASKB", np.concatenate([nle, nge], 1))
    put("SMF", np.concatenate([-gt, -lt], 1))
    put("SMB", np.concatenate([-lt, -gt], 1))
    put("DPOS", np.maximum(f64 - pp, 0))
    put("DNEG", np.maximum(pp - f64, 0))
    put("GE", ge)
    put("LE", le)
    put("POS1", np.broadcast_to(f64 + 1.0, (128, 64)))
    put("POS64M", np.broadcast_to(64.0 - f64, (128, 64)))
    put("I128", (p == f128))
    put("TRILO128", (p <= f128))
    put("TRIUP128", (p >= f128))
    put("NEGONES128", -np.ones((128, 128)))
    ge8 = (f128 >= p).astype(np.float32)
    le8 = (f128 <= p).astype(np.float32)
    gt8 = (f128 > p).astype(np.float32)
    lt8 = (f128 < p).astype(np.float32)
    put("BLKM", (f128 // 64 == p // 64))
    put("MASKF128", np.concatenate([-BIG * (1 - ge8), -BIG * (1 - le8)], 1))
    put("MASKB128", np.concatenate([-BIG * (1 - le8), -BIG * (1 - ge8)], 1))
    put("SMF128", np.concatenate([-gt8, -lt8], 1))
    put("SMB128", np.concatenate([-lt8, -gt8], 1))
    put("DPOS8", np.maximum(f128 - p, 0))
    put("DNEG8", np.maximum(p - f128, 0))
    put("GE8", (f128 >= p))
    put("LE8", (f128 <= p))
    put("POS1_8", np.broadcast_to(f128 + 1.0, (128, 128)))
    put("POS128M", np.broadcast_to(128.0 - f128, (128, 128)))
    put("KPF", 127.0 - p)
    put("KPB", p * 1.0)
    put("EPS", np.full((128, 1), EPS))
    put("EPS128", np.full((128, 1), 128 * EPS))
    put("ONE", np.ones((128, 1)))
    put("C64", np.full((128, 1), 64.0))
    return c


class Sched:
    EPOCH = 3000
    NDMA = 24

    def __init__(self, nc, es):
        self.nc = nc
        self.es = es
        self.eng = {"pe": nc.tensor, "dve": nc.vector, "act": nc.scalar, "pool": nc.gpsimd, "sp": nc.sync}
        self.count = {e: 0 for e in self.eng}
        self.sems = {e: [] for e in self.eng}
        self.waited = {e: {} for e in self.eng}
        self.res = {}
        self.pending = {e: ([], []) for e in self.eng}
        self.dsem = [es.enter_context(nc.semaphore("dsem%d" % i)) for i in range(self.NDMA)]
        self.duse = [0] * self.NDMA
        self.qpool = {"sp": list(range(0, 16)), "pool": list(range(16, self.NDMA)), "act": list(range(0, 16))}
        self.qi = {"sp": 0, "pool": 0, "act": 0}
        self.di = 0
        self.semid = {}
        self.all_dma_events = []

    def _sid(self, sem):
        k = id(sem)
        if k not in self.semid:
            self.semid[k] = len(self.semid)
        return self.semid[k]

    def _tl_sem(self, e, epoch):
        while len(self.sems[e]) <= epoch:
            self.sems[e].append(self.es.enter_context(self.nc.semaphore("tl_%s_%d" % (e, len(self.sems[e])))))
        return self.sems[e][epoch]

    def _wait(self, e, ev):
        if ev is None:
            return
        sem, val = ev[0], ev[1]
        sid = self._sid(sem)
        if self.waited[e].get(sid, 0) >= val:
            return
        self.eng[e].wait_ge(sem, val)
        self.waited[e][sid] = val

    def _deps(self, e, reads, writes):
        evs = []
        for k in reads:
            r = self.res.get(k)
            if r is not None and r["w"] is not None:
                evs.append(r["w"])
        for k in writes:
            r = self.res.get(k)
            if r is not None:
                if r["w"] is not None:
                    evs.append(r["w"])
                evs.extend(r["r"].values())
        for ev in evs:
            if ev[0] == "PENDING":
                if ev[1] != e:
                    raise RuntimeError("dependency on pending (no-inc) instruction of %s" % ev[1])
                continue
            self._wait(e, ev)

    def _commit(self, ev, reads, writes, tl):
        for k in writes:
            self.res[k] = {"w": ev, "r": {}}
        for k in reads:
            r = self.res.setdefault(k, {"w": None, "r": {}})
            r["r"][tl] = ev

    def op(self, e, method, reads, writes, inc=True, **kw):
        if e != "pe":
            writes = list(writes) + [k for k in reads if k.startswith("ps")]
        self._deps(e, reads, writes)
        inst = getattr(self.eng[e], method)(**kw)
        if not inc:
            pr, pw = self.pending[e]
            pr.extend(reads)
            pw.extend(writes)
            for k in writes:
                self.res[k] = {"w": ("PENDING", e), "r": {}}
            return inst
        i = self.count[e]
        self.count[e] += 1
        sem = self._tl_sem(e, i // self.EPOCH)
        val = i % self.EPOCH + 1
        inst.then_inc(sem, 1)
        ev = (sem, val)
        pr, pw = self.pending[e]
        self._commit(ev, list(reads) + pr, list(writes) + pw, "tl_" + e + str(i // self.EPOCH))
        self.pending[e] = ([], [])
        return inst

    def dma(self, q, out, in_, reads, writes, **kw):
        self._deps(q, reads, writes)
        pl = self.qpool[q]
        s = pl[self.qi[q] % len(pl)]
        self.qi[q] += 1
        self.di += 1
        sem = self.dsem[s]
        self._wait(q, (sem, 16 * self.duse[s]))
        self.duse[s] += 1
        val = 16 * self.duse[s]
        self.eng[q].dma_start(out=out, in_=in_, **kw).then_inc(sem, 16)
        ev = (sem, val)
        self._commit(ev, reads, writes, "dma%d_%d" % (s, self.duse[s]))
        self.all_dma_events.append(ev)
        if len(self.all_dma_events) > 2 * self.NDMA:
            self.all_dma_events = self.all_dma_events[-self.NDMA:]

    def barrier(self):
        evs = []
        for e in self.eng:
            if self.count[e] > 0:
                i = self.count[e] - 1
                evs.append((self.sems[e][i // self.EPOCH], i % self.EPOCH + 1))
        for s in range(self.NDMA):
            if self.duse[s] > 0:
                evs.append((self.dsem[s], 16 * self.duse[s]))
        for e in self.eng:
            for ev in evs:
                self._wait(e, ev)
        self.res = {}


class _Stop(Exception):
    pass


def build_program(upto=99):
    nc = bass.Bass("TRN2", target_bir_lowering=False)
    try:
        _build_body(nc, upto)
    except _Stop:
        pass
    return nc


def _build_body(nc, upto):

    def din(name, shape, dt=F32):
        return nc.dram_tensor(name, shape, dt, kind="ExternalInput").ap()

    def dout(name, shape, dt=F32):
        return nc.dram_tensor(name, shape, dt, kind="ExternalOutput").ap()

    x_d = din("x", [T, D])
    cond_d = din("cond", [D])
    wmod_d = din("w_mod", [D, 6 * D])
    bmod_d = din("b_mod", [6 * D])
    normw_d = din("norm_w", [4 * D])
    wblk_d = din("wblk", [NBLK, D, D])
    wba_d = din("w_ba", [D, 16])
    wffo_d = din("w_ffo", [DFF, D])
    convw_d = din("conv_w", [3, 2048])
    rdl_d = din("ret_decay_logit", [8])
    gnw_d = din("ret_gn_w", [1024])
    alog_d = din("dn_a_log", [8])
    dtb_d = din("dn_dt_bias", [8])
    dnw_d = din("dn_norm_w", [256])
    sret_d = din("st_ret", [2, 4, 128, 256])
    sdn_d = din("st_dn", [2, 4, 128, 256])
    ropec_d = din("rope_c", [128, T])
    ropes_d = din("rope_s", [128, T])
    flag_d = din("flag", [128, 1])
    cst_d = din("cst", [128, NCST])
    ident_d = din("ident", [128, 128])
    y_d = dout("y", [T, D])
    sret_o = dout("sret_o", [4, 2, 4, 128, 256])
    sdn_o = dout("sdn_o", [4, 2, 4, 128, 256])
    yrg_d = nc.dram_tensor("yrg_scr", [8, 128, T], BF16, kind="Internal").ap()
    modscr_d = nc.dram_tensor("mod_scr", [1, 4 * D], F32, kind="Internal").ap()

    with ExitStack() as es:
        S = Sched(nc, es)

        dbgn = [0]

        def DBG(name, ap, shape, dt, reads):
            d = nc.dram_tensor("dbg_" + name, shape, dt, kind="ExternalOutput").ap()
            S.dma("sp", d, ap, reads, [])

        def STOP(k):
            if upto == k:
                S.barrier()
                print("STOP at", k, "INSTR", dict(S.count), "dma", S.di)
                raise _Stop()

        def MEM(tag):
            pass

        sbn = [0]

        def SB(stack, name, shape, dt):
            sbn[0] += 1
            return stack.enter_context(nc.sbuf_tensor("s%d_%s" % (sbn[0], name), shape, dt))

        psb = [es.enter_context(nc.psum_tensor("psb%d" % i, [128, 512], F32)) for i in range(7)]
        psT = es.enter_context(nc.psum_tensor("psT", [128, 1024], BF16))
        psi = [0]

        held = set()

        def PS():
            while True:
                i = psi[0] % 7
                psi[0] += 1
                if i not in held:
                    return psb[i], "psb%d" % i

        cst = SB(es, "cst", [128, NCST], F32)
        S.dma("sp", cst[:], cst_d[:, :], [], ["cst"])

        def C(name, rows=slice(0, 128), w=None):
            o, ww = CST_OFF[name]
            if w is None:
                w = ww
            return cst[rows, o:o + w]

        identb = SB(es, "identb", [128, 128], BF16)
        S.dma("pool", identb[:], ident_d[:, :], [], ["identb"])
        flag = SB(es, "flag", [128, 1], F32)
        S.dma("sp", flag[:], flag_d[:, :], [], ["flag"])
        hT = SB(es, "hT", [128, 8, T], BF16)
        wbuf = [SB(es, "wbuf%d" % i, [128, 8, D], BF16) for i in range(2)]
        wslot = [0]
        s01 = ExitStack()
        s01.__enter__()
        modA = SB(s01, "modA", [128, 2 * D], F32)
        modb = SB(s01, "modb", [128, 4 * D], F32)

        def load_w(blk):
            i = wslot[0] % 2
            wslot[0] += 1
            S.dma("pool", wbuf[i][:], wblk_d[blk].rearrange("(k p) c -> p k c", p=128), [], ["wbuf%d" % i])
            return wbuf[i], "wbuf%d" % i

        def mm(ps, pk, lhsT, rhs, rk, start, stop, last=None):
            if last is None:
                last = stop
            S.op("pe", "matmul", rk + ["cst"], [pk], inc=last, out=ps, lhsT=lhsT, rhs=rhs, start=start, stop=stop)

        MEM("global")
        s0 = ExitStack()
        s0.__enter__()
        normw = SB(s0, "normw", [128, 4 * D], F32)
        wm = [SB(s0, "wm%d" % i, [128, 8, 512], F32) for i in range(2)]
        cond = SB(s0, "cond", [128, 8], F32)
        csil = SB(s0, "csil", [128, 8], F32)
        cb = SB(s0, "cb", [128, 8, 128], F32)
        S.dma("sp", modA[:], bmod_d[0:2 * D].partition_broadcast(128), [], ["modA"])
        S.dma("sp", modb[:], bmod_d[2 * D:6 * D].partition_broadcast(128), [], ["modb"])
        S.dma("sp", normw[:], normw_d.partition_broadcast(128), [], ["normw"])
        S.dma("sp", cond[:], cond_d.rearrange("(k p) -> p k", p=128), [], ["cond"], allow_slow_non_contiguous=True)
        S.op("act", "activation", ["cond"], ["csil"], out=csil[:], in_=cond[:], func=AF.Silu)
        S.op("dve", "tensor_copy", ["csil"], ["cb"], out=cb[:], in_=csil[:].unsqueeze(2).to_broadcast([128, 8, 128]))

        def mod_cols(cg):
            w_ = wm[cg % 2]
            wk = "wm%d" % (cg % 2)
            S.dma("sp", w_[:], wmod_d[:, cg * 512:(cg + 1) * 512].rearrange("(k p) c -> p k c", p=128), [], [wk])
            ps, pk = PS()
            for kc in range(8):
                mm(ps[:, :], pk, cb[:, kc, :], w_[:, kc, :], ["cb", wk], kc == 0, kc == 7)
            if cg < 4:
                msl = modA[:, cg * 512:(cg + 1) * 512]
                mk = "modA"
            else:
                msl = modb[:, (cg - 4) * 512:(cg - 3) * 512]
                mk = "modb"
            S.op("dve", "tensor_tensor", [pk, mk], [mk], out=msl, in0=msl, in1=ps[:, :], op=ALU.add)

        for cg in range(4):
            mod_cols(cg)
        S.op("dve", "scalar_tensor_tensor", ["modA", "normw"], ["modA"], out=modA[:, 1024:2048], in0=modA[:, 1024:2048],
             scalar=1.0, in1=normw[:, 0:1024], op0=ALU.add, op1=ALU.mult)

        def stage0b():
            for cg in range(4, 12):
                mod_cols(cg)
                yield
            S.op("dve", "tensor_tensor", ["modb", "normw"], ["modb"], out=modb[:, 0:1024], in0=modb[:, 0:1024],
                 in1=normw[:, 1024:2048], op=ALU.mult)
            S.op("dve", "scalar_tensor_tensor", ["modb", "normw"], ["modb"], out=modb[:, 2048:3072], in0=modb[:, 2048:3072],
                 scalar=1.0, in1=normw[:, 2048:3072], op0=ALU.add, op1=ALU.mult)
            S.op("dve", "tensor_tensor", ["modb", "normw"], ["modb"], out=modb[:, 3072:4096], in0=modb[:, 3072:4096],
                 in1=normw[:, 3072:4096], op=ALU.mult)
            S.dma("sp", modscr_d[0:1, :], modb[0:1, :], ["modb"], ["modscr"])
            yield

        B1 = modA[:, 0:1024]
        A1 = modA[:, 1024:2048]

        def interleave(factories, width, extra=None, extras=None):
            extras = list(extras or [])
            active = []
            free = list(range(width))
            it = iter(factories)
            pending = True
            while pending or active:
                while free and pending:
                    try:
                        f = next(it)
                    except StopIteration:
                        pending = False
                        break
                    bi = free.pop(0)
                    active.append((f(bi), bi))
                if extra is not None:
                    try:
                        next(extra)
                    except StopIteration:
                        extra = None
                for eg_ in list(extras):
                    try:
                        next(eg_)
                    except StopIteration:
                        extras.remove(eg_)
                for item in list(active):
                    try:
                        next(item[0])
                    except StopIteration:
                        active.remove(item)
                        free.append(item[1])
            while extra is not None:
                try:
                    next(extra)
                except StopIteration:
                    extra = None
            while extras:
                for eg_ in list(extras):
                    try:
                        next(eg_)
                    except StopIteration:
                        extras.remove(eg_)

        def norm_to_hT_gen(src, srck, tt, A, B, tmp32, tk, hb, hk, small, smk, modk="modb"):
            S.op("act", "activation", [srck], [tk, smk + "0"], out=tmp32[:], in_=src, func=AF.Square, accum_out=small[:, 0:1])
            yield
            S.op("act", "activation", [smk + "0", "cst"], [smk + "1"], out=small[:, 1:2], in_=small[:, 0:1], func=AF.Sqrt,
                 scale=1.0 / D, bias=C("EPS"))
            yield
            S.op("dve", "reciprocal", [smk + "1"], [smk + "2"], out=small[:, 2:3], in_=small[:, 1:2])
            S.op("dve", "scalar_tensor_tensor", [srck, smk + "2", modk], [tk], out=tmp32[:], in0=src, scalar=small[:, 2:3],
                 in1=A, op0=ALU.mult, op1=ALU.mult)
            yield
            S.op("dve", "tensor_tensor", [tk, modk], [hk], out=hb[:], in0=tmp32[:], in1=B, op=ALU.add)
            yield
            for kc in range(8):
                S.op("pe", "transpose", [hk, "identb"], ["psT"], inc=(kc == 7), out=psT[:, kc * 128:(kc + 1) * 128],
                     in_=hb[:, kc * 128:(kc + 1) * 128], identity=identb[:])
            S.op("act", "activation", ["psT"], ["hT"], out=hT[:, :, tt * 128:(tt + 1) * 128],
                 in_=psT[:, :].rearrange("p (k t) -> p k t", k=8), func=AF.Copy)
            yield

        preR0 = load_w(BLK_R)
        with ExitStack() as s1:
            W1 = 4
            xb = [SB(s1, "xb%d" % i, [128, D], F32) for i in range(W1)]
            tmp32 = [SB(s1, "tmp32_%d" % i, [128, D], F32) for i in range(W1)]
            hb = [SB(s1, "hb%d" % i, [128, D], BF16) for i in range(W1)]
            small = [SB(s1, "small%d" % i, [128, 4], F32) for i in range(W1)]

            def st1(tt):
                def g(bi):
                    xk = "xb%d" % bi
                    S.dma("sp", xb[bi][:], x_d[tt * 128:(tt + 1) * 128, :], [], [xk])
                    yield
                    yield from norm_to_hT_gen(xb[bi][:], xk, tt, A1, B1, tmp32[bi], "tmp%d" % bi, hb[bi], "hb%d" % bi, small[bi], "sm%d_" % bi, "modA")
                return g
            interleave([st1(tt) for tt in range(NT)], W1, extra=stage0b())
            S.barrier()
        s0.__exit__(None, None, None)
        s01.__exit__(None, None, None)
        STOP(1)

        MEM("before mixers")
        with ExitStack() as sB:
            zT = SB(sB, "zT", [128, 8, T], BF16)
            par = SB(sB, "par", [128, 64], F32)
            lg = SB(sB, "lg", [128, 8], F32)
            dnw = SB(sB, "dnw", [128, 256], F32)
            S.dma("sp", dnw[:], dnw_d.partition_broadcast(128), [], ["dnw"])
            STOP(1.5)
            S32 = [[SB(sB, "S32_%d_%d" % (r, i), [128, 256], F32) for i in range(2)] for r in range(2)]
            S16 = [[SB(sB, "S16_%d_%d" % (r, i), [128, 256], BF16) for i in range(2)] for r in range(2)]

            def state_boundary(r, cur, seg, h, out_d):
                k = "S32_%d_%d" % (r, cur)
                k2 = "S32_%d_%d" % (r, 1 - cur)
                if seg < 4:
                    S.dma("sp", out_d[seg, r, h], S32[r][cur][:], [k], [])
                S.op("dve", "tensor_scalar", [k, "flag"], [k2], out=S32[r][1 - cur][:], in0=S32[r][cur][:], scalar1=flag[:, 0:1],
                     scalar2=None, op0=ALU.mult)
                return 1 - cur

            MEM("before sR")
            with ExitStack() as sR:
                cdec = SB(sR, "cdec", [128, 8], F32)
                kdec = SB(sR, "kdec", [128, 8], F32)
                qpat = SB(sR, "qpat", [128, 8, 128], F32)
                MT = SB(sR, "MT", [128, 4, 128], F32)
                S.dma("sp", par[:, 0:8], rdl_d.partition_broadcast(128), [], ["par"])
                S.op("act", "activation", ["par"], ["par1"], out=par[:, 8:16], in_=par[:, 0:8], func=AF.Exp, scale=-1.0)
                S.op("act", "activation", ["par1", "cst"], ["par2"], out=par[:, 16:24], in_=par[:, 8:16], func=AF.Ln, bias=C("ONE"))
                S.op("dve", "tensor_scalar", ["par2"], ["lg"], out=lg[:], in0=par[:, 16:24], scalar1=-1.0, scalar2=None, op0=ALU.mult)
                S.op("act", "activation", ["lg"], ["cdec"], out=cdec[:], in_=lg[:], func=AF.Exp, scale=128.0)
                mtmp = SB(sR, "mtmp", [128, 2, 128], F32)
                for h in range(4):
                    S.op("act", "activation", ["lg", "cst"], ["kdec"], out=kdec[:, h:h + 1], in_=C("KPF"), func=AF.Exp, scale=lg[:, h:h + 1])
                    S.op("act", "activation", ["lg", "cst"], ["kdec"], out=kdec[:, 4 + h:5 + h], in_=C("KPB"), func=AF.Exp, scale=lg[:, 4 + h:5 + h])
                    S.op("act", "activation", ["lg", "cst"], ["qpat"], out=qpat[:, h, :], in_=C("POS1_8"), func=AF.Exp, scale=lg[:, h:h + 1])
                    S.op("act", "activation", ["lg", "cst"], ["qpat"], out=qpat[:, 4 + h, :], in_=C("POS128M"), func=AF.Exp, scale=lg[:, 4 + h:5 + h])
                    S.op("act", "activation", ["lg", "cst"], ["mtmp"], out=mtmp[:, 0, :], in_=C("DPOS8"), func=AF.Exp, scale=lg[:, h:h + 1])
                    S.op("act", "activation", ["lg", "cst"], ["mtmp"], out=mtmp[:, 1, :], in_=C("DNEG8"), func=AF.Exp, scale=lg[:, 4 + h:5 + h])
                    S.op("dve", "tensor_tensor", ["mtmp", "cst"], ["mtmp"], out=mtmp[:, 0, :], in0=mtmp[:, 0, :], in1=C("GE8"), op=ALU.mult)
                    S.op("dve", "tensor_tensor", ["mtmp", "cst"], ["mtmp"], out=mtmp[:, 1, :], in0=mtmp[:, 1, :], in1=C("LE8"), op=ALU.mult)
                    S.op("dve", "tensor_tensor", ["mtmp"], ["MT"], out=MT[:, h, :], in0=mtmp[:, 0, :], in1=mtmp[:, 1, :], op=ALU.add)

                gnw = SB(sR, "gnw", [128, 1024], F32)
                S.dma("sp", gnw[:], gnw_d.partition_broadcast(128), [], ["gnw"])
                ropec = SB(sR, "ropec", [128, T], BF16)
                ropes = SB(sR, "ropes", [128, T], BF16)
                S.dma("pool", ropec[:], ropec_d[:, :], [], ["ropec"])
                S.dma("pool", ropes[:], ropes_d[:, :], [], ["ropes"])
                qT = SB(sR, "qT", [128, T], BF16)
                kT = SB(sR, "kT", [128, T], BF16)
                qfT = SB(sR, "qfT", [128, T], BF16)
                qbT = SB(sR, "qbT", [128, T], BF16)
                kft = SB(sR, "kft", [128, NT, 128], BF16)
                kbt = SB(sR, "kbt", [128, NT, 128], BF16)
                vt = SB(sR, "vt", [128, NT, 256], BF16)
                WB = 5
                rg = [SB(sR, "rg%d" % i, [128, 256], BF16) for i in range(WB)]
                of = SB(sR, "of", [128, NT, 256], BF16)
                ob = SB(sR, "ob", [128, NT, 256], BF16)
                sT = SB(sR, "sT", [128, NT, 128], BF16)
                tA = SB(sR, "tA", [128, 512], F32)
                tB = SB(sR, "tB", [128, 512], F32)
                o32 = [SB(sR, "o32_%d" % i, [128, 256], F32) for i in range(WB)]
                t32 = [SB(sR, "t32_%d" % i, [128, 256], F32) for i in range(WB)]
                zb = [SB(sR, "zb%d" % i, [128, 256], BF16) for i in range(WB)]
                sm = [SB(sR, "sm%d" % i, [128, 8], F32) for i in range(WB)]
                MEM("sR after alloc")
                pendingRB = []
                nextW = preR0
                for h in range(4):
                    W, wk = nextW
                    def ret_pre(h=h, W=W, wk=wk):
                        for tg in range(4):
                            ts_ = slice(tg * 512, (tg + 1) * 512)
                            for (dst, dk_, c0, scale) in ((qT, "qT", 0, 128.0 ** -0.5), (kT, "kT", 256, 1.0)):
                                psa, pka = PS()
                                for kc in range(8):
                                    mm(psa[:, :], pka, W[:, kc, c0:c0 + 128], hT[:, kc, ts_], [wk, "hT"], kc == 0, kc == 7)
                                psb2, pkb = PS()
                                for kc in range(8):
                                    mm(psb2[:, :], pkb, W[:, kc, c0 + 128:c0 + 256], hT[:, kc, ts_], [wk, "hT"], kc == 0, kc == 7)
                                S.op("dve", "scalar_tensor_tensor", [pka, "ropec"], ["tA"], out=tA[:], in0=psa[:, :], scalar=scale,
                                     in1=ropec[:, ts_], op0=ALU.mult, op1=ALU.mult)
                                S.op("dve", "scalar_tensor_tensor", [pkb, "ropes"], ["tB"], out=tB[:], in0=psb2[:, :], scalar=scale,
                                     in1=ropes[:, ts_], op0=ALU.mult, op1=ALU.mult)
                                S.op("dve", "tensor_tensor", ["tA", "tB"], [dk_], out=dst[:, ts_], in0=tA[:], in1=tB[:], op=ALU.add)
                                yield
                        for tt in range(NT):
                            ps, pk = PS()
                            for kc in range(8):
                                mm(ps[:, 0:256], pk, hT[:, kc, tt * 128:(tt + 1) * 128], W[:, kc, 512:768], [wk, "hT"], kc == 0, kc == 7)
                            S.op("act", "activation", [pk], ["vt"], out=vt[:, tt, :], in_=ps[:, 0:256], func=AF.Copy)
                            if tt % 2 == 1:
                                yield
                        for g4 in range(4):
                            for j in range(4):
                                tt = g4 * 4 + j
                                S.op("pe", "transpose", ["kT", "identb"], ["psT"], inc=(j == 3), out=psT[:, j * 128:(j + 1) * 128],
                                     in_=kT[:, tt * 128:(tt + 1) * 128], identity=identb[:])
                            pv = psT[:, 0:512].rearrange("p (t d) -> p t d", t=4)
                            S.op("dve", "tensor_scalar", ["psT", "kdec"], ["kft"], out=kft[:, g4 * 4:(g4 + 1) * 4, :], in0=pv,
                                 scalar1=kdec[:, h:h + 1], scalar2=None, op0=ALU.mult)
                            S.op("act", "activation", ["psT", "kdec"], ["kbt"], out=kbt[:, g4 * 4:(g4 + 1) * 4, :], in_=pv, func=AF.Copy,
                                 scale=kdec[:, 4 + h:5 + h])
                            yield
                        S.op("dve", "tensor_tensor", ["qT", "qpat"], ["qfT"], out=qfT[:, :].rearrange("p (n i) -> p n i", i=128),
                             in0=qT[:, :].rearrange("p (n i) -> p n i", i=128),
                             in1=qpat[:, h:h + 1, :].to_broadcast([128, 16, 128]), op=ALU.mult)
                        S.op("dve", "tensor_tensor", ["qT", "qpat"], ["qbT"], out=qbT[:, :].rearrange("p (n i) -> p n i", i=128),
                             in0=qT[:, :].rearrange("p (n i) -> p n i", i=128),
                             in1=qpat[:, 4 + h:5 + h, :].to_broadcast([128, 16, 128]), op=ALU.mult)
                        yield
                        for g4 in range(4):
                            ps, pk = PS()
                            for c in range(4):
                                tt = g4 * 4 + c
                                tsl = slice(tt * 128, (tt + 1) * 128)
                                mm(ps[:, c * 128:(c + 1) * 128], pk, kT[:, tsl], qT[:, tsl], ["kT", "qT"], True, True, last=(c == 3))
                            S.op("dve", "tensor_tensor", [pk, "MT"], ["sT"], out=sT[:, g4 * 4:(g4 + 1) * 4, :], in0=ps[:, :].rearrange("p (t i) -> p t i", t=4),
                                 in1=MT[:, h:h + 1, :].to_broadcast([128, 4, 128]), op=ALU.mult)
                            yield
                    interleave([lambda bi: ret_pre()], 1, extras=pendingRB)
                    pendingRB = []
                    nextW = load_w(BLK_R + h + 1) if h < 3 else load_w(BLK_RO)
                    STOP(1.8)
                    cur = [0, 0]
                    c16 = [0, 0]
                    for r in range(2):
                        S.dma("sp", S32[r][0][:], sret_d[r, h], [], ["S32_%d_0" % r])
                        S.op("act", "activation", ["S32_%d_0" % r], ["S16_%d_0" % r], out=S16[r][0][:], in_=S32[r][0][:], func=AF.Copy)
                    for t in range(NT):
                        for r in range(2):
                            tt = t if r == 0 else NT - 1 - t
                            tsl = slice(tt * 128, (tt + 1) * 128)
                            sk = "S32_%d_%d" % (r, cur[r])
                            s16k = "S16_%d_%d" % (r, c16[r])
                            s16n = "S16_%d_%d" % (r, 1 - c16[r])
                            Scur = S16[r][c16[r]]
                            Snew = S16[r][1 - c16[r]]
                            pss, pks = PS()
                            kk = kft if r == 0 else kbt
                            mm(pss[:, 0:256], pks, kk[:, tt, :], vt[:, tt, :], ["kft", "kbt", "vt"], True, True)
                            pso, pko = PS()
                            if r == 0:
                                mm(pso[:, 0:256], pko, sT[:, tt, :], vt[:, tt, :], ["sT", "vt"], True, False)
                                mm(pso[:, 0:256], pko, qfT[:, tsl], Scur[:], ["qfT", s16k], False, True)
                                S.op("act", "activation", [pko], ["of"], out=of[:, tt, :], in_=pso[:, 0:256], func=AF.Copy)
                            else:
                                mm(pso[:, 0:256], pko, qbT[:, tsl], Scur[:], ["qbT", s16k], True, True)
                                S.op("act", "activation", [pko], ["ob"], out=ob[:, tt, :], in_=pso[:, 0:256], func=AF.Copy)
                            bnd = (tt % 2 == 1) if r == 0 else (tt % 2 == 0)
                            if not bnd:
                                S.op("dve", "scalar_tensor_tensor", [pks, sk, "cdec"], [s16n], out=Snew[:], in0=S32[r][cur[r]][:],
                                     scalar=cdec[:, 4 * r + h:4 * r + h + 1], in1=pss[:, 0:256], op0=ALU.mult, op1=ALU.add)
                            S.op("dve", "scalar_tensor_tensor", [pks, sk, "cdec"], [sk], out=S32[r][cur[r]][:], in0=S32[r][cur[r]][:],
                                 scalar=cdec[:, 4 * r + h:4 * r + h + 1], in1=pss[:, 0:256], op0=ALU.mult, op1=ALU.add)
                            if bnd:
                                cur[r] = state_boundary(r, cur[r], tt // 2, h, sret_o)
                                S.op("act", "activation", ["S32_%d_%d" % (r, cur[r])], [s16n], out=Snew[:], in_=S32[r][cur[r]][:], func=AF.Copy)
                            c16[r] = 1 - c16[r]
                    STOP(1.9)
                    def retB(tt, h=h, W=W, wk=wk):
                        def g(bi):
                            o32_, t32_, zb_, sm_, rg_ = o32[bi], t32[bi], zb[bi], sm[bi], rg[bi]
                            q_ = "_%d" % bi
                            ps, pk = PS()
                            for kc in range(8):
                                mm(ps[:, 0:256], pk, hT[:, kc, tt * 128:(tt + 1) * 128], W[:, kc, 768:1024], [wk, "hT"], kc == 0, kc == 7)
                            S.op("act", "activation", [pk], ["rg" + q_], out=rg_[:], in_=ps[:, 0:256], func=AF.Silu)
                            S.op("dve", "scalar_tensor_tensor", ["of", "ob"], ["o32" + q_, "sm0" + q_], out=o32_[:], in0=of[:, tt, :], scalar=1.0,
                                 in1=ob[:, tt, :], op0=ALU.mult, op1=ALU.add, accum_out=sm_[:, 0:1])
                            yield
                            S.op("dve", "tensor_scalar", ["sm0" + q_], ["sm1" + q_], out=sm_[:, 1:2], in0=sm_[:, 0:1], scalar1=-1.0 / 256, scalar2=None, op0=ALU.mult)
                            yield
                            S.op("act", "activation", ["o32" + q_, "sm1" + q_], ["t32" + q_, "sm2" + q_], out=t32_[:], in_=o32_[:], func=AF.Square, bias=sm_[:, 1:2],
                                 accum_out=sm_[:, 2:3])
                            yield
                            S.op("act", "activation", ["sm2" + q_, "cst"], ["sm3" + q_], out=sm_[:, 3:4], in_=sm_[:, 2:3], func=AF.Sqrt, scale=1.0 / 256, bias=C("EPS"))
                            yield
                            S.op("dve", "reciprocal", ["sm3" + q_], ["sm4" + q_], out=sm_[:, 4:5], in_=sm_[:, 3:4])
                            S.op("dve", "scalar_tensor_tensor", ["o32" + q_, "sm1" + q_, "gnw"], ["t32" + q_], out=t32_[:], in0=o32_[:], scalar=sm_[:, 1:2],
                                 in1=gnw[:, h * 256:(h + 1) * 256], op0=ALU.add, op1=ALU.mult)
                            S.op("dve", "scalar_tensor_tensor", ["t32" + q_, "sm4" + q_, "rg" + q_], ["zb" + q_], out=zb_[:], in0=t32_[:], scalar=sm_[:, 4:5],
                                 in1=rg_[:], op0=ALU.mult, op1=ALU.mult)
                            yield
                            for c in range(2):
                                S.op("pe", "transpose", ["zb" + q_, "identb"], ["psT"], inc=(c == 1), out=psT[:, c * 128:(c + 1) * 128],
                                     in_=zb_[:, c * 128:(c + 1) * 128], identity=identb[:])
                            S.op("act", "activation", ["psT"], ["zT"], out=zT[:, 2 * h:2 * h + 2, tt * 128:(tt + 1) * 128],
                                 in_=psT[:, 0:256].rearrange("p (c t) -> p c t", c=2), func=AF.Copy)
                            yield
                        return g
                    def laneRB(tts, bi, retB=retB):
                        for tt in tts:
                            yield from retB(tt)(bi)
                    lanesR = [laneRB(list(range(bi, NT, WB)), bi) for bi in range(WB)]
                    if h < 3:
                        pendingRB = lanesR
                    else:
                        interleave([], 1, extras=lanesR)
                S.barrier()

            def out_gate(blk_o, blk_g, first, mT=None, pre=None):
                with ExitStack() as sg:
                    sig = SB(sg, "sig", [128, 512], F32)
                    yt = [SB(sg, "yt%d" % i, [128, 512], BF16) for i in range(2)]
                    tg32 = SB(sg, "tg32", [128, 512], F32)
                    Wo, wko = pre if pre is not None else load_w(blk_o)
                    Wg, wkg = load_w(blk_g)
                    it = 0
                    for fo in range(8):
                        for tg in range(4):
                            ts_ = slice(tg * 512, (tg + 1) * 512)
                            ps1, pk1 = PS()
                            for kc in range(8):
                                mm(ps1[:, :], pk1, Wo[:, kc, fo * 128:(fo + 1) * 128], zT[:, kc, ts_], [wko, "zT"], kc == 0, kc == 7)
                            ps2, pk2 = PS()
                            for kc in range(8):
                                mm(ps2[:, :], pk2, Wg[:, kc, fo * 128:(fo + 1) * 128], hT[:, kc, ts_], [wkg, "hT"], kc == 0, kc == 7)
                            S.op("act", "activation", [pk2], ["sig"], out=sig[:], in_=ps2[:, :], func=AF.Sigmoid)
                            yk = "yt%d" % (it % 2)
                            y_ = yt[it % 2]
                            it += 1
                            if first:
                                S.op("dve", "tensor_tensor", [pk1, "sig"], [yk], out=y_[:], in0=ps1[:, :], in1=sig[:], op=ALU.mult)
                                S.dma("sp", yrg_d[fo, :, ts_], y_[:], [yk], ["yrg%d_%d" % (fo, tg)])
                            else:
                                S.dma("sp", y_[:], yrg_d[fo, :, ts_], ["yrg%d_%d" % (fo, tg)], [yk])
                                S.op("dve", "tensor_tensor", [pk1, "sig"], ["tg32"], out=tg32[:], in0=ps1[:, :], in1=sig[:], op=ALU.mult)
                                S.op("dve", "tensor_tensor", ["tg32", yk], ["mT"], out=mT[:, fo, ts_], in0=tg32[:], in1=y_[:], op=ALU.add)
                    S.barrier()

            STOP(2)
            out_gate(BLK_RO, BLK_GR, True, pre=nextW)
            STOP(3)

            with ExitStack() as sD:
                preD0 = load_w(BLK_D)
                bap = SB(sD, "bap", [128, NT, 16], F32)
                beta = SB(sD, "beta", [128, NT, 8], F32)
                la = SB(sD, "la", [128, NT, 8], F32)
                Gtok = SB(sD, "Gtok", [128, NT, 8], F32)
                eg = SB(sD, "eg", [128, NT, 8], F32)
                kh = SB(sD, "kh", [128, NT, 8], F32)
                bw = SB(sD, "bw", [128, NT, 8], F32)
                EGL = SB(sD, "EGL", [128, NT, 8], F32)
                wba = SB(sD, "wba", [128, 8, 16], BF16)
                dpar = SB(sD, "dpar", [128, 32], F32)
                convw = SB(sD, "convw", [128, 3, 16], F32)
                ncw = SB(sD, "ncw", [128, 2, 16], F32)
                S.dma("pool", wba[:], wba_d.rearrange("(k p) c -> p k c", p=128), [], ["wba"])
                S.dma("sp", dpar[:, 0:8], alog_d.partition_broadcast(128), [], ["dpar"])
                S.dma("sp", dpar[:, 8:16], dtb_d.partition_broadcast(128), [], ["dpar"])
                S.dma("sp", convw[:], convw_d.rearrange("t (c p) -> p t c", p=128), [], ["convw"], allow_slow_non_contiguous=True)
                S.op("dve", "tensor_scalar", ["flag"], ["dpar2"], out=dpar[:, 24:25], in0=flag[:, 0:1], scalar1=-1.0, scalar2=None, op0=ALU.add)
                S.op("dve", "tensor_scalar", ["convw", "dpar2"], ["ncw"], out=ncw[:, 0, :], in0=convw[:, 0, :], scalar1=dpar[:, 24:25], scalar2=None, op0=ALU.mult)
                S.op("dve", "tensor_scalar", ["convw", "dpar2"], ["ncw"], out=ncw[:, 1, :], in0=convw[:, 2, :], scalar1=dpar[:, 24:25], scalar2=None, op0=ALU.mult)
                S.op("act", "activation", ["dpar"], ["dpar1"], out=dpar[:, 16:24], in_=dpar[:, 0:8], func=AF.Exp)
                for tt in range(NT):
                    ps, pk = PS()
                    for kc in range(8):
                        mm(ps[:, 0:16], pk, hT[:, kc, tt * 128:(tt + 1) * 128], wba[:, kc, :], ["wba", "hT"], kc == 0, kc == 7)
                    S.op("act", "activation", [pk], ["bap"], out=bap[:, tt, :], in_=ps[:, 0:16], func=AF.Copy)
                S.op("act", "activation", ["bap"], ["beta"], out=beta[:], in_=bap[:, :, 0:8], func=AF.Sigmoid)
                S.op("dve", "tensor_tensor", ["bap", "dpar"], ["la"], out=la[:], in0=bap[:, :, 8:16],
                     in1=dpar[:, 8:16].unsqueeze(1).to_broadcast([128, NT, 8]), op=ALU.add)
                S.op("act", "activation", ["la"], ["la"], out=la[:], in_=la[:], func=AF.Exp)
                S.op("act", "activation", ["la", "cst"], ["la"], out=la[:], in_=la[:], func=AF.Ln, bias=C("ONE"))
                S.op("dve", "scalar_tensor_tensor", ["la", "dpar1"], ["la"], out=la[:], in0=la[:], scalar=-1.0,
                     in1=dpar[:, 16:24].unsqueeze(1).to_broadcast([128, NT, 8]), op0=ALU.mult, op1=ALU.mult)
                ps, pk = PS()
                mm(ps[:, 0:64].rearrange("p (t c) -> p t c", c=4), pk, C("TRILO128"), la[:, :, 0:4], ["la"], True, True, last=False)
                mm(ps[:, 64:128].rearrange("p (t c) -> p t c", c=4), pk, C("TRIUP128"), la[:, :, 4:8], ["la"], True, True)
                S.op("act", "activation", [pk], ["Gtok"], out=Gtok[:, :, 0:4], in_=ps[:, 0:64].rearrange("p (t c) -> p t c", c=4), func=AF.Copy)
                S.op("act", "activation", [pk], ["Gtok"], out=Gtok[:, :, 4:8], in_=ps[:, 64:128].rearrange("p (t c) -> p t c", c=4), func=AF.Copy)
                ps, pk = PS()
                mm(ps[:, 0:128], pk, C("ONES"), la[:].rearrange("p t c -> p (t c)"), ["la"], True, True)
                S.op("dve", "tensor_tensor", [pk, "Gtok"], ["kh"], out=kh[:].rearrange("p t c -> p (t c)"), in0=ps[:, 0:128],
                     in1=Gtok[:].rearrange("p t c -> p (t c)"), op=ALU.subtract)
                S.op("act", "activation", [pk], ["EGL"], out=EGL[:].rearrange("p t c -> p (t c)"), in_=ps[:, 0:128], func=AF.Exp)
                S.op("act", "activation", ["kh"], ["kh"], out=kh[:], in_=kh[:], func=AF.Exp)
                S.op("act", "activation", ["Gtok"], ["eg"], out=eg[:], in_=Gtok[:], func=AF.Exp)
                S.op("dve", "tensor_tensor", ["eg", "beta"], ["bw"], out=bw[:], in0=eg[:], in1=beta[:], op=ALU.mult)

                dqT = SB(sD, "dqT", [128, T], BF16)
                dkT = SB(sD, "dkT", [128, T], BF16)
                ktok = SB(sD, "ktok", [128, NT, 128], BF16)
                vt = SB(sD, "dvt", [128, NT, 256], BF16)
                WBD = 4
                zg = [SB(sD, "zg%d" % i, [128, 256], BF16) for i in range(WBD)]
                odf = SB(sD, "odf", [128, NT, 256], BF16)
                odb = SB(sD, "odb", [128, NT, 256], BF16)
                o32 = [SB(sD, "do32_%d" % i, [128, 256], F32) for i in range(WBD)]
                t32 = [SB(sD, "dt32_%d" % i, [128, 256], F32) for i in range(WBD)]
                zb = [SB(sD, "dzb%d" % i, [128, 256], BF16) for i in range(WBD)]
                sm = [SB(sD, "dsm%d" % i, [128, 8], F32) for i in range(WBD)]
                RING = 3
                NSET = 4

                pendingB = []
                nextW = preD0
                for h in range(4):
                    W, wk = nextW
                    with ExitStack() as sconv:
                        WC = 2
                        xpre_l = [SB(sconv, "xpre%d" % i, [128, T], BF16) for i in range(WC)]
                        cv_l = [SB(sconv, "cv%d" % i, [128, T], F32) for i in range(WC)]
                        cvr_l = [SB(sconv, "cvr%d" % i, [128, 512], F32R) for i in range(WC)]
                        ysb_l = xpre_l
                        rin_l = [SB(sconv, "rin%d" % i, [128, 512], F32) for i in range(WC)]
                        onesR2 = SB(sconv, "onesR2", [128, 128], F32R)
                        S.op("dve", "tensor_copy", ["cst"], ["onesR2"], out=onesR2[:], in_=C("ONES"))

                        def conv_job(kind, ct, c0, h=h, W=W, wk=wk):
                            def g(bi):
                                xpre, cv, ysb, rin, cvr = xpre_l[bi], cv_l[bi], ysb_l[bi], rin_l[bi], cvr_l[bi]
                                q_ = "_%d" % bi
                                xk, ck, yk, rk_ = "xpre" + q_, "cv" + q_, "xpre" + q_, "rin" + q_
                                for tg in range(4):
                                    ts_ = slice(tg * 512, (tg + 1) * 512)
                                    ps, pk = PS()
                                    for kc in range(8):
                                        mm(ps[:, :], pk, W[:, kc, c0:c0 + 128], hT[:, kc, ts_], [wk, "hT"], kc == 0, kc == 7)
                                    S.op("act", "activation", [pk], [xk], out=xpre[:, ts_], in_=ps[:, :], func=AF.Copy)
                                    yield
                                w0 = convw[:, 0, ct:ct + 1]
                                w1 = convw[:, 1, ct:ct + 1]
                                w2 = convw[:, 2, ct:ct + 1]
                                S.op("dve", "tensor_scalar", [xk, "convw"], [ck], out=cv[:], in0=xpre[:], scalar1=w1, scalar2=None, op0=ALU.mult)
                                S.op("dve", "scalar_tensor_tensor", [xk, "convw", ck], [ck], out=cv[:, 1:T], in0=xpre[:, 0:T - 1], scalar=w0,
                                     in1=cv[:, 1:T], op0=ALU.mult, op1=ALU.add)
                                S.op("dve", "scalar_tensor_tensor", [xk, "convw", ck], [ck], out=cv[:, 0:T - 1], in0=xpre[:, 1:T], scalar=w2,
                                     in1=cv[:, 0:T - 1], op0=ALU.mult, op1=ALU.add)
                                xv = xpre[:, :].rearrange("p (s t) -> p s t", t=256)
                                cvv = cv[:, :].rearrange("p (s t) -> p s t", t=256)
                                S.op("dve", "scalar_tensor_tensor", [xk, "ncw", ck], [ck], out=cvv[:, 1:8, 0:1], in0=xv[:, 0:7, 255:256],
                                     scalar=ncw[:, 0, ct:ct + 1], in1=cvv[:, 1:8, 0:1], op0=ALU.mult, op1=ALU.add)
                                S.op("dve", "scalar_tensor_tensor", [xk, "ncw", ck], [ck], out=cvv[:, 0:7, 255:256], in0=xv[:, 1:8, 0:1],
                                     scalar=ncw[:, 1, ct:ct + 1], in1=cvv[:, 0:7, 255:256], op0=ALU.mult, op1=ALU.add)
                                yield
                                S.op("act", "activation", [ck], [yk], out=ysb[:], in_=cv[:], func=AF.Silu)
                                yield
                                if kind in ("q", "k"):
                                    dst, dk_ = (dqT, "dqT") if kind == "q" else (dkT, "dkT")
                                    epsn, sc = ("EPS128", 128.0) if kind == "q" else ("EPS", 1.0)
                                    for tg in range(4):
                                        ts_ = slice(tg * 512, (tg + 1) * 512)
                                        S.op("act", "activation", [yk], ["cvr" + q_], out=cvr[:], in_=ysb[:, ts_], func=AF.Square)
                                        yield
                                        ps, pk = PS()
                                        mm(ps[:, :], pk, onesR2[:], cvr[:], ["cvr" + q_, "onesR2"], True, True)
                                        S.op("act", "activation", [pk, "cst"], [rk_], out=rin[:], in_=ps[:, :], func=AF.Sqrt, scale=sc, bias=C(epsn))
                                        yield
                                        S.op("dve", "reciprocal", [rk_], [rk_], out=rin[:], in_=rin[:])
                                        S.op("dve", "tensor_tensor", [yk, rk_], [dk_], out=dst[:, ts_], in0=ysb[:, ts_], in1=rin[:], op=ALU.mult)
                                        yield
                                    if kind == "k":
                                        for g4 in range(4):
                                            for j in range(4):
                                                tt = g4 * 4 + j
                                                S.op("pe", "transpose", ["dkT", "identb"], ["psT"], inc=(j == 3), out=psT[:, j * 128:(j + 1) * 128],
                                                     in_=dkT[:, tt * 128:(tt + 1) * 128], identity=identb[:])
                                            S.op("act", "activation", ["psT"], ["ktok"], out=ktok[:, g4 * 4:(g4 + 1) * 4, :],
                                                 in_=psT[:, 0:512].rearrange("p (t d) -> p t d", t=4), func=AF.Copy)
                                            yield
                                else:
                                    c = kind
                                    for g4 in range(4):
                                        for j in range(4):
                                            tt = g4 * 4 + j
                                            S.op("pe", "transpose", [yk, "identb"], ["psT"], inc=(j == 3), out=psT[:, j * 128:(j + 1) * 128],
                                                 in_=ysb[:, tt * 128:(tt + 1) * 128], identity=identb[:])
                                        S.op("act", "activation", ["psT"], ["dvt"], out=vt[:, g4 * 4:(g4 + 1) * 4, c * 128:(c + 1) * 128],
                                             in_=psT[:, 0:512].rearrange("p (t d) -> p t d", t=4), func=AF.Copy)
                                        yield
                            return g
                        interleave([conv_job("q", h, 0), conv_job("k", 4 + h, 128), conv_job(0, 8 + 2 * h, 256), conv_job(1, 9 + 2 * h, 384)], WC,
                                   extras=pendingB)
                        pendingB = []
                        S.barrier()
                    nextW = load_w(BLK_D + h + 1) if h < 3 else load_w(BLK_DO)

                    with ExitStack() as sscan:
                        ring = {}
                        for r in range(2):
                            for i in range(RING):
                                ring[(r, i)] = dict(
                                    sT=SB(sscan, "r_sT%d%d" % (r, i), [128, 128], BF16),
                                    nwT=SB(sscan, "r_nwT%d%d" % (r, i), [128, 128], BF16),
                                    u=SB(sscan, "r_u%d%d" % (r, i), [128, 256], F32),
                                    qgT=SB(sscan, "r_qgT%d%d" % (r, i), [128, 128], BF16),
                                    khat=SB(sscan, "r_kh%d%d" % (r, i), [128, 128], BF16))
                        TS = []
                        for i in range(NSET):
                            TS.append(dict(
                                aTri=SB(sscan, "aTri%d" % i, [128, 128], F32R),
                                DD=SB(sscan, "DD%d" % i, [128, 256], F32),
                                Z=SB(sscan, "Z%d" % i, [128, 384], F32R),
                                Noff=SB(sscan, "Noff%d" % i, [128, 128], F32R),
                                bv=SB(sscan, "bv%d" % i, [128, 256], F32R), kbeg=SB(sscan, "kbeg%d" % i, [128, 128], F32R),
                                egrow=SB(sscan, "egrow%d" % i, [128, 128], F32)))
                        vn = [SB(sscan, "vn%d" % r, [128, 256], BF16) for r in range(2)]
                        identR = SB(sscan, "identR", [128, 128], F32R)
                        S.op("dve", "tensor_copy", ["cst"], ["identR"], out=identR[:], in_=C("I128"))
                        onesR = SB(sscan, "onesR", [128, 128], F32R)
                        negonesR = SB(sscan, "negonesR", [128, 128], F32R)
                        S.op("dve", "tensor_copy", ["cst"], ["onesR"], out=onesR[:], in_=C("ONES"))
                        S.op("dve", "tensor_copy", ["cst"], ["negonesR"], out=negonesR[:], in_=C("NEGONES128"))

                        def prep_gen(k, r, tsi):
                            tt = k if r == 0 else NT - 1 - k
                            c = 4 * r + h
                            X = TS[tsi]
                            q_ = "_%d" % tsi
                            slot = ring[(r, k % RING)]
                            sk = "ring%d%d" % (r, k % RING)
                            tsl = slice(tt * 128, (tt + 1) * 128)
                            tri = C("TRILO128") if r == 0 else C("TRIUP128")
                            aTri, DD, Z = X["aTri"], X["DD"], X["Z"]
                            DS = DD[:, 128:256]
                            bv, kbeg, egrow = X["bv"], X["kbeg"], X["egrow"]
                            S.op("act", "activation", ["la", "cst"], ["aTri" + q_], out=aTri[:], in_=tri, func=AF.Copy, scale=la[:, tt, c:c + 1])
                            S.op("act", "activation", ["ktok", "kh"], [sk + "khat"], out=slot["khat"][:], in_=ktok[:, tt, :], func=AF.Copy, scale=kh[:, tt, c:c + 1])
                            S.op("dve", "tensor_scalar", ["ktok", "bw"], ["kbeg" + q_], out=kbeg[:], in0=ktok[:, tt, :], scalar1=bw[:, tt, c:c + 1], scalar2=None, op0=ALU.mult)
                            S.op("act", "activation", ["dvt", "beta"], ["bv" + q_], out=bv[:], in_=vt[:, tt, :], func=AF.Copy, scale=beta[:, tt, c:c + 1])
                            yield
                            while nhold[0] >= 2:
                                yield
                            nhold[0] += 1
                            psQ, pkQ = PS()
                            held.add(int(pkQ[3:]))
                            psG, pkG = PS()
                            held.add(int(pkG[3:]))
                            mm(psQ[:, 0:128], pkQ, onesR[:], aTri[:], ["aTri" + q_, "onesR"], True, False)
                            mm(psQ[:, 0:128], pkQ, aTri[:], negonesR[:], ["aTri" + q_, "negonesR"], False, True, last=False)
                            mm(psQ[:, 128:256], pkQ, onesR[:], aTri[:], ["aTri" + q_, "onesR"], True, True)
                            mm(psG[:, 0:128], pkG, dkT[:, tsl], dkT[:, tsl], ["dkT"], True, True, last=False)
                            mm(psG[:, 128:256], pkG, dkT[:, tsl], dqT[:, tsl], ["dkT", "dqT"], True, True)
                            mask = C("MASKF128") if r == 0 else C("MASKB128")
                            smask = C("SMF128") if r == 0 else C("SMB128")
                            S.op("dve", "tensor_tensor", [pkQ, "cst"], ["DD" + q_], out=DD[:, 0:128], in0=psQ[:, 0:128], in1=mask[:, 0:128], op=ALU.add)
                            S.op("dve", "scalar_tensor_tensor", [pkQ, "cst"], ["DD" + q_], out=DD[:, 128:256], in0=psQ[:, 0:128], scalar=-1.0,
                                 in1=mask[:, 128:256], op0=ALU.mult, op1=ALU.add)
                            S.op("act", "activation", [pkQ], ["egrow" + q_], out=egrow[:], in_=psQ[:, 128:256], func=AF.Exp)
                            S.op("act", "activation", ["DD" + q_], ["DD" + q_], out=DD[:], in_=DD[:], func=AF.Exp)
                            held.discard(int(pkQ[3:]))
                            yield
                            S.op("dve", "tensor_tensor", ["DD" + q_, "cst"], ["DD" + q_], out=DS, in0=DD[:, 128:256], in1=smask[:, 128:256], op=ALU.mult)
                            S.op("dve", "tensor_tensor", [pkG, "DD" + q_], [sk + "sT"], out=slot["sT"][:], in0=psG[:, 128:256], in1=DD[:, 0:128], op=ALU.mult)
                            S.op("dve", "tensor_tensor", ["dqT", "egrow" + q_], [sk + "qgT"], out=slot["qgT"][:], in0=dqT[:, tsl], in1=egrow[:], op=ALU.mult)
                            kZ = "Z" + q_
                            Noff = X["Noff"][:]
                            S.op("dve", "scalar_tensor_tensor", [pkG, "beta", "DD" + q_], ["egrow" + q_], out=egrow[:], in0=psG[:, 0:128], scalar=beta[:, tt, c:c + 1],
                                 in1=DS, op0=ALU.mult, op1=ALU.mult)
                            S.op("dve", "tensor_tensor", ["egrow" + q_, "cst"], [kZ], out=Z[:, 256:384], in0=egrow[:], in1=C("BLKM"), op=ALU.mult)
                            S.op("dve", "tensor_tensor", ["egrow" + q_, kZ], ["Noff" + q_], out=Noff, in0=egrow[:], in1=Z[:, 256:384], op=ALU.subtract)
                            mm(psG[:, 256:384], pkG, Z[:, 256:384], identR[:], [kZ, "identR"], True, True)
                            S.op("act", "activation", [pkG], [kZ], out=Z[:, 0:128], in_=psG[:, 256:384], func=AF.Copy)
                            S.op("dve", "tensor_tensor", [kZ, "identR"], [kZ], out=Z[:, 128:256], in0=Z[:, 0:128], in1=identR[:], op=ALU.add)
                            held.discard(int(pkG[3:]))
                            nhold[0] -= 1
                            yield
                            for lev in range(1, 7):
                                psA, pkA = PS()
                                if lev == 1:
                                    mm(psA[:, 0:128], pkA, Z[:, 256:384], Z[:, 0:128], [kZ], True, True, last=False)
                                else:
                                    mm(psA[:, 0:256], pkA, Z[:, 256:384], Z[:, 0:256], [kZ], True, True, last=(lev == 6))
                                if lev < 6:
                                    mm(psA[:, 256:384], pkA, Z[:, 0:128], Z[:, 256:384], [kZ], True, True)
                                    S.op("act", "activation", [pkA], [kZ], out=Z[:, :].rearrange("p (b f) -> p b f", b=3)[:, 0:3:2, :],
                                         in_=psA[:, 0:384].rearrange("p (b f) -> p b f", b=3)[:, 0:3:2, :], func=AF.Copy)
                                if lev >= 2:
                                    S.op("dve", "tensor_tensor", [pkA, kZ], [kZ], out=Z[:, 128:256], in0=psA[:, 128:256], in1=Z[:, 128:256], op=ALU.add)
                                yield
                            psM, pkM = PS()
                            mm(psM[:, 0:128], pkM, Noff, Z[:, 128:256], ["Noff" + q_, kZ], True, True, last=False)
                            mm(psM[:, 256:384], pkM, Z[:, 128:256], identR[:], [kZ, "identR"], True, True)
                            S.op("act", "activation", [pkM], [kZ], out=Z[:, :].rearrange("p (b f) -> p b f", b=3)[:, 0:3:2, :],
                                 in_=psM[:, 0:384].rearrange("p (b f) -> p b f", b=3)[:, 0:3:2, :], func=AF.Copy)
                            yield
                            psM2, pkM2 = PS()
                            mm(psM2[:, 0:128], pkM2, Z[:, 256:384], Z[:, 0:128], [kZ], True, True)
                            S.op("dve", "tensor_tensor", [pkM2, kZ], [kZ], out=Z[:, 128:256], in0=psM2[:, 0:128], in1=Z[:, 128:256], op=ALU.add)
                            yield
                            TT_ = Z[:, 128:256]
                            tk = kZ
                            psU, pkU = PS()
                            mm(psU[:, 0:256], pkU, TT_, bv[:], [tk, "bv" + q_], True, True, last=False)
                            mm(psU[:, 256:384], pkU, kbeg[:], TT_, [tk, "kbeg" + q_], True, True)
                            S.op("dve", "tensor_copy", [pkU], [sk + "u"], out=slot["u"][:], in_=psU[:, 0:256])
                            S.op("act", "activation", [pkU], [sk + "nwT"], out=slot["nwT"][:], in_=psU[:, 256:384], func=AF.Copy, scale=-1.0)
                            yield

                        cur = [0, 0]
                        c16 = [0, 0]
                        ready = {}
                        done_tiles = [0, 0]
                        nhold = [0]

                        def chain_gen(r):
                            c = 4 * r + h
                            od = odf if r == 0 else odb
                            odk = "odf" if r == 0 else "odb"
                            vk = "vn%d" % r
                            for k in range(NT):
                                tt = k if r == 0 else NT - 1 - k
                                while not ready.get((k, r)):
                                    yield
                                slot = ring[(r, k % RING)]
                                sk = "ring%d%d" % (r, k % RING)
                                s32k = "S32_%d_%d" % (r, cur[r])
                                s16k = "S16_%d_%d" % (r, c16[r])
                                s16n = "S16_%d_%d" % (r, 1 - c16[r])
                                Scur = S16[r][c16[r]]
                                Snew = S16[r][1 - c16[r]]
                                psV, pkV = PS()
                                mm(psV[:, 0:256], pkV, slot["nwT"][:], Scur[:], [sk + "nwT", s16k], True, True)
                                S.op("dve", "tensor_tensor", [pkV, sk + "u"], [vk], out=vn[r][:], in0=psV[:, 0:256], in1=slot["u"][:], op=ALU.add)
                                yield
                                psO, pkO = PS()
                                psS, pkS = PS()
                                mm(psS[:, 0:256], pkS, slot["khat"][:], vn[r][:], [sk + "khat", vk], True, True)
                                mm(psO[:, 0:256], pkO, slot["qgT"][:], Scur[:], [sk + "qgT", s16k], True, False)
                                mm(psO[:, 0:256], pkO, slot["sT"][:], vn[r][:], [sk + "sT", vk], False, True)
                                bnd = (tt % 2 == 1) if r == 0 else (tt % 2 == 0)
                                S.op("dve", "scalar_tensor_tensor", [pkS, s32k, "EGL"], [s32k], out=S32[r][cur[r]][:], in0=S32[r][cur[r]][:],
                                     scalar=EGL[:, tt, c:c + 1], in1=psS[:, 0:256], op0=ALU.mult, op1=ALU.add)
                                if bnd:
                                    cur[r] = state_boundary(r, cur[r], tt // 2, h, sdn_o)
                                S.op("act", "activation", ["S32_%d_%d" % (r, cur[r])], [s16n], out=Snew[:], in_=S32[r][cur[r]][:], func=AF.Copy)
                                S.op("act", "activation", [pkO], [odk], out=od[:, tt, :], in_=psO[:, 0:256], func=AF.Copy)
                                c16[r] = 1 - c16[r]
                                yield
                                done_tiles[r] += 1

                        for r in range(2):
                            S.dma("sp", S32[r][0][:], sdn_d[r, h], [], ["S32_%d_0" % r])
                            S.op("act", "activation", ["S32_%d_0" % r], ["S16_%d_0" % r], out=S16[r][0][:], in_=S32[r][0][:], func=AF.Copy)
                        chains = [chain_gen(0), chain_gen(1)]
                        chain_alive = [True, True]
                        nextk = [0, 0]
                        free_sets = list(range(NSET))
                        active = []
                        turn = 0
                        while any(chain_alive):
                            launched = True
                            while free_sets and launched:
                                launched = False
                                for rr in ((0, 1) if turn % 2 == 0 else (1, 0)):
                                    if free_sets and nextk[rr] < NT and nextk[rr] < done_tiles[rr] + RING:
                                        tsi = free_sets.pop(0)
                                        active.append((prep_gen(nextk[rr], rr, tsi), nextk[rr], rr, tsi))
                                        nextk[rr] += 1
                                        launched = True
                                        turn += 1
                            def chain_turn():
                                for r in range(2):
                                    if chain_alive[r]:
                                        try:
                                            next(chains[r])
                                        except StopIteration:
                                            chain_alive[r] = False
                            chain_turn()
                            CT = 2
                            for ii, item in enumerate(list(active)):
                                if ii > 0 and ii % CT == 0:
                                    chain_turn()
                                g, k, rr, tsi = item
                                try:
                                    next(g)
                                except StopIteration:
                                    active.remove(item)
                                    free_sets.append(tsi)
                                    ready[(k, rr)] = True
                        S.barrier()
                    def delB(tt, h=h, W=W, wk=wk):
                        def g(bi):
                            o32_, t32_, zb_, sm_, zg_ = o32[bi], t32[bi], zb[bi], sm[bi], zg[bi]
                            q_ = "_%d" % bi
                            ps, pk = PS()
                            for kc in range(8):
                                mm(ps[:, 0:256], pk, hT[:, kc, tt * 128:(tt + 1) * 128], W[:, kc, 512:768], [wk, "hT"], kc == 0, kc == 7)
                            S.op("act", "activation", [pk], ["zg" + q_], out=zg_[:], in_=ps[:, 0:256], func=AF.Silu)
                            S.op("dve", "tensor_tensor", ["odf", "odb"], ["do32" + q_], out=o32_[:], in0=odf[:, tt, :], in1=odb[:, tt, :], op=ALU.add)
                            yield
                            S.op("act", "activation", ["do32" + q_], ["dt32" + q_, "dsm0" + q_], out=t32_[:], in_=o32_[:], func=AF.Square, accum_out=sm_[:, 0:1])
                            yield
                            S.op("act", "activation", ["dsm0" + q_, "cst"], ["dsm1" + q_], out=sm_[:, 1:2], in_=sm_[:, 0:1], func=AF.Sqrt, scale=1.0 / 256, bias=C("EPS"))
                            yield
                            S.op("dve", "reciprocal", ["dsm1" + q_], ["dsm2" + q_], out=sm_[:, 2:3], in_=sm_[:, 1:2])
                            S.op("dve", "scalar_tensor_tensor", ["do32" + q_, "dsm2" + q_, "dnw"], ["dt32" + q_], out=t32_[:], in0=o32_[:], scalar=sm_[:, 2:3],
                                 in1=dnw[:], op0=ALU.mult, op1=ALU.mult)
                            S.op("dve", "tensor_tensor", ["dt32" + q_, "zg" + q_], ["dzb" + q_], out=zb_[:], in0=t32_[:], in1=zg_[:], op=ALU.mult)
                            yield
                            for c2 in range(2):
                                S.op("pe", "transpose", ["dzb" + q_, "identb"], ["psT"], inc=(c2 == 1), out=psT[:, c2 * 128:(c2 + 1) * 128],
                                     in_=zb_[:, c2 * 128:(c2 + 1) * 128], identity=identb[:])
                            S.op("act", "activation", ["psT"], ["zT"], out=zT[:, 2 * h:2 * h + 2, tt * 128:(tt + 1) * 128],
                                 in_=psT[:, 0:256].rearrange("p (c t) -> p c t", c=2), func=AF.Copy)
                            yield
                        return g
                    def laneB(tts, bi, delB=delB):
                        for tt in tts:
                            yield from delB(tt)(bi)
                    lanes = [laneB(list(range(bi, NT, WBD)), bi) for bi in range(WBD)]
                    if h < 3:
                        pendingB = lanes
                    else:
                        interleave([], 1, extras=lanes)
                S.barrier()

            STOP(4)
            mT = SB(sB, "mT", [128, 8, T], BF16)
            out_gate(BLK_DO, BLK_GD, False, mT, pre=nextW)

            with ExitStack() as s4:
                W4 = 4
                xb = [SB(s4, "x4b%d" % i, [128, D], F32) for i in range(W4)]
                x1 = xb
                tmp32 = [SB(s4, "tmp32b%d" % i, [128, D], F32) for i in range(W4)]
                hb = [SB(s4, "hb4_%d" % i, [128, D], BF16) for i in range(W4)]
                small = [SB(s4, "small4_%d" % i, [128, 8], F32) for i in range(W4)]
                small_n = [SB(s4, "small_n%d" % i, [128, 4], F32) for i in range(W4)]
                junk4 = hb
                modb4 = SB(s4, "modb4", [128, 3 * D], F32)
                S.dma("sp", modb4[:], modscr_d[0, 0:3 * D].partition_broadcast(128), [], ["modb"])
                G1 = modb4[:, 0:1024]
                B2 = modb4[:, 1024:2048]
                A2 = modb4[:, 2048:3072]
                Wo, wko = load_w(BLK_OUT)
                preFF = load_w(BLK_FF)

                def st4(tt):
                    def g(bi):
                        q_ = "_%d" % bi
                        xk = "x4b" + q_
                        x1k = xk
                        tk = "tmp4" + q_
                        sm_ = small[bi]
                        tsl = slice(tt * 128, (tt + 1) * 128)
                        S.dma("sp", xb[bi][:], x_d[tsl, :], [], [xk])
                        pa, pka = PS()
                        pb_, pkb = PS()
                        for kc in range(8):
                            mm(pa[:, :], pka, mT[:, kc, tsl], Wo[:, kc, 0:512], ["mT", wko], kc == 0, kc == 7)
                        for kc in range(8):
                            mm(pb_[:, :], pkb, mT[:, kc, tsl], Wo[:, kc, 512:1024], ["mT", wko], kc == 0, kc == 7)
                        S.op("act", "activation", [pka], ["hb4" + q_, "sa" + q_], out=junk4[bi][:, 0:512], in_=pa[:, :], func=AF.Square, accum_out=sm_[:, 0:1])
                        S.op("dve", "tensor_copy", [pka], [tk], out=tmp32[bi][:, 0:512], in_=pa[:, :])
                        S.op("act", "activation", [pkb], ["hb4" + q_, "sb" + q_], out=junk4[bi][:, 512:1024], in_=pb_[:, :], func=AF.Square, accum_out=sm_[:, 1:2])
                        S.op("dve", "tensor_copy", [pkb], [tk], out=tmp32[bi][:, 512:1024], in_=pb_[:, :])
                        yield
                        S.op("dve", "tensor_tensor", ["sa" + q_, "sb" + q_], ["sc" + q_], out=sm_[:, 2:3], in0=sm_[:, 0:1], in1=sm_[:, 1:2], op=ALU.add)
                        yield
                        S.op("act", "activation", ["sc" + q_, "cst"], ["sd" + q_], out=sm_[:, 3:4], in_=sm_[:, 2:3], func=AF.Sqrt, scale=1.0 / D, bias=C("EPS"))
                        yield
                        S.op("dve", "reciprocal", ["sd" + q_], ["se" + q_], out=sm_[:, 4:5], in_=sm_[:, 3:4])
                        S.op("dve", "scalar_tensor_tensor", ["se" + q_, "modb", tk], [tk], out=tmp32[bi][:], in0=tmp32[bi][:], scalar=sm_[:, 4:5],
                             in1=G1, op0=ALU.mult, op1=ALU.mult)
                        yield
                        S.op("dve", "tensor_tensor", [tk, xk], [x1k], out=x1[bi][:], in0=tmp32[bi][:], in1=xb[bi][:], op=ALU.add)
                        S.dma("sp", y_d[tsl, :], x1[bi][:], [x1k], ["y%d" % tt])
                        yield
                        yield from norm_to_hT_gen(x1[bi][:], x1k, tt, A2, B2, tmp32[bi], tk, hb[bi], "hb4" + q_, small_n[bi], "sn" + q_)
                    return g
                interleave([st4(tt) for tt in range(NT)], W4)
                S.barrier()

        STOP(6)
        with ExitStack() as s5:
            wffo = SB(s5, "wffo", [128, NFF, D], BF16)
            S.dma("pool", wffo[:], wffo_d.rearrange("(j p) c -> p j c", p=128), [], ["wffo"])
            aT = SB(s5, "aT", [128, NFF, 1024], BF16)
            sg = SB(s5, "sg", [128, 512], F32)
            x1b = [SB(s5, "x5b%d" % i, [128, D], F32) for i in range(2)]
            yb = [SB(s5, "y5b%d" % i, [128, D], F32) for i in range(2)]
            tmp32 = SB(s5, "tmp32c", [128, D], F32)
            small = SB(s5, "small5", [128, 8], F32)
            G2 = SB(s5, "modb5", [128, D], F32)
            S.dma("sp", G2[:], modscr_d[0, 3 * D:4 * D].partition_broadcast(128), [], ["modb"])
            for th in range(2):
                for b in range(6):
                    W, wk = preFF if (th == 0 and b == 0) else load_w(BLK_FF + b)
                    nj = 4 if b < 5 else 2
                    for jj in range(nj):
                        j = b * 4 + jj
                        for tg in range(2):
                            ts_ = slice(th * 1024 + tg * 512, th * 1024 + (tg + 1) * 512)
                            pg, pkg = PS()
                            pu, pku = PS()
                            for kc in range(8):
                                mm(pg[:, :], pkg, W[:, kc, jj * 128:(jj + 1) * 128], hT[:, kc, ts_], [wk, "hT"], kc == 0, kc == 7)
                            for kc in range(8):
                                mm(pu[:, :], pku, W[:, kc, 512 + jj * 128:512 + (jj + 1) * 128], hT[:, kc, ts_], [wk, "hT"], kc == 0, kc == 7)
                            S.op("act", "activation", [pkg], ["sg"], out=sg[:], in_=pg[:, :], func=AF.Silu)
                            S.op("dve", "tensor_tensor", [pku, "sg"], ["aT"], out=aT[:, j, tg * 512:(tg + 1) * 512], in0=pu[:, :], in1=sg[:], op=ALU.mult)
                for t8 in range(8):
                    tt = th * 8 + t8
                    tsl = slice(tt * 128, (tt + 1) * 128)
                    xk = "x5b%d" % (tt % 2)
                    yk = "y5b%d" % (tt % 2)
                    S.dma("sp", x1b[tt % 2][:], y_d[tsl, :], ["y%d" % tt], [xk])
                    pa, pka = PS()
                    pb_, pkb = PS()
                    for j in range(NFF):
                        mm(pa[:, :], pka, aT[:, j, t8 * 128:(t8 + 1) * 128], wffo[:, j, 0:512], ["aT", "wffo"], j == 0, j == NFF - 1)
                    for j in range(NFF):
                        mm(pb_[:, :], pkb, aT[:, j, t8 * 128:(t8 + 1) * 128], wffo[:, j, 512:1024], ["aT", "wffo"], j == 0, j == NFF - 1)
                    S.op("act", "activation", [pka], ["junk5", "sm5a"], out=tmp32[:, 0:512], in_=pa[:, :], func=AF.Square, accum_out=small[:, 0:1])
                    S.op("act", "activation", [pkb], ["junk5", "sm5b"], out=tmp32[:, 512:1024], in_=pb_[:, :], func=AF.Square, accum_out=small[:, 1:2])
                    S.op("dve", "tensor_tensor", ["sm5a", "sm5b"], ["sm5c"], out=small[:, 2:3], in0=small[:, 0:1], in1=small[:, 1:2], op=ALU.add)
                    S.op("act", "activation", ["sm5c", "cst"], ["sm5d"], out=small[:, 3:4], in_=small[:, 2:3], func=AF.Sqrt, scale=1.0 / D, bias=C("EPS"))
                    S.op("dve", "reciprocal", ["sm5d"], ["sm5e"], out=small[:, 4:5], in_=small[:, 3:4])
                    S.op("dve", "scalar_tensor_tensor", [pka, "sm5e", "modb", "junk5"], ["junk5"], out=tmp32[:, 0:512], in0=pa[:, :], scalar=small[:, 4:5],
                         in1=G2[:, 0:512], op0=ALU.mult, op1=ALU.mult)
                    S.op("dve", "scalar_tensor_tensor", [pkb, "sm5e", "modb", "junk5"], ["junk5"], out=tmp32[:, 512:1024], in0=pb_[:, :], scalar=small[:, 4:5],
                         in1=G2[:, 512:1024], op0=ALU.mult, op1=ALU.mult)
                    S.op("dve", "tensor_tensor", ["junk5", xk], [yk], out=yb[tt % 2][:], in0=tmp32[:], in1=x1b[tt % 2][:], op=ALU.add)
                    S.dma("sp", y_d[tsl, :], yb[tt % 2][:], [yk], ["y%d" % tt])
            S.barrier()
        print("INSTR", dict(S.count), "dma", S.di)


_PROG = {}
_RETURN_MAPS = [False]


def _rope_tables():
    L = 2048
    rows = L // 64
    row_idx = np.repeat(np.arange(rows, dtype=np.float32), 64)
    col_idx = (np.arange(L) % 64).astype(np.float32)
    n_freq = 32
    freqs = (np.float32(10000.0) ** (-np.arange(n_freq, dtype=np.float32) / np.float32(n_freq))).astype(np.float32)
    ang = np.concatenate([row_idx[:, None] * freqs, col_idx[:, None] * freqs], axis=-1).astype(np.float32)
    cos = np.cos(ang).astype(np.float32)
    sin = np.sin(ang).astype(np.float32)
    d = np.arange(128)
    c = cos[:, d // 2].T.copy()
    s = sin[:, d // 2].T.copy()
    s[0::2, :] *= -1.0
    return np.ascontiguousarray(c), np.ascontiguousarray(s)


def _pack_weights(w_in, w_ret_o, w_dn_o, w_out, w_ffn_in):
    blk = np.zeros((NBLK, D, D), np.float32)
    swap = np.arange(128) ^ 1
    for h in range(4):
        q = w_in[:, h * 128:(h + 1) * 128]
        k = w_in[:, 512 + h * 128:512 + (h + 1) * 128]
        blk[BLK_R + h, :, 0:128] = q
        blk[BLK_R + h, :, 128:256] = q[:, swap]
        blk[BLK_R + h, :, 256:384] = k
        blk[BLK_R + h, :, 384:512] = k[:, swap]
        blk[BLK_R + h, :, 512:768] = w_in[:, 1024 + h * 256:1024 + (h + 1) * 256]
        blk[BLK_R + h, :, 768:1024] = w_in[:, 2048 + h * 256:2048 + (h + 1) * 256]
        blk[BLK_D + h, :, 0:128] = w_in[:, 3072 + h * 128:3072 + (h + 1) * 128]
        blk[BLK_D + h, :, 128:256] = w_in[:, 3584 + h * 128:3584 + (h + 1) * 128]
        blk[BLK_D + h, :, 256:512] = w_in[:, 4096 + h * 256:4096 + (h + 1) * 256]
        blk[BLK_D + h, :, 512:768] = w_in[:, 5120 + h * 256:5120 + (h + 1) * 256]
    blk[BLK_GR] = w_in[:, 6160:7184]
    blk[BLK_GD] = w_in[:, 7184:8208]
    blk[BLK_RO] = w_ret_o
    blk[BLK_DO] = w_dn_o
    blk[BLK_OUT] = w_out
    for b in range(6):
        n = 512 if b < 5 else 256
        blk[BLK_FF + b, :, 0:n] = w_ffn_in[:, b * 512:b * 512 + n]
        blk[BLK_FF + b, :, 512:512 + n] = w_ffn_in[:, DFF + b * 512:DFF + b * 512 + n]
    w_ba = np.ascontiguousarray(w_in[:, 6144:6160])
    return blk, w_ba


def kernel(x_prompt, x_sample, c, state_ret, state_dn, c_ctx, w_mod, b_mod, norm_w, w_in, conv_w, ret_decay_logit,
           ret_gn_w, dn_a_log, dn_dt_bias, dn_norm_w, w_ret_o, w_dn_o, w_out, w_ffn_in, w_ffn_out):
    f = lambda a: np.ascontiguousarray(np.asarray(a, dtype=np.float32))
    x_prompt, x_sample, c, state_ret, state_dn, c_ctx = map(f, (x_prompt, x_sample, c, state_ret, state_dn, c_ctx))
    w_in0 = f(w_in)[0]
    blk, w_ba = _pack_weights(w_in0, f(w_ret_o)[0], f(w_dn_o)[0], f(w_out)[0], f(w_ffn_in)[0])
    ropec, ropes = _rope_tables()
    ones_c = np.ones((128, T), np.float32)
    zeros_c = np.zeros((128, T), np.float32)
    shared = {
        "w_mod": f(w_mod)[0], "b_mod": f(b_mod)[0], "norm_w": f(norm_w)[0].reshape(-1), "wblk": blk, "w_ba": w_ba,
        "w_ffo": f(w_ffn_out)[0], "conv_w": f(conv_w)[0], "ret_decay_logit": f(ret_decay_logit)[0].reshape(-1),
        "ret_gn_w": f(ret_gn_w)[0], "dn_a_log": f(dn_a_log)[0].reshape(-1), "dn_dt_bias": f(dn_dt_bias)[0].reshape(-1),
        "dn_norm_w": f(dn_norm_w)[0], "cst": _build_cst(), "ident": np.eye(128, dtype=np.float32),
    }
    in_maps = []
    for core in range(8):
        m = dict(shared)
        if core < 4:
            m["x"] = x_sample[core]
            m["cond"] = c[core]
            m["st_ret"] = np.ascontiguousarray(state_ret[core, 0])
            m["st_dn"] = np.ascontiguousarray(state_dn[core, 0])
            m["rope_c"] = ropec
            m["rope_s"] = ropes
            m["flag"] = np.ones((128, 1), np.float32)
        else:
            j = core - 4
            xs = np.zeros((T, D), np.float32)
            xs[0:1024] = x_prompt[4 * j:4 * j + 4].reshape(1024, D)
            xs[1024:2048] = xs[0:1024]
            m["x"] = xs
            m["cond"] = c_ctx
            m["st_ret"] = np.zeros((2, 4, 128, 256), np.float32)
            m["st_dn"] = np.zeros((2, 4, 128, 256), np.float32)
            m["rope_c"] = ones_c
            m["rope_s"] = zeros_c
            m["flag"] = np.zeros((128, 1), np.float32)
        in_maps.append(m)
    if _RETURN_MAPS[0]:
        return in_maps
    if "nc" not in _PROG:
        _PROG["nc"] = build_program()
    res = run_bass_kernel_spmd(_PROG["nc"], in_maps, core_ids=list(range(8)))
    rr = res.results
    y_sample = np.stack([rr[i]["y"] for i in range(4)], 0)
    y_prompt = np.concatenate([rr[4 + j]["y"][0:1024].reshape(4, 256, D) for j in range(4)], 0)
    new_ret = np.concatenate([rr[4 + j]["sret_o"] for j in range(4)], 0)[:, None]
    new_dn = np.concatenate([rr[4 + j]["sdn_o"] for j in range(4)], 0)[:, None]
    return (y_prompt.astype(np.float32), y_sample.astype(np.float32), new_ret.astype(np.float32), new_dn.astype(np.float32))
```

```python
import numpy as np
from contextlib import ExitStack
import concourse.bass as bass
import concourse.mybir as mybir
from concourse.bass_utils import run_bass_kernel_spmd

F32 = mybir.dt.float32
F32R = mybir.dt.float32r
BF16 = mybir.dt.bfloat16
AF = mybir.ActivationFunctionType
ALU = mybir.AluOpType

T = 2048
D = 1024
NT = 16
EPS = 1e-6
BIG = 30000.0
DFF = 2816
NFF = 22
DBGR = 0

BLK_R = 0
BLK_D = 4
BLK_GR = 8
BLK_GD = 9
BLK_RO = 10
BLK_DO = 11
BLK_OUT = 12
BLK_FF = 13
NBLK = 19


def _cst_layout():
    names = [("ONES", 128),
             ("BLKM", 128), ("I128", 128), ("TRILO128", 128), ("TRIUP128", 128), ("NEGONES128", 128),
             ("MASKF128", 256), ("MASKB128", 256), ("SMF128", 256), ("SMB128", 256),
             ("DPOS8", 128), ("DNEG8", 128), ("GE8", 128), ("LE8", 128), ("POS1_8", 128), ("POS128M", 128),
             ("KPF", 1), ("KPB", 1), ("EPS", 1), ("EPS128", 1), ("ONE", 1), ("C64", 1), ("ZERO", 1)]
    off = {}
    o = 0
    for n, w in names:
        off[n] = (o, w)
        o += w
    return off, o


CST_OFF, NCST = _cst_layout()


def _build_cst():
    c = np.zeros((128, NCST), np.float32)
    p = np.arange(128)[:, None]
    pp = p % 64

    def put(name, arr):
        if name not in CST_OFF:
            return
        o, w = CST_OFF[name]
        c[:, o:o + w] = arr

    f64 = np.arange(64)[None, :]
    f128 = np.arange(128)[None, :]
    put("I2", (pp == f64))
    put("TRILO", (pp <= f64))
    put("TRIUP", (pp >= f64))
    put("ONES", np.ones((128, 128)))
    put("NEGONES", -np.ones((128, 64)))
    put("SELH0", np.broadcast_to(p < 64, (128, 128)))
    put("SELH1", np.broadcast_to(p >= 64, (128, 128)))
    put("BD1", (p // 64 == f128 // 64))
    put("BDLO", (p // 64 == f128 // 64) & (pp <= f128 % 64))
    put("BDUP", (p // 64 == f128 // 64) & (pp >= f128 % 64))
    ge = (f64 >= pp).astype(np.float32)
    le = (f64 <= pp).astype(np.float32)
    gt = (f64 > pp).astype(np.float32)
    lt = (f64 < pp).astype(np.float32)
    nge = -BIG * (1 - ge)
    nle = -BIG * (1 - le)
    put("MASKF", np.concatenate([nge, nle], 1))
    put("MASKB", np.concatenate([nle, nge], 1))
    put("SMF", np.concatenate([-gt, -lt], 1))
    put("SMB", np.concatenate([-lt, -gt], 1))
    put("DPOS", np.maximum(f64 - pp, 0))
    put("DNEG", np.maximum(pp - f64, 0))
    put("GE", ge)
    put("LE", le)
    put("POS1", np.broadcast_to(f64 + 1.0, (128, 64)))
    put("POS64M", np.broadcast_to(64.0 - f64, (128, 64)))
    put("I128", (p == f128))
    put("TRILO128", (p <= f128))
    put("TRIUP128", (p >= f128))
    put("NEGONES128", -np.ones((128, 128)))
    ge8 = (f128 >= p).astype(np.float32)
    le8 = (f128 <= p).astype(np.float32)
    gt8 = (f128 > p).astype(np.float32)
    lt8 = (f128 < p).astype(np.float32)
    put("BLKM", (f128 // 64 == p // 64))
    put("MASKF128", np.concatenate([-BIG * (1 - ge8), -BIG * (1 - le8)], 1))
    put("MASKB128", np.concatenate([-BIG * (1 - le8), -BIG * (1 - ge8)], 1))
    put("SMF128", np.concatenate([-gt8, -lt8], 1))
    put("SMB128", np.concatenate([-lt8, -gt8], 1))
    put("DPOS8", np.maximum(f128 - p, 0))
    put("DNEG8", np.maximum(p - f128, 0))
    put("GE8", (f128 >= p))
    put("LE8", (f128 <= p))
    put("POS1_8", np.broadcast_to(f128 + 1.0, (128, 128)))
    put("POS128M", np.broadcast_to(128.0 - f128, (128, 128)))
    put("KPF", 127.0 - p)
    put("KPB", p * 1.0)
    put("EPS", np.full((128, 1), EPS))
    put("EPS128", np.full((128, 1), 128 * EPS))
    put("ONE", np.ones((128, 1)))
    put("C64", np.full((128, 1), 64.0))
    return c


class Sched:
    EPOCH = 3000
    NDMA = 24

    def __init__(self, nc, es):
        self.nc = nc
        self.es = es
        self.eng = {"pe": nc.tensor, "dve": nc.vector, "act": nc.scalar, "pool": nc.gpsimd, "sp": nc.sync}
        self.count = {e: 0 for e in self.eng}
        self.sems = {e: [] for e in self.eng}
        self.waited = {e: {} for e in self.eng}
        self.res = {}
        self.pending = {e: ([], []) for e in self.eng}
        self.dsem = [es.enter_context(nc.semaphore("dsem%d" % i)) for i in range(self.NDMA)]
        self.duse = [0] * self.NDMA
        self.qpool = {"sp": list(range(0, 16)), "pool": list(range(16, self.NDMA)), "act": list(range(0, 16))}
        self.qi = {"sp": 0, "pool": 0, "act": 0}
        self.di = 0
        self.semid = {}
        self.all_dma_events = []

    def _sid(self, sem):
        k = id(sem)
        if k not in self.semid:
            self.semid[k] = len(self.semid)
        return self.semid[k]

    def _tl_sem(self, e, epoch):
        while len(self.sems[e]) <= epoch:
            self.sems[e].append(self.es.enter_context(self.nc.semaphore("tl_%s_%d" % (e, len(self.sems[e])))))
        return self.sems[e][epoch]

    def _wait(self, e, ev):
        if ev is None:
            return
        sem, val = ev[0], ev[1]
        sid = self._sid(sem)
        if self.waited[e].get(sid, 0) >= val:
            return
        self.eng[e].wait_ge(sem, val)
        self.waited[e][sid] = val

    def _deps(self, e, reads, writes):
        evs = []
        for k in reads:
            r = self.res.get(k)
            if r is not None and r["w"] is not None:
                evs.append(r["w"])
        for k in writes:
            r = self.res.get(k)
            if r is not None:
                if r["w"] is not None:
                    evs.append(r["w"])
                evs.extend(r["r"].values())
        for ev in evs:
            if ev[0] == "PENDING":
                if ev[1] != e:
                    raise RuntimeError("dependency on pending (no-inc) instruction of %s" % ev[1])
                continue
            self._wait(e, ev)

    def _commit(self, ev, reads, writes, tl):
        for k in writes:
            self.res[k] = {"w": ev, "r": {}}
        for k in reads:
            r = self.res.setdefault(k, {"w": None, "r": {}})
            r["r"][tl] = ev

    def op(self, e, method, reads, writes, inc=True, **kw):
        if e != "pe":
            writes = list(writes) + [k for k in reads if k.startswith("ps")]
        self._deps(e, reads, writes)
        inst = getattr(self.eng[e], method)(**kw)
        if not inc:
            pr, pw = self.pending[e]
            pr.extend(reads)
            pw.extend(writes)
            for k in writes:
                self.res[k] = {"w": ("PENDING", e), "r": {}}
            return inst
        i = self.count[e]
        self.count[e] += 1
        sem = self._tl_sem(e, i // self.EPOCH)
        val = i % self.EPOCH + 1
        inst.then_inc(sem, 1)
        ev = (sem, val)
        pr, pw = self.pending[e]
        self._commit(ev, list(reads) + pr, list(writes) + pw, "tl_" + e + str(i // self.EPOCH))
        self.pending[e] = ([], [])
        return inst

    def dma(self, q, out, in_, reads, writes, **kw):
        self._deps(q, reads, writes)
        pl = self.qpool[q]
        s = pl[self.qi[q] % len(pl)]
        self.qi[q] += 1
        self.di += 1
        sem = self.dsem[s]
        self._wait(q, (sem, 16 * self.duse[s]))
        self.duse[s] += 1
        val = 16 * self.duse[s]
        self.eng[q].dma_start(out=out, in_=in_, **kw).then_inc(sem, 16)
        ev = (sem, val)
        self._commit(ev, reads, writes, "dma%d_%d" % (s, self.duse[s]))
        self.all_dma_events.append(ev)
        if len(self.all_dma_events) > 2 * self.NDMA:
            self.all_dma_events = self.all_dma_events[-self.NDMA:]

    def barrier(self):
        evs = []
        for e in self.eng:
            if self.count[e] > 0:
                i = self.count[e] - 1
                evs.append((self.sems[e][i // self.EPOCH], i % self.EPOCH + 1))
        for s in range(self.NDMA):
            if self.duse[s] > 0:
                evs.append((self.dsem[s], 16 * self.duse[s]))
        for e in self.eng:
            for ev in evs:
                self._wait(e, ev)
        self.res = {}


class _Stop(Exception):
    pass


def build_program(upto=99):
    nc = bass.Bass("TRN2", target_bir_lowering=False)
    try:
        _build_body(nc, upto)
    except _Stop:
        pass
    return nc


def _build_body(nc, upto):

    def din(name, shape, dt=F32):
        return nc.dram_tensor(name, shape, dt, kind="ExternalInput").ap()

    def dout(name, shape, dt=F32):
        return nc.dram_tensor(name, shape, dt, kind="ExternalOutput").ap()

    x_d = din("x", [T, D])
    cond_d = din("cond", [D])
    wmod_d = din("w_mod", [D, 6 * D])
    bmod_d = din("b_mod", [6 * D])
    normw_d = din("norm_w", [4 * D])
    wblk_d = din("wblk", [NBLK, D, D])
    wba_d = din("w_ba", [D, 16])
    wffo_d = din("w_ffo", [DFF, D])
    convw_d = din("conv_w", [3, 2048])
    rdl_d = din("ret_decay_logit", [8])
    gnw_d = din("ret_gn_w", [1024])
    alog_d = din("dn_a_log", [8])
    dtb_d = din("dn_dt_bias", [8])
    dnw_d = din("dn_norm_w", [256])
    sret_d = din("st_ret", [2, 4, 128, 256])
    sdn_d = din("st_dn", [2, 4, 128, 256])
    ropec_d = din("rope_c", [128, T])
    ropes_d = din("rope_s", [128, T])
    flag_d = din("flag", [128, 1])
    cst_d = din("cst", [128, NCST])
    ident_d = din("ident", [128, 128])
    y_d = dout("y", [T, D])
    sret_o = dout("sret_o", [4, 2, 4, 128, 256])
    sdn_o = dout("sdn_o", [4, 2, 4, 128, 256])
    yrg_d = nc.dram_tensor("yrg_scr", [8, 128, T], BF16, kind="Internal").ap()
    modscr_d = nc.dram_tensor("mod_scr", [1, 4 * D], F32, kind="Internal").ap()

    with ExitStack() as es:
        S = Sched(nc, es)

        dbgn = [0]

        def DBG(name, ap, shape, dt, reads):
            d = nc.dram_tensor("dbg_" + name, shape, dt, kind="ExternalOutput").ap()
            S.dma("sp", d, ap, reads, [])

        def STOP(k):
            if upto == k:
                S.barrier()
                print("STOP at", k, "INSTR", dict(S.count), "dma", S.di)
                raise _Stop()

        def MEM(tag):
            pass

        sbn = [0]

        def SB(stack, name, shape, dt):
            sbn[0] += 1
            return stack.enter_context(nc.sbuf_tensor("s%d_%s" % (sbn[0], name), shape, dt))

        psb = [es.enter_context(nc.psum_tensor("psb%d" % i, [128, 512], F32)) for i in range(7)]
        psT = es.enter_context(nc.psum_tensor("psT", [128, 1024], BF16))
        psi = [0]

        held = set()

        def PS():
            while True:
                i = psi[0] % 7
                psi[0] += 1
                if i not in held:
                    return psb[i], "psb%d" % i

        cst = SB(es, "cst", [128, NCST], F32)
        S.dma("sp", cst[:], cst_d[:, :], [], ["cst"])

        def C(name, rows=slice(0, 128), w=None):
            o, ww = CST_OFF[name]
            if w is None:
                w = ww
            return cst[rows, o:o + w]

        identb = SB(es, "identb", [128, 128], BF16)
        S.dma("pool", identb[:], ident_d[:, :], [], ["identb"])
        flag = SB(es, "flag", [128, 1], F32)
        S.dma("sp", flag[:], flag_d[:, :], [], ["flag"])
        hT = SB(es, "hT", [128, 8, T], BF16)
        wbuf = [SB(es, "wbuf%d" % i, [128, 8, D], BF16) for i in range(2)]
        wslot = [0]
        s01 = ExitStack()
        s01.__enter__()
        modA = SB(s01, "modA", [128, 2 * D], F32)
        modb = SB(s01, "modb", [128, 4 * D], F32)

        def load_w(blk):
            i = wslot[0] % 2
            wslot[0] += 1
            S.dma("pool", wbuf[i][:], wblk_d[blk].rearrange("(k p) c -> p k c", p=128), [], ["wbuf%d" % i])
            return wbuf[i], "wbuf%d" % i

        def mm(ps, pk, lhsT, rhs, rk, start, stop, last=None):
            if last is None:
                last = stop
            S.op("pe", "matmul", rk + ["cst"], [pk], inc=last, out=ps, lhsT=lhsT, rhs=rhs, start=start, stop=stop)

        MEM("global")
        s0 = ExitStack()
        s0.__enter__()
        normw = SB(s0, "normw", [128, 4 * D], F32)
        wm = [SB(s0, "wm%d" % i, [128, 8, 512], F32) for i in range(2)]
        cond = SB(s0, "cond", [128, 8], F32)
        csil = SB(s0, "csil", [128, 8], F32)
        cb = SB(s0, "cb", [128, 8, 128], F32)
        S.dma("sp", modA[:], bmod_d[0:2 * D].partition_broadcast(128), [], ["modA"])
        S.dma("sp", modb[:], bmod_d[2 * D:6 * D].partition_broadcast(128), [], ["modb"])
        S.dma("sp", normw[:], normw_d.partition_broadcast(128), [], ["normw"])
        S.dma("sp", cond[:], cond_d.rearrange("(k p) -> p k", p=128), [], ["cond"], allow_slow_non_contiguous=True)
        S.op("act", "activation", ["cond"], ["csil"], out=csil[:], in_=cond[:], func=AF.Silu)
        S.op("dve", "tensor_copy", ["csil"], ["cb"], out=cb[:], in_=csil[:].unsqueeze(2).to_broadcast([128, 8, 128]))

        def mod_cols(cg):
            w_ = wm[cg % 2]
            wk = "wm%d" % (cg % 2)
            S.dma("sp", w_[:], wmod_d[:, cg * 512:(cg + 1) * 512].rearrange("(k p) c -> p k c", p=128), [], [wk])
            ps, pk = PS()
            for kc in range(8):
                mm(ps[:, :], pk, cb[:, kc, :], w_[:, kc, :], ["cb", wk], kc == 0, kc == 7)
            if cg < 4:
                msl = modA[:, cg * 512:(cg + 1) * 512]
                mk = "modA"
            else:
                msl = modb[:, (cg - 4) * 512:(cg - 3) * 512]
                mk = "modb"
            S.op("dve", "tensor_tensor", [pk, mk], [mk], out=msl, in0=msl, in1=ps[:, :], op=ALU.add)

        for cg in range(4):
            mod_cols(cg)
        S.op("dve", "scalar_tensor_tensor", ["modA", "normw"], ["modA"], out=modA[:, 1024:2048], in0=modA[:, 1024:2048],
             scalar=1.0, in1=normw[:, 0:1024], op0=ALU.add, op1=ALU.mult)

        def stage0b():
            for cg in range(4, 12):
                mod_cols(cg)
                yield
            S.op("dve", "tensor_tensor", ["modb", "normw"], ["modb"], out=modb[:, 0:1024], in0=modb[:, 0:1024],
                 in1=normw[:, 1024:2048], op=ALU.mult)
            S.op("dve", "scalar_tensor_tensor", ["modb", "normw"], ["modb"], out=modb[:, 2048:3072], in0=modb[:, 2048:3072],
                 scalar=1.0, in1=normw[:, 2048:3072], op0=ALU.add, op1=ALU.mult)
            S.op("dve", "tensor_tensor", ["modb", "normw"], ["modb"], out=modb[:, 3072:4096], in0=modb[:, 3072:4096],
                 in1=normw[:, 3072:4096], op=ALU.mult)
            S.dma("sp", modscr_d[0:1, :], modb[0:1, :], ["modb"], ["modscr"])
            yield

        B1 = modA[:, 0:1024]
        A1 = modA[:, 1024:2048]

        def interleave(factories, width, extra=None, extras=None):
            extras = list(extras or [])
            active = []
            free = list(range(width))
            it = iter(factories)
            pending = True
            while pending or active:
                while free and pending:
                    try:
                        f = next(it)
                    except StopIteration:
                        pending = False
                        break
                    bi = free.pop(0)
                    active.append((f(bi), bi))
                if extra is not None:
                    try:
                        next(extra)
                    except StopIteration:
                        extra = None
                for eg_ in list(extras):
                    try:
                        next(eg_)
                    except StopIteration:
                        extras.remove(eg_)
                for item in list(active):
                    try:
                        next(item[0])
                    except StopIteration:
                        active.remove(item)
                        free.append(item[1])
            while extra is not None:
                try:
                    next(extra)
                except StopIteration:
                    extra = None
            while extras:
                for eg_ in list(extras):
                    try:
                        next(eg_)
                    except StopIteration:
                        extras.remove(eg_)

        def norm_to_hT_gen(src, srck, tt, A, B, tmp32, tk, hb, hk, small, smk, modk="modb"):
            S.op("act", "activation", [srck], [tk, smk + "0"], out=tmp32[:], in_=src, func=AF.Square, accum_out=small[:, 0:1])
            yield
            S.op("act", "activation", [smk + "0", "cst"], [smk + "1"], out=small[:, 1:2], in_=small[:, 0:1], func=AF.Sqrt,
                 scale=1.0 / D, bias=C("EPS"))
            yield
            S.op("dve", "reciprocal", [smk + "1"], [smk + "2"], out=small[:, 2:3], in_=small[:, 1:2])
            S.op("dve", "scalar_tensor_tensor", [srck, smk + "2", modk], [tk], out=tmp32[:], in0=src, scalar=small[:, 2:3],
                 in1=A, op0=ALU.mult, op1=ALU.mult)
            yield
            S.op("dve", "tensor_tensor", [tk, modk], [hk], out=hb[:], in0=tmp32[:], in1=B, op=ALU.add)
            yield
            for kc in range(8):
                S.op("pe", "transpose", [hk, "identb"], ["psT"], inc=(kc == 7), out=psT[:, kc * 128:(kc + 1) * 128],
                     in_=hb[:, kc * 128:(kc + 1) * 128], identity=identb[:])
            S.op("act", "activation", ["psT"], ["hT"], out=hT[:, :, tt * 128:(tt + 1) * 128],
                 in_=psT[:, :].rearrange("p (k t) -> p k t", k=8), func=AF.Copy)
            yield

        preR0 = load_w(BLK_R)
        with ExitStack() as s1:
            W1 = 4
            xb = [SB(s1, "xb%d" % i, [128, D], F32) for i in range(W1)]
            tmp32 = [SB(s1, "tmp32_%d" % i, [128, D], F32) for i in range(W1)]
            hb = [SB(s1, "hb%d" % i, [128, D], BF16) for i in range(W1)]
            small = [SB(s1, "small%d" % i, [128, 4], F32) for i in range(W1)]

            def st1(tt):
                def g(bi):
                    xk = "xb%d" % bi
                    S.dma("sp", xb[bi][:], x_d[tt * 128:(tt + 1) * 128, :], [], [xk])
                    yield
                    yield from norm_to_hT_gen(xb[bi][:], xk, tt, A1, B1, tmp32[bi], "tmp%d" % bi, hb[bi], "hb%d" % bi, small[bi], "sm%d_" % bi, "modA")
                return g
            interleave([st1(tt) for tt in range(NT)], W1, extra=stage0b())
            S.barrier()
        s0.__exit__(None, None, None)
        s01.__exit__(None, None, None)
        STOP(1)

        MEM("before mixers")
        with ExitStack() as sB:
            zT = SB(sB, "zT", [128, 8, T], BF16)
            par = SB(sB, "par", [128, 64], F32)
            lg = SB(sB, "lg", [128, 8], F32)
            dnw = SB(sB, "dnw", [128, 256], F32)
            S.dma("sp", dnw[:], dnw_d.partition_broadcast(128), [], ["dnw"])
            STOP(1.5)
            S32 = [[SB(sB, "S32_%d_%d" % (r, i), [128, 256], F32) for i in range(2)] for r in range(2)]
            S16 = [[SB(sB, "S16_%d_%d" % (r, i), [128, 256], BF16) for i in range(2)] for r in range(2)]

            def state_boundary(r, cur, seg, h, out_d):
                k = "S32_%d_%d" % (r, cur)
                k2 = "S32_%d_%d" % (r, 1 - cur)
                if seg < 4:
                    S.dma("sp", out_d[seg, r, h], S32[r][cur][:], [k], [])
                S.op("dve", "tensor_scalar", [k, "flag"], [k2], out=S32[r][1 - cur][:], in0=S32[r][cur][:], scalar1=flag[:, 0:1],
                     scalar2=None, op0=ALU.mult)
                return 1 - cur

            MEM("before sR")
            with ExitStack() as sR:
                cdec = SB(sR, "cdec", [128, 8], F32)
                kdec = SB(sR, "kdec", [128, 8], F32)
                qpat = SB(sR, "qpat", [128, 8, 128], F32)
                MT = SB(sR, "MT", [128, 4, 128], F32)
                S.dma("sp", par[:, 0:8], rdl_d.partition_broadcast(128), [], ["par"])
                S.op("act", "activation", ["par"], ["par1"], out=par[:, 8:16], in_=par[:, 0:8], func=AF.Exp, scale=-1.0)
                S.op("act", "activation", ["par1", "cst"], ["par2"], out=par[:, 16:24], in_=par[:, 8:16], func=AF.Ln, bias=C("ONE"))
                S.op("dve", "tensor_scalar", ["par2"], ["lg"], out=lg[:], in0=par[:, 16:24], scalar1=-1.0, scalar2=None, op0=ALU.mult)
                S.op("act", "activation", ["lg"], ["cdec"], out=cdec[:], in_=lg[:], func=AF.Exp, scale=128.0)
                mtmp = SB(sR, "mtmp", [128, 2, 128], F32)
                for h in range(4):
                    S.op("act", "activation", ["lg", "cst"], ["kdec"], out=kdec[:, h:h + 1], in_=C("KPF"), func=AF.Exp, scale=lg[:, h:h + 1])
                    S.op("act", "activation", ["lg", "cst"], ["kdec"], out=kdec[:, 4 + h:5 + h], in_=C("KPB"), func=AF.Exp, scale=lg[:, 4 + h:5 + h])
                    S.op("act", "activation", ["lg", "cst"], ["qpat"], out=qpat[:, h, :], in_=C("POS1_8"), func=AF.Exp, scale=lg[:, h:h + 1])
                    S.op("act", "activation", ["lg", "cst"], ["qpat"], out=qpat[:, 4 + h, :], in_=C("POS128M"), func=AF.Exp, scale=lg[:, 4 + h:5 + h])
                    S.op("act", "activation", ["lg", "cst"], ["mtmp"], out=mtmp[:, 0, :], in_=C("DPOS8"), func=AF.Exp, scale=lg[:, h:h + 1])
                    S.op("act", "activation", ["lg", "cst"], ["mtmp"], out=mtmp[:, 1, :], in_=C("DNEG8"), func=AF.Exp, scale=lg[:, 4 + h:5 + h])
                    S.op("dve", "tensor_tensor", ["mtmp", "cst"], ["mtmp"], out=mtmp[:, 0, :], in0=mtmp[:, 0, :], in1=C("GE8"), op=ALU.mult)
                    S.op("dve", "tensor_tensor", ["mtmp", "cst"], ["mtmp"], out=mtmp[:, 1, :], in0=mtmp[:, 1, :], in1=C("LE8"), op=ALU.mult)
                    S.op("dve", "tensor_tensor", ["mtmp"], ["MT"], out=MT[:, h, :], in0=mtmp[:, 0, :], in1=mtmp[:, 1, :], op=ALU.add)

                gnw = SB(sR, "gnw", [128, 1024], F32)
                S.dma("sp", gnw[:], gnw_d.partition_broadcast(128), [], ["gnw"])
                ropec = SB(sR, "ropec", [128, T], BF16)
                ropes = SB(sR, "ropes", [128, T], BF16)
                S.dma("pool", ropec[:], ropec_d[:, :], [], ["ropec"])
                S.dma("pool", ropes[:], ropes_d[:, :], [], ["ropes"])
                qT = SB(sR, "qT", [128, T], BF16)
                kT = SB(sR, "kT", [128, T], BF16)
                qfT = SB(sR, "qfT", [128, T], BF16)
                qbT = SB(sR, "qbT", [128, T], BF16)
                kft = SB(sR, "kft", [128, NT, 128], BF16)
                kbt = SB(sR, "kbt", [128, NT, 128], BF16)
                vt = SB(sR, "vt", [128, NT, 256], BF16)
                WB = 5
                rg = [SB(sR, "rg%d" % i, [128, 256], BF16) for i in range(WB)]
                of = SB(sR, "of", [128, NT, 256], BF16)
                ob = SB(sR, "ob", [128, NT, 256], BF16)
                sT = SB(sR, "sT", [128, NT, 128], BF16)
                tA = SB(sR, "tA", [128, 512], F32)
                tB = SB(sR, "tB", [128, 512], F32)
                o32 = [SB(sR, "o32_%d" % i, [128, 256], F32) for i in range(WB)]
                t32 = [SB(sR, "t32_%d" % i, [128, 256], F32) for i in range(WB)]
                zb = [SB(sR, "zb%d" % i, [128, 256], BF16) for i in range(WB)]
                sm = [SB(sR, "sm%d" % i, [128, 8], F32) for i in range(WB)]
                MEM("sR after alloc")
                pendingRB = []
                nextW = preR0
                for h in range(4):
                    W, wk = nextW
                    def ret_pre(h=h, W=W, wk=wk):
                        for tg in range(4):
                            ts_ = slice(tg * 512, (tg + 1) * 512)
                            for (dst, dk_, c0, scale) in ((qT, "qT", 0, 128.0 ** -0.5), (kT, "kT", 256, 1.0)):
                                psa, pka = PS()
                                for kc in range(8):
                                    mm(psa[:, :], pka, W[:, kc, c0:c0 + 128], hT[:, kc, ts_], [wk, "hT"], kc == 0, kc == 7)
                                psb2, pkb = PS()
                                for kc in range(8):
                                    mm(psb2[:, :], pkb, W[:, kc, c0 + 128:c0 + 256], hT[:, kc, ts_], [wk, "hT"], kc == 0, kc == 7)
                                S.op("dve", "scalar_tensor_tensor", [pka, "ropec"], ["tA"], out=tA[:], in0=psa[:, :], scalar=scale,
                                     in1=ropec[:, ts_], op0=ALU.mult, op1=ALU.mult)
                                S.op("dve", "scalar_tensor_tensor", [pkb, "ropes"], ["tB"], out=tB[:], in0=psb2[:, :], scalar=scale,
                                     in1=ropes[:, ts_], op0=ALU.mult, op1=ALU.mult)
                                S.op("dve", "tensor_tensor", ["tA", "tB"], [dk_], out=dst[:, ts_], in0=tA[:], in1=tB[:], op=ALU.add)
                                yield
                        for tt in range(NT):
                            ps, pk = PS()
                            for kc in range(8):
                                mm(ps[:, 0:256], pk, hT[:, kc, tt * 128:(tt + 1) * 128], W[:, kc, 512:768], [wk, "hT"], kc == 0, kc == 7)
                            S.op("act", "activation", [pk], ["vt"], out=vt[:, tt, :], in_=ps[:, 0:256], func=AF.Copy)
                            if tt % 2 == 1:
                                yield
                        for g4 in range(4):
                            for j in range(4):
                                tt = g4 * 4 + j
                                S.op("pe", "transpose", ["kT", "identb"], ["psT"], inc=(j == 3), out=psT[:, j * 128:(j + 1) * 128],
                                     in_=kT[:, tt * 128:(tt + 1) * 128], identity=identb[:])
                            pv = psT[:, 0:512].rearrange("p (t d) -> p t d", t=4)
                            S.op("dve", "tensor_scalar", ["psT", "kdec"], ["kft"], out=kft[:, g4 * 4:(g4 + 1) * 4, :], in0=pv,
                                 scalar1=kdec[:, h:h + 1], scalar2=None, op0=ALU.mult)
                            S.op("act", "activation", ["psT", "kdec"], ["kbt"], out=kbt[:, g4 * 4:(g4 + 1) * 4, :], in_=pv, func=AF.Copy,
                                 scale=kdec[:, 4 + h:5 + h])
                            yield
                        S.op("dve", "tensor_tensor", ["qT", "qpat"], ["qfT"], out=qfT[:, :].rearrange("p (n i) -> p n i", i=128),
                             in0=qT[:, :].rearrange("p (n i) -> p n i", i=128),
                             in1=qpat[:, h:h + 1, :].to_broadcast([128, 16, 128]), op=ALU.mult)
                        S.op("dve", "tensor_tensor", ["qT", "qpat"], ["qbT"], out=qbT[:, :].rearrange("p (n i) -> p n i", i=128),
                             in0=qT[:, :].rearrange("p (n i) -> p n i", i=128),
                             in1=qpat[:, 4 + h:5 + h, :].to_broadcast([128, 16, 128]), op=ALU.mult)
                        yield
                        for g4 in range(4):
                            ps, pk = PS()
                            for c in range(4):
                                tt = g4 * 4 + c
                                tsl = slice(tt * 128, (tt + 1) * 128)
                                mm(ps[:, c * 128:(c + 1) * 128], pk, kT[:, tsl], qT[:, tsl], ["kT", "qT"], True, True, last=(c == 3))
                            S.op("dve", "tensor_tensor", [pk, "MT"], ["sT"], out=sT[:, g4 * 4:(g4 + 1) * 4, :], in0=ps[:, :].rearrange("p (t i) -> p t i", t=4),
                                 in1=MT[:, h:h + 1, :].to_broadcast([128, 4, 128]), op=ALU.mult)
                            yield
                    interleave([lambda bi: ret_pre()], 1, extras=pendingRB)
                    pendingRB = []
                    nextW = load_w(BLK_R + h + 1) if h < 3 else load_w(BLK_RO)
                    STOP(1.8)
                    cur = [0, 0]
                    c16 = [0, 0]
                    for r in range(2):
                        S.dma("sp", S32[r][0][:], sret_d[r, h], [], ["S32_%d_0" % r])
                        S.op("act", "activation", ["S32_%d_0" % r], ["S16_%d_0" % r], out=S16[r][0][:], in_=S32[r][0][:], func=AF.Copy)
                    for t in range(NT):
                        for r in range(2):
                            tt = t if r == 0 else NT - 1 - t
                            tsl = slice(tt * 128, (tt + 1) * 128)
                            sk = "S32_%d_%d" % (r, cur[r])
                            s16k = "S16_%d_%d" % (r, c16[r])
                            s16n = "S16_%d_%d" % (r, 1 - c16[r])
                            Scur = S16[r][c16[r]]
                            Snew = S16[r][1 - c16[r]]
                            pss, pks = PS()
                            kk = kft if r == 0 else kbt
                            mm(pss[:, 0:256], pks, kk[:, tt, :], vt[:, tt, :], ["kft", "kbt", "vt"], True, True)
                            pso, pko = PS()
                            if r == 0:
                                mm(pso[:, 0:256], pko, sT[:, tt, :], vt[:, tt, :], ["sT", "vt"], True, False)
                                mm(pso[:, 0:256], pko, qfT[:, tsl], Scur[:], ["qfT", s16k], False, True)
                                S.op("act", "activation", [pko], ["of"], out=of[:, tt, :], in_=pso[:, 0:256], func=AF.Copy)
                            else:
                                mm(pso[:, 0:256], pko, qbT[:, tsl], Scur[:], ["qbT", s16k], True, True)
                                S.op("act", "activation", [pko], ["ob"], out=ob[:, tt, :], in_=pso[:, 0:256], func=AF.Copy)
                            bnd = (tt % 2 == 1) if r == 0 else (tt % 2 == 0)
                            if not bnd:
                                S.op("dve", "scalar_tensor_tensor", [pks, sk, "cdec"], [s16n], out=Snew[:], in0=S32[r][cur[r]][:],
                                     scalar=cdec[:, 4 * r + h:4 * r + h + 1], in1=pss[:, 0:256], op0=ALU.mult, op1=ALU.add)
                            S.op("dve", "scalar_tensor_tensor", [pks, sk, "cdec"], [sk], out=S32[r][cur[r]][:], in0=S32[r][cur[r]][:],
                                 scalar=cdec[:, 4 * r + h:4 * r + h + 1], in1=pss[:, 0:256], op0=ALU.mult, op1=ALU.add)
                            if bnd:
                                cur[r] = state_boundary(r, cur[r], tt // 2, h, sret_o)
                                S.op("act", "activation", ["S32_%d_%d" % (r, cur[r])], [s16n], out=Snew[:], in_=S32[r][cur[r]][:], func=AF.Copy)
                            c16[r] = 1 - c16[r]
                    STOP(1.9)
                    def retB(tt, h=h, W=W, wk=wk):
                        def g(bi):
                            o32_, t32_, zb_, sm_, rg_ = o32[bi], t32[bi], zb[bi], sm[bi], rg[bi]
                            q_ = "_%d" % bi
                            ps, pk = PS()
                            for kc in range(8):
                                mm(ps[:, 0:256], pk, hT[:, kc, tt * 128:(tt + 1) * 128], W[:, kc, 768:1024], [wk, "hT"], kc == 0, kc == 7)
                            S.op("act", "activation", [pk], ["rg" + q_], out=rg_[:], in_=ps[:, 0:256], func=AF.Silu)
                            S.op("dve", "scalar_tensor_tensor", ["of", "ob"], ["o32" + q_, "sm0" + q_], out=o32_[:], in0=of[:, tt, :], scalar=1.0,
                                 in1=ob[:, tt, :], op0=ALU.mult, op1=ALU.add, accum_out=sm_[:, 0:1])
                            yield
                            S.op("dve", "tensor_scalar", ["sm0" + q_], ["sm1" + q_], out=sm_[:, 1:2], in0=sm_[:, 0:1], scalar1=-1.0 / 256, scalar2=None, op0=ALU.mult)
                            yield
                            S.op("act", "activation", ["o32" + q_, "sm1" + q_], ["t32" + q_, "sm2" + q_], out=t32_[:], in_=o32_[:], func=AF.Square, bias=sm_[:, 1:2],
                                 accum_out=sm_[:, 2:3])
                            yield
                            S.op("act", "activation", ["sm2" + q_, "cst"], ["sm3" + q_], out=sm_[:, 3:4], in_=sm_[:, 2:3], func=AF.Sqrt, scale=1.0 / 256, bias=C("EPS"))
                            yield
                            S.op("dve", "reciprocal", ["sm3" + q_], ["sm4" + q_], out=sm_[:, 4:5], in_=sm_[:, 3:4])
                            S.op("dve", "scalar_tensor_tensor", ["o32" + q_, "sm1" + q_, "gnw"], ["t32" + q_], out=t32_[:], in0=o32_[:], scalar=sm_[:, 1:2],
                                 in1=gnw[:, h * 256:(h + 1) * 256], op0=ALU.add, op1=ALU.mult)
                            S.op("dve", "scalar_tensor_tensor", ["t32" + q_, "sm4" + q_, "rg" + q_], ["zb" + q_], out=zb_[:], in0=t32_[:], scalar=sm_[:, 4:5],
                                 in1=rg_[:], op0=ALU.mult, op1=ALU.mult)
                            yield
                            for c in range(2):
                                S.op("pe", "transpose", ["zb" + q_, "identb"], ["psT"], inc=(c == 1), out=psT[:, c * 128:(c + 1) * 128],
                                     in_=zb_[:, c * 128:(c + 1) * 128], identity=identb[:])
                            S.op("act", "activation", ["psT"], ["zT"], out=zT[:, 2 * h:2 * h + 2, tt * 128:(tt + 1) * 128],
                                 in_=psT[:, 0:256].rearrange("p (c t) -> p c t", c=2), func=AF.Copy)
                            yield
                        return g
                    def laneRB(tts, bi, retB=retB):
                        for tt in tts:
                            yield from retB(tt)(bi)
                    lanesR = [laneRB(list(range(bi, NT, WB)), bi) for bi in range(WB)]
                    if h < 3:
                        pendingRB = lanesR
                    else:
                        interleave([], 1, extras=lanesR)
                S.barrier()

            def out_gate(blk_o, blk_g, first, mT=None, pre=None):
                with ExitStack() as sg:
                    sig = SB(sg, "sig", [128, 512], F32)
                    yt = [SB(sg, "yt%d" % i, [128, 512], BF16) for i in range(2)]
                    tg32 = SB(sg, "tg32", [128, 512], F32)
                    Wo, wko = pre if pre is not None else load_w(blk_o)
                    Wg, wkg = load_w(blk_g)
                    it = 0
                    for fo in range(8):
                        for tg in range(4):
                            ts_ = slice(tg * 512, (tg + 1) * 512)
                            ps1, pk1 = PS()
                            for kc in range(8):
                                mm(ps1[:, :], pk1, Wo[:, kc, fo * 128:(fo + 1) * 128], zT[:, kc, ts_], [wko, "zT"], kc == 0, kc == 7)
                            ps2, pk2 = PS()
                            for kc in range(8):
                                mm(ps2[:, :], pk2, Wg[:, kc, fo * 128:(fo + 1) * 128], hT[:, kc, ts_], [wkg, "hT"], kc == 0, kc == 7)
                            S.op("act", "activation", [pk2], ["sig"], out=sig[:], in_=ps2[:, :], func=AF.Sigmoid)
                            yk = "yt%d" % (it % 2)
                            y_ = yt[it % 2]
                            it += 1
                            if first:
                                S.op("dve", "tensor_tensor", [pk1, "sig"], [yk], out=y_[:], in0=ps1[:, :], in1=sig[:], op=ALU.mult)
                                S.dma("sp", yrg_d[fo, :, ts_], y_[:], [yk], ["yrg%d_%d" % (fo, tg)])
                            else:
                                S.dma("sp", y_[:], yrg_d[fo, :, ts_], ["yrg%d_%d" % (fo, tg)], [yk])
                                S.op("dve", "tensor_tensor", [pk1, "sig"], ["tg32"], out=tg32[:], in0=ps1[:, :], in1=sig[:], op=ALU.mult)
                                S.op("dve", "tensor_tensor", ["tg32", yk], ["mT"], out=mT[:, fo, ts_], in0=tg32[:], in1=y_[:], op=ALU.add)
                    S.barrier()

            STOP(2)
            out_gate(BLK_RO, BLK_GR, True, pre=nextW)
            STOP(3)

            with ExitStack() as sD:
                preD0 = load_w(BLK_D)
                bap = SB(sD, "bap", [128, NT, 16], F32)
                beta = SB(sD, "beta", [128, NT, 8], F32)
                la = SB(sD, "la", [128, NT, 8], F32)
                Gtok = SB(sD, "Gtok", [128, NT, 8], F32)
                eg = SB(sD, "eg", [128, NT, 8], F32)
                kh = SB(sD, "kh", [128, NT, 8], F32)
                bw = SB(sD, "bw", [128, NT, 8], F32)
                EGL = SB(sD, "EGL", [128, NT, 8], F32)
                wba = SB(sD, "wba", [128, 8, 16], BF16)
                dpar = SB(sD, "dpar", [128, 32], F32)
                convw = SB(sD, "convw", [128, 3, 16], F32)
                ncw = SB(sD, "ncw", [128, 2, 16], F32)
                S.dma("pool", wba[:], wba_d.rearrange("(k p) c -> p k c", p=128), [], ["wba"])
                S.dma("sp", dpar[:, 0:8], alog_d.partition_broadcast(128), [], ["dpar"])
                S.dma("sp", dpar[:, 8:16], dtb_d.partition_broadcast(128), [], ["dpar"])
                S.dma("sp", convw[:], convw_d.rearrange("t (c p) -> p t c", p=128), [], ["convw"], allow_slow_non_contiguous=True)
                S.op("dve", "tensor_scalar", ["flag"], ["dpar2"], out=dpar[:, 24:25], in0=flag[:, 0:1], scalar1=-1.0, scalar2=None, op0=ALU.add)
                S.op("dve", "tensor_scalar", ["convw", "dpar2"], ["ncw"], out=ncw[:, 0, :], in0=convw[:, 0, :], scalar1=dpar[:, 24:25], scalar2=None, op0=ALU.mult)
                S.op("dve", "tensor_scalar", ["convw", "dpar2"], ["ncw"], out=ncw[:, 1, :], in0=convw[:, 2, :], scalar1=dpar[:, 24:25], scalar2=None, op0=ALU.mult)
                S.op("act", "activation", ["dpar"], ["dpar1"], out=dpar[:, 16:24], in_=dpar[:, 0:8], func=AF.Exp)
                for tt in range(NT):
                    ps, pk = PS()
                    for kc in range(8):
                        mm(ps[:, 0:16], pk, hT[:, kc, tt * 128:(tt + 1) * 128], wba[:, kc, :], ["wba", "hT"], kc == 0, kc == 7)
                    S.op("act", "activation", [pk], ["bap"], out=bap[:, tt, :], in_=ps[:, 0:16], func=AF.Copy)
                S.op("act", "activation", ["bap"], ["beta"], out=beta[:], in_=bap[:, :, 0:8], func=AF.Sigmoid)
                S.op("dve", "tensor_tensor", ["bap", "dpar"], ["la"], out=la[:], in0=bap[:, :, 8:16],
                     in1=dpar[:, 8:16].unsqueeze(1).to_broadcast([128, NT, 8]), op=ALU.add)
                S.op("act", "activation", ["la"], ["la"], out=la[:], in_=la[:], func=AF.Exp)
                S.op("act", "activation", ["la", "cst"], ["la"], out=la[:], in_=la[:], func=AF.Ln, bias=C("ONE"))
                S.op("dve", "scalar_tensor_tensor", ["la", "dpar1"], ["la"], out=la[:], in0=la[:], scalar=-1.0,
                     in1=dpar[:, 16:24].unsqueeze(1).to_broadcast([128, NT, 8]), op0=ALU.mult, op1=ALU.mult)
                ps, pk = PS()
                mm(ps[:, 0:64].rearrange("p (t c) -> p t c", c=4), pk, C("TRILO128"), la[:, :, 0:4], ["la"], True, True, last=False)
                mm(ps[:, 64:128].rearrange("p (t c) -> p t c", c=4), pk, C("TRIUP128"), la[:, :, 4:8], ["la"], True, True)
                S.op("act", "activation", [pk], ["Gtok"], out=Gtok[:, :, 0:4], in_=ps[:, 0:64].rearrange("p (t c) -> p t c", c=4), func=AF.Copy)
                S.op("act", "activation", [pk], ["Gtok"], out=Gtok[:, :, 4:8], in_=ps[:, 64:128].rearrange("p (t c) -> p t c", c=4), func=AF.Copy)
                ps, pk = PS()
                mm(ps[:, 0:128], pk, C("ONES"), la[:].rearrange("p t c -> p (t c)"), ["la"], True, True)
                S.op("dve", "tensor_tensor", [pk, "Gtok"], ["kh"], out=kh[:].rearrange("p t c -> p (t c)"), in0=ps[:, 0:128],
                     in1=Gtok[:].rearrange("p t c -> p (t c)"), op=ALU.subtract)
                S.op("act", "activation", [pk], ["EGL"], out=EGL[:].rearrange("p t c -> p (t c)"), in_=ps[:, 0:128], func=AF.Exp)
                S.op("act", "activation", ["kh"], ["kh"], out=kh[:], in_=kh[:], func=AF.Exp)
                S.op("act", "activation", ["Gtok"], ["eg"], out=eg[:], in_=Gtok[:], func=AF.Exp)
                S.op("dve", "tensor_tensor", ["eg", "beta"], ["bw"], out=bw[:], in0=eg[:], in1=beta[:], op=ALU.mult)

                dqT = SB(sD, "dqT", [128, T], BF16)
                dkT = SB(sD, "dkT", [128, T], BF16)
                ktok = SB(sD, "ktok", [128, NT, 128], BF16)
                vt = SB(sD, "dvt", [128, NT, 256], BF16)
                WBD = 4
                zg = [SB(sD, "zg%d" % i, [128, 256], BF16) for i in range(WBD)]
                odf = SB(sD, "odf", [128, NT, 256], BF16)
                odb = SB(sD, "odb", [128, NT, 256], BF16)
                o32 = [SB(sD, "do32_%d" % i, [128, 256], F32) for i in range(WBD)]
                t32 = [SB(sD, "dt32_%d" % i, [128, 256], F32) for i in range(WBD)]
                zb = [SB(sD, "dzb%d" % i, [128, 256], BF16) for i in range(WBD)]
                sm = [SB(sD, "dsm%d" % i, [128, 8], F32) for i in range(WBD)]
                RING = 3
                NSET = 4

                pendingB = []
                nextW = preD0
                for h in range(4):
                    W, wk = nextW
                    with ExitStack() as sconv:
                        WC = 2
                        xpre_l = [SB(sconv, "xpre%d" % i, [128, T], BF16) for i in range(WC)]
                        cv_l = [SB(sconv, "cv%d" % i, [128, T], F32) for i in range(WC)]
                        cvr_l = [SB(sconv, "cvr%d" % i, [128, 512], F32R) for i in range(WC)]
                        ysb_l = xpre_l
                        rin_l = [SB(sconv, "rin%d" % i, [128, 512], F32) for i in range(WC)]
                        onesR2 = SB(sconv, "onesR2", [128, 128], F32R)
                        S.op("dve", "tensor_copy", ["cst"], ["onesR2"], out=onesR2[:], in_=C("ONES"))

                        def conv_job(kind, ct, c0, h=h, W=W, wk=wk):
                            def g(bi):
                                xpre, cv, ysb, rin, cvr = xpre_l[bi], cv_l[bi], ysb_l[bi], rin_l[bi], cvr_l[bi]
                                q_ = "_%d" % bi
                                xk, ck, yk, rk_ = "xpre" + q_, "cv" + q_, "xpre" + q_, "rin" + q_
                                for tg in range(4):
                                    ts_ = slice(tg * 512, (tg + 1) * 512)
                                    ps, pk = PS()
                                    for kc in range(8):
                                        mm(ps[:, :], pk, W[:, kc, c0:c0 + 128], hT[:, kc, ts_], [wk, "hT"], kc == 0, kc == 7)
                                    S.op("act", "activation", [pk], [xk], out=xpre[:, ts_], in_=ps[:, :], func=AF.Copy)
                                    yield
                                w0 = convw[:, 0, ct:ct + 1]
                                w1 = convw[:, 1, ct:ct + 1]
                                w2 = convw[:, 2, ct:ct + 1]
                                S.op("dve", "tensor_scalar", [xk, "convw"], [ck], out=cv[:], in0=xpre[:], scalar1=w1, scalar2=None, op0=ALU.mult)
                                S.op("dve", "scalar_tensor_tensor", [xk, "convw", ck], [ck], out=cv[:, 1:T], in0=xpre[:, 0:T - 1], scalar=w0,
                                     in1=cv[:, 1:T], op0=ALU.mult, op1=ALU.add)
                                S.op("dve", "scalar_tensor_tensor", [xk, "convw", ck], [ck], out=cv[:, 0:T - 1], in0=xpre[:, 1:T], scalar=w2,
                                     in1=cv[:, 0:T - 1], op0=ALU.mult, op1=ALU.add)
                                xv = xpre[:, :].rearrange("p (s t) -> p s t", t=256)
                                cvv = cv[:, :].rearrange("p (s t) -> p s t", t=256)
                                S.op("dve", "scalar_tensor_tensor", [xk, "ncw", ck], [ck], out=cvv[:, 1:8, 0:1], in0=xv[:, 0:7, 255:256],
                                     scalar=ncw[:, 0, ct:ct + 1], in1=cvv[:, 1:8, 0:1], op0=ALU.mult, op1=ALU.add)
                                S.op("dve", "scalar_tensor_tensor", [xk, "ncw", ck], [ck], out=cvv[:, 0:7, 255:256], in0=xv[:, 1:8, 0:1],
                                     scalar=ncw[:, 1, ct:ct + 1], in1=cvv[:, 0:7, 255:256], op0=ALU.mult, op1=ALU.add)
                                yield
                                S.op("act", "activation", [ck], [yk], out=ysb[:], in_=cv[:], func=AF.Silu)
                                yield
                                if kind in ("q", "k"):
                                    dst, dk_ = (dqT, "dqT") if kind == "q" else (dkT, "dkT")
                                    epsn, sc = ("EPS128", 128.0) if kind == "q" else ("EPS", 1.0)
                                    for tg in range(4):
                                        ts_ = slice(tg * 512, (tg + 1) * 512)
                                        S.op("act", "activation", [yk], ["cvr" + q_], out=cvr[:], in_=ysb[:, ts_], func=AF.Square)
                                        yield
                                        ps, pk = PS()
                                        mm(ps[:, :], pk, onesR2[:], cvr[:], ["cvr" + q_, "onesR2"], True, True)
                                        S.op("act", "activation", [pk, "cst"], [rk_], out=rin[:], in_=ps[:, :], func=AF.Sqrt, scale=sc, bias=C(epsn))
                                        yield
                                        S.op("dve", "reciprocal", [rk_], [rk_], out=rin[:], in_=rin[:])
                                        S.op("dve", "tensor_tensor", [yk, rk_], [dk_], out=dst[:, ts_], in0=ysb[:, ts_], in1=rin[:], op=ALU.mult)
                                        yield
                                    if kind == "k":
                                        for g4 in range(4):
                                            for j in range(4):
                                                tt = g4 * 4 + j
                                                S.op("pe", "transpose", ["dkT", "identb"], ["psT"], inc=(j == 3), out=psT[:, j * 128:(j + 1) * 128],
                                                     in_=dkT[:, tt * 128:(tt + 1) * 128], identity=identb[:])
                                            S.op("act", "activation", ["psT"], ["ktok"], out=ktok[:, g4 * 4:(g4 + 1) * 4, :],
                                                 in_=psT[:, 0:512].rearrange("p (t d) -> p t d", t=4), func=AF.Copy)
                                            yield
                                else:
                                    c = kind
                                    for g4 in range(4):
                                        for j in range(4):
                                            tt = g4 * 4 + j
                                            S.op("pe", "transpose", [yk, "identb"], ["psT"], inc=(j == 3), out=psT[:, j * 128:(j + 1) * 128],
                                                 in_=ysb[:, tt * 128:(tt + 1) * 128], identity=identb[:])
                                        S.op("act", "activation", ["psT"], ["dvt"], out=vt[:, g4 * 4:(g4 + 1) * 4, c * 128:(c + 1) * 128],
                                             in_=psT[:, 0:512].rearrange("p (t d) -> p t d", t=4), func=AF.Copy)
                                        yield
                            return g
                        interleave([conv_job("q", h, 0), conv_job("k", 4 + h, 128), conv_job(0, 8 + 2 * h, 256), conv_job(1, 9 + 2 * h, 384)], WC,
                                   extras=pendingB)
                        pendingB = []
                        S.barrier()
                    nextW = load_w(BLK_D + h + 1) if h < 3 else load_w(BLK_DO)

                    with ExitStack() as sscan:
                        ring = {}
                        for r in range(2):
                            for i in range(RING):
                                ring[(r, i)] = dict(
                                    sT=SB(sscan, "r_sT%d%d" % (r, i), [128, 128], BF16),
                                    nwT=SB(sscan, "r_nwT%d%d" % (r, i), [128, 128], BF16),
                                    u=SB(sscan, "r_u%d%d" % (r, i), [128, 256], F32),
                                    qgT=SB(sscan, "r_qgT%d%d" % (r, i), [128, 128], BF16),
                                    khat=SB(sscan, "r_kh%d%d" % (r, i), [128, 128], BF16))
                        TS = []
                        for i in range(NSET):
                            TS.append(dict(
                                aTri=SB(sscan, "aTri%d" % i, [128, 128], F32R),
                                DD=SB(sscan, "DD%d" % i, [128, 256], F32),
                                Z=SB(sscan, "Z%d" % i, [128, 384], F32R),
                                Noff=SB(sscan, "Noff%d" % i, [128, 128], F32R),
                                bv=SB(sscan, "bv%d" % i, [128, 256], F32R), kbeg=SB(sscan, "kbeg%d" % i, [128, 128], F32R),
                                egrow=SB(sscan, "egrow%d" % i, [128, 128], F32)))
                        vn = [SB(sscan, "vn%d" % r, [128, 256], BF16) for r in range(2)]
                        identR = SB(sscan, "identR", [128, 128], F32R)
                        S.op("dve", "tensor_copy", ["cst"], ["identR"], out=identR[:], in_=C("I128"))
                        onesR = SB(sscan, "onesR", [128, 128], F32R)
                        negonesR = SB(sscan, "negonesR", [128, 128], F32R)
                        S.op("dve", "tensor_copy", ["cst"], ["onesR"], out=onesR[:], in_=C("ONES"))
                        S.op("dve", "tensor_copy", ["cst"], ["negonesR"], out=negonesR[:], in_=C("NEGONES128"))

                        def prep_gen(k, r, tsi):
                            tt = k if r == 0 else NT - 1 - k
                            c = 4 * r + h
                            X = TS[tsi]
                            q_ = "_%d" % tsi
                            slot = ring[(r, k % RING)]
                            sk = "ring%d%d" % (r, k % RING)
                            tsl = slice(tt * 128, (tt + 1) * 128)
                            tri = C("TRILO128") if r == 0 else C("TRIUP128")
                            aTri, DD, Z = X["aTri"], X["DD"], X["Z"]
                            DS = DD[:, 128:256]
                            bv, kbeg, egrow = X["bv"], X["kbeg"], X["egrow"]
                            S.op("act", "activation", ["la", "cst"], ["aTri" + q_], out=aTri[:], in_=tri, func=AF.Copy, scale=la[:, tt, c:c + 1])
                            S.op("act", "activation", ["ktok", "kh"], [sk + "khat"], out=slot["khat"][:], in_=ktok[:, tt, :], func=AF.Copy, scale=kh[:, tt, c:c + 1])
                            S.op("dve", "tensor_scalar", ["ktok", "bw"], ["kbeg" + q_], out=kbeg[:], in0=ktok[:, tt, :], scalar1=bw[:, tt, c:c + 1], scalar2=None, op0=ALU.mult)
                            S.op("act", "activation", ["dvt", "beta"], ["bv" + q_], out=bv[:], in_=vt[:, tt, :], func=AF.Copy, scale=beta[:, tt, c:c + 1])
                            yield
                            while nhold[0] >= 2:
                                yield
                            nhold[0] += 1
                            psQ, pkQ = PS()
                            held.add(int(pkQ[3:]))
                            psG, pkG = PS()
                            held.add(int(pkG[3:]))
                            mm(psQ[:, 0:128], pkQ, onesR[:], aTri[:], ["aTri" + q_, "onesR"], True, False)
                            mm(psQ[:, 0:128], pkQ, aTri[:], negonesR[:], ["aTri" + q_, "negonesR"], False, True, last=False)
                            mm(psQ[:, 128:256], pkQ, onesR[:], aTri[:], ["aTri" + q_, "onesR"], True, True)
                            mm(psG[:, 0:128], pkG, dkT[:, tsl], dkT[:, tsl], ["dkT"], True, True, last=False)
                            mm(psG[:, 128:256], pkG, dkT[:, tsl], dqT[:, tsl], ["dkT", "dqT"], True, True)
                            mask = C("MASKF128") if r == 0 else C("MASKB128")
                            smask = C("SMF128") if r == 0 else C("SMB128")
                            S.op("dve", "tensor_tensor", [pkQ, "cst"], ["DD" + q_], out=DD[:, 0:128], in0=psQ[:, 0:128], in1=mask[:, 0:128], op=ALU.add)
                            S.op("dve", "scalar_tensor_tensor", [pkQ, "cst"], ["DD" + q_], out=DD[:, 128:256], in0=psQ[:, 0:128], scalar=-1.0,
                                 in1=mask[:, 128:256], op0=ALU.mult, op1=ALU.add)
                            S.op("act", "activation", [pkQ], ["egrow" + q_], out=egrow[:], in_=psQ[:, 128:256], func=AF.Exp)
                            S.op("act", "activation", ["DD" + q_], ["DD" + q_], out=DD[:], in_=DD[:], func=AF.Exp)
                            held.discard(int(pkQ[3:]))
                            yield
                            S.op("dve", "tensor_tensor", ["DD" + q_, "cst"], ["DD" + q_], out=DS, in0=DD[:, 128:256], in1=smask[:, 128:256], op=ALU.mult)
                            S.op("dve", "tensor_tensor", [pkG, "DD" + q_], [sk + "sT"], out=slot["sT"][:], in0=psG[:, 128:256], in1=DD[:, 0:128], op=ALU.mult)
                            S.op("dve", "tensor_tensor", ["dqT", "egrow" + q_], [sk + "qgT"], out=slot["qgT"][:], in0=dqT[:, tsl], in1=egrow[:], op=ALU.mult)
                            kZ = "Z" + q_
                            Noff = X["Noff"][:]
                            S.op("dve", "scalar_tensor_tensor", [pkG, "beta", "DD" + q_], ["egrow" + q_], out=egrow[:], in0=psG[:, 0:128], scalar=beta[:, tt, c:c + 1],
                                 in1=DS, op0=ALU.mult, op1=ALU.mult)
                            S.op("dve", "tensor_tensor", ["egrow" + q_, "cst"], [kZ], out=Z[:, 256:384], in0=egrow[:], in1=C("BLKM"), op=ALU.mult)
                            S.op("dve", "tensor_tensor", ["egrow" + q_, kZ], ["Noff" + q_], out=Noff, in0=egrow[:], in1=Z[:, 256:384], op=ALU.subtract)
                            mm(psG[:, 256:384], pkG, Z[:, 256:384], identR[:], [kZ, "identR"], True, True)
                            S.op("act", "activation", [pkG], [kZ], out=Z[:, 0:128], in_=psG[:, 256:384], func=AF.Copy)
                            S.op("dve", "tensor_tensor", [kZ, "identR"], [kZ], out=Z[:, 128:256], in0=Z[:, 0:128], in1=identR[:], op=ALU.add)
                            held.discard(int(pkG[3:]))
                            nhold[0] -= 1
                            yield
                            for lev in range(1, 7):
                                psA, pkA = PS()
                                if lev == 1:
                                    mm(psA[:, 0:128], pkA, Z[:, 256:384], Z[:, 0:128], [kZ], True, True, last=False)
                                else:
                                    mm(psA[:, 0:256], pkA, Z[:, 256:384], Z[:, 0:256], [kZ], True, True, last=(lev == 6))
                                if lev < 6:
                                    mm(psA[:, 256:384], pkA, Z[:, 0:128], Z[:, 256:384], [kZ], True, True)
                                    S.op("act", "activation", [pkA], [kZ], out=Z[:, :].rearrange("p (b f) -> p b f", b=3)[:, 0:3:2, :],
                                         in_=psA[:, 0:384].rearrange("p (b f) -> p b f", b=3)[:, 0:3:2, :], func=AF.Copy)
                                if lev >= 2:
                                    S.op("dve", "tensor_tensor", [pkA, kZ], [kZ], out=Z[:, 128:256], in0=psA[:, 128:256], in1=Z[:, 128:256], op=ALU.add)
                                yield
                            psM, pkM = PS()
                            mm(psM[:, 0:128], pkM, Noff, Z[:, 128:256], ["Noff" + q_, kZ], True, True, last=False)
                            mm(psM[:, 256:384], pkM, Z[:, 128:256], identR[:], [kZ, "identR"], True, True)
                            S.op("act", "activation", [pkM], [kZ], out=Z[:, :].rearrange("p (b f) -> p b f", b=3)[:, 0:3:2, :],
                                 in_=psM[:, 0:384].rearrange("p (b f) -> p b f", b=3)[:, 0:3:2, :], func=AF.Copy)
                            yield
                            psM2, pkM2 = PS()
                            mm(psM2[:, 0:128], pkM2, Z[:, 256:384], Z[:, 0:128], [kZ], True, True)
                            S.op("dve", "tensor_tensor", [pkM2, kZ], [kZ], out=Z[:, 128:256], in0=psM2[:, 0:128], in1=Z[:, 128:256], op=ALU.add)
                            yield
                            TT_ = Z[:, 128:256]
                            tk = kZ
                            psU, pkU = PS()
                            mm(psU[:, 0:256], pkU, TT_, bv[:], [tk, "bv" + q_], True, True, last=False)
                            mm(psU[:, 256:384], pkU, kbeg[:], TT_, [tk, "kbeg" + q_], True, True)
                            S.op("dve", "tensor_copy", [pkU], [sk + "u"], out=slot["u"][:], in_=psU[:, 0:256])
                            S.op("act", "activation", [pkU], [sk + "nwT"], out=slot["nwT"][:], in_=psU[:, 256:384], func=AF.Copy, scale=-1.0)
                            yield

                        cur = [0, 0]
                        c16 = [0, 0]
                        ready = {}
                        done_tiles = [0, 0]
                        nhold = [0]

                        def chain_gen(r):
                            c = 4 * r + h
                            od = odf if r == 0 else odb
                            odk = "odf" if r == 0 else "odb"
                            vk = "vn%d" % r
                            for k in range(NT):
                                tt = k if r == 0 else NT - 1 - k
                                while not ready.get((k, r)):
                                    yield
                                slot = ring[(r, k % RING)]
                                sk = "ring%d%d" % (r, k % RING)
                                s32k = "S32_%d_%d" % (r, cur[r])
                                s16k = "S16_%d_%d" % (r, c16[r])
                                s16n = "S16_%d_%d" % (r, 1 - c16[r])
                                Scur = S16[r][c16[r]]
                                Snew = S16[r][1 - c16[r]]
                                psV, pkV = PS()
                                mm(psV[:, 0:256], pkV, slot["nwT"][:], Scur[:], [sk + "nwT", s16k], True, True)
                                S.op("dve", "tensor_tensor", [pkV, sk + "u"], [vk], out=vn[r][:], in0=psV[:, 0:256], in1=slot["u"][:], op=ALU.add)
                                yield
                                psO, pkO = PS()
                                psS, pkS = PS()
                                mm(psS[:, 0:256], pkS, slot["khat"][:], vn[r][:], [sk + "khat", vk], True, True)
                                mm(psO[:, 0:256], pkO, slot["qgT"][:], Scur[:], [sk + "qgT", s16k], True, False)
                                mm(psO[:, 0:256], pkO, slot["sT"][:], vn[r][:], [sk + "sT", vk], False, True)
                                bnd = (tt % 2 == 1) if r == 0 else (tt % 2 == 0)
                                S.op("dve", "scalar_tensor_tensor", [pkS, s32k, "EGL"], [s32k], out=S32[r][cur[r]][:], in0=S32[r][cur[r]][:],
                                     scalar=EGL[:, tt, c:c + 1], in1=psS[:, 0:256], op0=ALU.mult, op1=ALU.add)
                                if bnd:
                                    cur[r] = state_boundary(r, cur[r], tt // 2, h, sdn_o)
                                S.op("act", "activation", ["S32_%d_%d" % (r, cur[r])], [s16n], out=Snew[:], in_=S32[r][cur[r]][:], func=AF.Copy)
                                S.op("act", "activation", [pkO], [odk], out=od[:, tt, :], in_=psO[:, 0:256], func=AF.Copy)
                                c16[r] = 1 - c16[r]
                                yield
                                done_tiles[r] += 1

                        for r in range(2):
                            S.dma("sp", S32[r][0][:], sdn_d[r, h], [], ["S32_%d_0" % r])
                            S.op("act", "activation", ["S32_%d_0" % r], ["S16_%d_0" % r], out=S16[r][0][:], in_=S32[r][0][:], func=AF.Copy)
                        chains = [chain_gen(0), chain_gen(1)]
                        chain_alive = [True, True]
                        nextk = [0, 0]
                        free_sets = list(range(NSET))
                        active = []
                        turn = 0
                        while any(chain_alive):
                            launched = True
                            while free_sets and launched:
                                launched = False
                                for rr in ((0, 1) if turn % 2 == 0 else (1, 0)):
                                    if free_sets and nextk[rr] < NT and nextk[rr] < done_tiles[rr] + RING:
                                        tsi = free_sets.pop(0)
                                        active.append((prep_gen(nextk[rr], rr, tsi), nextk[rr], rr, tsi))
                                        nextk[rr] += 1
                                        launched = True
                                        turn += 1
                            def chain_turn():
                                for r in range(2):
                                    if chain_alive[r]:
                                        try:
                                            next(chains[r])
                                        except StopIteration:
                                            chain_alive[r] = False
                            chain_turn()
                            CT = 2
                            for ii, item in enumerate(list(active)):
                                if ii > 0 and ii % CT == 0:
                                    chain_turn()
                                g, k, rr, tsi = item
                                try:
                                    next(g)
                                except StopIteration:
                                    active.remove(item)
                                    free_sets.append(tsi)
                                    ready[(k, rr)] = True
                        S.barrier()
                    def delB(tt, h=h, W=W, wk=wk):
                        def g(bi):
                            o32_, t32_, zb_, sm_, zg_ = o32[bi], t32[bi], zb[bi], sm[bi], zg[bi]
                            q_ = "_%d" % bi
                            ps, pk = PS()
                            for kc in range(8):
                                mm(ps[:, 0:256], pk, hT[:, kc, tt * 128:(tt + 1) * 128], W[:, kc, 512:768], [wk, "hT"], kc == 0, kc == 7)
                            S.op("act", "activation", [pk], ["zg" + q_], out=zg_[:], in_=ps[:, 0:256], func=AF.Silu)
                            S.op("dve", "tensor_tensor", ["odf", "odb"], ["do32" + q_], out=o32_[:], in0=odf[:, tt, :], in1=odb[:, tt, :], op=ALU.add)
                            yield
                            S.op("act", "activation", ["do32" + q_], ["dt32" + q_, "dsm0" + q_], out=t32_[:], in_=o32_[:], func=AF.Square, accum_out=sm_[:, 0:1])
                            yield
                            S.op("act", "activation", ["dsm0" + q_, "cst"], ["dsm1" + q_], out=sm_[:, 1:2], in_=sm_[:, 0:1], func=AF.Sqrt, scale=1.0 / 256, bias=C("EPS"))
                            yield
                            S.op("dve", "reciprocal", ["dsm1" + q_], ["dsm2" + q_], out=sm_[:, 2:3], in_=sm_[:, 1:2])
                            S.op("dve", "scalar_tensor_tensor", ["do32" + q_, "dsm2" + q_, "dnw"], ["dt32" + q_], out=t32_[:], in0=o32_[:], scalar=sm_[:, 2:3],
                                 in1=dnw[:], op0=ALU.mult, op1=ALU.mult)
                            S.op("dve", "tensor_tensor", ["dt32" + q_, "zg" + q_], ["dzb" + q_], out=zb_[:], in0=t32_[:], in1=zg_[:], op=ALU.mult)
                            yield
                            for c2 in range(2):
                                S.op("pe", "transpose", ["dzb" + q_, "identb"], ["psT"], inc=(c2 == 1), out=psT[:, c2 * 128:(c2 + 1) * 128],
                                     in_=zb_[:, c2 * 128:(c2 + 1) * 128], identity=identb[:])
                            S.op("act", "activation", ["psT"], ["zT"], out=zT[:, 2 * h:2 * h + 2, tt * 128:(tt + 1) * 128],
                                 in_=psT[:, 0:256].rearrange("p (c t) -> p c t", c=2), func=AF.Copy)
                            yield
                        return g
                    def laneB(tts, bi, delB=delB):
                        for tt in tts:
                            yield from delB(tt)(bi)
                    lanes = [laneB(list(range(bi, NT, WBD)), bi) for bi in range(WBD)]
                    if h < 3:
                        pendingB = lanes
                    else:
                        interleave([], 1, extras=lanes)
                S.barrier()

            STOP(4)
            mT = SB(sB, "mT", [128, 8, T], BF16)
            out_gate(BLK_DO, BLK_GD, False, mT, pre=nextW)

            with ExitStack() as s4:
                W4 = 4
                xb = [SB(s4, "x4b%d" % i, [128, D], F32) for i in range(W4)]
                x1 = xb
                tmp32 = [SB(s4, "tmp32b%d" % i, [128, D], F32) for i in range(W4)]
                hb = [SB(s4, "hb4_%d" % i, [128, D], BF16) for i in range(W4)]
                small = [SB(s4, "small4_%d" % i, [128, 8], F32) for i in range(W4)]
                small_n = [SB(s4, "small_n%d" % i, [128, 4], F32) for i in range(W4)]
                junk4 = hb
                modb4 = SB(s4, "modb4", [128, 3 * D], F32)
                S.dma("sp", modb4[:], modscr_d[0, 0:3 * D].partition_broadcast(128), [], ["modb"])
                G1 = modb4[:, 0:1024]
                B2 = modb4[:, 1024:2048]
                A2 = modb4[:, 2048:3072]
                Wo, wko = load_w(BLK_OUT)
                preFF = load_w(BLK_FF)

                def st4(tt):
                    def g(bi):
                        q_ = "_%d" % bi
                        xk = "x4b" + q_
                        x1k = xk
                        tk = "tmp4" + q_
                        sm_ = small[bi]
                        tsl = slice(tt * 128, (tt + 1) * 128)
                        S.dma("sp", xb[bi][:], x_d[tsl, :], [], [xk])
                        pa, pka = PS()
                        pb_, pkb = PS()
                        for kc in range(8):
                            mm(pa[:, :], pka, mT[:, kc, tsl], Wo[:, kc, 0:512], ["mT", wko], kc == 0, kc == 7)
                        for kc in range(8):
                            mm(pb_[:, :], pkb, mT[:, kc, tsl], Wo[:, kc, 512:1024], ["mT", wko], kc == 0, kc == 7)
                        S.op("act", "activation", [pka], ["hb4" + q_, "sa" + q_], out=junk4[bi][:, 0:512], in_=pa[:, :], func=AF.Square, accum_out=sm_[:, 0:1])
                        S.op("dve", "tensor_copy", [pka], [tk], out=tmp32[bi][:, 0:512], in_=pa[:, :])
                        S.op("act", "activation", [pkb], ["hb4" + q_, "sb" + q_], out=junk4[bi][:, 512:1024], in_=pb_[:, :], func=AF.Square, accum_out=sm_[:, 1:2])
                        S.op("dve", "tensor_copy", [pkb], [tk], out=tmp32[bi][:, 512:1024], in_=pb_[:, :])
                        yield
                        S.op("dve", "tensor_tensor", ["sa" + q_, "sb" + q_], ["sc" + q_], out=sm_[:, 2:3], in0=sm_[:, 0:1], in1=sm_[:, 1:2], op=ALU.add)
                        yield
                        S.op("act", "activation", ["sc" + q_, "cst"], ["sd" + q_], out=sm_[:, 3:4], in_=sm_[:, 2:3], func=AF.Sqrt, scale=1.0 / D, bias=C("EPS"))
                        yield
                        S.op("dve", "reciprocal", ["sd" + q_], ["se" + q_], out=sm_[:, 4:5], in_=sm_[:, 3:4])
                        S.op("dve", "scalar_tensor_tensor", ["se" + q_, "modb", tk], [tk], out=tmp32[bi][:], in0=tmp32[bi][:], scalar=sm_[:, 4:5],
                             in1=G1, op0=ALU.mult, op1=ALU.mult)
                        yield
                        S.op("dve", "tensor_tensor", [tk, xk], [x1k], out=x1[bi][:], in0=tmp32[bi][:], in1=xb[bi][:], op=ALU.add)
                        S.dma("sp", y_d[tsl, :], x1[bi][:], [x1k], ["y%d" % tt])
                        yield
                        yield from norm_to_hT_gen(x1[bi][:], x1k, tt, A2, B2, tmp32[bi], tk, hb[bi], "hb4" + q_, small_n[bi], "sn" + q_)
                    return g
                interleave([st4(tt) for tt in range(NT)], W4)
                S.barrier()

        STOP(6)
        with ExitStack() as s5:
            wffo = SB(s5, "wffo", [128, NFF, D], BF16)
            aT = SB(s5, "aT", [128, NFF, 1024], BF16)
            sg = SB(s5, "sg", [128, 512], F32)
            x1b = [SB(s5, "x5b%d" % i, [128, D], F32) for i in range(2)]
            yb = [SB(s5, "y5b%d" % i, [128, D], F32) for i in range(2)]
            tmp32 = SB(s5, "tmp32c", [128, D], F32)
            small = SB(s5, "small5", [128, 8], F32)
            G2 = SB(s5, "modb5", [128, D], F32)
            S.dma("sp", G2[:], modscr_d[0, 3 * D:4 * D].partition_broadcast(128), [], ["modb"])
            for th in range(2):
                for b in range(6):
                    W, wk = preFF if (th == 0 and b == 0) else load_w(BLK_FF + b)
                    if th == 0 and b == 2:
                        S.dma("pool", wffo[:], wffo_d.rearrange("(j p) c -> p j c", p=128), [], ["wffo"])
                    nj = 4 if b < 5 else 2
                    for jj in range(nj):
                        j = b * 4 + jj
                        for tg in range(2):
                            ts_ = slice(th * 1024 + tg * 512, th * 1024 + (tg + 1) * 512)
                            pg, pkg = PS()
                            pu, pku = PS()
                            for kc in range(8):
                                mm(pg[:, :], pkg, W[:, kc, jj * 128:(jj + 1) * 128], hT[:, kc, ts_], [wk, "hT"], kc == 0, kc == 7)
                            for kc in range(8):
                                mm(pu[:, :], pku, W[:, kc, 512 + jj * 128:512 + (jj + 1) * 128], hT[:, kc, ts_], [wk, "hT"], kc == 0, kc == 7)
                            S.op("act", "activation", [pkg], ["sg"], out=sg[:], in_=pg[:, :], func=AF.Silu)
                            S.op("dve", "tensor_tensor", [pku, "sg"], ["aT"], out=aT[:, j, tg * 512:(tg + 1) * 512], in0=pu[:, :], in1=sg[:], op=ALU.mult)
                for t8 in range(8):
                    tt = th * 8 + t8
                    tsl = slice(tt * 128, (tt + 1) * 128)
                    xk = "x5b%d" % (tt % 2)
                    yk = "y5b%d" % (tt % 2)
                    S.dma("sp", x1b[tt % 2][:], y_d[tsl, :], ["y%d" % tt], [xk])
                    pa, pka = PS()
                    pb_, pkb = PS()
                    for j in range(NFF):
                        mm(pa[:, :], pka, aT[:, j, t8 * 128:(t8 + 1) * 128], wffo[:, j, 0:512], ["aT", "wffo"], j == 0, j == NFF - 1)
                    for j in range(NFF):
                        mm(pb_[:, :], pkb, aT[:, j, t8 * 128:(t8 + 1) * 128], wffo[:, j, 512:1024], ["aT", "wffo"], j == 0, j == NFF - 1)
                    S.op("act", "activation", [pka], ["junk5", "sm5a"], out=tmp32[:, 0:512], in_=pa[:, :], func=AF.Square, accum_out=small[:, 0:1])
                    S.op("act", "activation", [pkb], ["junk5", "sm5b"], out=tmp32[:, 512:1024], in_=pb_[:, :], func=AF.Square, accum_out=small[:, 1:2])
                    S.op("dve", "tensor_tensor", ["sm5a", "sm5b"], ["sm5c"], out=small[:, 2:3], in0=small[:, 0:1], in1=small[:, 1:2], op=ALU.add)
                    S.op("act", "activation", ["sm5c", "cst"], ["sm5d"], out=small[:, 3:4], in_=small[:, 2:3], func=AF.Sqrt, scale=1.0 / D, bias=C("EPS"))
                    S.op("dve", "reciprocal", ["sm5d"], ["sm5e"], out=small[:, 4:5], in_=small[:, 3:4])
                    S.op("dve", "scalar_tensor_tensor", [pka, "sm5e", "modb", "junk5"], ["junk5"], out=tmp32[:, 0:512], in0=pa[:, :], scalar=small[:, 4:5],
                         in1=G2[:, 0:512], op0=ALU.mult, op1=ALU.mult)
                    S.op("dve", "scalar_tensor_tensor", [pkb, "sm5e", "modb", "junk5"], ["junk5"], out=tmp32[:, 512:1024], in0=pb_[:, :], scalar=small[:, 4:5],
                         in1=G2[:, 512:1024], op0=ALU.mult, op1=ALU.mult)
                    S.op("dve", "tensor_tensor", ["junk5", xk], [yk], out=yb[tt % 2][:], in0=tmp32[:], in1=x1b[tt % 2][:], op=ALU.add)
                    S.dma("sp", y_d[tsl, :], yb[tt % 2][:], [yk], ["y%d" % tt])
            S.barrier()
        print("INSTR", dict(S.count), "dma", S.di)


_PROG = {}
_RETURN_MAPS = [False]


def _rope_tables():
    L = 2048
    rows = L // 64
    row_idx = np.repeat(np.arange(rows, dtype=np.float32), 64)
    col_idx = (np.arange(L) % 64).astype(np.float32)
    n_freq = 32
    freqs = (np.float32(10000.0) ** (-np.arange(n_freq, dtype=np.float32) / np.float32(n_freq))).astype(np.float32)
    ang = np.concatenate([row_idx[:, None] * freqs, col_idx[:, None] * freqs], axis=-1).astype(np.float32)
    cos = np.cos(ang).astype(np.float32)
    sin = np.sin(ang).astype(np.float32)
    d = np.arange(128)
    c = cos[:, d // 2].T.copy()
    s = sin[:, d // 2].T.copy()
    s[0::2, :] *= -1.0
    return np.ascontiguousarray(c), np.ascontiguousarray(s)


def _pack_weights(w_in, w_ret_o, w_dn_o, w_out, w_ffn_in):
    blk = np.zeros((NBLK, D, D), np.float32)
    swap = np.arange(128) ^ 1
    for h in range(4):
        q = w_in[:, h * 128:(h + 1) * 128]
        k = w_in[:, 512 + h * 128:512 + (h + 1) * 128]
        blk[BLK_R + h, :, 0:128] = q
        blk[BLK_R + h, :, 128:256] = q[:, swap]
        blk[BLK_R + h, :, 256:384] = k
        blk[BLK_R + h, :, 384:512] = k[:, swap]
        blk[BLK_R + h, :, 512:768] = w_in[:, 1024 + h * 256:1024 + (h + 1) * 256]
        blk[BLK_R + h, :, 768:1024] = w_in[:, 2048 + h * 256:2048 + (h + 1) * 256]
        blk[BLK_D + h, :, 0:128] = w_in[:, 3072 + h * 128:3072 + (h + 1) * 128]
        blk[BLK_D + h, :, 128:256] = w_in[:, 3584 + h * 128:3584 + (h + 1) * 128]
        blk[BLK_D + h, :, 256:512] = w_in[:, 4096 + h * 256:4096 + (h + 1) * 256]
        blk[BLK_D + h, :, 512:768] = w_in[:, 5120 + h * 256:5120 + (h + 1) * 256]
    blk[BLK_GR] = w_in[:, 6160:7184]
    blk[BLK_GD] = w_in[:, 7184:8208]
    blk[BLK_RO] = w_ret_o
    blk[BLK_DO] = w_dn_o
    blk[BLK_OUT] = w_out
    for b in range(6):
        n = 512 if b < 5 else 256
        blk[BLK_FF + b, :, 0:n] = w_ffn_in[:, b * 512:b * 512 + n]
        blk[BLK_FF + b, :, 512:512 + n] = w_ffn_in[:, DFF + b * 512:DFF + b * 512 + n]
    w_ba = np.ascontiguousarray(w_in[:, 6144:6160])
    return blk, w_ba


def kernel(x_prompt, x_sample, c, state_ret, state_dn, c_ctx, w_mod, b_mod, norm_w, w_in, conv_w, ret_decay_logit,
           ret_gn_w, dn_a_log, dn_dt_bias, dn_norm_w, w_ret_o, w_dn_o, w_out, w_ffn_in, w_ffn_out):
    f = lambda a: np.ascontiguousarray(np.asarray(a, dtype=np.float32))
    x_prompt, x_sample, c, state_ret, state_dn, c_ctx = map(f, (x_prompt, x_sample, c, state_ret, state_dn, c_ctx))
    w_in0 = f(w_in)[0]
    blk, w_ba = _pack_weights(w_in0, f(w_ret_o)[0], f(w_dn_o)[0], f(w_out)[0], f(w_ffn_in)[0])
    ropec, ropes = _rope_tables()
    ones_c = np.ones((128, T), np.float32)
    zeros_c = np.zeros((128, T), np.float32)
    shared = {
        "w_mod": f(w_mod)[0], "b_mod": f(b_mod)[0], "norm_w": f(norm_w)[0].reshape(-1), "wblk": blk, "w_ba": w_ba,
        "w_ffo": f(w_ffn_out)[0], "conv_w": f(conv_w)[0], "ret_decay_logit": f(ret_decay_logit)[0].reshape(-1),
        "ret_gn_w": f(ret_gn_w)[0], "dn_a_log": f(dn_a_log)[0].reshape(-1), "dn_dt_bias": f(dn_dt_bias)[0].reshape(-1),
        "dn_norm_w": f(dn_norm_w)[0], "cst": _build_cst(), "ident": np.eye(128, dtype=np.float32),
    }
    in_maps = []
    for core in range(8):
        m = dict(shared)
        if core < 4:
            m["x"] = x_sample[core]
            m["cond"] = c[core]
            m["st_ret"] = np.ascontiguousarray(state_ret[core, 0])
            m["st_dn"] = np.ascontiguousarray(state_dn[core, 0])
            m["rope_c"] = ropec
            m["rope_s"] = ropes
            m["flag"] = np.ones((128, 1), np.float32)
        else:
            j = core - 4
            xs = np.zeros((T, D), np.float32)
            xs[0:1024] = x_prompt[4 * j:4 * j + 4].reshape(1024, D)
            xs[1024:2048] = xs[0:1024]
            m["x"] = xs
            m["cond"] = c_ctx
            m["st_ret"] = np.zeros((2, 4, 128, 256), np.float32)
            m["st_dn"] = np.zeros((2, 4, 128, 256), np.float32)
            m["rope_c"] = ones_c
            m["rope_s"] = zeros_c
            m["flag"] = np.zeros((128, 1), np.float32)
        in_maps.append(m)
    if _RETURN_MAPS[0]:
        return in_maps
    if "nc" not in _PROG:
        _PROG["nc"] = build_program()
    res = run_bass_kernel_spmd(_PROG["nc"], in_maps, core_ids=list(range(8)))
    rr = res.results
    y_sample = np.stack([rr[i]["y"] for i in range(4)], 0)
    y_prompt = np.concatenate([rr[4 + j]["y"][0:1024].reshape(4, 256, D) for j in range(4)], 0)
    new_ret = np.concatenate([rr[4 + j]["sret_o"] for j in range(4)], 0)[:, None]
    new_dn = np.concatenate([rr[4 + j]["sdn_o"] for j in range(4)], 0)[:, None]
    return (y_prompt.astype(np.float32), y_sample.astype(np.float32), new_ret.astype(np.float32), new_dn.astype(np.float32))
```

```python
import numpy as np
from contextlib import ExitStack
import concourse.bass as bass
import concourse.mybir as mybir
from concourse.bass_utils import run_bass_kernel_spmd

F32 = mybir.dt.float32
F32R = mybir.dt.float32r
BF16 = mybir.dt.bfloat16
AF = mybir.ActivationFunctionType
ALU = mybir.AluOpType

T = 2048
D = 1024
NT = 16
EPS = 1e-6
BIG = 30000.0
DFF = 2816
NFF = 22
DBGR = 0

BLK_R = 0
BLK_D = 4
BLK_GR = 8
BLK_GD = 9
BLK_RO = 10
BLK_DO = 11
BLK_OUT = 12
BLK_FF = 13
NBLK = 19


def _cst_layout():
    names = [("ONES", 128),
             ("BLKM", 128), ("I128", 128), ("TRILO128", 128), ("TRIUP128", 128), ("NEGONES128", 128),
             ("MASKF128", 256), ("MASKB128", 256), ("SMF128", 256), ("SMB128", 256),
             ("DPOS8", 128), ("DNEG8", 128), ("GE8", 128), ("LE8", 128), ("POS1_8", 128), ("POS128M", 128),
             ("KPF", 1), ("KPB", 1), ("EPS", 1), ("EPS128", 1), ("ONE", 1), ("C64", 1), ("ZERO", 1)]
    off = {}
    o = 0
    for n, w in names:
        off[n] = (o, w)
        o += w
    return off, o


CST_OFF, NCST = _cst_layout()


def _build_cst():
    c = np.zeros((128, NCST), np.float32)
    p = np.arange(128)[:, None]
    pp = p % 64

    def put(name, arr):
        if name not in CST_OFF:
            return
        o, w = CST_OFF[name]
        c[:, o:o + w] = arr

    f64 = np.arange(64)[None, :]
    f128 = np.arange(128)[None, :]
    put("I2", (pp == f64))
    put("TRILO", (pp <= f64))
    put("TRIUP", (pp >= f64))
    put("ONES", np.ones((128, 128)))
    put("NEGONES", -np.ones((128, 64)))
    put("SELH0", np.broadcast_to(p < 64, (128, 128)))
    put("SELH1", np.broadcast_to(p >= 64, (128, 128)))
    put("BD1", (p // 64 == f128 // 64))
    put("BDLO", (p // 64 == f128 // 64) & (pp <= f128 % 64))
    put("BDUP", (p // 64 == f128 // 64) & (pp >= f128 % 64))
    ge = (f64 >= pp).astype(np.float32)
    le = (f64 <= pp).astype(np.float32)
    gt = (f64 > pp).astype(np.float32)
    lt = (f64 < pp).astype(np.float32)
    nge = -BIG * (1 - ge)
    nle = -BIG * (1 - le)
    put("MASKF", np.concatenate([nge, nle], 1))
    put("MASKB", np.concatenate([nle, nge], 1))
    put("SMF", np.concatenate([-gt, -lt], 1))
    put("SMB", np.concatenate([-lt, -gt], 1))
    put("DPOS", np.maximum(f64 - pp, 0))
    put("DNEG", np.maximum(pp - f64, 0))
    put("GE", ge)
    put("LE", le)
    put("POS1", np.broadcast_to(f64 + 1.0, (128, 64)))
    put("POS64M", np.broadcast_to(64.0 - f64, (128, 64)))
    put("I128", (p == f128))
    put("TRILO128", (p <= f128))
    put("TRIUP128", (p >= f128))
    put("NEGONES128", -np.ones((128, 128)))
    ge8 = (f128 >= p).astype(np.float32)
    le8 = (f128 <= p).astype(np.float32)
    gt8 = (f128 > p).astype(np.float32)
    lt8 = (f128 < p).astype(np.float32)
    put("BLKM", (f128 // 64 == p // 64))
    put("MASKF128", np.concatenate([-BIG * (1 - ge8), -BIG * (1 - le8)], 1))
    put("MASKB128", np.concatenate([-BIG * (1 - le8), -BIG * (1 - ge8)], 1))
    put("SMF128", np.concatenate([-gt8, -lt8], 1))
    put("SMB128", np.concatenate([-lt8, -gt8], 1))
    put("DPOS8", np.maximum(f128 - p, 0))
    put("DNEG8", np.maximum(p - f128, 0))
    put("GE8", (f128 >= p))
    put("LE8", (f128 <= p))
    put("POS1_8", np.broadcast_to(f128 + 1.0, (128, 128)))
    put("POS128M", np.broadcast_to(128.0 - f128, (128, 128)))
    put("KPF", 127.0 - p)
    put("KPB", p * 1.0)
    put("EPS", np.full((128, 1), EPS))
    put("EPS128", np.full((128, 1), 128 * EPS))
    put("ONE", np.ones((128, 1)))
    put("C64", np.full((128, 1), 64.0))
    return c


class Sched:
    EPOCH = 3000
    NDMA = 24

    def __init__(self, nc, es):
        self.nc = nc
        self.es = es
        self.eng = {"pe": nc.tensor, "dve": nc.vector, "act": nc.scalar, "pool": nc.gpsimd, "sp": nc.sync}
        self.count = {e: 0 for e in self.eng}
        self.sems = {e: [] for e in self.eng}
        self.waited = {e: {} for e in self.eng}
        self.res = {}
        self.pending = {e: ([], []) for e in self.eng}
        self.dsem = [es.enter_context(nc.semaphore("dsem%d" % i)) for i in range(self.NDMA)]
        self.duse = [0] * self.NDMA
        self.qpool = {"sp": list(range(0, 16)), "pool": list(range(16, self.NDMA)), "act": list(range(0, 16))}
        self.qi = {"sp": 0, "pool": 0, "act": 0}
        self.di = 0
        self.semid = {}
        self.all_dma_events = []

    def _sid(self, sem):
        k = id(sem)
        if k not in self.semid:
            self.semid[k] = len(self.semid)
        return self.semid[k]

    def _tl_sem(self, e, epoch):
        while len(self.sems[e]) <= epoch:
            self.sems[e].append(self.es.enter_context(self.nc.semaphore("tl_%s_%d" % (e, len(self.sems[e])))))
        return self.sems[e][epoch]

    def _wait(self, e, ev):
        if ev is None:
            return
        sem, val = ev[0], ev[1]
        sid = self._sid(sem)
        if self.waited[e].get(sid, 0) >= val:
            return
        self.eng[e].wait_ge(sem, val)
        self.waited[e][sid] = val

    def _deps(self, e, reads, writes):
        evs = []
        for k in reads:
            r = self.res.get(k)
            if r is not None and r["w"] is not None:
                evs.append(r["w"])
        for k in writes:
            r = self.res.get(k)
            if r is not None:
                if r["w"] is not None:
                    evs.append(r["w"])
                evs.extend(r["r"].values())
        for ev in evs:
            if ev[0] == "PENDING":
                if ev[1] != e:
                    raise RuntimeError("dependency on pending (no-inc) instruction of %s" % ev[1])
                continue
            self._wait(e, ev)

    def _commit(self, ev, reads, writes, tl):
        for k in writes:
            self.res[k] = {"w": ev, "r": {}}
        for k in reads:
            r = self.res.setdefault(k, {"w": None, "r": {}})
            r["r"][tl] = ev

    def op(self, e, method, reads, writes, inc=True, **kw):
        if e != "pe":
            writes = list(writes) + [k for k in reads if k.startswith("ps")]
        self._deps(e, reads, writes)
        inst = getattr(self.eng[e], method)(**kw)
        if not inc:
            pr, pw = self.pending[e]
            pr.extend(reads)
            pw.extend(writes)
            for k in writes:
                self.res[k] = {"w": ("PENDING", e), "r": {}}
            return inst
        i = self.count[e]
        self.count[e] += 1
        sem = self._tl_sem(e, i // self.EPOCH)
        val = i % self.EPOCH + 1
        inst.then_inc(sem, 1)
        ev = (sem, val)
        pr, pw = self.pending[e]
        self._commit(ev, list(reads) + pr, list(writes) + pw, "tl_" + e + str(i // self.EPOCH))
        self.pending[e] = ([], [])
        return inst

    def dma(self, q, out, in_, reads, writes, **kw):
        self._deps(q, reads, writes)
        pl = self.qpool[q]
        s = pl[self.qi[q] % len(pl)]
        self.qi[q] += 1
        self.di += 1
        sem = self.dsem[s]
        self._wait(q, (sem, 16 * self.duse[s]))
        self.duse[s] += 1
        val = 16 * self.duse[s]
        self.eng[q].dma_start(out=out, in_=in_, **kw).then_inc(sem, 16)
        ev = (sem, val)
        self._commit(ev, reads, writes, "dma%d_%d" % (s, self.duse[s]))
        self.all_dma_events.append(ev)
        if len(self.all_dma_events) > 2 * self.NDMA:
            self.all_dma_events = self.all_dma_events[-self.NDMA:]

    def barrier(self):
        evs = []
        for e in self.eng:
            if self.count[e] > 0:
                i = self.count[e] - 1
                evs.append((self.sems[e][i // self.EPOCH], i % self.EPOCH + 1))
        for s in range(self.NDMA):
            if self.duse[s] > 0:
                evs.append((self.dsem[s], 16 * self.duse[s]))
        for e in self.eng:
            for ev in evs:
                self._wait(e, ev)
        self.res = {}


class _Stop(Exception):
    pass


def build_program(upto=99):
    nc = bass.Bass("TRN2", target_bir_lowering=False)
    try:
        _build_body(nc, upto)
    except _Stop:
        pass
    return nc


def _build_body(nc, upto):

    def din(name, shape, dt=F32):
        return nc.dram_tensor(name, shape, dt, kind="ExternalInput").ap()

    def dout(name, shape, dt=F32):
        return nc.dram_tensor(name, shape, dt, kind="ExternalOutput").ap()

    x_d = din("x", [T, D])
    cond_d = din("cond", [D])
    wmod_d = din("w_mod", [D, 6 * D])
    bmod_d = din("b_mod", [6 * D])
    normw_d = din("norm_w", [4 * D])
    wblk_d = din("wblk", [NBLK, D, D])
    wba_d = din("w_ba", [D, 16])
    wffo_d = din("w_ffo", [DFF, D])
    convw_d = din("conv_w", [3, 2048])
    rdl_d = din("ret_decay_logit", [8])
    gnw_d = din("ret_gn_w", [1024])
    alog_d = din("dn_a_log", [8])
    dtb_d = din("dn_dt_bias", [8])
    dnw_d = din("dn_norm_w", [256])
    sret_d = din("st_ret", [2, 4, 128, 256])
    sdn_d = din("st_dn", [2, 4, 128, 256])
    ropec_d = din("rope_c", [128, T])
    ropes_d = din("rope_s", [128, T])
    flag_d = din("flag", [128, 1])
    cst_d = din("cst", [128, NCST])
    ident_d = din("ident", [128, 128])
    y_d = dout("y", [T, D])
    sret_o = dout("sret_o", [4, 2, 4, 128, 256])
    sdn_o = dout("sdn_o", [4, 2, 4, 128, 256])
    yrg_d = nc.dram_tensor("yrg_scr", [8, 128, T], BF16, kind="Internal").ap()
    modscr_d = nc.dram_tensor("mod_scr", [1, 4 * D], F32, kind="Internal").ap()

    with ExitStack() as es:
        S = Sched(nc, es)

        dbgn = [0]

        def DBG(name, ap, shape, dt, reads):
            d = nc.dram_tensor("dbg_" + name, shape, dt, kind="ExternalOutput").ap()
            S.dma("sp", d, ap, reads, [])

        def STOP(k):
            if upto == k:
                S.barrier()
                print("STOP at", k, "INSTR", dict(S.count), "dma", S.di)
                raise _Stop()

        def MEM(tag):
            pass

        sbn = [0]

        def SB(stack, name, shape, dt):
            sbn[0] += 1
            return stack.enter_context(nc.sbuf_tensor("s%d_%s" % (sbn[0], name), shape, dt))

        psb = [es.enter_context(nc.psum_tensor("psb%d" % i, [128, 512], F32)) for i in range(7)]
        psT = es.enter_context(nc.psum_tensor("psT", [128, 1024], BF16))
        psi = [0]

        held = set()

        def PS():
            while True:
                i = psi[0] % 7
                psi[0] += 1
                if i not in held:
                    return psb[i], "psb%d" % i

        cst = SB(es, "cst", [128, NCST], F32)
        S.dma("sp", cst[:], cst_d[:, :], [], ["cst"])

        def C(name, rows=slice(0, 128), w=None):
            o, ww = CST_OFF[name]
            if w is None:
                w = ww
            return cst[rows, o:o + w]

        identb = SB(es, "identb", [128, 128], BF16)
        S.dma("pool", identb[:], ident_d[:, :], [], ["identb"])
        flag = SB(es, "flag", [128, 1], F32)
        S.dma("sp", flag[:], flag_d[:, :], [], ["flag"])
        hT = SB(es, "hT", [128, 8, T], BF16)
        wbuf = [SB(es, "wbuf%d" % i, [128, 8, D], BF16) for i in range(2)]
        wslot = [0]
        s01 = ExitStack()
        s01.__enter__()
        modA = SB(s01, "modA", [128, 2 * D], F32)
        modb = SB(s01, "modb", [128, 4 * D], F32)

        def load_w(blk):
            i = wslot[0] % 2
            wslot[0] += 1
            S.dma("pool", wbuf[i][:], wblk_d[blk].rearrange("(k p) c -> p k c", p=128), [], ["wbuf%d" % i])
            return wbuf[i], "wbuf%d" % i

        def mm(ps, pk, lhsT, rhs, rk, start, stop, last=None):
            if last is None:
                last = stop
            S.op("pe", "matmul", rk + ["cst"], [pk], inc=last, out=ps, lhsT=lhsT, rhs=rhs, start=start, stop=stop)

        MEM("global")
        s0 = ExitStack()
        s0.__enter__()
        normw = SB(s0, "normw", [128, 4 * D], F32)
        wm = [SB(s0, "wm%d" % i, [128, 8, 512], F32) for i in range(2)]
        cond = SB(s0, "cond", [128, 8], F32)
        csil = SB(s0, "csil", [128, 8], F32)
        cb = SB(s0, "cb", [128, 8, 128], F32)
        S.dma("sp", modA[:], bmod_d[0:2 * D].partition_broadcast(128), [], ["modA"])
        S.dma("sp", modb[:], bmod_d[2 * D:6 * D].partition_broadcast(128), [], ["modb"])
        S.dma("sp", normw[:], normw_d.partition_broadcast(128), [], ["normw"])
        S.dma("sp", cond[:], cond_d.rearrange("(k p) -> p k", p=128), [], ["cond"], allow_slow_non_contiguous=True)
        S.op("act", "activation", ["cond"], ["csil"], out=csil[:], in_=cond[:], func=AF.Silu)
        S.op("dve", "tensor_copy", ["csil"], ["cb"], out=cb[:], in_=csil[:].unsqueeze(2).to_broadcast([128, 8, 128]))

        def mod_cols(cg):
            w_ = wm[cg % 2]
            wk = "wm%d" % (cg % 2)
            S.dma("sp", w_[:], wmod_d[:, cg * 512:(cg + 1) * 512].rearrange("(k p) c -> p k c", p=128), [], [wk])
            ps, pk = PS()
            for kc in range(8):
                mm(ps[:, :], pk, cb[:, kc, :], w_[:, kc, :], ["cb", wk], kc == 0, kc == 7)
            if cg < 4:
                msl = modA[:, cg * 512:(cg + 1) * 512]
                mk = "modA"
            else:
                msl = modb[:, (cg - 4) * 512:(cg - 3) * 512]
                mk = "modb"
            S.op("dve", "tensor_tensor", [pk, mk], [mk], out=msl, in0=msl, in1=ps[:, :], op=ALU.add)

        for cg in range(4):
            mod_cols(cg)
        S.op("dve", "scalar_tensor_tensor", ["modA", "normw"], ["modA"], out=modA[:, 1024:2048], in0=modA[:, 1024:2048],
             scalar=1.0, in1=normw[:, 0:1024], op0=ALU.add, op1=ALU.mult)

        def stage0b():
            for cg in range(4, 12):
                mod_cols(cg)
                yield
            S.op("dve", "tensor_tensor", ["modb", "normw"], ["modb"], out=modb[:, 0:1024], in0=modb[:, 0:1024],
                 in1=normw[:, 1024:2048], op=ALU.mult)
            S.op("dve", "scalar_tensor_tensor", ["modb", "normw"], ["modb"], out=modb[:, 2048:3072], in0=modb[:, 2048:3072],
                 scalar=1.0, in1=normw[:, 2048:3072], op0=ALU.add, op1=ALU.mult)
            S.op("dve", "tensor_tensor", ["modb", "normw"], ["modb"], out=modb[:, 3072:4096], in0=modb[:, 3072:4096],
                 in1=normw[:, 3072:4096], op=ALU.mult)
            S.dma("sp", modscr_d[0:1, :], modb[0:1, :], ["modb"], ["modscr"])
            yield

        B1 = modA[:, 0:1024]
        A1 = modA[:, 1024:2048]

        def interleave(factories, width, extra=None, extras=None):
            extras = list(extras or [])
            active = []
            free = list(range(width))
            it = iter(factories)
            pending = True
            while pending or active:
                while free and pending:
                    try:
                        f = next(it)
                    except StopIteration:
                        pending = False
                        break
                    bi = free.pop(0)
                    active.append((f(bi), bi))
                if extra is not None:
                    try:
                        next(extra)
                    except StopIteration:
                        extra = None
                for eg_ in list(extras):
                    try:
                        next(eg_)
                    except StopIteration:
                        extras.remove(eg_)
                for item in list(active):
                    try:
                        next(item[0])
                    except StopIteration:
                        active.remove(item)
                        free.append(item[1])
            while extra is not None:
                try:
                    next(extra)
                except StopIteration:
                    extra = None
            while extras:
                for eg_ in list(extras):
                    try:
                        next(eg_)
                    except StopIteration:
                        extras.remove(eg_)

        def norm_to_hT_gen(src, srck, tt, A, B, tmp32, tk, hb, hk, small, smk, modk="modb"):
            S.op("act", "activation", [srck], [tk, smk + "0"], out=tmp32[:], in_=src, func=AF.Square, accum_out=small[:, 0:1])
            yield
            S.op("act", "activation", [smk + "0", "cst"], [smk + "1"], out=small[:, 1:2], in_=small[:, 0:1], func=AF.Sqrt,
                 scale=1.0 / D, bias=C("EPS"))
            yield
            S.op("dve", "reciprocal", [smk + "1"], [smk + "2"], out=small[:, 2:3], in_=small[:, 1:2])
            S.op("dve", "scalar_tensor_tensor", [srck, smk + "2", modk], [tk], out=tmp32[:], in0=src, scalar=small[:, 2:3],
                 in1=A, op0=ALU.mult, op1=ALU.mult)
            yield
            S.op("dve", "tensor_tensor", [tk, modk], [hk], out=hb[:], in0=tmp32[:], in1=B, op=ALU.add)
            yield
            for kc in range(8):
                S.op("pe", "transpose", [hk, "identb"], ["psT"], inc=(kc == 7), out=psT[:, kc * 128:(kc + 1) * 128],
                     in_=hb[:, kc * 128:(kc + 1) * 128], identity=identb[:])
            S.op("act", "activation", ["psT"], ["hT"], out=hT[:, :, tt * 128:(tt + 1) * 128],
                 in_=psT[:, :].rearrange("p (k t) -> p k t", k=8), func=AF.Copy)
            yield

        preR0 = load_w(BLK_R)
        with ExitStack() as s1:
            W1 = 4
            xb = [SB(s1, "xb%d" % i, [128, D], F32) for i in range(W1)]
            tmp32 = [SB(s1, "tmp32_%d" % i, [128, D], F32) for i in range(W1)]
            hb = [SB(s1, "hb%d" % i, [128, D], BF16) for i in range(W1)]
            small = [SB(s1, "small%d" % i, [128, 4], F32) for i in range(W1)]

            def st1(tt):
                def g(bi):
                    xk = "xb%d" % bi
                    S.dma("sp", xb[bi][:], x_d[tt * 128:(tt + 1) * 128, :], [], [xk])
                    yield
                    yield from norm_to_hT_gen(xb[bi][:], xk, tt, A1, B1, tmp32[bi], "tmp%d" % bi, hb[bi], "hb%d" % bi, small[bi], "sm%d_" % bi, "modA")
                return g
            interleave([st1(tt) for tt in range(NT)], W1, extra=stage0b())
            S.barrier()
        s0.__exit__(None, None, None)
        s01.__exit__(None, None, None)
        STOP(1)

        MEM("before mixers")
        with ExitStack() as sB:
            zT = SB(sB, "zT", [128, 8, T], BF16)
            par = SB(sB, "par", [128, 64], F32)
            lg = SB(sB, "lg", [128, 8], F32)
            dnw = SB(sB, "dnw", [128, 256], F32)
            S.dma("sp", dnw[:], dnw_d.partition_broadcast(128), [], ["dnw"])
            STOP(1.5)
            S32 = [[SB(sB, "S32_%d_%d" % (r, i), [128, 256], F32) for i in range(2)] for r in range(2)]
            S16 = [[SB(sB, "S16_%d_%d" % (r, i), [128, 256], BF16) for i in range(2)] for r in range(2)]

            def state_boundary(r, cur, seg, h, out_d):
                k = "S32_%d_%d" % (r, cur)
                k2 = "S32_%d_%d" % (r, 1 - cur)
                if seg < 4:
                    S.dma("sp", out_d[seg, r, h], S32[r][cur][:], [k], [])
                S.op("dve", "tensor_scalar", [k, "flag"], [k2], out=S32[r][1 - cur][:], in0=S32[r][cur][:], scalar1=flag[:, 0:1],
                     scalar2=None, op0=ALU.mult)
                return 1 - cur

            MEM("before sR")
            with ExitStack() as sR:
                cdec = SB(sR, "cdec", [128, 8], F32)
                kdec = SB(sR, "kdec", [128, 8], F32)
                qpat = SB(sR, "qpat", [128, 8, 128], F32)
                MT = SB(sR, "MT", [128, 4, 128], F32)
                S.dma("sp", par[:, 0:8], rdl_d.partition_broadcast(128), [], ["par"])
                S.op("act", "activation", ["par"], ["par1"], out=par[:, 8:16], in_=par[:, 0:8], func=AF.Exp, scale=-1.0)
                S.op("act", "activation", ["par1", "cst"], ["par2"], out=par[:, 16:24], in_=par[:, 8:16], func=AF.Ln, bias=C("ONE"))
                S.op("dve", "tensor_scalar", ["par2"], ["lg"], out=lg[:], in0=par[:, 16:24], scalar1=-1.0, scalar2=None, op0=ALU.mult)
                S.op("act", "activation", ["lg"], ["cdec"], out=cdec[:], in_=lg[:], func=AF.Exp, scale=128.0)
                mtmp = SB(sR, "mtmp", [128, 2, 128], F32)
                for h in range(4):
                    S.op("act", "activation", ["lg", "cst"], ["kdec"], out=kdec[:, h:h + 1], in_=C("KPF"), func=AF.Exp, scale=lg[:, h:h + 1])
                    S.op("act", "activation", ["lg", "cst"], ["kdec"], out=kdec[:, 4 + h:5 + h], in_=C("KPB"), func=AF.Exp, scale=lg[:, 4 + h:5 + h])
                    S.op("act", "activation", ["lg", "cst"], ["qpat"], out=qpat[:, h, :], in_=C("POS1_8"), func=AF.Exp, scale=lg[:, h:h + 1])
                    S.op("act", "activation", ["lg", "cst"], ["qpat"], out=qpat[:, 4 + h, :], in_=C("POS128M"), func=AF.Exp, scale=lg[:, 4 + h:5 + h])
                    S.op("act", "activation", ["lg", "cst"], ["mtmp"], out=mtmp[:, 0, :], in_=C("DPOS8"), func=AF.Exp, scale=lg[:, h:h + 1])
                    S.op("act", "activation", ["lg", "cst"], ["mtmp"], out=mtmp[:, 1, :], in_=C("DNEG8"), func=AF.Exp, scale=lg[:, 4 + h:5 + h])
                    S.op("dve", "tensor_tensor", ["mtmp", "cst"], ["mtmp"], out=mtmp[:, 0, :], in0=mtmp[:, 0, :], in1=C("GE8"), op=ALU.mult)
                    S.op("dve", "tensor_tensor", ["mtmp", "cst"], ["mtmp"], out=mtmp[:, 1, :], in0=mtmp[:, 1, :], in1=C("LE8"), op=ALU.mult)
                    S.op("dve", "tensor_tensor", ["mtmp"], ["MT"], out=MT[:, h, :], in0=mtmp[:, 0, :], in1=mtmp[:, 1, :], op=ALU.add)

                gnw = SB(sR, "gnw", [128, 1024], F32)
                S.dma("sp", gnw[:], gnw_d.partition_broadcast(128), [], ["gnw"])
                ropec = SB(sR, "ropec", [128, T], BF16)
                ropes = SB(sR, "ropes", [128, T], BF16)
                S.dma("pool", ropec[:], ropec_d[:, :], [], ["ropec"])
                S.dma("pool", ropes[:], ropes_d[:, :], [], ["ropes"])
                qT = SB(sR, "qT", [128, T], BF16)
                kT = SB(sR, "kT", [128, T], BF16)
                qfT = SB(sR, "qfT", [128, T], BF16)
                qbT = SB(sR, "qbT", [128, T], BF16)
                kft = SB(sR, "kft", [128, NT, 128], BF16)
                kbt = SB(sR, "kbt", [128, NT, 128], BF16)
                vt = SB(sR, "vt", [128, NT, 256], BF16)
                WB = 5
                rg = [SB(sR, "rg%d" % i, [128, 256], BF16) for i in range(WB)]
                of = SB(sR, "of", [128, NT, 256], BF16)
                ob = SB(sR, "ob", [128, NT, 256], BF16)
                sT = SB(sR, "sT", [128, NT, 128], BF16)
                tA = SB(sR, "tA", [128, 512], F32)
                tB = SB(sR, "tB", [128, 512], F32)
                o32 = [SB(sR, "o32_%d" % i, [128, 256], F32) for i in range(WB)]
                t32 = [SB(sR, "t32_%d" % i, [128, 256], F32) for i in range(WB)]
                zb = [SB(sR, "zb%d" % i, [128, 256], BF16) for i in range(WB)]
                sm = [SB(sR, "sm%d" % i, [128, 8], F32) for i in range(WB)]
                MEM("sR after alloc")
                pendingRB = []
                nextW = preR0
                for h in range(4):
                    W, wk = nextW
                    def ret_pre(h=h, W=W, wk=wk):
                        for tg in range(4):
                            ts_ = slice(tg * 512, (tg + 1) * 512)
                            for (dst, dk_, c0, scale) in ((qT, "qT", 0, 128.0 ** -0.5), (kT, "kT", 256, 1.0)):
                                psa, pka = PS()
                                for kc in range(8):
                                    mm(psa[:, :], pka, W[:, kc, c0:c0 + 128], hT[:, kc, ts_], [wk, "hT"], kc == 0, kc == 7)
                                psb2, pkb = PS()
                                for kc in range(8):
                                    mm(psb2[:, :], pkb, W[:, kc, c0 + 128:c0 + 256], hT[:, kc, ts_], [wk, "hT"], kc == 0, kc == 7)
                                S.op("dve", "scalar_tensor_tensor", [pka, "ropec"], ["tA"], out=tA[:], in0=psa[:, :], scalar=scale,
                                     in1=ropec[:, ts_], op0=ALU.mult, op1=ALU.mult)
                                S.op("dve", "scalar_tensor_tensor", [pkb, "ropes"], ["tB"], out=tB[:], in0=psb2[:, :], scalar=scale,
                                     in1=ropes[:, ts_], op0=ALU.mult, op1=ALU.mult)
                                S.op("dve", "tensor_tensor", ["tA", "tB"], [dk_], out=dst[:, ts_], in0=tA[:], in1=tB[:], op=ALU.add)
                                yield
                        for tt in range(NT):
                            ps, pk = PS()
                            for kc in range(8):
                                mm(ps[:, 0:256], pk, hT[:, kc, tt * 128:(tt + 1) * 128], W[:, kc, 512:768], [wk, "hT"], kc == 0, kc == 7)
                            S.op("act", "activation", [pk], ["vt"], out=vt[:, tt, :], in_=ps[:, 0:256], func=AF.Copy)
                            if tt % 2 == 1:
                                yield
                        for g4 in range(4):
                            for j in range(4):
                                tt = g4 * 4 + j
                                S.op("pe", "transpose", ["kT", "identb"], ["psT"], inc=(j == 3), out=psT[:, j * 128:(j + 1) * 128],
                                     in_=kT[:, tt * 128:(tt + 1) * 128], identity=identb[:])
                            pv = psT[:, 0:512].rearrange("p (t d) -> p t d", t=4)
                            S.op("dve", "tensor_scalar", ["psT", "kdec"], ["kft"], out=kft[:, g4 * 4:(g4 + 1) * 4, :], in0=pv,
                                 scalar1=kdec[:, h:h + 1], scalar2=None, op0=ALU.mult)
                            S.op("act", "activation", ["psT", "kdec"], ["kbt"], out=kbt[:, g4 * 4:(g4 + 1) * 4, :], in_=pv, func=AF.Copy,
                                 scale=kdec[:, 4 + h:5 + h])
                            yield
                        S.op("dve", "tensor_tensor", ["qT", "qpat"], ["qfT"], out=qfT[:, :].rearrange("p (n i) -> p n i", i=128),
                             in0=qT[:, :].rearrange("p (n i) -> p n i", i=128),
                             in1=qpat[:, h:h + 1, :].to_broadcast([128, 16, 128]), op=ALU.mult)
                        S.op("dve", "tensor_tensor", ["qT", "qpat"], ["qbT"], out=qbT[:, :].rearrange("p (n i) -> p n i", i=128),
                             in0=qT[:, :].rearrange("p (n i) -> p n i", i=128),
                             in1=qpat[:, 4 + h:5 + h, :].to_broadcast([128, 16, 128]), op=ALU.mult)
                        yield
                        for g4 in range(4):
                            ps, pk = PS()
                            for c in range(4):
                                tt = g4 * 4 + c
                                tsl = slice(tt * 128, (tt + 1) * 128)
                                mm(ps[:, c * 128:(c + 1) * 128], pk, kT[:, tsl], qT[:, tsl], ["kT", "qT"], True, True, last=(c == 3))
                            S.op("dve", "tensor_tensor", [pk, "MT"], ["sT"], out=sT[:, g4 * 4:(g4 + 1) * 4, :], in0=ps[:, :].rearrange("p (t i) -> p t i", t=4),
                                 in1=MT[:, h:h + 1, :].to_broadcast([128, 4, 128]), op=ALU.mult)
                            yield
                    interleave([lambda bi: ret_pre()], 1, extras=pendingRB)
                    pendingRB = []
                    nextW = load_w(BLK_R + h + 1) if h < 3 else load_w(BLK_RO)
                    STOP(1.8)
                    cur = [0, 0]
                    c16 = [0, 0]
                    for r in range(2):
                        S.dma("sp", S32[r][0][:], sret_d[r, h], [], ["S32_%d_0" % r])
                        S.op("act", "activation", ["S32_%d_0" % r], ["S16_%d_0" % r], out=S16[r][0][:], in_=S32[r][0][:], func=AF.Copy)
                    for t in range(NT):
                        for r in range(2):
                            tt = t if r == 0 else NT - 1 - t
                            tsl = slice(tt * 128, (tt + 1) * 128)
                            sk = "S32_%d_%d" % (r, cur[r])
                            s16k = "S16_%d_%d" % (r, c16[r])
                            s16n = "S16_%d_%d" % (r, 1 - c16[r])
                            Scur = S16[r][c16[r]]
                            Snew = S16[r][1 - c16[r]]
                            pss, pks = PS()
                            kk = kft if r == 0 else kbt
                            mm(pss[:, 0:256], pks, kk[:, tt, :], vt[:, tt, :], ["kft", "kbt", "vt"], True, True)
                            pso, pko = PS()
                            if r == 0:
                                mm(pso[:, 0:256], pko, sT[:, tt, :], vt[:, tt, :], ["sT", "vt"], True, False)
                                mm(pso[:, 0:256], pko, qfT[:, tsl], Scur[:], ["qfT", s16k], False, True)
                                S.op("act", "activation", [pko], ["of"], out=of[:, tt, :], in_=pso[:, 0:256], func=AF.Copy)
                            else:
                                mm(pso[:, 0:256], pko, qbT[:, tsl], Scur[:], ["qbT", s16k], True, True)
                                S.op("act", "activation", [pko], ["ob"], out=ob[:, tt, :], in_=pso[:, 0:256], func=AF.Copy)
                            bnd = (tt % 2 == 1) if r == 0 else (tt % 2 == 0)
                            if not bnd:
                                S.op("dve", "scalar_tensor_tensor", [pks, sk, "cdec"], [s16n], out=Snew[:], in0=S32[r][cur[r]][:],
                                     scalar=cdec[:, 4 * r + h:4 * r + h + 1], in1=pss[:, 0:256], op0=ALU.mult, op1=ALU.add)
                            S.op("dve", "scalar_tensor_tensor", [pks, sk, "cdec"], [sk], out=S32[r][cur[r]][:], in0=S32[r][cur[r]][:],
                                 scalar=cdec[:, 4 * r + h:4 * r + h + 1], in1=pss[:, 0:256], op0=ALU.mult, op1=ALU.add)
                            if bnd:
                                cur[r] = state_boundary(r, cur[r], tt // 2, h, sret_o)
                                S.op("act", "activation", ["S32_%d_%d" % (r, cur[r])], [s16n], out=Snew[:], in_=S32[r][cur[r]][:], func=AF.Copy)
                            c16[r] = 1 - c16[r]
                    STOP(1.9)
                    def retB(tt, h=h, W=W, wk=wk):
                        def g(bi):
                            o32_, t32_, zb_, sm_, rg_ = o32[bi], t32[bi], zb[bi], sm[bi], rg[bi]
                            q_ = "_%d" % bi
                            ps, pk = PS()
                            for kc in range(8):
                                mm(ps[:, 0:256], pk, hT[:, kc, tt * 128:(tt + 1) * 128], W[:, kc, 768:1024], [wk, "hT"], kc == 0, kc == 7)
                            S.op("act", "activation", [pk], ["rg" + q_], out=rg_[:], in_=ps[:, 0:256], func=AF.Silu)
                            S.op("dve", "scalar_tensor_tensor", ["of", "ob"], ["o32" + q_, "sm0" + q_], out=o32_[:], in0=of[:, tt, :], scalar=1.0,
                                 in1=ob[:, tt, :], op0=ALU.mult, op1=ALU.add, accum_out=sm_[:, 0:1])
                            yield
                            S.op("dve", "tensor_scalar", ["sm0" + q_], ["sm1" + q_], out=sm_[:, 1:2], in0=sm_[:, 0:1], scalar1=-1.0 / 256, scalar2=None, op0=ALU.mult)
                            yield
                            S.op("act", "activation", ["o32" + q_, "sm1" + q_], ["t32" + q_, "sm2" + q_], out=t32_[:], in_=o32_[:], func=AF.Square, bias=sm_[:, 1:2],
                                 accum_out=sm_[:, 2:3])
                            yield
                            S.op("act", "activation", ["sm2" + q_, "cst"], ["sm3" + q_], out=sm_[:, 3:4], in_=sm_[:, 2:3], func=AF.Sqrt, scale=1.0 / 256, bias=C("EPS"))
                            yield
                            S.op("dve", "reciprocal", ["sm3" + q_], ["sm4" + q_], out=sm_[:, 4:5], in_=sm_[:, 3:4])
                            S.op("dve", "scalar_tensor_tensor", ["o32" + q_, "sm1" + q_, "gnw"], ["t32" + q_], out=t32_[:], in0=o32_[:], scalar=sm_[:, 1:2],
                                 in1=gnw[:, h * 256:(h + 1) * 256], op0=ALU.add, op1=ALU.mult)
                            S.op("dve", "scalar_tensor_tensor", ["t32" + q_, "sm4" + q_, "rg" + q_], ["zb" + q_], out=zb_[:], in0=t32_[:], scalar=sm_[:, 4:5],
                                 in1=rg_[:], op0=ALU.mult, op1=ALU.mult)
                            yield
                            for c in range(2):
                                S.op("pe", "transpose", ["zb" + q_, "identb"], ["psT"], inc=(c == 1), out=psT[:, c * 128:(c + 1) * 128],
                                     in_=zb_[:, c * 128:(c + 1) * 128], identity=identb[:])
                            S.op("act", "activation", ["psT"], ["zT"], out=zT[:, 2 * h:2 * h + 2, tt * 128:(tt + 1) * 128],
                                 in_=psT[:, 0:256].rearrange("p (c t) -> p c t", c=2), func=AF.Copy)
                            yield
                        return g
                    def laneRB(tts, bi, retB=retB):
                        for tt in tts:
                            yield from retB(tt)(bi)
                    lanesR = [laneRB(list(range(bi, NT, WB)), bi) for bi in range(WB)]
                    if h < 3:
                        pendingRB = lanesR
                    else:
                        interleave([], 1, extras=lanesR)
                S.barrier()

            def out_gate(blk_o, blk_g, first, mT=None, pre=None):
                with ExitStack() as sg:
                    sig = SB(sg, "sig", [128, 512], F32)
                    yt = [SB(sg, "yt%d" % i, [128, 512], BF16) for i in range(2)]
                    tg32 = SB(sg, "tg32", [128, 512], F32)
                    Wo, wko = pre if pre is not None else load_w(blk_o)
                    Wg, wkg = load_w(blk_g)
                    it = 0
                    for fo in range(8):
                        for tg in range(4):
                            ts_ = slice(tg * 512, (tg + 1) * 512)
                            ps1, pk1 = PS()
                            for kc in range(8):
                                mm(ps1[:, :], pk1, Wo[:, kc, fo * 128:(fo + 1) * 128], zT[:, kc, ts_], [wko, "zT"], kc == 0, kc == 7)
                            ps2, pk2 = PS()
                            for kc in range(8):
                                mm(ps2[:, :], pk2, Wg[:, kc, fo * 128:(fo + 1) * 128], hT[:, kc, ts_], [wkg, "hT"], kc == 0, kc == 7)
                            S.op("act", "activation", [pk2], ["sig"], out=sig[:], in_=ps2[:, :], func=AF.Sigmoid)
                            yk = "yt%d" % (it % 2)
                            y_ = yt[it % 2]
                            it += 1
                            if first:
                                S.op("dve", "tensor_tensor", [pk1, "sig"], [yk], out=y_[:], in0=ps1[:, :], in1=sig[:], op=ALU.mult)
                                S.dma("sp", yrg_d[fo, :, ts_], y_[:], [yk], ["yrg%d_%d" % (fo, tg)])
                            else:
                                S.dma("sp", y_[:], yrg_d[fo, :, ts_], ["yrg%d_%d" % (fo, tg)], [yk])
                                S.op("dve", "tensor_tensor", [pk1, "sig"], ["tg32"], out=tg32[:], in0=ps1[:, :], in1=sig[:], op=ALU.mult)
                                S.op("dve", "tensor_tensor", ["tg32", yk], ["mT"], out=mT[:, fo, ts_], in0=tg32[:], in1=y_[:], op=ALU.add)
                    S.barrier()

            STOP(2)
            out_gate(BLK_RO, BLK_GR, True, pre=nextW)
            STOP(3)

            with ExitStack() as sD:
                preD0 = load_w(BLK_D)
                bap = SB(sD, "bap", [128, NT, 16], F32)
                beta = SB(sD, "beta", [128, NT, 8], F32)
                la = SB(sD, "la", [128, NT, 8], F32)
                Gtok = SB(sD, "Gtok", [128, NT, 8], F32)
                eg = SB(sD, "eg", [128, NT, 8], F32)
                kh = SB(sD, "kh", [128, NT, 8], F32)
                bw = SB(sD, "bw", [128, NT, 8], F32)
                EGL = SB(sD, "EGL", [128, NT, 8], F32)
                wba = SB(sD, "wba", [128, 8, 16], BF16)
                dpar = SB(sD, "dpar", [128, 32], F32)
                convw = SB(sD, "convw", [128, 3, 16], F32)
                ncw = SB(sD, "ncw", [128, 2, 16], F32)
                S.dma("pool", wba[:], wba_d.rearrange("(k p) c -> p k c", p=128), [], ["wba"])
                S.dma("sp", dpar[:, 0:8], alog_d.partition_broadcast(128), [], ["dpar"])
                S.dma("sp", dpar[:, 8:16], dtb_d.partition_broadcast(128), [], ["dpar"])
                S.dma("sp", convw[:], convw_d.rearrange("t (c p) -> p t c", p=128), [], ["convw"], allow_slow_non_contiguous=True)
                S.op("dve", "tensor_scalar", ["flag"], ["dpar2"], out=dpar[:, 24:25], in0=flag[:, 0:1], scalar1=-1.0, scalar2=None, op0=ALU.add)
                S.op("dve", "tensor_scalar", ["convw", "dpar2"], ["ncw"], out=ncw[:, 0, :], in0=convw[:, 0, :], scalar1=dpar[:, 24:25], scalar2=None, op0=ALU.mult)
                S.op("dve", "tensor_scalar", ["convw", "dpar2"], ["ncw"], out=ncw[:, 1, :], in0=convw[:, 2, :], scalar1=dpar[:, 24:25], scalar2=None, op0=ALU.mult)
                S.op("act", "activation", ["dpar"], ["dpar1"], out=dpar[:, 16:24], in_=dpar[:, 0:8], func=AF.Exp)
                for tt in range(NT):
                    ps, pk = PS()
                    for kc in range(8):
                        mm(ps[:, 0:16], pk, hT[:, kc, tt * 128:(tt + 1) * 128], wba[:, kc, :], ["wba", "hT"], kc == 0, kc == 7)
                    S.op("act", "activation", [pk], ["bap"], out=bap[:, tt, :], in_=ps[:, 0:16], func=AF.Copy)
                S.op("act", "activation", ["bap"], ["beta"], out=beta[:], in_=bap[:, :, 0:8], func=AF.Sigmoid)
                S.op("dve", "tensor_tensor", ["bap", "dpar"], ["la"], out=la[:], in0=bap[:, :, 8:16],
                     in1=dpar[:, 8:16].unsqueeze(1).to_broadcast([128, NT, 8]), op=ALU.add)
                S.op("act", "activation", ["la"], ["la"], out=la[:], in_=la[:], func=AF.Exp)
                S.op("act", "activation", ["la", "cst"], ["la"], out=la[:], in_=la[:], func=AF.Ln, bias=C("ONE"))
                S.op("dve", "scalar_tensor_tensor", ["la", "dpar1"], ["la"], out=la[:], in0=la[:], scalar=-1.0,
                     in1=dpar[:, 16:24].unsqueeze(1).to_broadcast([128, NT, 8]), op0=ALU.mult, op1=ALU.mult)
                ps, pk = PS()
                mm(ps[:, 0:64].rearrange("p (t c) -> p t c", c=4), pk, C("TRILO128"), la[:, :, 0:4], ["la"], True, True, last=False)
                mm(ps[:, 64:128].rearrange("p (t c) -> p t c", c=4), pk, C("TRIUP128"), la[:, :, 4:8], ["la"], True, True)
                S.op("act", "activation", [pk], ["Gtok"], out=Gtok[:, :, 0:4], in_=ps[:, 0:64].rearrange("p (t c) -> p t c", c=4), func=AF.Copy)
                S.op("act", "activation", [pk], ["Gtok"], out=Gtok[:, :, 4:8], in_=ps[:, 64:128].rearrange("p (t c) -> p t c", c=4), func=AF.Copy)
                ps, pk = PS()
                mm(ps[:, 0:128], pk, C("ONES"), la[:].rearrange("p t c -> p (t c)"), ["la"], True, True)
                S.op("dve", "tensor_tensor", [pk, "Gtok"], ["kh"], out=kh[:].rearrange("p t c -> p (t c)"), in0=ps[:, 0:128],
                     in1=Gtok[:].rearrange("p t c -> p (t c)"), op=ALU.subtract)
                S.op("act", "activation", [pk], ["EGL"], out=EGL[:].rearrange("p t c -> p (t c)"), in_=ps[:, 0:128], func=AF.Exp)
                S.op("act", "activation", ["kh"], ["kh"], out=kh[:], in_=kh[:], func=AF.Exp)
                S.op("act", "activation", ["Gtok"], ["eg"], out=eg[:], in_=Gtok[:], func=AF.Exp)
                S.op("dve", "tensor_tensor", ["eg", "beta"], ["bw"], out=bw[:], in0=eg[:], in1=beta[:], op=ALU.mult)

                dqT = SB(sD, "dqT", [128, T], BF16)
                dkT = SB(sD, "dkT", [128, T], BF16)
                ktok = SB(sD, "ktok", [128, NT, 128], BF16)
                vt = SB(sD, "dvt", [128, NT, 256], BF16)
                WBD = 4
                zg = [SB(sD, "zg%d" % i, [128, 256], BF16) for i in range(WBD)]
                odf = SB(sD, "odf", [128, NT, 256], BF16)
                odb = SB(sD, "odb", [128, NT, 256], BF16)
                o32 = [SB(sD, "do32_%d" % i, [128, 256], F32) for i in range(WBD)]
                t32 = [SB(sD, "dt32_%d" % i, [128, 256], F32) for i in range(WBD)]
                zb = [SB(sD, "dzb%d" % i, [128, 256], BF16) for i in range(WBD)]
                sm = [SB(sD, "dsm%d" % i, [128, 8], F32) for i in range(WBD)]
                RING = 3
                NSET = 4

                pendingB = []
                nextW = preD0
                for h in range(4):
                    W, wk = nextW
                    with ExitStack() as sconv:
                        WC = 2
                        xpre_l = [SB(sconv, "xpre%d" % i, [128, T], BF16) for i in range(WC)]
                        cv_l = [SB(sconv, "cv%d" % i, [128, T], F32) for i in range(WC)]
                        cvr_l = [SB(sconv, "cvr%d" % i, [128, 512], F32R) for i in range(WC)]
                        ysb_l = xpre_l
                        rin_l = [SB(sconv, "rin%d" % i, [128, 512], F32) for i in range(WC)]
                        onesR2 = SB(sconv, "onesR2", [128, 128], F32R)
                        S.op("dve", "tensor_copy", ["cst"], ["onesR2"], out=onesR2[:], in_=C("ONES"))

                        def conv_job(kind, ct, c0, h=h, W=W, wk=wk):
                            def g(bi):
                                xpre, cv, ysb, rin, cvr = xpre_l[bi], cv_l[bi], ysb_l[bi], rin_l[bi], cvr_l[bi]
                                q_ = "_%d" % bi
                                xk, ck, yk, rk_ = "xpre" + q_, "cv" + q_, "xpre" + q_, "rin" + q_
                                for tg in range(4):
                                    ts_ = slice(tg * 512, (tg + 1) * 512)
                                    ps, pk = PS()
                                    for kc in range(8):
                                        mm(ps[:, :], pk, W[:, kc, c0:c0 + 128], hT[:, kc, ts_], [wk, "hT"], kc == 0, kc == 7)
                                    if tg % 2 == 0:
                                        S.op("act", "activation", [pk], [xk], out=xpre[:, ts_], in_=ps[:, :], func=AF.Copy)
                                    else:
                                        S.op("dve", "tensor_copy", [pk], [xk], out=xpre[:, ts_], in_=ps[:, :])
                                    yield
                                w0 = convw[:, 0, ct:ct + 1]
                                w1 = convw[:, 1, ct:ct + 1]
                                w2 = convw[:, 2, ct:ct + 1]
                                S.op("dve", "tensor_scalar", [xk, "convw"], [ck], out=cv[:], in0=xpre[:], scalar1=w1, scalar2=None, op0=ALU.mult)
                                S.op("dve", "scalar_tensor_tensor", [xk, "convw", ck], [ck], out=cv[:, 1:T], in0=xpre[:, 0:T - 1], scalar=w0,
                                     in1=cv[:, 1:T], op0=ALU.mult, op1=ALU.add)
                                S.op("dve", "scalar_tensor_tensor", [xk, "convw", ck], [ck], out=cv[:, 0:T - 1], in0=xpre[:, 1:T], scalar=w2,
                                     in1=cv[:, 0:T - 1], op0=ALU.mult, op1=ALU.add)
                                xv = xpre[:, :].rearrange("p (s t) -> p s t", t=256)
                                cvv = cv[:, :].rearrange("p (s t) -> p s t", t=256)
                                S.op("dve", "scalar_tensor_tensor", [xk, "ncw", ck], [ck], out=cvv[:, 1:8, 0:1], in0=xv[:, 0:7, 255:256],
                                     scalar=ncw[:, 0, ct:ct + 1], in1=cvv[:, 1:8, 0:1], op0=ALU.mult, op1=ALU.add)
                                S.op("dve", "scalar_tensor_tensor", [xk, "ncw", ck], [ck], out=cvv[:, 0:7, 255:256], in0=xv[:, 1:8, 0:1],
                                     scalar=ncw[:, 1, ct:ct + 1], in1=cvv[:, 0:7, 255:256], op0=ALU.mult, op1=ALU.add)
                                yield
                                S.op("act", "activation", [ck], [yk], out=ysb[:], in_=cv[:], func=AF.Silu)
                                yield
                                if kind in ("q", "k"):
                                    dst, dk_ = (dqT, "dqT") if kind == "q" else (dkT, "dkT")
                                    epsn, sc = ("EPS128", 128.0) if kind == "q" else ("EPS", 1.0)
                                    for tg in range(4):
                                        ts_ = slice(tg * 512, (tg + 1) * 512)
                                        S.op("act", "activation", [yk], ["cvr" + q_], out=cvr[:], in_=ysb[:, ts_], func=AF.Square)
                                        yield
                                        ps, pk = PS()
                                        mm(ps[:, :], pk, onesR2[:], cvr[:], ["cvr" + q_, "onesR2"], True, True)
                                        S.op("act", "activation", [pk, "cst"], [rk_], out=rin[:], in_=ps[:, :], func=AF.Sqrt, scale=sc, bias=C(epsn))
                                        yield
                                        S.op("dve", "reciprocal", [rk_], [rk_], out=rin[:], in_=rin[:])
                                        S.op("dve", "tensor_tensor", [yk, rk_], [dk_], out=dst[:, ts_], in0=ysb[:, ts_], in1=rin[:], op=ALU.mult)
                                        yield
                                    if kind == "k":
                                        for g4 in range(4):
                                            for j in range(4):
                                                tt = g4 * 4 + j
                                                S.op("pe", "transpose", ["dkT", "identb"], ["psT"], inc=(j == 3), out=psT[:, j * 128:(j + 1) * 128],
                                                     in_=dkT[:, tt * 128:(tt + 1) * 128], identity=identb[:])
                                            S.op("act", "activation", ["psT"], ["ktok"], out=ktok[:, g4 * 4:(g4 + 1) * 4, :],
                                                 in_=psT[:, 0:512].rearrange("p (t d) -> p t d", t=4), func=AF.Copy)
                                            yield
                                else:
                                    c = kind
                                    for g4 in range(4):
                                        for j in range(4):
                                            tt = g4 * 4 + j
                                            S.op("pe", "transpose", [yk, "identb"], ["psT"], inc=(j == 3), out=psT[:, j * 128:(j + 1) * 128],
                                                 in_=ysb[:, tt * 128:(tt + 1) * 128], identity=identb[:])
                                        S.op("act", "activation", ["psT"], ["dvt"], out=vt[:, g4 * 4:(g4 + 1) * 4, c * 128:(c + 1) * 128],
                                             in_=psT[:, 0:512].rearrange("p (t d) -> p t d", t=4), func=AF.Copy)
                                        yield
                            return g
                        interleave([conv_job("q", h, 0), conv_job("k", 4 + h, 128), conv_job(0, 8 + 2 * h, 256), conv_job(1, 9 + 2 * h, 384)], WC,
                                   extras=pendingB)
                        pendingB = []
                        S.barrier()
                    nextW = load_w(BLK_D + h + 1) if h < 3 else load_w(BLK_DO)

                    with ExitStack() as sscan:
                        ring = {}
                        for r in range(2):
                            for i in range(RING):
                                ring[(r, i)] = dict(
                                    sT=SB(sscan, "r_sT%d%d" % (r, i), [128, 128], BF16),
                                    nwT=SB(sscan, "r_nwT%d%d" % (r, i), [128, 128], BF16),
                                    u=SB(sscan, "r_u%d%d" % (r, i), [128, 256], F32),
                                    qgT=SB(sscan, "r_qgT%d%d" % (r, i), [128, 128], BF16),
                                    khat=SB(sscan, "r_kh%d%d" % (r, i), [128, 128], BF16))
                        TS = []
                        for i in range(NSET):
                            TS.append(dict(
                                aTri=SB(sscan, "aTri%d" % i, [128, 128], F32R),
                                DD=SB(sscan, "DD%d" % i, [128, 256], F32),
                                Z=SB(sscan, "Z%d" % i, [128, 384], F32R),
                                Noff=SB(sscan, "Noff%d" % i, [128, 128], F32R),
                                bv=SB(sscan, "bv%d" % i, [128, 256], F32R), kbeg=SB(sscan, "kbeg%d" % i, [128, 128], F32R),
                                egrow=SB(sscan, "egrow%d" % i, [128, 128], F32)))
                        vn = [SB(sscan, "vn%d" % r, [128, 256], BF16) for r in range(2)]
                        identR = SB(sscan, "identR", [128, 128], F32R)
                        S.op("dve", "tensor_copy", ["cst"], ["identR"], out=identR[:], in_=C("I128"))
                        onesR = SB(sscan, "onesR", [128, 128], F32R)
                        negonesR = SB(sscan, "negonesR", [128, 128], F32R)
                        S.op("dve", "tensor_copy", ["cst"], ["onesR"], out=onesR[:], in_=C("ONES"))
                        S.op("dve", "tensor_copy", ["cst"], ["negonesR"], out=negonesR[:], in_=C("NEGONES128"))

                        def prep_gen(k, r, tsi):
                            tt = k if r == 0 else NT - 1 - k
                            c = 4 * r + h
                            X = TS[tsi]
                            q_ = "_%d" % tsi
                            slot = ring[(r, k % RING)]
                            sk = "ring%d%d" % (r, k % RING)
                            tsl = slice(tt * 128, (tt + 1) * 128)
                            tri = C("TRILO128") if r == 0 else C("TRIUP128")
                            aTri, DD, Z = X["aTri"], X["DD"], X["Z"]
                            DS = DD[:, 128:256]
                            bv, kbeg, egrow = X["bv"], X["kbeg"], X["egrow"]
                            S.op("act", "activation", ["la", "cst"], ["aTri" + q_], out=aTri[:], in_=tri, func=AF.Copy, scale=la[:, tt, c:c + 1])
                            S.op("act", "activation", ["ktok", "kh"], [sk + "khat"], out=slot["khat"][:], in_=ktok[:, tt, :], func=AF.Copy, scale=kh[:, tt, c:c + 1])
                            S.op("dve", "tensor_scalar", ["ktok", "bw"], ["kbeg" + q_], out=kbeg[:], in0=ktok[:, tt, :], scalar1=bw[:, tt, c:c + 1], scalar2=None, op0=ALU.mult)
                            S.op("act", "activation", ["dvt", "beta"], ["bv" + q_], out=bv[:], in_=vt[:, tt, :], func=AF.Copy, scale=beta[:, tt, c:c + 1])
                            yield
                            while nhold[0] >= 2:
                                yield
                            nhold[0] += 1
                            psQ, pkQ = PS()
                            held.add(int(pkQ[3:]))
                            psG, pkG = PS()
                            held.add(int(pkG[3:]))
                            mm(psQ[:, 0:128], pkQ, onesR[:], aTri[:], ["aTri" + q_, "onesR"], True, False)
                            mm(psQ[:, 0:128], pkQ, aTri[:], negonesR[:], ["aTri" + q_, "negonesR"], False, True, last=False)
                            mm(psQ[:, 128:256], pkQ, onesR[:], aTri[:], ["aTri" + q_, "onesR"], True, True)
                            mm(psG[:, 0:128], pkG, dkT[:, tsl], dkT[:, tsl], ["dkT"], True, True, last=False)
                            mm(psG[:, 128:256], pkG, dkT[:, tsl], dqT[:, tsl], ["dkT", "dqT"], True, True)
                            mask = C("MASKF128") if r == 0 else C("MASKB128")
                            smask = C("SMF128") if r == 0 else C("SMB128")
                            S.op("dve", "tensor_tensor", [pkQ, "cst"], ["DD" + q_], out=DD[:, 0:128], in0=psQ[:, 0:128], in1=mask[:, 0:128], op=ALU.add)
                            S.op("dve", "scalar_tensor_tensor", [pkQ, "cst"], ["DD" + q_], out=DD[:, 128:256], in0=psQ[:, 0:128], scalar=-1.0,
                                 in1=mask[:, 128:256], op0=ALU.mult, op1=ALU.add)
                            S.op("act", "activation", [pkQ], ["egrow" + q_], out=egrow[:], in_=psQ[:, 128:256], func=AF.Exp)
                            S.op("act", "activation", ["DD" + q_], ["DD" + q_], out=DD[:], in_=DD[:], func=AF.Exp)
                            held.discard(int(pkQ[3:]))
                            yield
                            S.op("dve", "tensor_tensor", ["DD" + q_, "cst"], ["DD" + q_], out=DS, in0=DD[:, 128:256], in1=smask[:, 128:256], op=ALU.mult)
                            S.op("dve", "tensor_tensor", [pkG, "DD" + q_], [sk + "sT"], out=slot["sT"][:], in0=psG[:, 128:256], in1=DD[:, 0:128], op=ALU.mult)
                            S.op("dve", "tensor_tensor", ["dqT", "egrow" + q_], [sk + "qgT"], out=slot["qgT"][:], in0=dqT[:, tsl], in1=egrow[:], op=ALU.mult)
                            kZ = "Z" + q_
                            Noff = X["Noff"][:]
                            S.op("dve", "scalar_tensor_tensor", [pkG, "beta", "DD" + q_], ["egrow" + q_], out=egrow[:], in0=psG[:, 0:128], scalar=beta[:, tt, c:c + 1],
                                 in1=DS, op0=ALU.mult, op1=ALU.mult)
                            S.op("dve", "tensor_tensor", ["egrow" + q_, "cst"], [kZ], out=Z[:, 256:384], in0=egrow[:], in1=C("BLKM"), op=ALU.mult)
                            S.op("dve", "tensor_tensor", ["egrow" + q_, kZ], ["Noff" + q_], out=Noff, in0=egrow[:], in1=Z[:, 256:384], op=ALU.subtract)
                            mm(psG[:, 256:384], pkG, Z[:, 256:384], identR[:], [kZ, "identR"], True, True)
                            S.op("act", "activation", [pkG], [kZ], out=Z[:, 0:128], in_=psG[:, 256:384], func=AF.Copy)
                            S.op("dve", "tensor_tensor", [kZ, "identR"], [kZ], out=Z[:, 128:256], in0=Z[:, 0:128], in1=identR[:], op=ALU.add)
                            held.discard(int(pkG[3:]))
                            nhold[0] -= 1
                            yield
                            for lev in range(1, 7):
                                psA, pkA = PS()
                                if lev == 1:
                                    mm(psA[:, 0:128], pkA, Z[:, 256:384], Z[:, 0:128], [kZ], True, True, last=False)
                                else:
                                    mm(psA[:, 0:256], pkA, Z[:, 256:384], Z[:, 0:256], [kZ], True, True, last=(lev == 6))
                                if lev < 6:
                                    mm(psA[:, 256:384], pkA, Z[:, 0:128], Z[:, 256:384], [kZ], True, True)
                                    S.op("act", "activation", [pkA], [kZ], out=Z[:, :].rearrange("p (b f) -> p b f", b=3)[:, 0:3:2, :],
                                         in_=psA[:, 0:384].rearrange("p (b f) -> p b f", b=3)[:, 0:3:2, :], func=AF.Copy)
                                if lev >= 2:
                                    S.op("dve", "tensor_tensor", [pkA, kZ], [kZ], out=Z[:, 128:256], in0=psA[:, 128:256], in1=Z[:, 128:256], op=ALU.add)
                                yield
                            psM, pkM = PS()
                            mm(psM[:, 0:128], pkM, Noff, Z[:, 128:256], ["Noff" + q_, kZ], True, True, last=False)
                            mm(psM[:, 256:384], pkM, Z[:, 128:256], identR[:], [kZ, "identR"], True, True)
                            S.op("act", "activation", [pkM], [kZ], out=Z[:, :].rearrange("p (b f) -> p b f", b=3)[:, 0:3:2, :],
                                 in_=psM[:, 0:384].rearrange("p (b f) -> p b f", b=3)[:, 0:3:2, :], func=AF.Copy)
                            yield
                            psM2, pkM2 = PS()
                            mm(psM2[:, 0:128], pkM2, Z[:, 256:384], Z[:, 0:128], [kZ], True, True)
                            S.op("dve", "tensor_tensor", [pkM2, kZ], [kZ], out=Z[:, 128:256], in0=psM2[:, 0:128], in1=Z[:, 128:256], op=ALU.add)
                            yield
                            TT_ = Z[:, 128:256]
                            tk = kZ
                            psU, pkU = PS()
                            mm(psU[:, 0:256], pkU, TT_, bv[:], [tk, "bv" + q_], True, True, last=False)
                            mm(psU[:, 256:384], pkU, kbeg[:], TT_, [tk, "kbeg" + q_], True, True)
                            S.op("dve", "tensor_copy", [pkU], [sk + "u"], out=slot["u"][:], in_=psU[:, 0:256])
                            S.op("act", "activation", [pkU], [sk + "nwT"], out=slot["nwT"][:], in_=psU[:, 256:384], func=AF.Copy, scale=-1.0)
                            yield

                        cur = [0, 0]
                        c16 = [0, 0]
                        ready = {}
                        done_tiles = [0, 0]
                        nhold = [0]

                        def chain_gen(r):
                            c = 4 * r + h
                            od = odf if r == 0 else odb
                            odk = "odf" if r == 0 else "odb"
                            vk = "vn%d" % r
                            for k in range(NT):
                                tt = k if r == 0 else NT - 1 - k
                                while not ready.get((k, r)):
                                    yield
                                slot = ring[(r, k % RING)]
                                sk = "ring%d%d" % (r, k % RING)
                                s32k = "S32_%d_%d" % (r, cur[r])
                                s16k = "S16_%d_%d" % (r, c16[r])
                                s16n = "S16_%d_%d" % (r, 1 - c16[r])
                                Scur = S16[r][c16[r]]
                                Snew = S16[r][1 - c16[r]]
                                psV, pkV = PS()
                                mm(psV[:, 0:256], pkV, slot["nwT"][:], Scur[:], [sk + "nwT", s16k], True, True)
                                S.op("dve", "tensor_tensor", [pkV, sk + "u"], [vk], out=vn[r][:], in0=psV[:, 0:256], in1=slot["u"][:], op=ALU.add)
                                yield
                                psO, pkO = PS()
                                psS, pkS = PS()
                                mm(psS[:, 0:256], pkS, slot["khat"][:], vn[r][:], [sk + "khat", vk], True, True)
                                mm(psO[:, 0:256], pkO, slot["qgT"][:], Scur[:], [sk + "qgT", s16k], True, False)
                                mm(psO[:, 0:256], pkO, slot["sT"][:], vn[r][:], [sk + "sT", vk], False, True)
                                bnd = (tt % 2 == 1) if r == 0 else (tt % 2 == 0)
                                S.op("dve", "scalar_tensor_tensor", [pkS, s32k, "EGL"], [s32k], out=S32[r][cur[r]][:], in0=S32[r][cur[r]][:],
                                     scalar=EGL[:, tt, c:c + 1], in1=psS[:, 0:256], op0=ALU.mult, op1=ALU.add)
                                if bnd:
                                    cur[r] = state_boundary(r, cur[r], tt // 2, h, sdn_o)
                                S.op("act", "activation", ["S32_%d_%d" % (r, cur[r])], [s16n], out=Snew[:], in_=S32[r][cur[r]][:], func=AF.Copy)
                                S.op("act", "activation", [pkO], [odk], out=od[:, tt, :], in_=psO[:, 0:256], func=AF.Copy)
                                c16[r] = 1 - c16[r]
                                yield
                                done_tiles[r] += 1

                        for r in range(2):
                            S.dma("sp", S32[r][0][:], sdn_d[r, h], [], ["S32_%d_0" % r])
                            S.op("act", "activation", ["S32_%d_0" % r], ["S16_%d_0" % r], out=S16[r][0][:], in_=S32[r][0][:], func=AF.Copy)
                        chains = [chain_gen(0), chain_gen(1)]
                        chain_alive = [True, True]
                        nextk = [0, 0]
                        free_sets = list(range(NSET))
                        active = []
                        turn = 0
                        while any(chain_alive):
                            launched = True
                            while free_sets and launched:
                                launched = False
                                for rr in ((0, 1) if turn % 2 == 0 else (1, 0)):
                                    if free_sets and nextk[rr] < NT and nextk[rr] < done_tiles[rr] + RING:
                                        tsi = free_sets.pop(0)
                                        active.append((prep_gen(nextk[rr], rr, tsi), nextk[rr], rr, tsi))
                                        nextk[rr] += 1
                                        launched = True
                                        turn += 1
                            def chain_turn():
                                for r in range(2):
                                    if chain_alive[r]:
                                        try:
                                            next(chains[r])
                                        except StopIteration:
                                            chain_alive[r] = False
                            chain_turn()
                            CT = 2
                            for ii, item in enumerate(list(active)):
                                if ii > 0 and ii % CT == 0:
                                    chain_turn()
                                g, k, rr, tsi = item
                                try:
                                    next(g)
                                except StopIteration:
                                    active.remove(item)
                                    free_sets.append(tsi)
                                    ready[(k, rr)] = True
                        S.barrier()
                    def delB(tt, h=h, W=W, wk=wk):
                        def g(bi):
                            o32_, t32_, zb_, sm_, zg_ = o32[bi], t32[bi], zb[bi], sm[bi], zg[bi]
                            q_ = "_%d" % bi
                            ps, pk = PS()
                            for kc in range(8):
                                mm(ps[:, 0:256], pk, hT[:, kc, tt * 128:(tt + 1) * 128], W[:, kc, 512:768], [wk, "hT"], kc == 0, kc == 7)
                            S.op("act", "activation", [pk], ["zg" + q_], out=zg_[:], in_=ps[:, 0:256], func=AF.Silu)
                            S.op("dve", "tensor_tensor", ["odf", "odb"], ["do32" + q_], out=o32_[:], in0=odf[:, tt, :], in1=odb[:, tt, :], op=ALU.add)
                            yield
                            S.op("act", "activation", ["do32" + q_], ["dt32" + q_, "dsm0" + q_], out=t32_[:], in_=o32_[:], func=AF.Square, accum_out=sm_[:, 0:1])
                            yield
                            S.op("act", "activation", ["dsm0" + q_, "cst"], ["dsm1" + q_], out=sm_[:, 1:2], in_=sm_[:, 0:1], func=AF.Sqrt, scale=1.0 / 256, bias=C("EPS"))
                            yield
                            S.op("dve", "reciprocal", ["dsm1" + q_], ["dsm2" + q_], out=sm_[:, 2:3], in_=sm_[:, 1:2])
                            S.op("dve", "scalar_tensor_tensor", ["do32" + q_, "dsm2" + q_, "dnw"], ["dt32" + q_], out=t32_[:], in0=o32_[:], scalar=sm_[:, 2:3],
                                 in1=dnw[:], op0=ALU.mult, op1=ALU.mult)
                            S.op("dve", "tensor_tensor", ["dt32" + q_, "zg" + q_], ["dzb" + q_], out=zb_[:], in0=t32_[:], in1=zg_[:], op=ALU.mult)
                            yield
                            for c2 in range(2):
                                S.op("pe", "transpose", ["dzb" + q_, "identb"], ["psT"], inc=(c2 == 1), out=psT[:, c2 * 128:(c2 + 1) * 128],
                                     in_=zb_[:, c2 * 128:(c2 + 1) * 128], identity=identb[:])
                            S.op("act", "activation", ["psT"], ["zT"], out=zT[:, 2 * h:2 * h + 2, tt * 128:(tt + 1) * 128],
                                 in_=psT[:, 0:256].rearrange("p (c t) -> p c t", c=2), func=AF.Copy)
                            yield
                        return g
                    def laneB(tts, bi, delB=delB):
                        for tt in tts:
                            yield from delB(tt)(bi)
                    lanes = [laneB(list(range(bi, NT, WBD)), bi) for bi in range(WBD)]
                    if h < 3:
                        pendingB = lanes
                    else:
                        interleave([], 1, extras=lanes)
                S.barrier()

            STOP(4)
            mT = SB(sB, "mT", [128, 8, T], BF16)
            out_gate(BLK_DO, BLK_GD, False, mT, pre=nextW)

            with ExitStack() as s4:
                W4 = 4
                xb = [SB(s4, "x4b%d" % i, [128, D], F32) for i in range(W4)]
                x1 = xb
                tmp32 = [SB(s4, "tmp32b%d" % i, [128, D], F32) for i in range(W4)]
                hb = [SB(s4, "hb4_%d" % i, [128, D], BF16) for i in range(W4)]
                small = [SB(s4, "small4_%d" % i, [128, 8], F32) for i in range(W4)]
                small_n = [SB(s4, "small_n%d" % i, [128, 4], F32) for i in range(W4)]
                junk4 = hb
                modb4 = SB(s4, "modb4", [128, 3 * D], F32)
                S.dma("sp", modb4[:], modscr_d[0, 0:3 * D].partition_broadcast(128), [], ["modb"])
                G1 = modb4[:, 0:1024]
                B2 = modb4[:, 1024:2048]
                A2 = modb4[:, 2048:3072]
                Wo, wko = load_w(BLK_OUT)
                preFF = load_w(BLK_FF)

                def st4(tt):
                    def g(bi):
                        q_ = "_%d" % bi
                        xk = "x4b" + q_
                        x1k = xk
                        tk = "tmp4" + q_
                        sm_ = small[bi]
                        tsl = slice(tt * 128, (tt + 1) * 128)
                        S.dma("sp", xb[bi][:], x_d[tsl, :], [], [xk])
                        pa, pka = PS()
                        pb_, pkb = PS()
                        for kc in range(8):
                            mm(pa[:, :], pka, mT[:, kc, tsl], Wo[:, kc, 0:512], ["mT", wko], kc == 0, kc == 7)
                        for kc in range(8):
                            mm(pb_[:, :], pkb, mT[:, kc, tsl], Wo[:, kc, 512:1024], ["mT", wko], kc == 0, kc == 7)
                        S.op("act", "activation", [pka], ["hb4" + q_, "sa" + q_], out=junk4[bi][:, 0:512], in_=pa[:, :], func=AF.Square, accum_out=sm_[:, 0:1])
                        S.op("dve", "tensor_copy", [pka], [tk], out=tmp32[bi][:, 0:512], in_=pa[:, :])
                        S.op("act", "activation", [pkb], ["hb4" + q_, "sb" + q_], out=junk4[bi][:, 512:1024], in_=pb_[:, :], func=AF.Square, accum_out=sm_[:, 1:2])
                        S.op("dve", "tensor_copy", [pkb], [tk], out=tmp32[bi][:, 512:1024], in_=pb_[:, :])
                        yield
                        S.op("dve", "tensor_tensor", ["sa" + q_, "sb" + q_], ["sc" + q_], out=sm_[:, 2:3], in0=sm_[:, 0:1], in1=sm_[:, 1:2], op=ALU.add)
                        yield
                        S.op("act", "activation", ["sc" + q_, "cst"], ["sd" + q_], out=sm_[:, 3:4], in_=sm_[:, 2:3], func=AF.Sqrt, scale=1.0 / D, bias=C("EPS"))
                        yield
                        S.op("dve", "reciprocal", ["sd" + q_], ["se" + q_], out=sm_[:, 4:5], in_=sm_[:, 3:4])
                        S.op("dve", "scalar_tensor_tensor", ["se" + q_, "modb", tk], [tk], out=tmp32[bi][:], in0=tmp32[bi][:], scalar=sm_[:, 4:5],
                             in1=G1, op0=ALU.mult, op1=ALU.mult)
                        yield
                        S.op("dve", "tensor_tensor", [tk, xk], [x1k], out=x1[bi][:], in0=tmp32[bi][:], in1=xb[bi][:], op=ALU.add)
                        S.dma("sp", y_d[tsl, :], x1[bi][:], [x1k], ["y%d" % tt])
                        yield
                        yield from norm_to_hT_gen(x1[bi][:], x1k, tt, A2, B2, tmp32[bi], tk, hb[bi], "hb4" + q_, small_n[bi], "sn" + q_)
                    return g
                interleave([st4(tt) for tt in range(NT)], W4)
                S.barrier()

        STOP(6)
        with ExitStack() as s5:
            wffo = SB(s5, "wffo", [128, NFF, D], BF16)
            aT = SB(s5, "aT", [128, NFF, 1024], BF16)
            sg = SB(s5, "sg", [128, 512], F32)
            x1b = [SB(s5, "x5b%d" % i, [128, D], F32) for i in range(2)]
            yb = [SB(s5, "y5b%d" % i, [128, D], F32) for i in range(2)]
            tmp32 = SB(s5, "tmp32c", [128, D], F32)
            small = SB(s5, "small5", [128, 8], F32)
            G2 = SB(s5, "modb5", [128, D], F32)
            S.dma("sp", G2[:], modscr_d[0, 3 * D:4 * D].partition_broadcast(128), [], ["modb"])
            for th in range(2):
                for b in range(6):
                    W, wk = preFF if (th == 0 and b == 0) else load_w(BLK_FF + b)
                    if th == 0 and b == 2:
                        S.dma("pool", wffo[:], wffo_d.rearrange("(j p) c -> p j c", p=128), [], ["wffo"])
                    nj = 4 if b < 5 else 2
                    for jj in range(nj):
                        j = b * 4 + jj
                        for tg in range(2):
                            ts_ = slice(th * 1024 + tg * 512, th * 1024 + (tg + 1) * 512)
                            pg, pkg = PS()
                            pu, pku = PS()
                            for kc in range(8):
                                mm(pg[:, :], pkg, W[:, kc, jj * 128:(jj + 1) * 128], hT[:, kc, ts_], [wk, "hT"], kc == 0, kc == 7)
                            for kc in range(8):
                                mm(pu[:, :], pku, W[:, kc, 512 + jj * 128:512 + (jj + 1) * 128], hT[:, kc, ts_], [wk, "hT"], kc == 0, kc == 7)
                            S.op("act", "activation", [pkg], ["sg"], out=sg[:], in_=pg[:, :], func=AF.Silu)
                            S.op("dve", "tensor_tensor", [pku, "sg"], ["aT"], out=aT[:, j, tg * 512:(tg + 1) * 512], in0=pu[:, :], in1=sg[:], op=ALU.mult)
                for t8 in range(8):
                    tt = th * 8 + t8
                    tsl = slice(tt * 128, (tt + 1) * 128)
                    xk = "x5b%d" % (tt % 2)
                    yk = "y5b%d" % (tt % 2)
                    S.dma("sp", x1b[tt % 2][:], y_d[tsl, :], ["y%d" % tt], [xk])
                    pa, pka = PS()
                    pb_, pkb = PS()
                    for j in range(NFF):
                        mm(pa[:, :], pka, aT[:, j, t8 * 128:(t8 + 1) * 128], wffo[:, j, 0:512], ["aT", "wffo"], j == 0, j == NFF - 1)
                    for j in range(NFF):
                        mm(pb_[:, :], pkb, aT[:, j, t8 * 128:(t8 + 1) * 128], wffo[:, j, 512:1024], ["aT", "wffo"], j == 0, j == NFF - 1)
                    S.op("act", "activation", [pka], ["junk5", "sm5a"], out=tmp32[:, 0:512], in_=pa[:, :], func=AF.Square, accum_out=small[:, 0:1])
                    S.op("act", "activation", [pkb], ["junk5", "sm5b"], out=tmp32[:, 512:1024], in_=pb_[:, :], func=AF.Square, accum_out=small[:, 1:2])
                    S.op("dve", "tensor_tensor", ["sm5a", "sm5b"], ["sm5c"], out=small[:, 2:3], in0=small[:, 0:1], in1=small[:, 1:2], op=ALU.add)
                    S.op("act", "activation", ["sm5c", "cst"], ["sm5d"], out=small[:, 3:4], in_=small[:, 2:3], func=AF.Sqrt, scale=1.0 / D, bias=C("EPS"))
                    S.op("dve", "reciprocal", ["sm5d"], ["sm5e"], out=small[:, 4:5], in_=small[:, 3:4])
                    S.op("dve", "scalar_tensor_tensor", [pka, "sm5e", "modb", "junk5"], ["junk5"], out=tmp32[:, 0:512], in0=pa[:, :], scalar=small[:, 4:5],
                         in1=G2[:, 0:512], op0=ALU.mult, op1=ALU.mult)
                    S.op("dve", "scalar_tensor_tensor", [pkb, "sm5e", "modb", "junk5"], ["junk5"], out=tmp32[:, 512:1024], in0=pb_[:, :], scalar=small[:, 4:5],
                         in1=G2[:, 512:1024], op0=ALU.mult, op1=ALU.mult)
                    S.op("dve", "tensor_tensor", ["junk5", xk], [yk], out=yb[tt % 2][:], in0=tmp32[:], in1=x1b[tt % 2][:], op=ALU.add)
                    S.dma("sp", y_d[tsl, :], yb[tt % 2][:], [yk], ["y%d" % tt])
            S.barrier()
        print("INSTR", dict(S.count), "dma", S.di)


_PROG = {}
_RETURN_MAPS = [False]


def _rope_tables():
    L = 2048
    rows = L // 64
    row_idx = np.repeat(np.arange(rows, dtype=np.float32), 64)
    col_idx = (np.arange(L) % 64).astype(np.float32)
    n_freq = 32
    freqs = (np.float32(10000.0) ** (-np.arange(n_freq, dtype=np.float32) / np.float32(n_freq))).astype(np.float32)
    ang = np.concatenate([row_idx[:, None] * freqs, col_idx[:, None] * freqs], axis=-1).astype(np.float32)
    cos = np.cos(ang).astype(np.float32)
    sin = np.sin(ang).astype(np.float32)
    d = np.arange(128)
    c = cos[:, d // 2].T.copy()
    s = sin[:, d // 2].T.copy()
    s[0::2, :] *= -1.0
    return np.ascontiguousarray(c), np.ascontiguousarray(s)


def _pack_weights(w_in, w_ret_o, w_dn_o, w_out, w_ffn_in):
    blk = np.zeros((NBLK, D, D), np.float32)
    swap = np.arange(128) ^ 1
    for h in range(4):
        q = w_in[:, h * 128:(h + 1) * 128]
        k = w_in[:, 512 + h * 128:512 + (h + 1) * 128]
        blk[BLK_R + h, :, 0:128] = q
        blk[BLK_R + h, :, 128:256] = q[:, swap]
        blk[BLK_R + h, :, 256:384] = k
        blk[BLK_R + h, :, 384:512] = k[:, swap]
        blk[BLK_R + h, :, 512:768] = w_in[:, 1024 + h * 256:1024 + (h + 1) * 256]
        blk[BLK_R + h, :, 768:1024] = w_in[:, 2048 + h * 256:2048 + (h + 1) * 256]
        blk[BLK_D + h, :, 0:128] = w_in[:, 3072 + h * 128:3072 + (h + 1) * 128]
        blk[BLK_D + h, :, 128:256] = w_in[:, 3584 + h * 128:3584 + (h + 1) * 128]
        blk[BLK_D + h, :, 256:512] = w_in[:, 4096 + h * 256:4096 + (h + 1) * 256]
        blk[BLK_D + h, :, 512:768] = w_in[:, 5120 + h * 256:5120 + (h + 1) * 256]
    blk[BLK_GR] = w_in[:, 6160:7184]
    blk[BLK_GD] = w_in[:, 7184:8208]
    blk[BLK_RO] = w_ret_o
    blk[BLK_DO] = w_dn_o
    blk[BLK_OUT] = w_out
    for b in range(6):
        n = 512 if b < 5 else 256
        blk[BLK_FF + b, :, 0:n] = w_ffn_in[:, b * 512:b * 512 + n]
        blk[BLK_FF + b, :, 512:512 + n] = w_ffn_in[:, DFF + b * 512:DFF + b * 512 + n]
    w_ba = np.ascontiguousarray(w_in[:, 6144:6160])
    return blk, w_ba


def kernel(x_prompt, x_sample, c, state_ret, state_dn, c_ctx, w_mod, b_mod, norm_w, w_in, conv_w, ret_decay_logit,
           ret_gn_w, dn_a_log, dn_dt_bias, dn_norm_w, w_ret_o, w_dn_o, w_out, w_ffn_in, w_ffn_out):
    f = lambda a: np.ascontiguousarray(np.asarray(a, dtype=np.float32))
    x_prompt, x_sample, c, state_ret, state_dn, c_ctx = map(f, (x_prompt, x_sample, c, state_ret, state_dn, c_ctx))
    w_in0 = f(w_in)[0]
    blk, w_ba = _pack_weights(w_in0, f(w_ret_o)[0], f(w_dn_o)[0], f(w_out)[0], f(w_ffn_in)[0])
    ropec, ropes = _rope_tables()
    ones_c = np.ones((128, T), np.float32)
    zeros_c = np.zeros((128, T), np.float32)
    shared = {
        "w_mod": f(w_mod)[0], "b_mod": f(b_mod)[0], "norm_w": f(norm_w)[0].reshape(-1), "wblk": blk, "w_ba": w_ba,
        "w_ffo": f(w_ffn_out)[0], "conv_w": f(conv_w)[0], "ret_decay_logit": f(ret_decay_logit)[0].reshape(-1),
        "ret_gn_w": f(ret_gn_w)[0], "dn_a_log": f(dn_a_log)[0].reshape(-1), "dn_dt_bias": f(dn_dt_bias)[0].reshape(-1),
        "dn_norm_w": f(dn_norm_w)[0], "cst": _build_cst(), "ident": np.eye(128, dtype=np.float32),
    }
    in_maps = []
    for core in range(8):
        m = dict(shared)
        if core < 4:
            m["x"] = x_sample[core]
            m["cond"] = c[core]
            m["st_ret"] = np.ascontiguousarray(state_ret[core, 0])
            m["st_dn"] = np.ascontiguousarray(state_dn[core, 0])
            m["rope_c"] = ropec
            m["rope_s"] = ropes
            m["flag"] = np.ones((128, 1), np.float32)
        else:
            j = core - 4
            xs = np.zeros((T, D), np.float32)
            xs[0:1024] = x_prompt[4 * j:4 * j + 4].reshape(1024, D)
            xs[1024:2048] = xs[0:1024]
            m["x"] = xs
            m["cond"] = c_ctx
            m["st_ret"] = np.zeros((2, 4, 128, 256), np.float32)
            m["st_dn"] = np.zeros((2, 4, 128, 256), np.float32)
            m["rope_c"] = ones_c
            m["rope_s"] = zeros_c
            m["flag"] = np.zeros((128, 1), np.float32)
        in_maps.append(m)
    if _RETURN_MAPS[0]:
        return in_maps
    if "nc" not in _PROG:
        _PROG["nc"] = build_program()
    res = run_bass_kernel_spmd(_PROG["nc"], in_maps, core_ids=list(range(8)))
    rr = res.results
    y_sample = np.stack([rr[i]["y"] for i in range(4)], 0)
    y_prompt = np.concatenate([rr[4 + j]["y"][0:1024].reshape(4, 256, D) for j in range(4)], 0)
    new_ret = np.concatenate([rr[4 + j]["sret_o"] for j in range(4)], 0)[:, None]
    new_dn = np.concatenate([rr[4 + j]["sdn_o"] for j in range(4)], 0)[:, None]
    return (y_prompt.astype(np.float32), y_sample.astype(np.float32), new_ret.astype(np.float32), new_dn.astype(np.float32))
```
